# Optimizing a Trainium2 kernel written in Bass

```python
import math
import jax, jax.numpy as jnp
from jax import lax
import numpy as np

D_MODEL = 1024
BATCH = 32
SEQ = 2048
DEPTH = 1
DEC_BATCH = 8
DEC_SEQ = 16
PAST_LEN = 2048

CHUNK = 64
Q_BLOCK = 128
N_MEM = 256
H_A = 4
DK_A = 64
DV_A = 2 * DK_A
W_A = H_A * DV_A
H_B = 4
DH_B = 64
W_B = H_B * DH_B
H_M = 4
DH_M = 64
W_M = H_M * DH_M
D_MIX = W_A + W_B + W_M
EPS = 1e-6
ALIBI_SLOPES = tuple(2.0 ** (-8.0 * (h + 1) / H_A) for h in range(H_A))
IN_SIZES = (W_A, W_A, W_A, W_A, W_B, W_B, W_B, W_B, H_B, H_B, W_B, W_M, W_M)
N_IN = 4 * W_A + 5 * W_B + 2 * H_B + 2 * W_M

kernel_name = "hymba_diffattn_mlstm_stream_step"

F32 = jnp.float32


def rms_norm(x, g):
    xf = x.astype(F32)
    y = xf * lax.rsqrt(jnp.mean(xf * xf, axis=-1, keepdims=True) + EPS)
    return (y * g.astype(F32)).astype(x.dtype)


def split_cols(z):
    out, start = [], 0
    for size in IN_SIZES:
        out.append(z[..., start:start + size])
        start += size
    return out


def diff_attention(q, k, v, q_pos, k_pos, lam):
    slopes = jnp.asarray(ALIBI_SLOPES, F32)
    scale = DK_A ** -0.5
    k_chunk = k_pos // CHUNK

    def block(args):
        qb, qp = args
        s = jnp.einsum('bqhmd,bkhmd->bhmqk', qb, k, preferred_element_type=F32) * scale
        dist = jnp.abs(qp[:, None] - k_pos[None, :]).astype(F32)
        bias = -slopes[:, None, None, None] * dist
        allowed = k_chunk[None, :] <= (qp // CHUNK)[:, None]
        s = jnp.where(allowed, s + bias, -jnp.inf)
        p = jax.nn.softmax(s, axis=-1)
        w = p[:, :, 0] - lam * p[:, :, 1]
        return jnp.einsum('bhqk,bkhd->bqhd', w.astype(v.dtype), v)

    B, Sq = q.shape[0], q.shape[1]
    if Sq <= Q_BLOCK:
        return block((q, q_pos))
    nb = Sq // Q_BLOCK
    qs = q.reshape((B, nb, Q_BLOCK) + q.shape[2:]).swapaxes(0, 1)
    ps = q_pos.reshape(nb, Q_BLOCK)
    out = lax.map(block, (qs, ps))
    return out.swapaxes(0, 1).reshape(B, Sq, H_A, DV_A)


def mlstm_chunk(carry, inp):
    C0, n0, m0 = carry
    q, k, v, ig, lf = inp
    L = q.shape[2]
    b = jnp.cumsum(lf, axis=-1)
    causal = jnp.tril(jnp.ones((L, L), dtype=bool))
    D = jnp.where(causal, b[..., :, None] - b[..., None, :] + ig[..., None, :], -jnp.inf)
    g = b + m0[..., None]
    m = jnp.maximum(g, jnp.max(D, axis=-1))
    Dw = jnp.exp(D - m[..., None])
    gw = jnp.exp(g - m)
    A = jnp.einsum('bhtd,bhsd->bhts', q, k) * Dw
    num = gw[..., None] * jnp.einsum('bhtd,bhde->bhte', q, C0) + jnp.einsum('bhts,bhse->bhte', A, v)
    den = gw * jnp.einsum('bhtd,bhd->bht', q, n0) + jnp.sum(A, axis=-1)
    h = num / jnp.maximum(jnp.abs(den), jnp.exp(-m))[..., None]
    mL = m[..., -1]
    wS = jnp.exp(b[..., -1:] - b + ig - mL[..., None])
    decay = jnp.exp(b[..., -1] + m0 - mL)
    C = decay[..., None, None] * C0 + jnp.einsum('bhs,bhsd,bhse->bhde', wS, k, v)
    n = decay[..., None] * n0 + jnp.einsum('bhs,bhsd->bhd', wS, k)
    return (C, n, mL), h


def mlstm_run(q, k, v, ig, lf, state):
    B, S = q.shape[0], q.shape[1]
    q, k, v, ig, lf = [jnp.moveaxis(a.astype(F32), 1, 2) for a in (q, k, v, ig, lf)]
    state = tuple(s.astype(F32) for s in state)
    if S <= CHUNK:
        state, h = mlstm_chunk(state, (q, k, v, ig, lf))
    else:
        nc = S // CHUNK

        def chunks(a):
            a = a.reshape(a.shape[:2] + (nc, CHUNK) + a.shape[3:])
            return jnp.moveaxis(a, 2, 0)

        state, h = lax.scan(mlstm_chunk, state, tuple(chunks(a) for a in (q, k, v, ig, lf)))
        h = jnp.moveaxis(h, 0, 2).reshape(B, H_B, S, DH_B)
    return jnp.moveaxis(h, 2, 1), state


def memory_kv(mem, g_mem, w_mk, w_mv, g_km):
    B, N, _ = mem.shape
    hm = rms_norm(mem, g_mem)
    mk = rms_norm(jnp.einsum('bnd,de->bne', hm, w_mk).reshape(B, N, H_M, DH_M), g_km)
    mv = jnp.einsum('bnd,de->bne', hm, w_mv).reshape(B, N, H_M, DH_M)
    return mk, mv


def mixer_layer(x, q_pos, past_k, past_v, mlstm_state, mem_k, mem_v, lam_init,
                g_norm, w_in, w_out, g_qa, g_ka, lam_q1, lam_k1, lam_q2, lam_k2, g_subln,
                b_i, b_f, g_mh, g_qm):
    B, S, _ = x.shape
    h = rms_norm(x, g_norm)
    z = jnp.einsum('bsd,de->bse', h, w_in)
    qa, ka, va, ga, qb, kb, vb, ob, ib, fb, gb, qm, gm = split_cols(z)

    qa = rms_norm(qa.reshape(B, S, H_A, 2, DK_A), g_qa)
    ka = rms_norm(ka.reshape(B, S, H_A, 2, DK_A), g_ka)
    va = va.reshape(B, S, H_A, DV_A)
    new_k = ka.reshape(B, S, H_A, 2 * DK_A)
    if past_k is None:
        keys, vals, k_pos = ka, va, q_pos
    else:
        P = past_k.shape[1]
        keys = jnp.concatenate([past_k.reshape(B, P, H_A, 2, DK_A).astype(ka.dtype), ka], axis=1)
        vals = jnp.concatenate([past_v.astype(va.dtype), va], axis=1)
        k_pos = jnp.arange(P + S, dtype=jnp.int32)
    lam = (jnp.exp(jnp.sum(lam_q1.astype(F32) * lam_k1.astype(F32)))
           - jnp.exp(jnp.sum(lam_q2.astype(F32) * lam_k2.astype(F32))) + lam_init)
    oa = diff_attention(qa, keys, vals, q_pos, k_pos, lam)
    oa = rms_norm(oa, g_subln) * (1.0 - lam_init)
    oa = oa.reshape(B, S, W_A) * jax.nn.silu(ga)

    qb = qb.reshape(B, S, H_B, DH_B)
    kb = kb.reshape(B, S, H_B, DH_B) * (DH_B ** -0.5)
    vb = vb.reshape(B, S, H_B, DH_B)
    ig = (ib + b_i).astype(F32)
    lf = jax.nn.log_sigmoid((fb + b_f).astype(F32))
    hb, new_state = mlstm_run(qb, kb, vb, ig, lf, mlstm_state)
    hb = rms_norm(hb.astype(x.dtype), g_mh) * jax.nn.sigmoid(ob.reshape(B, S, H_B, DH_B))
    hb = hb.reshape(B, S, W_B) * jax.nn.silu(gb)

    qm = rms_norm(qm.reshape(B, S, H_M, DH_M), g_qm)
    sm = jnp.einsum('bshd,bnhd->bhsn', qm, mem_k.astype(qm.dtype), preferred_element_type=F32) * (DH_M ** -0.5)
    pm = jax.nn.softmax(sm, axis=-1)
    om = jnp.einsum('bhsn,bnhd->bshd', pm.astype(x.dtype), mem_v.astype(x.dtype)).reshape(B, S, W_M)
    om = om * jax.nn.silu(gm)

    mix = jnp.concatenate([oa, hb, om], axis=-1)
    y = x + jnp.einsum('bse,ed->bsd', mix, w_out)
    return y, new_k, va, new_state


def setup_inputs(seed: int = 0) -> dict:
    key = jax.random.key(seed)
    ks = iter(jax.random.split(key, 40))

    def nrm(shape, s=1.0):
        return s * jax.random.normal(next(ks), shape, F32)

    def gain(shape):
        return 1.0 + 0.05 * jax.random.normal(next(ks), shape, F32)

    return {
        "x_prompt": nrm((BATCH, SEQ, D_MODEL)),
        "x_sample": nrm((DEC_BATCH, DEC_SEQ, D_MODEL)),
        "cache_attn_k": nrm((DEPTH, DEC_BATCH, PAST_LEN, H_A, 2 * DK_A)),
        "cache_attn_v": nrm((DEPTH, DEC_BATCH, PAST_LEN, H_A, DV_A)),
        "state_mlstm_C": nrm((DEPTH, DEC_BATCH, H_B, DH_B, DH_B), 0.1),
        "state_mlstm_n": nrm((DEPTH, DEC_BATCH, H_B, DH_B), 0.1),
        "state_mlstm_m": nrm((DEPTH, DEC_BATCH, H_B), 0.5),
        "cache_mem_k": nrm((DEPTH, DEC_BATCH, N_MEM, H_M, DH_M)),
        "cache_mem_v": nrm((DEPTH, DEC_BATCH, N_MEM, H_M, DH_M)),
        "mem_prompt": nrm((BATCH, N_MEM, D_MODEL)),
        "g_norm": gain((DEPTH, D_MODEL)),
        "w_in": nrm((DEPTH, D_MODEL, N_IN), D_MODEL ** -0.5),
        "w_out": nrm((DEPTH, D_MIX, D_MODEL), D_MIX ** -0.5),
        "g_qa": gain((DEPTH, DK_A)),
        "g_ka": gain((DEPTH, DK_A)),
        "lam_q1": nrm((DEPTH, DK_A), 0.1),
        "lam_k1": nrm((DEPTH, DK_A), 0.1),
        "lam_q2": nrm((DEPTH, DK_A), 0.1),
        "lam_k2": nrm((DEPTH, DK_A), 0.1),
        "g_subln": gain((DEPTH, DV_A)),
        "b_i": nrm((DEPTH, H_B), 0.1),
        "b_f": jnp.linspace(3.0, 6.0, H_B, dtype=F32)[None, :] + nrm((DEPTH, H_B), 0.1),
        "g_mh": gain((DEPTH, DH_B)),
        "g_qm": gain((DEPTH, DH_M)),
        "g_km": gain((DEPTH, DH_M)),
        "g_mem": gain((DEPTH, D_MODEL)),
        "w_mk": nrm((DEPTH, D_MODEL, W_M), D_MODEL ** -0.5),
        "w_mv": nrm((DEPTH, D_MODEL, W_M), D_MODEL ** -0.5),
    }


def reference(x_prompt, x_sample, cache_attn_k, cache_attn_v, state_mlstm_C, state_mlstm_n,
              state_mlstm_m, cache_mem_k, cache_mem_v, mem_prompt, g_norm, w_in, w_out, g_qa,
              g_ka, lam_q1, lam_k1, lam_q2, lam_k2, g_subln, b_i, b_f, g_mh, g_qm, g_km, g_mem,
              w_mk, w_mv):
    Bp, Sp = x_prompt.shape[0], x_prompt.shape[1]
    Ss = x_sample.shape[1]
    P = cache_attn_k.shape[2]
    pos_p = jnp.arange(Sp, dtype=jnp.int32)
    pos_s = P + jnp.arange(Ss, dtype=jnp.int32)
    xp, xs = x_prompt, x_sample
    pk, pv, pC, pn, pm, pmk, pmv = [], [], [], [], [], [], []
    sk, sv, sC, sn, sm = [], [], [], [], []
    for l in range(DEPTH):
        lam_init = 0.8 - 0.6 * math.exp(-0.3 * l)
        w = (g_norm[l], w_in[l], w_out[l], g_qa[l], g_ka[l], lam_q1[l], lam_k1[l], lam_q2[l],
             lam_k2[l], g_subln[l], b_i[l], b_f[l], g_mh[l], g_qm[l])
        mk, mv = memory_kv(mem_prompt, g_mem[l], w_mk[l], w_mv[l], g_km[l])
        zero_state = (jnp.zeros((Bp, H_B, DH_B, DH_B), F32), jnp.zeros((Bp, H_B, DH_B), F32),
                      jnp.zeros((Bp, H_B), F32))
        xp, k_p, v_p, st_p = mixer_layer(xp, pos_p, None, None, zero_state, mk, mv, lam_init, *w)
        xs, k_s, v_s, st_s = mixer_layer(
            xs, pos_s, cache_attn_k[l], cache_attn_v[l],
            (state_mlstm_C[l], state_mlstm_n[l], state_mlstm_m[l]),
            cache_mem_k[l], cache_mem_v[l], lam_init, *w)
        pk.append(k_p); pv.append(v_p)
        pC.append(st_p[0].astype(x_prompt.dtype)); pn.append(st_p[1].astype(x_prompt.dtype))
        pm.append(st_p[2].astype(x_prompt.dtype))
        pmk.append(mk); pmv.append(mv)
        sk.append(k_s); sv.append(v_s)
        sC.append(st_s[0].astype(state_mlstm_C.dtype)); sn.append(st_s[1].astype(state_mlstm_n.dtype))
        sm.append(st_s[2].astype(state_mlstm_m.dtype))
    return (xp, xs, jnp.stack(pk), jnp.stack(pv), jnp.stack(pC), jnp.stack(pn), jnp.stack(pm),
            jnp.stack(pmk), jnp.stack(pmv), jnp.stack(sk), jnp.stack(sv), jnp.stack(sC),
            jnp.stack(sn), jnp.stack(sm))
```

```python
import math
from contextlib import ExitStack

import numpy as np
import concourse.bass as bass
import concourse.mybir as mybir
from concourse.bass_utils import run_bass_kernel_spmd

F32 = mybir.dt.float32
BF16 = mybir.dt.bfloat16
AF = mybir.ActivationFunctionType
ALU = mybir.AluOpType
AX = mybir.AxisListType

N_DMA_SEMS = 40
EPS = 1e-6
SLOPES = [2.0 ** (-8.0 * (h + 1) / 4) for h in range(4)]
LAM_INIT = 0.8 - 0.6 * math.exp(0.0)
NIN = 3848
BIG = 240000.0


class _Op:
    __slots__ = ("eng", "fn", "deps", "has_dep", "is_dma", "semkey", "val", "ie")


class Sched:
    def __init__(self, nc):
        self.nc = nc
        self.ops = []
        self.lw = {}
        self.rd = {}
        self.eng_n = {"pe": 0, "act": 0, "dve": 0, "pool": 0, "sp": 0}

    def add(self, eng, fn, reads=(), writes=(), dma=False):
        i = len(self.ops)
        deps = set()
        for k in reads:
            w = self.lw.get(k)
            if w is not None:
                deps.add(w)
            if isinstance(k, tuple) and k[0] == "pb":
                for r in self.rd.get(k, ()):
                    if self.ops[r].eng != eng:
                        deps.add(r)
        for k in writes:
            w = self.lw.get(k)
            if w is not None:
                deps.add(w)
            for r in self.rd.get(k, ()):
                deps.add(r)
        op = _Op()
        op.eng = eng
        op.fn = fn
        op.is_dma = dma
        op.has_dep = False
        op.semkey = None
        op.val = 0
        op.ie = self.eng_n[eng]
        self.eng_n[eng] += 1
        keep = []
        for d in deps:
            p = self.ops[d]
            if p.eng == eng and not p.is_dma:
                if eng == "pe" or eng == "sp":
                    continue
                if op.ie - p.ie > 2:
                    continue
            p.has_dep = True
            keep.append(d)
        op.deps = keep
        for k in reads:
            self.rd.setdefault(k, []).append(i)
        for k in writes:
            self.lw[k] = i
            self.rd[k] = []
        self.ops.append(op)
        return i

    def emit(self, stack):
        nc = self.nc
        engobj = {"pe": nc.tensor, "act": nc.scalar, "dve": nc.vector,
                  "pool": nc.gpsimd, "sp": nc.sync}
        esem = {e: stack.enter_context(nc.semaphore("s_" + e))
                for e in ("pe", "act", "dve", "pool")}
        dsem = [stack.enter_context(nc.semaphore("d%d" % i)) for i in range(N_DMA_SEMS)]
        waited = {e: {} for e in engobj}
        cnt = {e: 0 for e in engobj}
        dcnt = [0] * N_DMA_SEMS
        rr = 0
        rrp = 0
        for op in self.ops:
            E = engobj[op.eng]
            need = {}
            for d in op.deps:
                p = self.ops[d]
                if need.get(p.semkey, 0) < p.val:
                    need[p.semkey] = p.val
            s = None
            if op.is_dma:
                if op.eng == "pool":
                    s = N_DMA_SEMS - 8 + (rrp % 8)
                    rrp += 1
                else:
                    s = rr % (N_DMA_SEMS - 8)
                    rr += 1
                if dcnt[s] > 0 and need.get(("d", s), 0) < dcnt[s]:
                    need[("d", s)] = dcnt[s]
                dcnt[s] += 16
                op.semkey = ("d", s)
                op.val = dcnt[s]
            w = waited[op.eng]
            for key, val in need.items():
                if w.get(key, 0) >= val:
                    continue
                so = dsem[key[1]] if key[0] == "d" else esem[key[1]]
                E.wait_ge(so, val)
                w[key] = val
            inst = op.fn()
            if op.is_dma:
                inst.then_inc(dsem[s], 16)
            elif op.has_dep:
                cnt[op.eng] += 1
                op.semkey = ("e", op.eng)
                op.val = cnt[op.eng]
                inst.then_inc(esem[op.eng], 1)
        for s in range(N_DMA_SEMS):
            if dcnt[s] > 0:
                nc.sync.wait_ge(dsem[s], dcnt[s])
        return cnt


class Rot:
    def __init__(self, aps, name):
        self.aps = aps
        self.name = name
        self.i = 0

    def next(self):
        k = self.i % len(self.aps)
        self.i += 1
        return self.aps[k], (self.name, k)


def _cp_layout():
    off = {}
    c = 0
    for name, n in [("ident", 128), ("maskU", 128), ("corr", 512), ("AB", 64), ("ABS", 68),
                    ("gqa_pp", 1), ("gqm_pp", 1), ("gkm_rep", 64), ("gka_rep", 64),
                    ("gnorm_pp", 8), ("gmem_pp", 8), ("rows_pp", 8), ("lam4", 256),
                    ("bi_pp", 1), ("bf_pp", 1), ("sm_pp", 1), ("SEL", 128), ("SEL2", 4)]:
        off[name] = (c, c + n)
        c += n
    return off, c


CP_OFF, NCP = _cp_layout()


def _make_cp(inp, core):
    cp = np.zeros((128, NCP), np.float32)

    def put(name, arr):
        a, b = CP_OFF[name]
        cp[:, a:b] = arr

    p = np.arange(128)
    put("ident", np.eye(128, dtype=np.float32))
    put("maskU", (p[:, None] <= p[None, :]).astype(np.float32))
    corr = np.zeros((128, 4, 128), np.float32)
    k = p[:, None]
    q = p[None, :]
    for h in range(4):
        c_ = np.where(k > q, -16.0 * SLOPES[h] * (k - q), 0.0)
        c_ = np.where((k // 64) > (q // 64), -BIG, c_)
        corr[:, h, :] = c_
    put("corr", corr.reshape(128, 512))
    AB = np.zeros((128, 4, 16), np.float32)
    for h in range(4):
        for r in range(16):
            AB[:, h, r] = SLOPES[h] * (p - 127 - 128 * r)
    put("AB", AB.reshape(128, 64))
    ABS = np.zeros((128, 4, 17), np.float32)
    for h in range(4):
        for t in range(17):
            ABS[:, h, t] = SLOPES[h] * np.minimum(128 * t + p - 2063, 0)
    put("ABS", ABS.reshape(128, 68))
    put("gqa_pp", inp["g_qa"][0][p % 64][:, None])
    put("gqm_pp", inp["g_qm"][0][p % 64][:, None])
    put("gkm_rep", np.broadcast_to(inp["g_km"][0][None, :], (128, 64)))
    put("gka_rep", np.broadcast_to(inp["g_ka"][0][None, :], (128, 64)))
    put("gnorm_pp", inp["g_norm"][0].reshape(8, 128).T)
    put("gmem_pp", inp["g_mem"][0].reshape(8, 128).T)
    rows = np.ones((128, 8), np.float32)
    rows[:, 0:4] = inp["g_subln"][0][:, None]
    rows[:, 4:6] = inp["g_mh"][0][p % 64][:, None]
    put("rows_pp", rows)
    lam4 = np.concatenate([inp["lam_q1"][0], inp["lam_k1"][0], inp["lam_q2"][0], inp["lam_k2"][0]])
    put("lam4", np.broadcast_to(lam4[None, :], (128, 256)))
    put("bi_pp", inp["b_i"][0][p % 4][:, None])
    put("bf_pp", inp["b_f"][0][p % 4][:, None])
    put("sm_pp", inp["state_mlstm_m"][0, core][p % 4][:, None])
    SEL = np.zeros((128, 128), np.float32)
    SEL[0:4, :] = 1.0
    put("SEL", SEL)
    SEL2 = np.zeros((128, 4), np.float32)
    SEL2[0:4, 0:4] = np.eye(4, dtype=np.float32)
    put("SEL2", SEL2)
    return cp


class _Stop(Exception):
    pass


def build_program(NSEQ=4, NBLK=4, DO_SAMPLE=True, STAGE=99):
    nc = bass.Bass("TRN2", target_bir_lowering=False)
    S = Sched(nc)

    def din(name, shape):
        return nc.dram_tensor(name, shape, F32, kind="ExternalInput").ap()

    def dout(name, shape):
        return nc.dram_tensor(name, shape, F32, kind="ExternalOutput").ap()

    NS = max(NSEQ, 1)
    xp = din("xp", [NS, 2048, 1024])
    memp = din("memp", [NS, 256, 1024])
    xs = din("xs", [16, 1024])
    ck = din("ck", [2048, 512])
    cv = din("cv", [2048, 512])
    sC = din("sC", [4, 64, 64])
    sn = din("sn", [4, 64])
    cmk = din("cmk", [256, 256])
    cmv = din("cmv", [256, 256])
    w_in = din("w_in", [1024, NIN])
    w_out = din("w_out", [1024, 1024])
    w_mk = din("w_mk", [1024, 256])
    w_mv = din("w_mv", [1024, 256])
    cpd = din("cp", [128, NCP])

    yp = dout("yp", [NS, 2048, 1024])
    pk = dout("pk", [NS, 2048, 512])
    pv = dout("pv", [NS, 2048, 512])
    pC = dout("pC", [NS, 4, 64, 64])
    pn = dout("pn", [NS, 4, 64])
    pm = dout("pm", [NS, 4])
    pmk = dout("pmk", [NS, 256, 256])
    pmv = dout("pmv", [NS, 256, 256])
    ys = dout("ys", [16, 1024])
    sk = dout("sk", [16, 512])
    sv = dout("sv", [16, 512])
    sCo = dout("sCo", [4, 64, 64])
    sno = dout("sno", [4, 64])
    smo = dout("smo", [1, 4])

    st = ExitStack()

    def chk(n):
        if STAGE == n:
            raise _Stop()

    def sb(name, shape, dt=F32):
        return st.enter_context(nc.sbuf_tensor("sb_" + name, shape, dt))

    cp = sb("cp", [128, NCP])
    w_in_bf = sb("w_in_bf", [128, 8, NIN], BF16)
    w_out_bf = sb("w_out_bf", [128, 8, 1024], BF16)
    w_mkv_bf = sb("w_mkv_bf", [128, 8, 512], BF16)
    identb = sb("identb", [128, 128], BF16)
    corrb = sb("corrb", [128, 4, 128], BF16)
    zerob = sb("zerob", [128, 512], BF16)
    cm05 = sb("cm05", [128, 8])
    lamt = sb("lamt", [128, 8])
    rs8 = sb("rs8", [128, 8])
    nbf = sb("nbf", [128, 1])

    kT = sb("kT", [128, 4, 2064], BF16)
    v_aug = sb("v_aug", [128, 17, 4, 130], BF16)
    mkT = sb("mkT", [128, 2, 256], BF16)
    mv_aug = sb("mv_aug", [128, 2, 4, 66], BF16)

    qT = sb("qT", [128, 4, 512], BF16)
    mix = sb("mix", [128, 4, 1024], BF16)
    OBG = sb("OBG", [128, 4, 256], BF16)
    QKB = sb("QKB", [128, 4, 512], BF16)
    QKT = sb("QKT", [64, 8, 512], BF16)
    vb_aug = sb("vb_aug", [128, 4, 4, 66], BF16)
    qmT = sb("qmT", [128, 2, 512], BF16)
    GIF = sb("GIF", [128, 4, 8])
    WE = sb("WE", [128, 4, 8])
    C0B = sb("C0B", [64, 4, 4])
    Sst = sb("Sst", [64, 4, 66])
    Sd = sb("Sd", [64, 4, 66])
    Cs = sb("Cs", [64, 4, 66], BF16)
    Bprev = sb("Bprev", [4, 1])
    MUALL = sb("MUALL", [4, 8])
    UM = sb("UM", [4, 4])
    C0 = sb("C0", [4, 4])
    CD = sb("CD", [4, 4, 4])
    mfin = sb("mfin", [4, 1])

    XT = Rot([sb("xt%d" % i, [128, 1024])[:] for i in range(2)], "xt")
    xn = sb("xn", [128, 1024], BF16)
    HT = Rot([sb("hT%d" % i, [128, 8, 128], BF16)[:] for i in range(2)], "hT")
    ZF = Rot([sb("zf%d" % i, [128, 512])[:] for i in range(2)], "zf")
    SQ = sb("sq", [128, 512])
    KOUT = Rot([sb("kvout%d" % i, [128, 512])[:] for i in range(3)], "kvout")
    VOUT = KOUT
    TH = Rot([sb("th%d" % i, [128, 512])[:] for i in range(2)], "th")
    NB = Rot([sb("nb%d" % i, [128, 512], BF16)[:] for i in range(2)], "nb")
    PT = Rot([sb("pt%d" % i, [128, 512], BF16)[:] for i in range(3)], "pt")
    STT = Rot([sb("stat%d" % i, [128, 72])[:] for i in range(4)], "stat")
    O1 = Rot([sb("o1s%d" % i, [128, 128])[:] for i in range(2)], "o1s")
    OO = Rot([sb("oo%d" % i, [128, 128])[:] for i in range(2)], "oo")
    mixT = sb("mixT", [128, 8, 128], BF16)
    AT = sb("AT", [128, 4, 128], BF16)
    vw = sb("vw", [128, 4, 66], BF16)
    HN = sb("HN", [128, 4, 64])
    OM = sb("OM", [128, 4, 64])
    GA, KGA = SQ[0:4, :], "sq"
    GB, KGB = TH.aps[0][0:4, :], ("th", 0)
    GU, KGU = TH.aps[1][0:4, :], ("th", 1)
    GW, KGW = ZF.aps[0][0:4, :], ("zf", 0)
    GE, KGE = ZF.aps[1][0:4, :], ("zf", 1)

    PB = [st.enter_context(nc.psum_tensor("pb%d" % i, [128, 512], F32)) for i in range(8)]
    MM = Rot([PB[0][:], PB[1][:], PB[2][:]], "pb_mm")
    MM.keys = [("pb", 0), ("pb", 1), ("pb", 2)]
    ACC = [PB[3], PB[4], PB[5]]
    ACCK = [("pb", 3), ("pb", 4), ("pb", 5)]
    TBf = PB[6]
    TB = PB[6][:].bitcast(BF16)
    TBK = ("pb", 6)
    TF = PB[7]
    TFK = ("pb", 7)

    def mmnext():
        k = MM.i % 3
        MM.i += 1
        return PB[k], ("pb", k)

    def cpc(name, a=None, b=None):
        o, e = CP_OFF[name]
        if a is None:
            return cp[:, o:e]
        return cp[:, o + a:o + b]

    def dma(out, in_, r=(), w=(), q="sp", **kw):
        if q == "sp":
            S.add("sp", lambda: nc.sync.dma_start(out=out, in_=in_, **kw), r, w, dma=True)
        else:
            S.add("pool", lambda: nc.gpsimd.dma_start(out=out, in_=in_, **kw), r, w, dma=True)

    def mm(out, lhsT, rhs, start, stop, r, w):
        S.add("pe", lambda: nc.tensor.matmul(out, lhsT=lhsT, rhs=rhs, start=start, stop=stop,
                                             skip_group_check=True), r, w)

    def tr(out, in_, ident, r, w):
        S.add("pe", lambda: nc.tensor.transpose(out=out, in_=in_, identity=ident), r, w)

    def act(out, in_, func, r, w, bias=0.0, scale=1.0, accum=None):
        if accum is None:
            S.add("act", lambda: nc.scalar.activation(out=out, in_=in_, func=func, bias=bias, scale=scale), r, w)
        else:
            S.add("act", lambda: nc.scalar.activation(out=out, in_=in_, func=func, bias=bias, scale=scale,
                                                      accum_out=accum), r, w)

    def engo(e):
        return nc.vector if e == "dve" else nc.gpsimd

    def ts(e, out, in0, s1, s2, op0, op1, r, w):
        if s2 is None:
            S.add(e, lambda: engo(e).tensor_scalar(out=out, in0=in0, scalar1=s1, scalar2=None, op0=op0), r, w)
        else:
            S.add(e, lambda: engo(e).tensor_scalar(out=out, in0=in0, scalar1=s1, scalar2=s2, op0=op0, op1=op1), r, w)

    def tt(e, out, in0, in1, op, r, w):
        S.add(e, lambda: engo(e).tensor_tensor(out=out, in0=in0, in1=in1, op=op), r, w)

    def stt(out, in0, scalar, in1, op0, op1, r, w):
        S.add("dve", lambda: nc.vector.scalar_tensor_tensor(out=out, in0=in0, scalar=scalar, in1=in1,
                                                            op0=op0, op1=op1), r, w)

    def cpy(e, out, in_, r, w):
        if e == "act":
            S.add("act", lambda: nc.scalar.copy(out=out, in_=in_), r, w)
        else:
            S.add(e, lambda: engo(e).tensor_copy(out=out, in_=in_), r, w)

    def memset(e, ap, val, w):
        S.add(e, lambda: engo(e).memset(ap, val), (), w)

    def red(out, in_, op, r, w):
        S.add("dve", lambda: nc.vector.tensor_reduce(out=out, in_=in_, axis=AX.X, op=op), r, w)

    def recip(out, in_, r, w):
        S.add("dve", lambda: nc.vector.reciprocal(out=out, in_=in_), r, w)

    def scan(out, d0, d1, init, op0, op1, r, w):
        S.add("dve", lambda: nc.vector.tensor_tensor_scan(out=out, data0=d0, data1=d1, initial=init,
                                                          op0=op0, op1=op1), r, w)

    def rstd_from_ss(stt_ap, kst, c_ss, c_tmp, c_out, n, T, inv):
        ts("dve", stt_ap[0:T, c_tmp:c_tmp + n], stt_ap[0:T, c_ss:c_ss + n], inv, EPS, ALU.mult, ALU.add,
           [kst], [kst])
        tt("pool", stt_ap[0:T, c_out:c_out + n], stt_ap[0:T, c_tmp:c_tmp + n], cm05[0:T, 0:n], ALU.pow,
           [kst, "cm05"], [kst])

    dma(cp[:], cpd, w=["cp"])
    cpy("dve", identb[:], cpc("ident"), ["cp"], ["identb"])
    identf = cpc("ident")
    cpy("dve", corrb[:].rearrange("p h q -> p (h q)"), cpc("corr"), ["cp"], ["corrb"])
    memset("pool", zerob[:], 0.0, ["zerob"])
    memset("pool", cm05[:], -0.5, ["cm05"])
    memset("pool", v_aug[:, :, :, 128:129], 1.0, ["v_ones"])
    memset("pool", vb_aug[:, :, :, 64:65], 1.0, ["vb_ones"])
    memset("pool", mv_aug[:, :, :, 64:65], 1.0, ["mv_ones"])
    wv = w_in.rearrange("(k p) n -> p k n", p=128)
    for kc in range(8):
        dma(w_in_bf[:, kc, 0:1536], wv[:, kc, 0:1536], w=[("w_in", kc, 0)], q="pool")
        dma(w_in_bf[:, kc, 1536:3072], wv[:, kc, 1536:3072], w=[("w_in", kc, 1)], q="pool")
        dma(w_in_bf[:, kc, 3072:3840], wv[:, kc, 3080:3848], w=[("w_in", kc, 2)], q="pool")
        dma(w_in_bf[:, kc, 3840:3848], wv[:, kc, 3072:3080], w=[("w_in", kc, 3)], q="pool")
    wov = w_out.rearrange("(k p) n -> p k n", p=128)
    wkv = w_mk.rearrange("(k p) n -> p k n", p=128)
    wvv = w_mv.rearrange("(k p) n -> p k n", p=128)
    for kc in range(8):
        dma(w_out_bf[:, kc, :], wov[:, kc, :], w=[("w_out", kc)], q="pool")
        dma(w_mkv_bf[:, kc, 0:256], wkv[:, kc, :], w=[("w_mkv", kc, 0)], q="pool")
        dma(w_mkv_bf[:, kc, 256:512], wvv[:, kc, :], w=[("w_mkv", kc, 1)], q="pool")
    ts("dve", rs8[:, 0:4], cpc("rows_pp", 0, 4), 0.5 * (1.0 - LAM_INIT), None, ALU.mult, None, ["cp"], ["rs8"])
    ts("dve", rs8[:, 4:6], cpc("rows_pp", 4, 6), 0.25, None, ALU.mult, None, ["cp"], ["rs8"])
    ts("dve", rs8[:, 6:8], cpc("rows_pp", 6, 8), 0.5, None, ALU.mult, None, ["cp"], ["rs8"])
    ts("dve", nbf[:], cpc("bf_pp"), -1.0, None, ALU.mult, None, ["cp"], ["nbf"])
    for kc in range(8):
        e = "dve" if kc % 2 == 0 else "pool"
        wk = [("w_in", kc, i) for i in range(4)]
        g = cpc("gnorm_pp", kc, kc + 1)
        ts(e, w_in_bf[:, kc, 0:2304], w_in_bf[:, kc, 0:2304], g, None, ALU.mult, None, ["cp"] + wk, wk)
        ts(e, w_in_bf[:, kc, 2304:2560], w_in_bf[:, kc, 2304:2560], g, 0.125, ALU.mult, ALU.mult, ["cp"] + wk, wk)
        ts(e, w_in_bf[:, kc, 2560:NIN], w_in_bf[:, kc, 2560:NIN], g, None, ALU.mult, None, ["cp"] + wk, wk)
        ts(e, w_out_bf[:, kc, :], w_out_bf[:, kc, :], rs8[:, kc:kc + 1], None, ALU.mult, None,
           ["rs8", ("w_out", kc)], [("w_out", kc)])
        wk2 = [("w_mkv", kc, 0), ("w_mkv", kc, 1)]
        ts(e, w_mkv_bf[:, kc, :], w_mkv_bf[:, kc, :], cpc("gmem_pp", kc, kc + 1), None, ALU.mult, None,
           ["cp"] + wk2, wk2)
    W_IN_K = [("w_in", kc, i) for kc in range(8) for i in range(4)]
    W_OUT_K = [("w_out", kc) for kc in range(8)]
    W_MKV_K = [("w_mkv", kc, i) for kc in range(8) for i in range(2)]
    l4 = cpc("lam4")
    tt("dve", SQ[:, 0:64], l4[:, 0:64], l4[:, 64:128], ALU.mult, ["cp"], ["sq"])
    red(lamt[:, 0:1], SQ[:, 0:64], ALU.add, ["sq"], ["lamt"])
    tt("dve", SQ[:, 64:128], l4[:, 128:192], l4[:, 192:256], ALU.mult, ["cp"], ["sq"])
    red(lamt[:, 1:2], SQ[:, 64:128], ALU.add, ["sq"], ["lamt"])
    act(lamt[:, 2:4], lamt[:, 0:2], AF.Exp, ["lamt"], ["lamt"])
    stt(lamt[:, 4:5], lamt[:, 2:3], LAM_INIT, lamt[:, 3:4], ALU.add, ALU.subtract, ["lamt"], ["lamt"])
    ts("dve", lamt[:, 5:6], lamt[:, 4:5], -1.0, None, ALU.mult, None, ["lamt"], ["lamt"])

    def front(src_ap, T):
        xt, kx = XT.next()
        dma(xt[0:T, :], src_ap, w=[kx])
        sa, ks = STT.next()
        act(xn[0:T, :], xt[0:T, :], AF.Square, [kx], ["xn", ks], accum=sa[0:T, 0:1])
        rstd_from_ss(sa, ks, 0, 1, 2, 1, T, 1.0 / 1024)
        ts("dve", xn[0:T, :], xt[0:T, :], sa[0:T, 2:3], None, ALU.mult, None, [kx, ks], ["xn"])
        for kc in range(8):
            tr(TB[:, kc * 128:kc * 128 + T], xn[0:T, kc * 128:(kc + 1) * 128], identb[0:T, 0:T],
               ["xn", "identb"], [TBK])
        hT, kh = HT.next()
        cpy("act", hT[:, :, 0:T], TB.rearrange("p (k t) -> p k t", k=8)[:, :, 0:T], [TBK], [kh])
        return xt, kx, hT, kh, sa, ks

    def group_norm_rstd(src, ksrc, T, ng, sa, ks, base):
        tt("pool", SQ[0:T, 0:ng * 64], src, src, ALU.mult, [ksrc], ["sq"])
        red(sa[0:T, base:base + ng], SQ[0:T, 0:ng * 64].rearrange("p (g d) -> p g d", d=64), ALU.add,
            ["sq"], [ks])
        rstd_from_ss(sa, ks, base, base + ng, base + 2 * ng, ng, T, 1.0 / 64)
        return sa[0:T, base + 2 * ng:base + 3 * ng]

    def phase1_tile(src_ap, T, ti, j, k_dst, v_dst):
        xt, kx, hT, kh, sa, ks = front(src_ap, T)
        chk(40)
        tok = slice(j * 128, j * 128 + T)
        ktok = slice(ti * 128, ti * 128 + T)
        for g in range(8):
            chk(41 + g)
            c0 = g * 512
            n = min(512, NIN - c0)
            ps, kp = mmnext()
            for kc in range(8):
                mm(ps[0:T, 0:n], hT[:, kc, 0:T], w_in_bf[:, kc, c0:c0 + n], kc == 0, kc == 7,
                   [kh] + W_IN_K, [kp])
            if g == 0 or g == 1:
                zf, kz = ZF.next()
                cpy("act", zf[0:T, :], ps[0:T, 0:512], [kp], [kz])
                r8 = group_norm_rstd(zf[0:T, :], kz, T, 8, sa, ks, 8 if g == 0 else 32)
                nb, kn = NB.next()
                if g == 0:
                    tt("dve", nb[0:T, :].rearrange("p (g d) -> p g d", d=64),
                       zf[0:T, :].rearrange("p (g d) -> p g d", d=64),
                       r8.unsqueeze(2).to_broadcast([T, 8, 64]), ALU.mult, [kz, ks], [kn])
                else:
                    tt("dve", zf[0:T, :].rearrange("p (g d) -> p g d", d=64),
                       zf[0:T, :].rearrange("p (g d) -> p g d", d=64),
                       r8.unsqueeze(2).to_broadcast([T, 8, 64]), ALU.mult, [kz, ks], [kz])
                    ko, kk = KOUT.next()
                    tt("dve", ko[0:T, :].rearrange("p (g d) -> p g d", d=64),
                       zf[0:T, :].rearrange("p (g d) -> p g d", d=64),
                       cpc("gka_rep")[0:T, :].unsqueeze(1).to_broadcast([T, 8, 64]), ALU.mult,
                       [kz, "cp"], [kk])
                    dma(k_dst, ko[0:T, :], r=[kk])
                    cpy("pool", nb[0:T, :], ko[0:T, :], [kk], [kn])
                for h in range(4):
                    tr(TB[:, h * 128:h * 128 + T], nb[0:T, h * 128:(h + 1) * 128], identb[0:T, 0:T],
                       [kn, "identb"], [TBK])
                src = TB[:, 0:512].rearrange("p (h t) -> p h t", h=4)[:, :, 0:T]
                if g == 0:
                    ts("dve", qT[:, :, tok], src, cpc("gqa_pp"), None, ALU.mult, None, [TBK, "cp"], [("qT", j)])
                else:
                    cpy("act", kT[:, :, ktok], src, [TBK], [("kT", ti)])
            elif g == 2:
                vo, kv = VOUT.next()
                cpy("act", vo[0:T, :], ps[0:T, 0:512], [kp], [kv])
                dma(v_dst, vo[0:T, :], r=[kv])
                cpy("pool", v_aug[0:T, ti, :, 0:128], vo[0:T, :].rearrange("p (h d) -> p h d", h=4),
                    [kv, "v_ones"], [("v", ti)])
            elif g == 3:
                th, kt = TH.next()
                act(th[0:T, :], ps[0:T, 0:512], AF.Tanh, [kp], [kt], scale=0.5)
                stt(mix[0:T, j, 0:512], th[0:T, :], 1.0, ps[0:T, 0:512], ALU.add, ALU.mult,
                    [kt, kp], [("mix", j, 0)])
            elif g == 4:
                cpy("act", QKB[0:T, j, :], ps[0:T, 0:512], [kp], [("QKB", j)])
                for i in range(8):
                    tr(TB[0:64, i * 128:i * 128 + T], QKB[0:T, j, i * 64:(i + 1) * 64], identb[0:T, 0:T],
                       [("QKB", j), "identb"], [TBK])
                cpy("dve", QKT[:, :, tok], TB[0:64, :].rearrange("p (h t) -> p h t", h=8)[:, :, 0:T],
                    [TBK], [("QKT", j)])
            elif g == 5:
                import os
                if os.environ.get("G5SKIP") != "dve":
                    cpy("dve", vb_aug[0:T, j, :, 0:64], ps[0:T, 0:256].rearrange("p (h d) -> p h d", h=4),
                        [kp, "vb_ones"], [("vb", j)])
                if os.environ.get("G5SKIP") != "act":
                    act(OBG[0:T, j, :], ps[0:T, 256:512], AF.Tanh, [kp], [("OBG", j)], scale=0.5)
            elif g == 6:
                th, kt = TH.next()
                act(th[0:T, 0:256], ps[0:T, 0:256], AF.Tanh, [kp], [kt], scale=0.5)
                stt(th[0:T, 256:512], th[0:T, 0:256], 1.0, ps[0:T, 0:256], ALU.add, ALU.mult, [kt, kp], [kt])
                stt(mix[0:T, j, 512:768], OBG[0:T, j, :], 1.0, th[0:T, 256:512], ALU.add, ALU.mult,
                    [("OBG", j), kt], [("mix", j, 1)])
                zf, kz = ZF.next()
                cpy("act", zf[0:T, 0:256], ps[0:T, 256:512], [kp], [kz])
                r4 = group_norm_rstd(zf[0:T, 0:256], kz, T, 4, sa, ks, 56)
                nb, kn = NB.next()
                tt("dve", nb[0:T, 0:256].rearrange("p (g d) -> p g d", d=64),
                   zf[0:T, 0:256].rearrange("p (g d) -> p g d", d=64),
                   r4.unsqueeze(2).to_broadcast([T, 4, 64]), ALU.mult, [kz, ks], [kn])
                for hp in range(2):
                    tr(TB[:, hp * 128:hp * 128 + T], nb[0:T, hp * 128:(hp + 1) * 128], identb[0:T, 0:T],
                       [kn, "identb"], [TBK])
                ts("dve", qmT[:, :, tok], TB[:, 0:256].rearrange("p (h t) -> p h t", h=2)[:, :, 0:T],
                   cpc("gqm_pp"), None, ALU.mult, None, [TBK, "cp"], [("qmT", j)])
            else:
                th, kt = TH.next()
                act(th[0:T, 0:256], ps[0:T, 0:256], AF.Tanh, [kp], [kt], scale=0.5)
                stt(mix[0:T, j, 768:1024], th[0:T, 0:256], 1.0, ps[0:T, 0:256], ALU.add, ALU.mult,
                    [kt, kp], [("mix", j, 2)])
                cpy("dve", GIF[0:T, j, :], ps[0:T, 256:264], [kp], [("GIF", j)])

    def gates_block(T, ntile):
        NQ = ntile * 128
        ibT = TF[0:4, 0:NQ]
        fbT = TBf[0:4, 0:NQ]
        for j in range(ntile):
            tr(TF[0:4, j * 128:j * 128 + T], GIF[0:T, j, 0:4], identf[0:T, 0:T], [("GIF", j), "cp"], [TFK])
            tr(TBf[0:4, j * 128:j * 128 + T], GIF[0:T, j, 4:8], identf[0:T, 0:T], [("GIF", j), "cp"], [TBK])
        if T < 128:
            memset("dve", GA[:, 0:NQ], 0.0, [KGA])
        cs = slice(0, T) if ntile == 1 else slice(0, NQ)
        act(GA[:, cs], fbT[:, cs], AF.Exp, [TBK, "nbf"], [KGA], bias=nbf[0:4, 0:1], scale=-1.0)
        act(GA[:, cs], GA[:, cs], AF.Ln, [KGA], [KGA], bias=1.0)
        scan(GB[:, cs], zerob[0:4, cs], GA[:, cs], Bprev[:, 0:1], ALU.add, ALU.subtract,
             ["zerob", KGA, "Bprev"], [KGB])
        stt(GU[:, cs], ibT[:, cs], cpc("bi_pp")[0:4, :], GB[:, cs], ALU.add, ALU.subtract,
            [TFK, "cp", KGB], [KGU])
        if ntile == 1:
            red(UM[:, 0:1], GU[:, cs], ALU.max, [KGU], ["UM"])
        else:
            red(UM[:, 0:ntile], GU[:, cs].rearrange("p (c t) -> p c t", c=ntile), ALU.max, [KGU], ["UM"])
        scan(MUALL[:, 1:1 + ntile], UM[:, 0:ntile], UM[:, 0:ntile], MUALL[:, 0:1], ALU.max, ALU.max,
             ["UM", "MUALL"], ["MUALL"])
        tt("dve", C0[:, 0:ntile], MUALL[:, 0:ntile], MUALL[:, 1:1 + ntile], ALU.subtract, ["MUALL"], ["C0"])
        act(C0[:, 0:ntile], C0[:, 0:ntile], AF.Exp, ["C0"], ["C0"])
        if ntile == 1:
            mub = MUALL[:, 1:2].to_broadcast([4, T])
            tt("dve", GW[:, cs], GU[:, cs], mub, ALU.subtract, [KGU, "MUALL"], [KGW])
            tt("dve", GE[:, cs], GB[:, cs], mub, ALU.add, [KGB, "MUALL"], [KGE])
        else:
            mub = MUALL[:, 1:1 + ntile].unsqueeze(2).to_broadcast([4, ntile, 128])
            tt("dve", GW[:, cs].rearrange("p (c t) -> p c t", c=ntile),
               GU[:, cs].rearrange("p (c t) -> p c t", c=ntile), mub, ALU.subtract, [KGU, "MUALL"], [KGW])
            tt("dve", GE[:, cs].rearrange("p (c t) -> p c t", c=ntile),
               GB[:, cs].rearrange("p (c t) -> p c t", c=ntile), mub, ALU.add, [KGB, "MUALL"], [KGE])
        act(GW[:, cs], GW[:, cs], AF.Exp, [KGW], [KGW])
        act(GE[:, cs], GE[:, cs], AF.Exp, [KGE], [KGE], scale=-1.0)
        for j in range(ntile):
            tr(TF[0:T, j * 8:j * 8 + 4], GW[0:4, j * 128:j * 128 + T], identf[0:4, 0:4], [KGW, "cp"], [TFK])
            tr(TF[0:T, j * 8 + 4:j * 8 + 8], GE[0:4, j * 128:j * 128 + T], identf[0:4, 0:4], [KGE, "cp"], [TFK])
        cpy("dve", WE[0:T, 0:ntile, :], TF[0:T, 0:ntile * 8].rearrange("p (c e) -> p c e", e=8), [TFK], ["WE"])
        tt("dve", CD[:, 0:ntile, :], C0[:, 0:ntile].unsqueeze(2).to_broadcast([4, ntile, 4]),
           cpc("SEL2")[0:4, :].unsqueeze(1).to_broadcast([4, ntile, 4]), ALU.mult, ["C0", "cp"], ["CD"])
        mm(TBf[0:64, 0:ntile * 4], cpc("SEL")[0:4, 0:64], CD[:, 0:ntile, :].rearrange("p c e -> p (c e)"), True, True,
           ["cp", "CD"], [TBK])
        cpy("dve", C0B[:, 0:ntile, :], TBf[0:64, 0:ntile * 4].rearrange("p (c e) -> p c e", e=4), [TBK], ["C0B"])
        last = T - 1 if ntile == 1 else NQ - 1
        cpy("dve", Bprev[:, 0:1], GB[:, last:last + 1], [KGB], ["Bprev"])
        cpy("dve", MUALL[:, 0:1], MUALL[:, ntile:ntile + 1], ["MUALL"], ["MUALL"])

    def attention_block(T, ntile, ktiles, sample):
        NQ = ntile * T
        for h in range(4):
            for b in range(3):
                mm(ACC[b][0:T, :], zerob[:, 0:T], zerob[:, :], True, True, ["zerob"], [ACCK[b]])
            for (t, nk, dsub) in ktiles:
                c0 = 0 if dsub is None else dsub * T
                ncol = NQ - c0
                for m in range(2):
                    ps, kp = mmnext()
                    pr = slice(64 * m, 64 * m + 64)
                    mm(ps[0:nk, 0:ncol], kT[pr, h, t * 128:t * 128 + nk], qT[pr, h, c0:NQ],
                       True, dsub is None, [("kT", t)] + [("qT", jj) for jj in range(ntile)], [kp])
                    if dsub is not None:
                        mm(ps[0:nk, 0:T], identb[0:nk, 0:nk], corrb[0:nk, h, 0:T], False, True,
                           ["identb", "corrb"], [kp])
                    pt, kpt = PT.next()
                    if sample:
                        calls = [(c0, NQ, None)]
                    elif h == 0:
                        calls = [(c, c + 128, None) for c in range(c0, NQ, 128)]
                    else:
                        calls = [(c0, NQ, None)]
                    for (ca, cb, _) in calls:
                        if sample:
                            bias = cpc("ABS")[0:nk, h * 17 + t:h * 17 + t + 1]
                        else:
                            tref = ktiles[-1][0] - (ntile - 1) + (cb - 1) // 128
                            bias = cpc("AB")[0:nk, h * 16 + (tref - t):h * 16 + (tref - t) + 1]
                        act(pt[0:nk, ca - c0:cb - c0], ps[0:nk, ca - c0:cb - c0], AF.Exp, [kp, "cp"], [kpt],
                            bias=bias, scale=0.125)
                    for i in range(ntile):
                        if i * T < c0:
                            continue
                        a = m * 4 + i
                        b = a // 3
                        o = (a % 3) * 129
                        mm(ACC[b][0:T, o:o + 129], pt[0:nk, i * T - c0:(i + 1) * T - c0],
                           v_aug[0:nk, t, h, 0:129], False, True, [kpt, ("v", t), "v_ones"], [ACCK[b]])
            sa, ks = STT.next()
            if ntile == 4:
                for b in range(3):
                    nacc = 3 if b < 2 else 2
                    recip(sa[0:T, 3 * b:3 * b + nacc], ACC[b][0:T, 128:129 * nacc:129], [ACCK[b]], [ks])
            else:
                recip(sa[0:T, 0:1], ACC[0][0:T, 128:129], [ACCK[0]], [ks])
                recip(sa[0:T, 4:5], ACC[1][0:T, 129 + 128:129 + 129], [ACCK[1]], [ks])
            tt("dve", sa[0:T, 4:4 + ntile], sa[0:T, 4:4 + ntile], lamt[0:T, 5:6].to_broadcast([T, ntile]), ALU.mult,
               [ks, "lamt"], [ks])
            oos = []
            for i in range(ntile):
                a1, a2 = i, 4 + i
                o1, k1 = O1.next()
                act(o1[0:T, :], ACC[a1 // 3][0:T, (a1 % 3) * 129:(a1 % 3) * 129 + 128], AF.Copy,
                    [ACCK[a1 // 3], ks], [k1], scale=sa[0:T, a1:a1 + 1])
                oo, ko = OO.next()
                stt(oo[0:T, :], ACC[a2 // 3][0:T, (a2 % 3) * 129:(a2 % 3) * 129 + 128], sa[0:T, a2:a2 + 1],
                    o1[0:T, :], ALU.mult, ALU.add, [ACCK[a2 // 3], ks, k1], [ko])
                act(o1[0:T, :], oo[0:T, :], AF.Square, [ko], [k1, ks], accum=sa[0:T, 8 + i:9 + i])
                oos.append((oo, ko))
                if i % 2 == 1 or i == ntile - 1:
                    i0 = i - (i % 2)
                    nn = i - i0 + 1
                    ts("dve", sa[0:T, 12 + i0:12 + i0 + nn], sa[0:T, 8 + i0:8 + i0 + nn], 1.0 / 128, EPS,
                       ALU.mult, ALU.add, [ks], [ks])
                    tt("pool", sa[0:T, 16 + i0:16 + i0 + nn], sa[0:T, 12 + i0:12 + i0 + nn], cm05[0:T, 0:nn],
                       ALU.pow, [ks, "cm05"], [ks])
                    for ii in range(i0, i + 1):
                        oo2, ko2 = oos[ii]
                        stt(mix[0:T, ii, h * 128:(h + 1) * 128], oo2[0:T, :], sa[0:T, 16 + ii:17 + ii],
                            mix[0:T, ii, h * 128:(h + 1) * 128], ALU.mult, ALU.mult,
                            [ko2, ks, ("mix", ii, 0)], [("mix", ii, 0)])

    def mem_block(T, ntile):
        NQ = ntile * T
        banks = [(ACC[0], ACCK[0]), (ACC[1], ACCK[1]), (ACC[2], ACCK[2]), (TF, TFK)]
        for i in range(ntile):
            mm(banks[i][0][0:T, :], zerob[:, 0:T], zerob[:, :], True, True, ["zerob"], [banks[i][1]])
        for h in range(4):
            pr = slice(64 * (h % 2), 64 * (h % 2) + 64)
            for nt in range(2):
                ps, kp = mmnext()
                mm(ps[:, 0:NQ], mkT[pr, h // 2, nt * 128:(nt + 1) * 128], qmT[pr, h // 2, 0:NQ], True, True,
                   ["mkT"] + [("qmT", jj) for jj in range(ntile)], [kp])
                pt, kpt = PT.next()
                act(pt[:, 0:NQ], ps[:, 0:NQ], AF.Exp, [kp], [kpt], scale=0.125)
                for i in range(ntile):
                    mm(banks[i][0][0:T, h * 65:(h + 1) * 65], pt[:, i * T:(i + 1) * T], mv_aug[:, nt, h, 0:65],
                       False, True, [kpt, "mv", "mv_ones"], [banks[i][1]])
        for i in range(ntile):
            bk, kb = banks[i]
            sa, ks = STT.next()
            recip(sa[0:T, 0:4], bk[0:T, 64:260:65], [kb], [ks])
            tt("dve", OM[0:T, :, :], bk[0:T, 0:260].rearrange("p (h e) -> p h e", e=65)[:, :, 0:64],
               sa[0:T, 0:4].unsqueeze(2).to_broadcast([T, 4, 64]), ALU.mult, [kb, ks], ["OM"])
            tt("dve", mix[0:T, i, 768:1024], OM[0:T, :, :].rearrange("p h d -> p (h d)"),
               mix[0:T, i, 768:1024], ALU.mult, ["OM", ("mix", i, 2)], [("mix", i, 2)])

    def mlstm_block(T, ntile):
        for j in range(ntile):
            tok = slice(j * 128, j * 128 + T)
            tt("dve", Sd[:, :, 0:65], Sst[:, :, 0:65], C0B[:, j, :].unsqueeze(2).to_broadcast([64, 4, 65]), ALU.mult,
               ["Sst", "C0B"], ["Sd"])
            cpy("pool", Cs[:, :, 0:65], Sd[:, :, 0:65], ["Sd"], ["Cs"])
            tt("pool", vw[0:T, :, 0:65], vb_aug[0:T, j, :, 0:65], WE[0:T, j, 0:4].unsqueeze(2).to_broadcast([T, 4, 65]),
               ALU.mult, [("vb", j), "vb_ones", "WE"], ["vw"])
            chk(80)
            psA, kA = mmnext()
            for h in range(4):
                mm(psA[0:T, h * 128:h * 128 + T], QKT[:, 4 + h, tok], QKT[:, h, tok], True, True,
                   [("QKT", j)], [kA])
            tt("dve", AT[0:T, :, 0:T], psA[0:T, :].rearrange("p (h t) -> p h t", h=4)[:, :, 0:T],
               cpc("maskU")[0:T, 0:T].unsqueeze(1).to_broadcast([T, 4, T]), ALU.mult, [kA, "cp"], ["AT"])
            chk(81)
            psO, kO = ACC[j % 2], ACCK[j % 2]
            for h in range(4):
                mm(psO[0:T, h * 65:(h + 1) * 65], AT[0:T, h, 0:T], vw[0:T, h, 0:65], True, False, ["AT", "vw"], [kO])
                mm(psO[0:T, h * 65:(h + 1) * 65], QKT[:, h, tok], Cs[:, h, 0:65], False, True,
                   [("QKT", j), "Cs"], [kO])
            chk(82)
            psS, kS = ACC[2], ACCK[2]
            for h in range(4):
                mm(psS[0:64, h * 65:(h + 1) * 65], QKB[0:T, j, 256 + h * 64:256 + (h + 1) * 64],
                   vw[0:T, h, 0:65], True, True, [("QKB", j), "vw"], [kS])
            chk(83)
            tt("dve", Sst[:, :, 0:65], Sd[:, :, 0:65], psS[0:64, 0:260].rearrange("p (c e) -> p c e", e=65), ALU.add,
               ["Sd", kS], ["Sst"])
            chk(84)
            sa, ks = STT.next()
            cpy("dve", sa[0:T, 20:24], psO[0:T, 64:260:65], [kO], [ks])
            stt(sa[0:T, 24:28], sa[0:T, 20:24], -1.0, sa[0:T, 20:24], ALU.mult, ALU.max, [ks], [ks])
            tt("dve", sa[0:T, 0:4], sa[0:T, 24:28], WE[0:T, j, 4:8], ALU.max, [ks, "WE"], [ks])
            recip(sa[0:T, 4:8], sa[0:T, 0:4], [ks], [ks])
            tt("dve", HN[0:T, :, :], psO[0:T, 0:260].rearrange("p (h e) -> p h e", e=65)[:, :, 0:64],
               sa[0:T, 4:8].unsqueeze(2).to_broadcast([T, 4, 64]), ALU.mult, [kO, ks], ["HN"])
            r4 = group_norm_rstd(HN[0:T, :, :].rearrange("p h d -> p (h d)"), "HN", T, 4, sa, ks, 8)
            tt("dve", HN[0:T, :, :], HN[0:T, :, :], r4.unsqueeze(2).to_broadcast([T, 4, 64]), ALU.mult,
               ["HN", ks], ["HN"])
            tt("dve", mix[0:T, j, 512:768], HN[0:T, :, :].rearrange("p h d -> p (h d)"), mix[0:T, j, 512:768],
               ALU.mult, ["HN", ("mix", j, 1)], [("mix", j, 1)])

    def phase3_tile(src_ap, dst_ap, T, j):
        for kc in range(8):
            tr(TB[:, kc * 128:kc * 128 + T], mix[0:T, j, kc * 128:(kc + 1) * 128], identb[0:T, 0:T],
               [("mix", j, 0), ("mix", j, 1), ("mix", j, 2), "identb"], [TBK])
        cpy("act", mixT[:, :, 0:T], TB.rearrange("p (k t) -> p k t", k=8)[:, :, 0:T], [TBK], ["mixT"])
        xt, kx = XT.next()
        dma(xt[0:T, :], src_ap, w=[kx])
        for half in range(2):
            ps, kp = ACC[half], ACCK[half]
            for kc in range(8):
                mm(ps[0:T, :], mixT[:, kc, 0:T], w_out_bf[:, kc, half * 512:(half + 1) * 512], kc == 0, kc == 7,
                   ["mixT"] + W_OUT_K, [kp])
            tt("dve", xt[0:T, half * 512:(half + 1) * 512], ps[0:T, :], xt[0:T, half * 512:(half + 1) * 512],
               ALU.add, [kp, kx], [kx])
        dma(dst_ap, xt[0:T, :], r=[kx])

    def memkv_seq(s):
        for nt in range(2):
            xt, kx, hT, kh, sa, ks = front(memp[s, nt * 128:(nt + 1) * 128, :], 128)
            ps, kp = mmnext()
            for kc in range(8):
                mm(ps[:, :], hT[:, kc, :], w_mkv_bf[:, kc, :], kc == 0, kc == 7, [kh] + W_MKV_K, [kp])
            zf, kz = ZF.next()
            cpy("act", zf[:, :], ps[:, :], [kp], [kz])
            dma(pmv[s, nt * 128:(nt + 1) * 128, :], zf[:, 256:512], r=[kz])
            cpy("pool", mv_aug[:, nt, :, 0:64], zf[:, 256:512].rearrange("p (h d) -> p h d", h=4),
                [kz, "mv_ones"], ["mv"])
            r4 = group_norm_rstd(zf[:, 0:256], kz, 128, 4, sa, ks, 8)
            tt("dve", zf[:, 0:256].rearrange("p (g d) -> p g d", d=64),
               zf[:, 0:256].rearrange("p (g d) -> p g d", d=64),
               r4.unsqueeze(2).to_broadcast([128, 4, 64]), ALU.mult, [kz, ks], [kz])
            ko, kk = KOUT.next()
            tt("dve", ko[:, 0:256].rearrange("p (g d) -> p g d", d=64),
               zf[:, 0:256].rearrange("p (g d) -> p g d", d=64),
               cpc("gkm_rep").unsqueeze(1).to_broadcast([128, 4, 64]), ALU.mult, [kz, "cp"], [kk])
            dma(pmk[s, nt * 128:(nt + 1) * 128, :], ko[:, 0:256], r=[kk])
            nb, kn = NB.next()
            cpy("pool", nb[:, 0:256], ko[:, 0:256], [kk], [kn])
            for hp in range(2):
                tr(TB[:, hp * 128:(hp + 1) * 128], nb[:, hp * 128:(hp + 1) * 128], identb[:, :],
                   [kn, "identb"], [TBK])
            cpy("act", mkT[:, :, nt * 128:(nt + 1) * 128], TB[:, 0:256].rearrange("p (h t) -> p h t", h=2),
                [TBK], ["mkT"])

    def state_out(dC, dn, dm):
        for h in range(4):
            dma(dC[h], Sst[:, h, 0:64], r=["Sst"])
            dma(dn[h].unsqueeze(1), Sst[:, h, 64:65], r=["Sst"])
        tt("dve", mfin[:, :], Bprev[:, 0:1], MUALL[:, 0:1], ALU.add, ["Bprev", "MUALL"], ["mfin"])
        dma(dm, mfin[:, :], r=["mfin"])

    try:
        chk(1)
        for s in range(NSEQ):
            memkv_seq(s)
            memset("dve", Sst[:, :, :], 0.0, ["Sst"])
            memset("dve", Bprev[:, :], 0.0, ["Bprev"])
            memset("dve", MUALL[:, 0:1], 0.0, ["MUALL"])
            for b in range(NBLK):
                for j in range(4):
                    ti = 4 * b + j
                    rows = slice(ti * 128, (ti + 1) * 128)
                    phase1_tile(xp[s, rows, :], 128, ti, j, pk[s, rows, :], pv[s, rows, :])
                gates_block(128, 4)
                ktiles = [(t, 128, None) for t in range(4 * b)] + [(4 * b + i, 128, i) for i in range(4)]
                attention_block(128, 4, ktiles, False)
                mem_block(128, 4)
                mlstm_block(128, 4)
                for j in range(4):
                    ti = 4 * b + j
                    rows = slice(ti * 128, (ti + 1) * 128)
                    phase3_tile(xp[s, rows, :], yp[s, rows, :], 128, j)
            state_out(pC[s], pn[s], pm[s].unsqueeze(1))

        if DO_SAMPLE:
            for t in range(16):
                xt, kx = XT.next()
                dma(xt[:, 0:512], ck[t * 128:(t + 1) * 128, :], w=[kx])
                dma(xt[:, 512:1024], cv[t * 128:(t + 1) * 128, :], w=[kx])
                nb, kn = NB.next()
                cpy("dve", nb[:, :], xt[:, 0:512], [kx], [kn])
                cpy("pool", v_aug[:, t, :, 0:128], xt[:, 512:1024].rearrange("p (h d) -> p h d", h=4),
                    [kx, "v_ones"], [("v", t)])
                for h in range(4):
                    tr(TB[:, h * 128:(h + 1) * 128], nb[:, h * 128:(h + 1) * 128], identb[:, :], [kn, "identb"], [TBK])
                cpy("act", kT[:, :, t * 128:(t + 1) * 128], TB[:, 0:512].rearrange("p (h t) -> p h t", h=4),
                    [TBK], [("kT", t)])
            for nt in range(2):
                xt, kx = XT.next()
                dma(xt[:, 0:256], cmk[nt * 128:(nt + 1) * 128, :], w=[kx])
                dma(xt[:, 256:512], cmv[nt * 128:(nt + 1) * 128, :], w=[kx])
                nb, kn = NB.next()
                cpy("dve", nb[:, 0:256], xt[:, 0:256], [kx], [kn])
                cpy("pool", mv_aug[:, nt, :, 0:64], xt[:, 256:512].rearrange("p (h d) -> p h d", h=4),
                    [kx, "mv_ones"], ["mv"])
                for hp in range(2):
                    tr(TB[:, hp * 128:(hp + 1) * 128], nb[:, hp * 128:(hp + 1) * 128], identb[:, :],
                       [kn, "identb"], [TBK])
                cpy("act", mkT[:, :, nt * 128:(nt + 1) * 128], TB[:, 0:256].rearrange("p (h t) -> p h t", h=2),
                    [TBK], ["mkT"])
            chk(2)
            for h in range(4):
                dma(Sst[:, h, 0:64], sC[h], w=["Sst"])
                dma(Sst[:, h, 64:65], sn[h].unsqueeze(1), w=["Sst"])
            memset("dve", Bprev[:, :], 0.0, ["Bprev"])
            cpy("dve", MUALL[:, 0:1], cpc("sm_pp")[0:4, :], ["cp"], ["MUALL"])
            chk(3)
            phase1_tile(xs[:, :], 16, 16, 0, sk[:, :], sv[:, :])
            chk(4)
            gates_block(16, 1)
            chk(5)
            attention_block(16, 1, [(t, 128, None) for t in range(16)] + [(16, 16, 0)], True)
            chk(6)
            mem_block(16, 1)
            chk(7)
            mlstm_block(16, 1)
            chk(8)
            phase3_tile(xs[:, :], ys[:, :], 16, 0)
            chk(9)
            state_out(sCo, sno, smo.rearrange("o h -> h o"))

    except _Stop:
        pass

    S.emit(st)
    st.close()
    return nc


_NC_CACHE = {}


def _get_nc(key=(4, 4, True)):
    if key not in _NC_CACHE:
        _NC_CACHE[key] = build_program(*key)
    return _NC_CACHE[key]


def _in_maps(inp, NSEQ=4):
    maps = []
    c32 = lambda a: np.ascontiguousarray(a, dtype=np.float32)
    for c in range(8):
        m = {
            "xp": c32(inp["x_prompt"][4 * c:4 * c + max(NSEQ, 1)]),
            "memp": c32(inp["mem_prompt"][4 * c:4 * c + max(NSEQ, 1)]),
            "xs": c32(inp["x_sample"][c]),
            "ck": c32(inp["cache_attn_k"][0, c].reshape(2048, 512)),
            "cv": c32(inp["cache_attn_v"][0, c].reshape(2048, 512)),
            "sC": c32(inp["state_mlstm_C"][0, c]),
            "sn": c32(inp["state_mlstm_n"][0, c]),
            "cmk": c32(inp["cache_mem_k"][0, c].reshape(256, 256)),
            "cmv": c32(inp["cache_mem_v"][0, c].reshape(256, 256)),
            "w_in": c32(inp["w_in"][0]),
            "w_out": c32(inp["w_out"][0]),
            "w_mk": c32(inp["w_mk"][0]),
            "w_mv": c32(inp["w_mv"][0]),
            "cp": _make_cp(inp, c),
        }
        maps.append(m)
    return maps


def kernel(**inputs):
    inp = {k: np.asarray(v) for k, v in inputs.items()}
    nc = _get_nc()
    res = run_bass_kernel_spmd(nc, _in_maps(inp), core_ids=list(range(8)))
    R = res.results
    cat = lambda name: np.concatenate([np.asarray(r[name]) for r in R], axis=0)
    stk = lambda name: np.stack([np.asarray(r[name]) for r in R], axis=0)
    y_prompt = cat("yp")
    y_sample = stk("ys")
    p_attn_k = cat("pk").reshape(1, 32, 2048, 4, 128)
    p_attn_v = cat("pv").reshape(1, 32, 2048, 4, 128)
    p_C = cat("pC").reshape(1, 32, 4, 64, 64)
    p_n = cat("pn").reshape(1, 32, 4, 64)
    p_m = cat("pm").reshape(1, 32, 4)
    p_mk = cat("pmk").reshape(1, 32, 256, 4, 64)
    p_mv = cat("pmv").reshape(1, 32, 256, 4, 64)
    s_k = stk("sk").reshape(1, 8, 16, 4, 128)
    s_v = stk("sv").reshape(1, 8, 16, 4, 128)
    s_C = stk("sCo").reshape(1, 8, 4, 64, 64)
    s_n = stk("sno").reshape(1, 8, 4, 64)
    s_m = stk("smo").reshape(1, 8, 4)
    return (y_prompt, y_sample, p_attn_k, p_attn_v, p_C, p_n, p_m, p_mk, p_mv,
            s_k, s_v, s_C, s_n, s_m)
```

```python
import math
from contextlib import ExitStack

import numpy as np
import concourse.bass as bass
import concourse.mybir as mybir
from concourse.bass_utils import run_bass_kernel_spmd

F32 = mybir.dt.float32
BF16 = mybir.dt.bfloat16
AF = mybir.ActivationFunctionType
ALU = mybir.AluOpType
AX = mybir.AxisListType

N_DMA_SEMS = 40
EPS = 1e-6
SLOPES = [2.0 ** (-8.0 * (h + 1) / 4) for h in range(4)]
LAM_INIT = 0.8 - 0.6 * math.exp(0.0)
NIN = 3848
BIG = 240000.0


class _Op:
    __slots__ = ("eng", "fn", "deps", "has_dep", "is_dma", "semkey", "val", "ie")


class Sched:
    def __init__(self, nc):
        self.nc = nc
        self.ops = []
        self.lw = {}
        self.rd = {}
        self.eng_n = {"pe": 0, "act": 0, "dve": 0, "pool": 0, "sp": 0}

    def add(self, eng, fn, reads=(), writes=(), dma=False):
        i = len(self.ops)
        deps = set()
        for k in reads:
            w = self.lw.get(k)
            if w is not None:
                deps.add(w)
            if isinstance(k, tuple) and k[0] == "pb":
                for r in self.rd.get(k, ()):
                    if self.ops[r].eng != eng:
                        deps.add(r)
        for k in writes:
            w = self.lw.get(k)
            if w is not None:
                deps.add(w)
            for r in self.rd.get(k, ()):
                deps.add(r)
        op = _Op()
        op.eng = eng
        op.fn = fn
        op.is_dma = dma
        op.has_dep = False
        op.semkey = None
        op.val = 0
        op.ie = self.eng_n[eng]
        self.eng_n[eng] += 1
        keep = []
        for d in deps:
            p = self.ops[d]
            if p.eng == eng and not p.is_dma:
                if eng == "pe" or eng == "sp":
                    continue
                if eng != "pool" and op.ie - p.ie > 2:
                    continue
            p.has_dep = True
            keep.append(d)
        op.deps = keep
        for k in reads:
            self.rd.setdefault(k, []).append(i)
        for k in writes:
            self.lw[k] = i
            self.rd[k] = []
        self.ops.append(op)
        return i

    def emit(self, stack):
        nc = self.nc
        engobj = {"pe": nc.tensor, "act": nc.scalar, "dve": nc.vector,
                  "pool": nc.gpsimd, "sp": nc.sync}
        esem = {e: stack.enter_context(nc.semaphore("s_" + e))
                for e in ("pe", "act", "dve", "pool")}
        dsem = [stack.enter_context(nc.semaphore("d%d" % i)) for i in range(N_DMA_SEMS)]
        waited = {e: {} for e in engobj}
        cnt = {e: 0 for e in engobj}
        dcnt = [0] * N_DMA_SEMS
        rr = 0
        rrp = 0
        for op in self.ops:
            E = engobj[op.eng]
            need = {}
            for d in op.deps:
                p = self.ops[d]
                if need.get(p.semkey, 0) < p.val:
                    need[p.semkey] = p.val
            s = None
            if op.is_dma:
                if op.eng == "pool":
                    s = N_DMA_SEMS - 8 + (rrp % 8)
                    rrp += 1
                else:
                    s = rr % (N_DMA_SEMS - 8)
                    rr += 1
                if dcnt[s] > 0 and need.get(("d", s), 0) < dcnt[s]:
                    need[("d", s)] = dcnt[s]
                dcnt[s] += 16
                op.semkey = ("d", s)
                op.val = dcnt[s]
            w = waited[op.eng]
            for key, val in need.items():
                if w.get(key, 0) >= val:
                    continue
                so = dsem[key[1]] if key[0] == "d" else esem[key[1]]
                E.wait_ge(so, val)
                w[key] = val
            inst = op.fn()
            if op.is_dma:
                inst.then_inc(dsem[s], 16)
            elif op.has_dep:
                cnt[op.eng] += 1
                op.semkey = ("e", op.eng)
                op.val = cnt[op.eng]
                inst.then_inc(esem[op.eng], 1)
        for s in range(N_DMA_SEMS):
            if dcnt[s] > 0:
                nc.sync.wait_ge(dsem[s], dcnt[s])
        return cnt


class Rot:
    def __init__(self, aps, name):
        self.aps = aps
        self.name = name
        self.i = 0

    def next(self):
        k = self.i % len(self.aps)
        self.i += 1
        return self.aps[k], (self.name, k)


def _cp_layout():
    off = {}
    c = 0
    for name, n in [("ident", 128), ("maskU", 128), ("corr", 512), ("AB", 64), ("ABS", 68),
                    ("gqa_pp", 1), ("gqm_pp", 1), ("gkm_rep", 64), ("gka_rep", 64),
                    ("gnorm_pp", 8), ("gmem_pp", 8), ("rows_pp", 8), ("lam4", 256),
                    ("bi_pp", 1), ("bf_pp", 1), ("sm_pp", 1), ("SEL", 128), ("SEL2", 4)]:
        off[name] = (c, c + n)
        c += n
    return off, c


CP_OFF, NCP = _cp_layout()


def _make_cp(inp, core):
    cp = np.zeros((128, NCP), np.float32)

    def put(name, arr):
        a, b = CP_OFF[name]
        cp[:, a:b] = arr

    p = np.arange(128)
    put("ident", np.eye(128, dtype=np.float32))
    put("maskU", (p[:, None] <= p[None, :]).astype(np.float32))
    corr = np.zeros((128, 4, 128), np.float32)
    k = p[:, None]
    q = p[None, :]
    for h in range(4):
        c_ = np.where(k > q, -16.0 * SLOPES[h] * (k - q), 0.0)
        c_ = np.where((k // 64) > (q // 64), -BIG, c_)
        corr[:, h, :] = c_
    put("corr", corr.reshape(128, 512))
    AB = np.zeros((128, 4, 16), np.float32)
    for h in range(4):
        for r in range(16):
            AB[:, h, r] = SLOPES[h] * (p - 127 - 128 * r)
    put("AB", AB.reshape(128, 64))
    ABS = np.zeros((128, 4, 17), np.float32)
    for h in range(4):
        for t in range(17):
            ABS[:, h, t] = SLOPES[h] * np.minimum(128 * t + p - 2063, 0)
    put("ABS", ABS.reshape(128, 68))
    put("gqa_pp", inp["g_qa"][0][p % 64][:, None])
    put("gqm_pp", inp["g_qm"][0][p % 64][:, None])
    put("gkm_rep", np.broadcast_to(inp["g_km"][0][None, :], (128, 64)))
    put("gka_rep", np.broadcast_to(inp["g_ka"][0][None, :], (128, 64)))
    put("gnorm_pp", inp["g_norm"][0].reshape(8, 128).T)
    put("gmem_pp", inp["g_mem"][0].reshape(8, 128).T)
    rows = np.ones((128, 8), np.float32)
    rows[:, 0:4] = inp["g_subln"][0][:, None]
    rows[:, 4:6] = inp["g_mh"][0][p % 64][:, None]
    put("rows_pp", rows)
    lam4 = np.concatenate([inp["lam_q1"][0], inp["lam_k1"][0], inp["lam_q2"][0], inp["lam_k2"][0]])
    put("lam4", np.broadcast_to(lam4[None, :], (128, 256)))
    put("bi_pp", inp["b_i"][0][p % 4][:, None])
    put("bf_pp", inp["b_f"][0][p % 4][:, None])
    put("sm_pp", inp["state_mlstm_m"][0, core][p % 4][:, None])
    SEL = np.zeros((128, 128), np.float32)
    SEL[0:4, :] = 1.0
    put("SEL", SEL)
    SEL2 = np.zeros((128, 4), np.float32)
    SEL2[0:4, 0:4] = np.eye(4, dtype=np.float32)
    put("SEL2", SEL2)
    return cp


class _Stop(Exception):
    pass


def build_program(NSEQ=4, NBLK=4, DO_SAMPLE=True, STAGE=99):
    nc = bass.Bass("TRN2", target_bir_lowering=False)
    S = Sched(nc)

    def din(name, shape):
        return nc.dram_tensor(name, shape, F32, kind="ExternalInput").ap()

    def dout(name, shape):
        return nc.dram_tensor(name, shape, F32, kind="ExternalOutput").ap()

    NS = max(NSEQ, 1)
    xp = din("xp", [NS, 2048, 1024])
    memp = din("memp", [NS, 256, 1024])
    xs = din("xs", [16, 1024])
    ck = din("ck", [2048, 512])
    cv = din("cv", [2048, 512])
    sC = din("sC", [4, 64, 64])
    sn = din("sn", [4, 64])
    cmk = din("cmk", [256, 256])
    cmv = din("cmv", [256, 256])
    w_in = din("w_in", [1024, NIN])
    w_out = din("w_out", [1024, 1024])
    w_mk = din("w_mk", [1024, 256])
    w_mv = din("w_mv", [1024, 256])
    cpd = din("cp", [128, NCP])

    yp = dout("yp", [NS, 2048, 1024])
    pk = dout("pk", [NS, 2048, 512])
    pv = dout("pv", [NS, 2048, 512])
    pC = dout("pC", [NS, 4, 64, 64])
    pn = dout("pn", [NS, 4, 64])
    pm = dout("pm", [NS, 4])
    pmk = dout("pmk", [NS, 256, 256])
    pmv = dout("pmv", [NS, 256, 256])
    ys = dout("ys", [16, 1024])
    sk = dout("sk", [16, 512])
    sv = dout("sv", [16, 512])
    sCo = dout("sCo", [4, 64, 64])
    sno = dout("sno", [4, 64])
    smo = dout("smo", [1, 4])

    st = ExitStack()

    def chk(n):
        if STAGE == n:
            raise _Stop()

    def sb(name, shape, dt=F32):
        return st.enter_context(nc.sbuf_tensor("sb_" + name, shape, dt))

    cp = sb("cp", [128, NCP])
    w_in_bf = sb("w_in_bf", [128, 8, NIN], BF16)
    w_out_bf = sb("w_out_bf", [128, 8, 1024], BF16)
    w_mkv_bf = sb("w_mkv_bf", [128, 8, 512], BF16)
    identb = sb("identb", [128, 128], BF16)
    corrb = sb("corrb", [128, 4, 128], BF16)
    zerob = sb("zerob", [128, 512], BF16)
    cm05 = sb("cm05", [128, 8])
    lamt = sb("lamt", [128, 8])
    rs8 = sb("rs8", [128, 8])
    nbf = sb("nbf", [128, 1])

    kT = sb("kT", [128, 4, 2064], BF16)
    v_aug = sb("v_aug", [128, 17, 4, 130], BF16)
    mkT = sb("mkT", [128, 2, 256], BF16)
    mv_aug = sb("mv_aug", [128, 2, 4, 66], BF16)

    qT = sb("qT", [128, 4, 512], BF16)
    mix = sb("mix", [128, 4, 1024], BF16)
    OBG = sb("OBG", [128, 4, 256], BF16)
    QKB = sb("QKB", [128, 4, 512], BF16)
    QKT = sb("QKT", [64, 8, 512], BF16)
    vb_aug = sb("vb_aug", [128, 4, 4, 66], BF16)
    qmT = sb("qmT", [128, 2, 512], BF16)
    GIF = sb("GIF", [128, 4, 8])
    WE = sb("WE", [128, 4, 8])
    C0B = sb("C0B", [64, 4, 4])
    Sst = sb("Sst", [64, 4, 66])
    Sd = sb("Sd", [64, 4, 66])
    Cs = sb("Cs", [64, 4, 66], BF16)
    Bprev = sb("Bprev", [4, 1])
    MUALL = sb("MUALL", [4, 8])
    UM = sb("UM", [4, 4])
    C0 = sb("C0", [4, 4])
    CD = sb("CD", [4, 4, 4])
    mfin = sb("mfin", [4, 1])

    XT = Rot([sb("xt%d" % i, [128, 1024])[:] for i in range(2)], "xt")
    xn = sb("xn", [128, 1024], BF16)
    HT = Rot([sb("hT%d" % i, [128, 8, 128], BF16)[:] for i in range(2)], "hT")
    ZF = Rot([sb("zf%d" % i, [128, 512])[:] for i in range(2)], "zf")
    SQ = sb("sq", [128, 512])
    KOUT = Rot([sb("kvout%d" % i, [128, 512])[:] for i in range(3)], "kvout")
    VOUT = KOUT
    TH = Rot([sb("th%d" % i, [128, 512])[:] for i in range(2)], "th")
    NB = Rot([sb("nb%d" % i, [128, 512], BF16)[:] for i in range(2)], "nb")
    PT = Rot([sb("pt%d" % i, [128, 512], BF16)[:] for i in range(3)], "pt")
    STT = Rot([sb("stat%d" % i, [128, 72])[:] for i in range(4)], "stat")
    O1 = Rot([sb("o1s%d" % i, [128, 128])[:] for i in range(2)], "o1s")
    OO = Rot([sb("oo%d" % i, [128, 128])[:] for i in range(4)], "oo")
    mixT = sb("mixT", [128, 8, 128], BF16)
    AT = sb("AT", [128, 4, 128], BF16)
    vw = sb("vw", [128, 4, 66], BF16)
    HN = sb("HN", [128, 4, 64])
    OM = sb("OM", [128, 4, 64])
    GA, KGA = SQ[0:4, :], "sq"
    GB, KGB = TH.aps[0][0:4, :], ("th", 0)
    GU, KGU = TH.aps[1][0:4, :], ("th", 1)
    GW, KGW = ZF.aps[0][0:4, :], ("zf", 0)
    GE, KGE = ZF.aps[1][0:4, :], ("zf", 1)

    PB = [st.enter_context(nc.psum_tensor("pb%d" % i, [128, 512], F32)) for i in range(8)]
    MM = Rot([PB[0][:], PB[1][:], PB[2][:]], "pb_mm")
    MM.keys = [("pb", 0), ("pb", 1), ("pb", 2)]
    ACC = [PB[3], PB[4], PB[5]]
    ACCK = [("pb", 3), ("pb", 4), ("pb", 5)]
    TBf = PB[6]
    TB = PB[6][:].bitcast(BF16)
    TBK = ("pb", 6)
    TF = PB[7]
    TFK = ("pb", 7)

    def mmnext():
        k = MM.i % 3
        MM.i += 1
        return PB[k], ("pb", k)

    def cpc(name, a=None, b=None):
        o, e = CP_OFF[name]
        if a is None:
            return cp[:, o:e]
        return cp[:, o + a:o + b]

    def dma(out, in_, r=(), w=(), q="sp", **kw):
        if q == "sp":
            S.add("sp", lambda: nc.sync.dma_start(out=out, in_=in_, **kw), r, w, dma=True)
        else:
            S.add("pool", lambda: nc.gpsimd.dma_start(out=out, in_=in_, **kw), r, w, dma=True)

    def mm(out, lhsT, rhs, start, stop, r, w):
        S.add("pe", lambda: nc.tensor.matmul(out, lhsT=lhsT, rhs=rhs, start=start, stop=stop,
                                             skip_group_check=True), r, w)

    def tr(out, in_, ident, r, w):
        S.add("pe", lambda: nc.tensor.transpose(out=out, in_=in_, identity=ident), r, w)

    def act(out, in_, func, r, w, bias=0.0, scale=1.0, accum=None):
        if accum is None:
            S.add("act", lambda: nc.scalar.activation(out=out, in_=in_, func=func, bias=bias, scale=scale), r, w)
        else:
            S.add("act", lambda: nc.scalar.activation(out=out, in_=in_, func=func, bias=bias, scale=scale,
                                                      accum_out=accum), r, w)

    def engo(e):
        return nc.vector if e == "dve" else nc.gpsimd

    def ts(e, out, in0, s1, s2, op0, op1, r, w):
        if s2 is None:
            S.add(e, lambda: engo(e).tensor_scalar(out=out, in0=in0, scalar1=s1, scalar2=None, op0=op0), r, w)
        else:
            S.add(e, lambda: engo(e).tensor_scalar(out=out, in0=in0, scalar1=s1, scalar2=s2, op0=op0, op1=op1), r, w)

    def tt(e, out, in0, in1, op, r, w):
        S.add(e, lambda: engo(e).tensor_tensor(out=out, in0=in0, in1=in1, op=op), r, w)

    def stt(out, in0, scalar, in1, op0, op1, r, w):
        S.add("dve", lambda: nc.vector.scalar_tensor_tensor(out=out, in0=in0, scalar=scalar, in1=in1,
                                                            op0=op0, op1=op1), r, w)

    def cpy(e, out, in_, r, w):
        if e == "act":
            S.add("act", lambda: nc.scalar.copy(out=out, in_=in_), r, w)
        else:
            S.add(e, lambda: engo(e).tensor_copy(out=out, in_=in_), r, w)

    def memset(e, ap, val, w):
        S.add(e, lambda: engo(e).memset(ap, val), (), w)

    def red(out, in_, op, r, w):
        S.add("dve", lambda: nc.vector.tensor_reduce(out=out, in_=in_, axis=AX.X, op=op), r, w)

    def recip(out, in_, r, w):
        S.add("dve", lambda: nc.vector.reciprocal(out=out, in_=in_), r, w)

    def scan(out, d0, d1, init, op0, op1, r, w):
        S.add("dve", lambda: nc.vector.tensor_tensor_scan(out=out, data0=d0, data1=d1, initial=init,
                                                          op0=op0, op1=op1), r, w)

    def rstd_from_ss(stt_ap, kst, c_ss, c_tmp, c_out, n, T, inv):
        ts("dve", stt_ap[0:T, c_tmp:c_tmp + n], stt_ap[0:T, c_ss:c_ss + n], inv, EPS, ALU.mult, ALU.add,
           [kst], [kst])
        tt("pool", stt_ap[0:T, c_out:c_out + n], stt_ap[0:T, c_tmp:c_tmp + n], cm05[0:T, 0:n], ALU.pow,
           [kst, "cm05"], [kst])

    dma(cp[:], cpd, w=["cp"])
    cpy("dve", identb[:], cpc("ident"), ["cp"], ["identb"])
    identf = cpc("ident")
    cpy("dve", corrb[:].rearrange("p h q -> p (h q)"), cpc("corr"), ["cp"], ["corrb"])
    memset("pool", zerob[:], 0.0, ["zerob"])
    memset("pool", cm05[:], -0.5, ["cm05"])
    memset("pool", v_aug[:, :, :, 128:129], 1.0, ["v_ones"])
    memset("pool", vb_aug[:, :, :, 64:65], 1.0, ["vb_ones"])
    memset("pool", mv_aug[:, :, :, 64:65], 1.0, ["mv_ones"])
    ts("dve", rs8[:, 0:4], cpc("rows_pp", 0, 4), 0.5 * (1.0 - LAM_INIT), None, ALU.mult, None, ["cp"], ["rs8"])
    ts("dve", rs8[:, 4:6], cpc("rows_pp", 4, 6), 0.25, None, ALU.mult, None, ["cp"], ["rs8"])
    ts("dve", rs8[:, 6:8], cpc("rows_pp", 6, 8), 0.5, None, ALU.mult, None, ["cp"], ["rs8"])
    ts("dve", nbf[:], cpc("bf_pp"), -1.0, None, ALU.mult, None, ["cp"], ["nbf"])
    wv = w_in.rearrange("(k p) n -> p k n", p=128)
    wov = w_out.rearrange("(k p) n -> p k n", p=128)
    wkv = w_mk.rearrange("(k p) n -> p k n", p=128)
    wvv = w_mv.rearrange("(k p) n -> p k n", p=128)

    def wload(dst, src, n, scal, key, extra=None):
        xt, kx = XT.next()
        dma(xt[:, 0:n], src, w=[kx])
        if extra is None:
            ts("dve", dst, xt[:, 0:n], scal, None, ALU.mult, None, [kx, "cp", "rs8"], [key])
        else:
            ts("dve", dst, xt[:, 0:n], scal, extra, ALU.mult, ALU.mult, [kx, "cp", "rs8"], [key])

    for kc in range(8):
        wload(w_mkv_bf[:, kc, 0:256], wkv[:, kc, :], 256, cpc("gmem_pp", kc, kc + 1), ("w_mkv", kc, 0))
        wload(w_mkv_bf[:, kc, 256:512], wvv[:, kc, :], 256, cpc("gmem_pp", kc, kc + 1), ("w_mkv", kc, 1))
    for kc in range(8):
        g = cpc("gnorm_pp", kc, kc + 1)
        wload(w_in_bf[:, kc, 0:1024], wv[:, kc, 0:1024], 1024, g, ("w_in", kc, 0))
        wload(w_in_bf[:, kc, 1024:2048], wv[:, kc, 1024:2048], 1024, g, ("w_in", kc, 1))
        wload(w_in_bf[:, kc, 2048:2304], wv[:, kc, 2048:2304], 256, g, ("w_in", kc, 2))
        wload(w_in_bf[:, kc, 2304:2560], wv[:, kc, 2304:2560], 256, g, ("w_in", kc, 2), 0.125)
        wload(w_in_bf[:, kc, 2560:3072], wv[:, kc, 2560:3072], 512, g, ("w_in", kc, 2))
        wload(w_in_bf[:, kc, 3072:3840], wv[:, kc, 3080:3848], 768, g, ("w_in", kc, 3))
        wload(w_in_bf[:, kc, 3840:3848], wv[:, kc, 3072:3080], 8, g, ("w_in", kc, 3))
    for kc in range(8):
        wload(w_out_bf[:, kc, :], wov[:, kc, :], 1024, rs8[:, kc:kc + 1], ("w_out", kc))
    W_IN_K = [("w_in", kc, i) for kc in range(8) for i in range(4)]
    W_OUT_K = [("w_out", kc) for kc in range(8)]
    W_MKV_K = [("w_mkv", kc, i) for kc in range(8) for i in range(2)]
    l4 = cpc("lam4")
    tt("dve", SQ[:, 0:64], l4[:, 0:64], l4[:, 64:128], ALU.mult, ["cp"], ["sq"])
    red(lamt[:, 0:1], SQ[:, 0:64], ALU.add, ["sq"], ["lamt"])
    tt("dve", SQ[:, 64:128], l4[:, 128:192], l4[:, 192:256], ALU.mult, ["cp"], ["sq"])
    red(lamt[:, 1:2], SQ[:, 64:128], ALU.add, ["sq"], ["lamt"])
    act(lamt[:, 2:4], lamt[:, 0:2], AF.Exp, ["lamt"], ["lamt"])
    stt(lamt[:, 4:5], lamt[:, 2:3], LAM_INIT, lamt[:, 3:4], ALU.add, ALU.subtract, ["lamt"], ["lamt"])
    ts("dve", lamt[:, 5:6], lamt[:, 4:5], -1.0, None, ALU.mult, None, ["lamt"], ["lamt"])

    def front_load(src_ap, T):
        xt, kx = XT.next()
        dma(xt[0:T, :], src_ap, w=[kx])
        return xt, kx

    def front_compute(xt, kx, T):
        sa, ks = STT.next()
        act(xn[0:T, :], xt[0:T, :], AF.Square, [kx], ["xn", ks], accum=sa[0:T, 0:1])
        rstd_from_ss(sa, ks, 0, 1, 2, 1, T, 1.0 / 1024)
        ts("dve", xn[0:T, :], xt[0:T, :], sa[0:T, 2:3], None, ALU.mult, None, [kx, ks], ["xn"])
        for kc in range(8):
            tr(TB[:, kc * 128:kc * 128 + T], xn[0:T, kc * 128:(kc + 1) * 128], identb[0:T, 0:T],
               ["xn", "identb"], [TBK])
        hT, kh = HT.next()
        cpy("act", hT[:, :, 0:T], TB.rearrange("p (k t) -> p k t", k=8)[:, :, 0:T], [TBK], [kh])
        return xt, kx, hT, kh, sa, ks

    def front(src_ap, T):
        xt, kx = front_load(src_ap, T)
        return front_compute(xt, kx, T)

    def group_norm_rstd(src, ksrc, T, ng, sa, ks, base):
        tt("dve", SQ[0:T, 0:ng * 64], src, src, ALU.mult, [ksrc], ["sq"])
        red(sa[0:T, base:base + ng], SQ[0:T, 0:ng * 64].rearrange("p (g d) -> p g d", d=64), ALU.add,
            ["sq"], [ks])
        rstd_from_ss(sa, ks, base, base + ng, base + 2 * ng, ng, T, 1.0 / 64)
        return sa[0:T, base + 2 * ng:base + 3 * ng]

    def phase1_tile(fr, T, ti, j, k_dst, v_dst, later, hook=None):
        xt, kx, hT, kh, sa, ks = fr
        tok = slice(j * 128, j * 128 + T)
        ktok = slice(ti * 128, ti * 128 + T)
        for g in range(8):
            c0 = g * 512
            n = min(512, NIN - c0)
            ps, kp = mmnext()
            for kc in range(8):
                mm(ps[0:T, 0:n], hT[:, kc, 0:T], w_in_bf[:, kc, c0:c0 + n], kc == 0, kc == 7,
                   [kh] + W_IN_K, [kp])
            for f_ in later:
                f_()
            del later[:]
            if g == 1 and hook is not None:
                hook()
            if g == 0 or g == 1:
                zf, kz = ZF.next()
                cpy("act", zf[0:T, :], ps[0:T, 0:512], [kp], [kz])
                r8 = group_norm_rstd(zf[0:T, :], kz, T, 8, sa, ks, 8 if g == 0 else 32)
                nb, kn = NB.next()
                if g == 0:
                    tt("dve", nb[0:T, :].rearrange("p (g d) -> p g d", d=64),
                       zf[0:T, :].rearrange("p (g d) -> p g d", d=64),
                       r8.unsqueeze(2).to_broadcast([T, 8, 64]), ALU.mult, [kz, ks], [kn])
                else:
                    tt("dve", zf[0:T, :].rearrange("p (g d) -> p g d", d=64),
                       zf[0:T, :].rearrange("p (g d) -> p g d", d=64),
                       r8.unsqueeze(2).to_broadcast([T, 8, 64]), ALU.mult, [kz, ks], [kz])
                    ko, kk = KOUT.next()
                    tt("dve", ko[0:T, :].rearrange("p (g d) -> p g d", d=64),
                       zf[0:T, :].rearrange("p (g d) -> p g d", d=64),
                       cpc("gka_rep")[0:T, :].unsqueeze(1).to_broadcast([T, 8, 64]), ALU.mult,
                       [kz, "cp"], [kk])
                    dma(k_dst, ko[0:T, :], r=[kk])
                    cpy("dve", nb[0:T, :], ko[0:T, :], [kk], [kn])

                def d01(g=g, nb=nb, kn=kn):
                    for h in range(4):
                        tr(TB[:, h * 128:h * 128 + T], nb[0:T, h * 128:(h + 1) * 128], identb[0:T, 0:T],
                           [kn, "identb"], [TBK])
                    src = TB[:, 0:512].rearrange("p (h t) -> p h t", h=4)[:, :, 0:T]
                    if g == 0:
                        ts("dve", qT[:, :, tok], src, cpc("gqa_pp"), None, ALU.mult, None, [TBK, "cp"], [("qT", j)])
                    else:
                        cpy("act", kT[:, :, ktok], src, [TBK], [("kT", ti)])
                later.append(d01)
            elif g == 2:
                vo, kv = VOUT.next()
                cpy("act", vo[0:T, :], ps[0:T, 0:512], [kp], [kv])
                dma(v_dst, vo[0:T, :], r=[kv])
                cpy("pool", v_aug[0:T, ti, :, 0:128], vo[0:T, :].rearrange("p (h d) -> p h d", h=4),
                    [kv, "v_ones"], [("v", ti)])
            elif g == 3:
                th, kt = TH.next()
                act(th[0:T, :], ps[0:T, 0:512], AF.Tanh, [kp], [kt], scale=0.5)
                stt(mix[0:T, j, 0:512], th[0:T, :], 1.0, ps[0:T, 0:512], ALU.add, ALU.mult,
                    [kt, kp], [("mix", j, 0)])
            elif g == 4:
                cpy("act", QKB[0:T, j, :], ps[0:T, 0:512], [kp], [("QKB", j)])

                def d4():
                    for i in range(8):
                        tr(TB[0:64, i * 128:i * 128 + T], QKB[0:T, j, i * 64:(i + 1) * 64], identb[0:T, 0:T],
                           [("QKB", j), "identb"], [TBK])
                    cpy("dve", QKT[:, :, tok], TB[0:64, :].rearrange("p (h t) -> p h t", h=8)[:, :, 0:T],
                        [TBK], [("QKT", j)])
                later.append(d4)
            elif g == 5:
                cpy("dve", vb_aug[0:T, j, :, 0:64], ps[0:T, 0:256].rearrange("p (h d) -> p h d", h=4),
                    [kp, "vb_ones"], [("vb", j)])
                act(OBG[0:T, j, :], ps[0:T, 256:512], AF.Tanh, [kp], [("OBG", j)], scale=0.5)
            elif g == 6:
                th, kt = TH.next()
                act(th[0:T, 0:256], ps[0:T, 0:256], AF.Tanh, [kp], [kt], scale=0.5)
                stt(th[0:T, 256:512], th[0:T, 0:256], 1.0, ps[0:T, 0:256], ALU.add, ALU.mult, [kt, kp], [kt])
                stt(mix[0:T, j, 512:768], OBG[0:T, j, :], 1.0, th[0:T, 256:512], ALU.add, ALU.mult,
                    [("OBG", j), kt], [("mix", j, 1)])
                zf, kz = ZF.next()
                cpy("act", zf[0:T, 0:256], ps[0:T, 256:512], [kp], [kz])
                r4 = group_norm_rstd(zf[0:T, 0:256], kz, T, 4, sa, ks, 56)
                nb, kn = NB.next()
                tt("dve", nb[0:T, 0:256].rearrange("p (g d) -> p g d", d=64),
                   zf[0:T, 0:256].rearrange("p (g d) -> p g d", d=64),
                   r4.unsqueeze(2).to_broadcast([T, 4, 64]), ALU.mult, [kz, ks], [kn])

                def d6(nb=nb, kn=kn):
                    for hp in range(2):
                        tr(TB[:, hp * 128:hp * 128 + T], nb[0:T, hp * 128:(hp + 1) * 128], identb[0:T, 0:T],
                           [kn, "identb"], [TBK])
                    ts("dve", qmT[:, :, tok], TB[:, 0:256].rearrange("p (h t) -> p h t", h=2)[:, :, 0:T],
                       cpc("gqm_pp"), None, ALU.mult, None, [TBK, "cp"], [("qmT", j)])
                later.append(d6)
            else:
                th, kt = TH.next()
                act(th[0:T, 0:256], ps[0:T, 0:256], AF.Tanh, [kp], [kt], scale=0.5)
                stt(mix[0:T, j, 768:1024], th[0:T, 0:256], 1.0, ps[0:T, 0:256], ALU.add, ALU.mult,
                    [kt, kp], [("mix", j, 2)])
                cpy("dve", GIF[0:T, j, :], ps[0:T, 256:264], [kp], [("GIF", j)])

    def phase1_block(tiles, T):
        later = []
        xt, kx = front_load(tiles[0][0], T)
        fr = front_compute(xt, kx, T)
        for idx, (src_ap, ti, j, k_dst, v_dst) in enumerate(tiles):
            nxt = {}
            hook = None
            if idx + 1 < len(tiles):
                nx = front_load(tiles[idx + 1][0], T)

                def hook(nx=nx, nxt=nxt):
                    nxt["fr"] = front_compute(nx[0], nx[1], T)
            phase1_tile(fr, T, ti, j, k_dst, v_dst, later, hook)
            fr = nxt.get("fr")
        for f_ in later:
            f_()
        del later[:]

    def gates_block(T, ntile):
        NQ = ntile * 128
        ibT = TF[0:4, 0:NQ]
        fbT = TBf[0:4, 0:NQ]
        for j in range(ntile):
            tr(TF[0:4, j * 128:j * 128 + T], GIF[0:T, j, 0:4], identf[0:T, 0:T], [("GIF", j), "cp"], [TFK])
            tr(TBf[0:4, j * 128:j * 128 + T], GIF[0:T, j, 4:8], identf[0:T, 0:T], [("GIF", j), "cp"], [TBK])
        if T < 128:
            memset("dve", GA[:, 0:NQ], 0.0, [KGA])
        cs = slice(0, T) if ntile == 1 else slice(0, NQ)
        act(GA[:, cs], fbT[:, cs], AF.Exp, [TBK, "nbf"], [KGA], bias=nbf[0:4, 0:1], scale=-1.0)
        act(GA[:, cs], GA[:, cs], AF.Ln, [KGA], [KGA], bias=1.0)
        scan(GB[:, cs], zerob[0:4, cs], GA[:, cs], Bprev[:, 0:1], ALU.add, ALU.subtract,
             ["zerob", KGA, "Bprev"], [KGB])
        stt(GU[:, cs], ibT[:, cs], cpc("bi_pp")[0:4, :], GB[:, cs], ALU.add, ALU.subtract,
            [TFK, "cp", KGB], [KGU])
        if ntile == 1:
            red(UM[:, 0:1], GU[:, cs], ALU.max, [KGU], ["UM"])
        else:
            red(UM[:, 0:ntile], GU[:, cs].rearrange("p (c t) -> p c t", c=ntile), ALU.max, [KGU], ["UM"])
        scan(MUALL[:, 1:1 + ntile], UM[:, 0:ntile], UM[:, 0:ntile], MUALL[:, 0:1], ALU.max, ALU.max,
             ["UM", "MUALL"], ["MUALL"])
        tt("dve", C0[:, 0:ntile], MUALL[:, 0:ntile], MUALL[:, 1:1 + ntile], ALU.subtract, ["MUALL"], ["C0"])
        act(C0[:, 0:ntile], C0[:, 0:ntile], AF.Exp, ["C0"], ["C0"])
        if ntile == 1:
            mub = MUALL[:, 1:2].to_broadcast([4, T])
            tt("dve", GW[:, cs], GU[:, cs], mub, ALU.subtract, [KGU, "MUALL"], [KGW])
            tt("dve", GE[:, cs], GB[:, cs], mub, ALU.add, [KGB, "MUALL"], [KGE])
        else:
            mub = MUALL[:, 1:1 + ntile].unsqueeze(2).to_broadcast([4, ntile, 128])
            tt("dve", GW[:, cs].rearrange("p (c t) -> p c t", c=ntile),
               GU[:, cs].rearrange("p (c t) -> p c t", c=ntile), mub, ALU.subtract, [KGU, "MUALL"], [KGW])
            tt("dve", GE[:, cs].rearrange("p (c t) -> p c t", c=ntile),
               GB[:, cs].rearrange("p (c t) -> p c t", c=ntile), mub, ALU.add, [KGB, "MUALL"], [KGE])
        act(GW[:, cs], GW[:, cs], AF.Exp, [KGW], [KGW])
        act(GE[:, cs], GE[:, cs], AF.Exp, [KGE], [KGE], scale=-1.0)
        for j in range(ntile):
            tr(TF[0:T, j * 8:j * 8 + 4], GW[0:4, j * 128:j * 128 + T], identf[0:4, 0:4], [KGW, "cp"], [TFK])
            tr(TF[0:T, j * 8 + 4:j * 8 + 8], GE[0:4, j * 128:j * 128 + T], identf[0:4, 0:4], [KGE, "cp"], [TFK])
        cpy("dve", WE[0:T, 0:ntile, :], TF[0:T, 0:ntile * 8].rearrange("p (c e) -> p c e", e=8), [TFK], ["WE"])
        tt("dve", CD[:, 0:ntile, :], C0[:, 0:ntile].unsqueeze(2).to_broadcast([4, ntile, 4]),
           cpc("SEL2")[0:4, :].unsqueeze(1).to_broadcast([4, ntile, 4]), ALU.mult, ["C0", "cp"], ["CD"])
        mm(TBf[0:64, 0:ntile * 4], cpc("SEL")[0:4, 0:64], CD[:, 0:ntile, :].rearrange("p c e -> p (c e)"), True, True,
           ["cp", "CD"], [TBK])
        cpy("dve", C0B[:, 0:ntile, :], TBf[0:64, 0:ntile * 4].rearrange("p (c e) -> p c e", e=4), [TBK], ["C0B"])
        last = T - 1 if ntile == 1 else NQ - 1
        cpy("dve", Bprev[:, 0:1], GB[:, last:last + 1], [KGB], ["Bprev"])
        cpy("dve", MUALL[:, 0:1], MUALL[:, ntile:ntile + 1], ["MUALL"], ["MUALL"])

    def attention_block(T, ntile, ktiles, sample):
        NQ = ntile * T
        SETS = [[3, 4, 5], [6, 7, 2]]
        SBK = [0, 1]
        qkeys = [("qT", jj) for jj in range(ntile)]
        steps = [(h, t, nk, dsub, m) for h in range(4) for (t, nk, dsub) in ktiles for m in range(2)]
        state = {"n": 0}

        def emit_S(st_):
            h, t, nk, dsub, m = st_
            c0 = 0 if dsub is None else dsub * T
            ncol = NQ - c0
            bk = SBK[state["n"] % 2]
            state["n"] += 1
            ps, kp = PB[bk], ("pb", bk)
            pr = slice(64 * m, 64 * m + 64)
            mm(ps[0:nk, 0:ncol], kT[pr, h, t * 128:t * 128 + nk], qT[pr, h, c0:NQ],
               True, dsub is None, [("kT", t)] + qkeys, [kp])
            if dsub is not None:
                mm(ps[0:nk, 0:T], identb[0:nk, 0:nk], corrb[0:nk, h, 0:T], False, True,
                   ["identb", "corrb"], [kp])
            pt, kpt = PT.next()
            if sample or h != 0:
                calls = [(c0, NQ)]
            else:
                calls = [(c, c + 128) for c in range(c0, NQ, 128)]
            for (ca, cb) in calls:
                if sample:
                    bias = cpc("ABS")[0:nk, h * 17 + t:h * 17 + t + 1]
                else:
                    tref = ktiles[-1][0] - (ntile - 1) + (cb - 1) // 128
                    bias = cpc("AB")[0:nk, h * 16 + (tref - t):h * 16 + (tref - t) + 1]
                act(pt[0:nk, ca - c0:cb - c0], ps[0:nk, ca - c0:cb - c0], AF.Exp, [kp, "cp"], [kpt],
                    bias=bias, scale=0.125)
            return pt, kpt, c0

        def emit_PV(st_, pt, kpt, c0, bset):
            h, t, nk, dsub, m = st_
            for i in range(ntile):
                if i * T < c0:
                    continue
                a_ = m * 4 + i
                bnk = bset[a_ // 3]
                o = (a_ % 3) * 129
                mm(PB[bnk][0:T, o:o + 129], pt[0:nk, i * T - c0:(i + 1) * T - c0],
                   v_aug[0:nk, t, h, 0:129], False, True, [kpt, ("v", t), "v_ones"], [("pb", bnk)])

        def evac(h, bset):
            sa, ks = STT.next()
            if ntile == 4:
                for b_ in range(3):
                    nacc = 3 if b_ < 2 else 2
                    recip(sa[0:T, 3 * b_:3 * b_ + nacc], PB[bset[b_]][0:T, 128:129 * nacc:129],
                          [("pb", bset[b_])], [ks])
            else:
                recip(sa[0:T, 0:1], PB[bset[0]][0:T, 128:129], [("pb", bset[0])], [ks])
                recip(sa[0:T, 4:5], PB[bset[1]][0:T, 129 + 128:129 + 129], [("pb", bset[1])], [ks])
            tt("dve", sa[0:T, 4:4 + ntile], sa[0:T, 4:4 + ntile], lamt[0:T, 5:6].to_broadcast([T, ntile]),
               ALU.mult, [ks, "lamt"], [ks])
            oos = []
            for i in range(ntile):
                a1, a2 = i, 4 + i
                o1, k1 = O1.next()
                b1, b2 = bset[a1 // 3], bset[a2 // 3]
                ts("dve", o1[0:T, :], PB[b1][0:T, (a1 % 3) * 129:(a1 % 3) * 129 + 128], sa[0:T, a1:a1 + 1], None,
                   ALU.mult, None, [("pb", b1), ks], [k1])
                oo, ko = OO.next()
                stt(oo[0:T, :], PB[b2][0:T, (a2 % 3) * 129:(a2 % 3) * 129 + 128], sa[0:T, a2:a2 + 1],
                    o1[0:T, :], ALU.mult, ALU.add, [("pb", b2), ks, k1], [ko])
                S.add("dve", (lambda o1=o1, oo=oo, sa=sa, i=i: nc.vector.scalar_tensor_tensor(
                    out=o1[0:T, :], in0=oo[0:T, :], scalar=1.0, in1=oo[0:T, :], op0=ALU.mult, op1=ALU.mult,
                    accum_out=sa[0:T, 8 + i:9 + i])), [ko], [k1, ks])
                oos.append((oo, ko))
            ts("dve", sa[0:T, 12:12 + ntile], sa[0:T, 8:8 + ntile], 1.0 / 128, EPS, ALU.mult, ALU.add, [ks], [ks])
            tt("pool", sa[0:T, 16:16 + ntile], sa[0:T, 12:12 + ntile], cm05[0:T, 0:ntile], ALU.pow,
               [ks, "cm05"], [ks])
            for ii in range(ntile):
                oo2, ko2 = oos[ii]
                stt(mix[0:T, ii, h * 128:(h + 1) * 128], oo2[0:T, :], sa[0:T, 16 + ii:17 + ii],
                    mix[0:T, ii, h * 128:(h + 1) * 128], ALU.mult, ALU.mult,
                    [ko2, ks, ("mix", ii, 0)], [("mix", ii, 0)])

        prev = None
        pending_evac = None
        for k, st_ in enumerate(steps):
            h = st_[0]
            bset = SETS[h % 2]
            if k == 0 or steps[k - 1][0] != h:
                for b_ in range(3):
                    mm(PB[bset[b_]][0:T, :], zerob[:, 0:T], zerob[:, :], True, True, ["zerob"], [("pb", bset[b_])])
            pt, kpt, c0 = emit_S(st_)
            if prev is not None:
                emit_PV(*prev)
                if prev[0][0] != h:
                    evac(prev[0][0], prev[4])
            prev = (st_, pt, kpt, c0, bset)
        emit_PV(*prev)
        evac(prev[0][0], prev[4])

    def mem_block(T, ntile):
        NQ = ntile * T
        banks = [(ACC[0], ACCK[0]), (ACC[1], ACCK[1]), (ACC[2], ACCK[2]), (TF, TFK)]
        for i in range(ntile):
            mm(banks[i][0][0:T, :], zerob[:, 0:T], zerob[:, :], True, True, ["zerob"], [banks[i][1]])
        for h in range(4):
            pr = slice(64 * (h % 2), 64 * (h % 2) + 64)
            for nt in range(2):
                ps, kp = mmnext()
                mm(ps[:, 0:NQ], mkT[pr, h // 2, nt * 128:(nt + 1) * 128], qmT[pr, h // 2, 0:NQ], True, True,
                   ["mkT"] + [("qmT", jj) for jj in range(ntile)], [kp])
                pt, kpt = PT.next()
                act(pt[:, 0:NQ], ps[:, 0:NQ], AF.Exp, [kp], [kpt], scale=0.125)
                for i in range(ntile):
                    mm(banks[i][0][0:T, h * 65:(h + 1) * 65], pt[:, i * T:(i + 1) * T], mv_aug[:, nt, h, 0:65],
                       False, True, [kpt, "mv", "mv_ones"], [banks[i][1]])
        for i in range(ntile):
            bk, kb = banks[i]
            sa, ks = STT.next()
            recip(sa[0:T, 0:4], bk[0:T, 64:260:65], [kb], [ks])
            tt("dve", OM[0:T, :, :], bk[0:T, 0:260].rearrange("p (h e) -> p h e", e=65)[:, :, 0:64],
               sa[0:T, 0:4].unsqueeze(2).to_broadcast([T, 4, 64]), ALU.mult, [kb, ks], ["OM"])
            tt("dve", mix[0:T, i, 768:1024], OM[0:T, :, :].rearrange("p h d -> p (h d)"),
               mix[0:T, i, 768:1024], ALU.mult, ["OM", ("mix", i, 2)], [("mix", i, 2)])

    def mlstm_block(T, ntile):
        for j in range(ntile):
            tok = slice(j * 128, j * 128 + T)
            tt("dve", Sd[:, :, 0:65], Sst[:, :, 0:65], C0B[:, j, :].unsqueeze(2).to_broadcast([64, 4, 65]), ALU.mult,
               ["Sst", "C0B"], ["Sd"])
            cpy("pool", Cs[:, :, 0:65], Sd[:, :, 0:65], ["Sd"], ["Cs"])
            tt("pool", vw[0:T, :, 0:65], vb_aug[0:T, j, :, 0:65], WE[0:T, j, 0:4].unsqueeze(2).to_broadcast([T, 4, 65]),
               ALU.mult, [("vb", j), "vb_ones", "WE"], ["vw"])
            chk(80)
            psA, kA = mmnext()
            for h in range(4):
                mm(psA[0:T, h * 128:h * 128 + T], QKT[:, 4 + h, tok], QKT[:, h, tok], True, True,
                   [("QKT", j)], [kA])
            tt("dve", AT[0:T, :, 0:T], psA[0:T, :].rearrange("p (h t) -> p h t", h=4)[:, :, 0:T],
               cpc("maskU")[0:T, 0:T].unsqueeze(1).to_broadcast([T, 4, T]), ALU.mult, [kA, "cp"], ["AT"])
            chk(81)
            psO, kO = ACC[j % 2], ACCK[j % 2]
            for h in range(4):
                mm(psO[0:T, h * 65:(h + 1) * 65], AT[0:T, h, 0:T], vw[0:T, h, 0:65], True, False, ["AT", "vw"], [kO])
                mm(psO[0:T, h * 65:(h + 1) * 65], QKT[:, h, tok], Cs[:, h, 0:65], False, True,
                   [("QKT", j), "Cs"], [kO])
            chk(82)
            psS, kS = ACC[2], ACCK[2]
            for h in range(4):
                mm(psS[0:64, h * 65:(h + 1) * 65], QKB[0:T, j, 256 + h * 64:256 + (h + 1) * 64],
                   vw[0:T, h, 0:65], True, True, [("QKB", j), "vw"], [kS])
            chk(83)
            tt("dve", Sst[:, :, 0:65], Sd[:, :, 0:65], psS[0:64, 0:260].rearrange("p (c e) -> p c e", e=65), ALU.add,
               ["Sd", kS], ["Sst"])
            chk(84)
            sa, ks = STT.next()
            cpy("dve", sa[0:T, 20:24], psO[0:T, 64:260:65], [kO], [ks])
            stt(sa[0:T, 24:28], sa[0:T, 20:24], -1.0, sa[0:T, 20:24], ALU.mult, ALU.max, [ks], [ks])
            tt("dve", sa[0:T, 0:4], sa[0:T, 24:28], WE[0:T, j, 4:8], ALU.max, [ks, "WE"], [ks])
            recip(sa[0:T, 4:8], sa[0:T, 0:4], [ks], [ks])
            tt("dve", HN[0:T, :, :], psO[0:T, 0:260].rearrange("p (h e) -> p h e", e=65)[:, :, 0:64],
               sa[0:T, 4:8].unsqueeze(2).to_broadcast([T, 4, 64]), ALU.mult, [kO, ks], ["HN"])
            r4 = group_norm_rstd(HN[0:T, :, :].rearrange("p h d -> p (h d)"), "HN", T, 4, sa, ks, 8)
            tt("dve", HN[0:T, :, :], HN[0:T, :, :], r4.unsqueeze(2).to_broadcast([T, 4, 64]), ALU.mult,
               ["HN", ks], ["HN"])
            tt("dve", mix[0:T, j, 512:768], HN[0:T, :, :].rearrange("p h d -> p (h d)"), mix[0:T, j, 512:768],
               ALU.mult, ["HN", ("mix", j, 1)], [("mix", j, 1)])

    def phase3_tile(src_ap, dst_ap, T, j):
        for kc in range(8):
            tr(TB[:, kc * 128:kc * 128 + T], mix[0:T, j, kc * 128:(kc + 1) * 128], identb[0:T, 0:T],
               [("mix", j, 0), ("mix", j, 1), ("mix", j, 2), "identb"], [TBK])
        cpy("act", mixT[:, :, 0:T], TB.rearrange("p (k t) -> p k t", k=8)[:, :, 0:T], [TBK], ["mixT"])
        xt, kx = XT.next()
        dma(xt[0:T, :], src_ap, w=[kx])
        for half in range(2):
            ps, kp = ACC[half], ACCK[half]
            for kc in range(8):
                mm(ps[0:T, :], mixT[:, kc, 0:T], w_out_bf[:, kc, half * 512:(half + 1) * 512], kc == 0, kc == 7,
                   ["mixT"] + W_OUT_K, [kp])
            tt("dve", xt[0:T, half * 512:(half + 1) * 512], ps[0:T, :], xt[0:T, half * 512:(half + 1) * 512],
               ALU.add, [kp, kx], [kx])
        dma(dst_ap, xt[0:T, :], r=[kx])

    def memkv_seq(s):
        for nt in range(2):
            xt, kx, hT, kh, sa, ks = front(memp[s, nt * 128:(nt + 1) * 128, :], 128)
            ps, kp = mmnext()
            for kc in range(8):
                mm(ps[:, :], hT[:, kc, :], w_mkv_bf[:, kc, :], kc == 0, kc == 7, [kh] + W_MKV_K, [kp])
            zf, kz = ZF.next()
            cpy("act", zf[:, :], ps[:, :], [kp], [kz])
            dma(pmv[s, nt * 128:(nt + 1) * 128, :], zf[:, 256:512], r=[kz])
            cpy("pool", mv_aug[:, nt, :, 0:64], zf[:, 256:512].rearrange("p (h d) -> p h d", h=4),
                [kz, "mv_ones"], ["mv"])
            r4 = group_norm_rstd(zf[:, 0:256], kz, 128, 4, sa, ks, 8)
            tt("dve", zf[:, 0:256].rearrange("p (g d) -> p g d", d=64),
               zf[:, 0:256].rearrange("p (g d) -> p g d", d=64),
               r4.unsqueeze(2).to_broadcast([128, 4, 64]), ALU.mult, [kz, ks], [kz])
            ko, kk = KOUT.next()
            tt("dve", ko[:, 0:256].rearrange("p (g d) -> p g d", d=64),
               zf[:, 0:256].rearrange("p (g d) -> p g d", d=64),
               cpc("gkm_rep").unsqueeze(1).to_broadcast([128, 4, 64]), ALU.mult, [kz, "cp"], [kk])
            dma(pmk[s, nt * 128:(nt + 1) * 128, :], ko[:, 0:256], r=[kk])
            nb, kn = NB.next()
            cpy("pool", nb[:, 0:256], ko[:, 0:256], [kk], [kn])
            for hp in range(2):
                tr(TB[:, hp * 128:(hp + 1) * 128], nb[:, hp * 128:(hp + 1) * 128], identb[:, :],
                   [kn, "identb"], [TBK])
            cpy("act", mkT[:, :, nt * 128:(nt + 1) * 128], TB[:, 0:256].rearrange("p (h t) -> p h t", h=2),
                [TBK], ["mkT"])

    def state_out(dC, dn, dm):
        for h in range(4):
            dma(dC[h], Sst[:, h, 0:64], r=["Sst"])
            dma(dn[h].unsqueeze(1), Sst[:, h, 64:65], r=["Sst"])
        tt("dve", mfin[:, :], Bprev[:, 0:1], MUALL[:, 0:1], ALU.add, ["Bprev", "MUALL"], ["mfin"])
        dma(dm, mfin[:, :], r=["mfin"])

    try:
        chk(1)
        for s in range(NSEQ):
            memkv_seq(s)
            memset("dve", Sst[:, :, :], 0.0, ["Sst"])
            memset("dve", Bprev[:, :], 0.0, ["Bprev"])
            memset("dve", MUALL[:, 0:1], 0.0, ["MUALL"])
            for b in range(NBLK):
                tl = []
                for j in range(4):
                    ti = 4 * b + j
                    rows = slice(ti * 128, (ti + 1) * 128)
                    tl.append((xp[s, rows, :], ti, j, pk[s, rows, :], pv[s, rows, :]))
                phase1_block(tl, 128)
                gates_block(128, 4)
                ktiles = [(t, 128, None) for t in range(4 * b)] + [(4 * b + i, 128, i) for i in range(4)]
                attention_block(128, 4, ktiles, False)
                mem_block(128, 4)
                mlstm_block(128, 4)
                for j in range(4):
                    ti = 4 * b + j
                    rows = slice(ti * 128, (ti + 1) * 128)
                    phase3_tile(xp[s, rows, :], yp[s, rows, :], 128, j)
            state_out(pC[s], pn[s], pm[s].unsqueeze(1))

        if DO_SAMPLE:
            for t in range(16):
                xt, kx = XT.next()
                dma(xt[:, 0:512], ck[t * 128:(t + 1) * 128, :], w=[kx])
                dma(xt[:, 512:1024], cv[t * 128:(t + 1) * 128, :], w=[kx])
                nb, kn = NB.next()
                cpy("dve", nb[:, :], xt[:, 0:512], [kx], [kn])
                cpy("pool", v_aug[:, t, :, 0:128], xt[:, 512:1024].rearrange("p (h d) -> p h d", h=4),
                    [kx, "v_ones"], [("v", t)])
                for h in range(4):
                    tr(TB[:, h * 128:(h + 1) * 128], nb[:, h * 128:(h + 1) * 128], identb[:, :], [kn, "identb"], [TBK])
                cpy("act", kT[:, :, t * 128:(t + 1) * 128], TB[:, 0:512].rearrange("p (h t) -> p h t", h=4),
                    [TBK], [("kT", t)])
            for nt in range(2):
                xt, kx = XT.next()
                dma(xt[:, 0:256], cmk[nt * 128:(nt + 1) * 128, :], w=[kx])
                dma(xt[:, 256:512], cmv[nt * 128:(nt + 1) * 128, :], w=[kx])
                nb, kn = NB.next()
                cpy("dve", nb[:, 0:256], xt[:, 0:256], [kx], [kn])
                cpy("pool", mv_aug[:, nt, :, 0:64], xt[:, 256:512].rearrange("p (h d) -> p h d", h=4),
                    [kx, "mv_ones"], ["mv"])
                for hp in range(2):
                    tr(TB[:, hp * 128:(hp + 1) * 128], nb[:, hp * 128:(hp + 1) * 128], identb[:, :],
                       [kn, "identb"], [TBK])
                cpy("act", mkT[:, :, nt * 128:(nt + 1) * 128], TB[:, 0:256].rearrange("p (h t) -> p h t", h=2),
                    [TBK], ["mkT"])
            chk(2)
            for h in range(4):
                dma(Sst[:, h, 0:64], sC[h], w=["Sst"])
                dma(Sst[:, h, 64:65], sn[h].unsqueeze(1), w=["Sst"])
            memset("dve", Bprev[:, :], 0.0, ["Bprev"])
            cpy("dve", MUALL[:, 0:1], cpc("sm_pp")[0:4, :], ["cp"], ["MUALL"])
            chk(3)
            phase1_block([(xs[:, :], 16, 0, sk[:, :], sv[:, :])], 16)
            chk(4)
            gates_block(16, 1)
            chk(5)
            attention_block(16, 1, [(t, 128, None) for t in range(16)] + [(16, 16, 0)], True)
            chk(6)
            mem_block(16, 1)
            chk(7)
            mlstm_block(16, 1)
            chk(8)
            phase3_tile(xs[:, :], ys[:, :], 16, 0)
            chk(9)
            state_out(sCo, sno, smo.rearrange("o h -> h o"))

    except _Stop:
        pass

    S.emit(st)
    st.close()
    return nc


_NC_CACHE = {}


def _get_nc(key=(4, 4, True)):
    if key not in _NC_CACHE:
        _NC_CACHE[key] = build_program(*key)
    return _NC_CACHE[key]


def _in_maps(inp, NSEQ=4):
    maps = []
    c32 = lambda a: np.ascontiguousarray(a, dtype=np.float32)
    for c in range(8):
        m = {
            "xp": c32(inp["x_prompt"][4 * c:4 * c + max(NSEQ, 1)]),
            "memp": c32(inp["mem_prompt"][4 * c:4 * c + max(NSEQ, 1)]),
            "xs": c32(inp["x_sample"][c]),
            "ck": c32(inp["cache_attn_k"][0, c].reshape(2048, 512)),
            "cv": c32(inp["cache_attn_v"][0, c].reshape(2048, 512)),
            "sC": c32(inp["state_mlstm_C"][0, c]),
            "sn": c32(inp["state_mlstm_n"][0, c]),
            "cmk": c32(inp["cache_mem_k"][0, c].reshape(256, 256)),
            "cmv": c32(inp["cache_mem_v"][0, c].reshape(256, 256)),
            "w_in": c32(inp["w_in"][0]),
            "w_out": c32(inp["w_out"][0]),
            "w_mk": c32(inp["w_mk"][0]),
            "w_mv": c32(inp["w_mv"][0]),
            "cp": _make_cp(inp, c),
        }
        maps.append(m)
    return maps


def kernel(**inputs):
    inp = {k: np.asarray(v) for k, v in inputs.items()}
    nc = _get_nc()
    res = run_bass_kernel_spmd(nc, _in_maps(inp), core_ids=list(range(8)))
    R = res.results
    cat = lambda name: np.concatenate([np.asarray(r[name]) for r in R], axis=0)
    stk = lambda name: np.stack([np.asarray(r[name]) for r in R], axis=0)
    y_prompt = cat("yp")
    y_sample = stk("ys")
    p_attn_k = cat("pk").reshape(1, 32, 2048, 4, 128)
    p_attn_v = cat("pv").reshape(1, 32, 2048, 4, 128)
    p_C = cat("pC").reshape(1, 32, 4, 64, 64)
    p_n = cat("pn").reshape(1, 32, 4, 64)
    p_m = cat("pm").reshape(1, 32, 4)
    p_mk = cat("pmk").reshape(1, 32, 256, 4, 64)
    p_mv = cat("pmv").reshape(1, 32, 256, 4, 64)
    s_k = stk("sk").reshape(1, 8, 16, 4, 128)
    s_v = stk("sv").reshape(1, 8, 16, 4, 128)
    s_C = stk("sCo").reshape(1, 8, 4, 64, 64)
    s_n = stk("sno").reshape(1, 8, 4, 64)
    s_m = stk("smo").reshape(1, 8, 4)
    return (y_prompt, y_sample, p_attn_k, p_attn_v, p_C, p_n, p_m, p_mk, p_mv,
            s_k, s_v, s_C, s_n, s_m)
```

```python
import math
from contextlib import ExitStack

import numpy as np
import concourse.bass as bass
import concourse.mybir as mybir
from concourse.bass_utils import run_bass_kernel_spmd

F32 = mybir.dt.float32
BF16 = mybir.dt.bfloat16
AF = mybir.ActivationFunctionType
ALU = mybir.AluOpType
AX = mybir.AxisListType

N_DMA_SEMS = 40
DELAY = 4
EPS = 1e-6
SLOPES = [2.0 ** (-8.0 * (h + 1) / 4) for h in range(4)]
LAM_INIT = 0.8 - 0.6 * math.exp(0.0)
NIN = 3848
BIG = 240000.0


class _Op:
    __slots__ = ("eng", "fn", "deps", "has_dep", "is_dma", "semkey", "val", "ie")


class Sched:
    def __init__(self, nc):
        self.nc = nc
        self.ops = []
        self.lw = {}
        self.rd = {}
        self.eng_n = {"pe": 0, "act": 0, "dve": 0, "pool": 0, "sp": 0}

    def add(self, eng, fn, reads=(), writes=(), dma=False):
        i = len(self.ops)
        deps = set()
        for k in reads:
            w = self.lw.get(k)
            if w is not None:
                deps.add(w)
            if isinstance(k, tuple) and k[0] == "pb":
                for r in self.rd.get(k, ()):
                    if self.ops[r].eng != eng:
                        deps.add(r)
        for k in writes:
            w = self.lw.get(k)
            if w is not None:
                deps.add(w)
            for r in self.rd.get(k, ()):
                deps.add(r)
        op = _Op()
        op.eng = eng
        op.fn = fn
        op.is_dma = dma
        op.has_dep = False
        op.semkey = None
        op.val = 0
        op.ie = self.eng_n[eng]
        self.eng_n[eng] += 1
        keep = []
        for d in deps:
            p = self.ops[d]
            if p.eng == eng and not p.is_dma:
                if eng == "pe" or eng == "sp":
                    continue
                if eng != "pool" and op.ie - p.ie > 2:
                    continue
            p.has_dep = True
            keep.append(d)
        op.deps = keep
        for k in reads:
            self.rd.setdefault(k, []).append(i)
        for k in writes:
            self.lw[k] = i
            self.rd[k] = []
        self.ops.append(op)
        return i

    def emit(self, stack):
        nc = self.nc
        engobj = {"pe": nc.tensor, "act": nc.scalar, "dve": nc.vector,
                  "pool": nc.gpsimd, "sp": nc.sync}
        esem = {e: stack.enter_context(nc.semaphore("s_" + e))
                for e in ("pe", "act", "dve", "pool")}
        dsem = [stack.enter_context(nc.semaphore("d%d" % i)) for i in range(N_DMA_SEMS)]
        waited = {e: {} for e in engobj}
        cnt = {e: 0 for e in engobj}
        dcnt = [0] * N_DMA_SEMS
        rr = 0
        rrp = 0
        for op in self.ops:
            E = engobj[op.eng]
            need = {}
            for d in op.deps:
                p = self.ops[d]
                if need.get(p.semkey, 0) < p.val:
                    need[p.semkey] = p.val
            s = None
            if op.is_dma:
                if op.eng == "pool":
                    s = N_DMA_SEMS - 8 + (rrp % 8)
                    rrp += 1
                else:
                    s = rr % (N_DMA_SEMS - 8)
                    rr += 1
                if dcnt[s] > 0 and need.get(("d", s), 0) < dcnt[s]:
                    need[("d", s)] = dcnt[s]
                dcnt[s] += 16
                op.semkey = ("d", s)
                op.val = dcnt[s]
            w = waited[op.eng]
            for key, val in need.items():
                if w.get(key, 0) >= val:
                    continue
                so = dsem[key[1]] if key[0] == "d" else esem[key[1]]
                E.wait_ge(so, val)
                w[key] = val
            inst = op.fn()
            if op.is_dma:
                inst.then_inc(dsem[s], 16)
            elif op.has_dep:
                cnt[op.eng] += 1
                op.semkey = ("e", op.eng)
                op.val = cnt[op.eng]
                inst.then_inc(esem[op.eng], 1)
        for s in range(N_DMA_SEMS):
            if dcnt[s] > 0:
                nc.sync.wait_ge(dsem[s], dcnt[s])
        return cnt


class Rot:
    def __init__(self, aps, name):
        self.aps = aps
        self.name = name
        self.i = 0

    def next(self):
        k = self.i % len(self.aps)
        self.i += 1
        return self.aps[k], (self.name, k)


def _cp_layout():
    off = {}
    c = 0
    for name, n in [("ident", 128), ("maskU", 128), ("AB", 64), ("ABS", 68),
                    ("gqa_pp", 1), ("gqm_pp", 1), ("gkm_rep", 64), ("gka_rep", 64),
                    ("gnorm_pp", 8), ("gmem_pp", 8), ("rows_pp", 8),
                    ("bi_pp", 1), ("bf_pp", 1), ("sm_pp", 1), ("SEL", 128), ("SEL2", 4)]:
        off[name] = (c, c + n)
        c += n
    return off, c


CP_OFF, NCP = _cp_layout()


def _make_cp(inp, core):
    cp = np.zeros((128, NCP), np.float32)

    def put(name, arr):
        a, b = CP_OFF[name]
        cp[:, a:b] = arr

    p = np.arange(128)
    put("ident", np.eye(128, dtype=np.float32))
    put("maskU", (p[:, None] <= p[None, :]).astype(np.float32))
    corr = np.zeros((128, 4, 128), np.float32)
    k = p[:, None]
    q = p[None, :]
    for h in range(4):
        c_ = np.where(k > q, -16.0 * SLOPES[h] * (k - q), 0.0)
        c_ = np.where((k // 64) > (q // 64), -BIG, c_)
        corr[:, h, :] = c_
    extra = {"corr": corr.reshape(128, 512)}
    AB = np.zeros((128, 4, 16), np.float32)
    for h in range(4):
        for r in range(16):
            AB[:, h, r] = SLOPES[h] * (p - 127 - 128 * r)
    put("AB", AB.reshape(128, 64))
    ABS = np.zeros((128, 4, 17), np.float32)
    for h in range(4):
        for t in range(17):
            ABS[:, h, t] = SLOPES[h] * np.minimum(128 * t + p - 2063, 0)
    put("ABS", ABS.reshape(128, 68))
    put("gqa_pp", inp["g_qa"][0][p % 64][:, None])
    put("gqm_pp", inp["g_qm"][0][p % 64][:, None])
    put("gkm_rep", np.broadcast_to(inp["g_km"][0][None, :], (128, 64)))
    put("gka_rep", np.broadcast_to(inp["g_ka"][0][None, :], (128, 64)))
    put("gnorm_pp", inp["g_norm"][0].reshape(8, 128).T)
    put("gmem_pp", inp["g_mem"][0].reshape(8, 128).T)
    rows = np.ones((128, 8), np.float32)
    rows[:, 0:4] = inp["g_subln"][0][:, None]
    rows[:, 4:6] = inp["g_mh"][0][p % 64][:, None]
    put("rows_pp", rows)
    lam4 = np.concatenate([inp["lam_q1"][0], inp["lam_k1"][0], inp["lam_q2"][0], inp["lam_k2"][0]])
    extra["lam4"] = np.ascontiguousarray(np.broadcast_to(lam4[None, :], (128, 256)))
    put("bi_pp", inp["b_i"][0][p % 4][:, None])
    put("bf_pp", inp["b_f"][0][p % 4][:, None])
    put("sm_pp", inp["state_mlstm_m"][0, core][p % 4][:, None])
    SEL = np.zeros((128, 128), np.float32)
    SEL[0:4, :] = 1.0
    put("SEL", SEL)
    SEL2 = np.zeros((128, 4), np.float32)
    SEL2[0:4, 0:4] = np.eye(4, dtype=np.float32)
    put("SEL2", SEL2)
    return cp, extra


class _Stop(Exception):
    pass


def build_program(NSEQ=4, NBLK=4, DO_SAMPLE=True, STAGE=99):
    nc = bass.Bass("TRN2", target_bir_lowering=False)
    S = Sched(nc)

    def din(name, shape):
        return nc.dram_tensor(name, shape, F32, kind="ExternalInput").ap()

    def dout(name, shape):
        return nc.dram_tensor(name, shape, F32, kind="ExternalOutput").ap()

    NS = max(NSEQ, 1)
    xp = din("xp", [NS, 2048, 1024])
    memp = din("memp", [NS, 256, 1024])
    xs = din("xs", [16, 1024])
    ck = din("ck", [2048, 512])
    cv = din("cv", [2048, 512])
    sC = din("sC", [4, 64, 64])
    sn = din("sn", [4, 64])
    cmk = din("cmk", [256, 256])
    cmv = din("cmv", [256, 256])
    w_in = din("w_in", [1024, NIN])
    w_out = din("w_out", [1024, 1024])
    w_mk = din("w_mk", [1024, 256])
    w_mv = din("w_mv", [1024, 256])
    cpd = din("cp", [128, NCP])
    corrd = din("corr", [128, 512])
    lam4d = din("lam4", [128, 256])

    yp = dout("yp", [NS, 2048, 1024])
    pk = dout("pk", [NS, 2048, 512])
    pv = dout("pv", [NS, 2048, 512])
    pC = dout("pC", [NS, 4, 64, 64])
    pn = dout("pn", [NS, 4, 64])
    pm = dout("pm", [NS, 4])
    pmk = dout("pmk", [NS, 256, 256])
    pmv = dout("pmv", [NS, 256, 256])
    ys = dout("ys", [16, 1024])
    sk = dout("sk", [16, 512])
    sv = dout("sv", [16, 512])
    sCo = dout("sCo", [4, 64, 64])
    sno = dout("sno", [4, 64])
    smo = dout("smo", [1, 4])

    st = ExitStack()

    def chk(n):
        if STAGE == n:
            raise _Stop()

    def sb(name, shape, dt=F32):
        return st.enter_context(nc.sbuf_tensor("sb_" + name, shape, dt))

    cp = sb("cp", [128, NCP])
    w_in_bf = sb("w_in_bf", [128, 8, NIN], BF16)
    w_out_bf = sb("w_out_bf", [128, 8, 1024], BF16)
    w_mkv_bf = sb("w_mkv_bf", [128, 8, 512], BF16)
    identb = sb("identb", [128, 128], BF16)
    corrb = sb("corrb", [128, 4, 128], BF16)
    zerob = sb("zerob", [128, 512], BF16)
    cm05 = sb("cm05", [128, 8])
    lamt = sb("lamt", [128, 8])
    rs8 = sb("rs8", [128, 8])
    nbf = sb("nbf", [128, 1])

    kT = sb("kT", [128, 4, 2064], BF16)
    v_aug = sb("v_aug", [128, 17, 4, 130], BF16)
    mkT = sb("mkT", [128, 2, 256], BF16)
    mv_aug = sb("mv_aug", [128, 2, 4, 66], BF16)

    qT = sb("qT", [128, 4, 512], BF16)
    mix = sb("mix", [128, 4, 1024], BF16)
    OBG = sb("OBG", [128, 4, 256], BF16)
    QKB = sb("QKB", [128, 4, 512], BF16)
    QKT = sb("QKT", [64, 8, 512], BF16)
    vb_aug = sb("vb_aug", [128, 4, 4, 66], BF16)
    qmT = sb("qmT", [128, 2, 512], BF16)
    GIF = sb("GIF", [128, 4, 8])
    WE = sb("WE", [128, 4, 8])
    C0B = sb("C0B", [64, 4, 4])
    Sst = sb("Sst", [64, 4, 66])
    Sd = sb("Sd", [64, 4, 66])
    Cs = sb("Cs", [64, 4, 66], BF16)
    Bprev = sb("Bprev", [4, 1])
    MUALL = sb("MUALL", [4, 8])
    UM = sb("UM", [4, 4])
    C0 = sb("C0", [4, 4])
    CD = sb("CD", [4, 4, 4])
    mfin = sb("mfin", [4, 1])

    XT = Rot([sb("xt%d" % i, [128, 1024])[:] for i in range(2)], "xt")
    xn = sb("xn", [128, 1024], BF16)
    HT = Rot([sb("hT%d" % i, [128, 8, 128], BF16)[:] for i in range(2)], "hT")
    ZF = Rot([sb("zf%d" % i, [128, 512])[:] for i in range(2)], "zf")
    SQ = sb("sq", [128, 512])
    KOUT = Rot([sb("kvout%d" % i, [128, 512])[:] for i in range(3)], "kvout")
    VOUT = KOUT
    TH = Rot([sb("th%d" % i, [128, 512])[:] for i in range(2)], "th")
    NB = Rot([sb("nb%d" % i, [128, 512], BF16)[:] for i in range(4)], "nb")
    PT = Rot([sb("pt%d" % i, [128, 2, 512], BF16)[:] for i in range(3)], "pt")
    STT = Rot([sb("stat%d" % i, [128, 72])[:] for i in range(4)], "stat")
    O1 = Rot([sb("o1s%d" % i, [128, 128])[:] for i in range(4)], "o1s")
    OO = Rot([sb("oo%d" % i, [128, 128])[:] for i in range(4)], "oo")
    mixT = sb("mixT", [128, 8, 128], BF16)
    AT = sb("AT", [128, 4, 128], BF16)
    vw = sb("vw", [128, 4, 66], BF16)
    HN = sb("HN", [128, 4, 64])
    OM = HN
    GA, KGA = SQ[0:4, :], "sq"
    GB, KGB = TH.aps[0][0:4, :], ("th", 0)
    GU, KGU = TH.aps[1][0:4, :], ("th", 1)
    GW, KGW = ZF.aps[0][0:4, :], ("zf", 0)
    GE, KGE = ZF.aps[1][0:4, :], ("zf", 1)

    PBIG = st.enter_context(nc.psum_tensor("pbig", [128, 4096], F32))
    PB = [PBIG[:, i * 512:(i + 1) * 512] for i in range(8)]
    MM = Rot([PB[0], PB[1], PB[2]], "pb_mm")
    MM.keys = [("pb", 0), ("pb", 1), ("pb", 2)]
    ACC = [PB[3], PB[4], PB[5]]
    ACCK = [("pb", 3), ("pb", 4), ("pb", 5)]
    TBf = PB[6]
    TB = PB[6].bitcast(BF16)
    TBK = ("pb", 6)
    TF = PB[7]
    TFK = ("pb", 7)

    def mmnext():
        k = MM.i % 3
        MM.i += 1
        return PB[k], ("pb", k)

    def cpc(name, a=None, b=None):
        o, e = CP_OFF[name]
        if a is None:
            return cp[:, o:e]
        return cp[:, o + a:o + b]

    def dma(out, in_, r=(), w=(), q="sp", **kw):
        if q == "sp":
            S.add("sp", lambda: nc.sync.dma_start(out=out, in_=in_, **kw), r, w, dma=True)
        else:
            S.add("pool", lambda: nc.gpsimd.dma_start(out=out, in_=in_, **kw), r, w, dma=True)

    def mm(out, lhsT, rhs, start, stop, r, w):
        S.add("pe", lambda: nc.tensor.matmul(out, lhsT=lhsT, rhs=rhs, start=start, stop=stop,
                                             skip_group_check=True), r, w)

    def tr(out, in_, ident, r, w):
        S.add("pe", lambda: nc.tensor.transpose(out=out, in_=in_, identity=ident), r, w)

    def act(out, in_, func, r, w, bias=0.0, scale=1.0, accum=None):
        if accum is None:
            S.add("act", lambda: nc.scalar.activation(out=out, in_=in_, func=func, bias=bias, scale=scale), r, w)
        else:
            S.add("act", lambda: nc.scalar.activation(out=out, in_=in_, func=func, bias=bias, scale=scale,
                                                      accum_out=accum), r, w)

    def engo(e):
        return nc.vector if e == "dve" else nc.gpsimd

    def ts(e, out, in0, s1, s2, op0, op1, r, w):
        if s2 is None:
            S.add(e, lambda: engo(e).tensor_scalar(out=out, in0=in0, scalar1=s1, scalar2=None, op0=op0), r, w)
        else:
            S.add(e, lambda: engo(e).tensor_scalar(out=out, in0=in0, scalar1=s1, scalar2=s2, op0=op0, op1=op1), r, w)

    def tt(e, out, in0, in1, op, r, w):
        S.add(e, lambda: engo(e).tensor_tensor(out=out, in0=in0, in1=in1, op=op), r, w)

    def stt(out, in0, scalar, in1, op0, op1, r, w):
        S.add("dve", lambda: nc.vector.scalar_tensor_tensor(out=out, in0=in0, scalar=scalar, in1=in1,
                                                            op0=op0, op1=op1), r, w)

    def cpy(e, out, in_, r, w):
        if e == "act":
            S.add("act", lambda: nc.scalar.copy(out=out, in_=in_), r, w)
        else:
            S.add(e, lambda: engo(e).tensor_copy(out=out, in_=in_), r, w)

    def memset(e, ap, val, w):
        S.add(e, lambda: engo(e).memset(ap, val), (), w)

    def red(out, in_, op, r, w):
        S.add("dve", lambda: nc.vector.tensor_reduce(out=out, in_=in_, axis=AX.X, op=op), r, w)

    def recip(out, in_, r, w):
        S.add("dve", lambda: nc.vector.reciprocal(out=out, in_=in_), r, w)

    def scan(out, d0, d1, init, op0, op1, r, w):
        S.add("dve", lambda: nc.vector.tensor_tensor_scan(out=out, data0=d0, data1=d1, initial=init,
                                                          op0=op0, op1=op1), r, w)

    def rstd_from_ss(stt_ap, kst, c_ss, c_tmp, c_out, n, T, inv):
        ts("dve", stt_ap[0:T, c_tmp:c_tmp + n], stt_ap[0:T, c_ss:c_ss + n], inv, EPS, ALU.mult, ALU.add,
           [kst], [kst])
        tt("pool", stt_ap[0:T, c_out:c_out + n], stt_ap[0:T, c_tmp:c_tmp + n], cm05[0:T, 0:n], ALU.pow,
           [kst, "cm05"], [kst])

    dma(cp[:], cpd, w=["cp"])
    cpy("dve", identb[:], cpc("ident"), ["cp"], ["identb"])
    identf = cpc("ident")
    dma(TH.aps[0][:, :], corrd, w=[("th", 0)])
    cpy("dve", corrb[:].rearrange("p h q -> p (h q)"), TH.aps[0][:, :], [("th", 0)], ["corrb"])
    dma(TH.aps[1][:, 0:256], lam4d, w=[("th", 1)])
    memset("pool", zerob[:], 0.0, ["zerob"])
    memset("pool", cm05[:], -0.5, ["cm05"])
    memset("pool", v_aug[:, :, :, 128:129], 1.0, ["v_ones"])
    memset("pool", vb_aug[:, :, :, 64:65], 1.0, ["vb_ones"])
    memset("pool", mv_aug[:, :, :, 64:65], 1.0, ["mv_ones"])
    ts("dve", rs8[:, 0:4], cpc("rows_pp", 0, 4), 0.5 * (1.0 - LAM_INIT), None, ALU.mult, None, ["cp"], ["rs8"])
    ts("dve", rs8[:, 4:6], cpc("rows_pp", 4, 6), 0.25, None, ALU.mult, None, ["cp"], ["rs8"])
    ts("dve", rs8[:, 6:8], cpc("rows_pp", 6, 8), 0.5, None, ALU.mult, None, ["cp"], ["rs8"])
    ts("dve", nbf[:], cpc("bf_pp"), -1.0, None, ALU.mult, None, ["cp"], ["nbf"])
    wv = w_in.rearrange("(k p) n -> p k n", p=128)
    wov = w_out.rearrange("(k p) n -> p k n", p=128)
    wkv = w_mk.rearrange("(k p) n -> p k n", p=128)
    wvv = w_mv.rearrange("(k p) n -> p k n", p=128)

    def wload(dst, src, n, scal, key, extra=None):
        xt, kx = XT.next()
        dma(xt[:, 0:n], src, w=[kx])
        if extra is None:
            ts("dve", dst, xt[:, 0:n], scal, None, ALU.mult, None, [kx, "cp", "rs8"], [key])
        else:
            ts("dve", dst, xt[:, 0:n], scal, extra, ALU.mult, ALU.mult, [kx, "cp", "rs8"], [key])

    for kc in range(8):
        wload(w_mkv_bf[:, kc, 0:256], wkv[:, kc, :], 256, cpc("gmem_pp", kc, kc + 1), ("w_mkv", kc, 0))
        wload(w_mkv_bf[:, kc, 256:512], wvv[:, kc, :], 256, cpc("gmem_pp", kc, kc + 1), ("w_mkv", kc, 1))
    for kc in range(8):
        g = cpc("gnorm_pp", kc, kc + 1)
        wload(w_in_bf[:, kc, 0:1024], wv[:, kc, 0:1024], 1024, g, ("w_in", kc, 0))
        wload(w_in_bf[:, kc, 1024:2048], wv[:, kc, 1024:2048], 1024, g, ("w_in", kc, 1))
        wload(w_in_bf[:, kc, 2048:2304], wv[:, kc, 2048:2304], 256, g, ("w_in", kc, 2))
        wload(w_in_bf[:, kc, 2304:2560], wv[:, kc, 2304:2560], 256, g, ("w_in", kc, 2), 0.125)
        wload(w_in_bf[:, kc, 2560:3072], wv[:, kc, 2560:3072], 512, g, ("w_in", kc, 2))
        wload(w_in_bf[:, kc, 3072:3840], wv[:, kc, 3080:3848], 768, g, ("w_in", kc, 3))
        wload(w_in_bf[:, kc, 3840:3848], wv[:, kc, 3072:3080], 8, g, ("w_in", kc, 3))
    for kc in range(8):
        wload(w_out_bf[:, kc, :], wov[:, kc, :], 1024, rs8[:, kc:kc + 1], ("w_out", kc))
    W_IN_K = [("w_in", kc, i) for kc in range(8) for i in range(4)]
    W_OUT_K = [("w_out", kc) for kc in range(8)]
    W_MKV_K = [("w_mkv", kc, i) for kc in range(8) for i in range(2)]
    l4 = TH.aps[1]
    tt("dve", SQ[:, 0:64], l4[:, 0:64], l4[:, 64:128], ALU.mult, [("th", 1)], ["sq"])
    red(lamt[:, 0:1], SQ[:, 0:64], ALU.add, ["sq"], ["lamt"])
    tt("dve", SQ[:, 64:128], l4[:, 128:192], l4[:, 192:256], ALU.mult, [("th", 1)], ["sq"])
    red(lamt[:, 1:2], SQ[:, 64:128], ALU.add, ["sq"], ["lamt"])
    act(lamt[:, 2:4], lamt[:, 0:2], AF.Exp, ["lamt"], ["lamt"])
    stt(lamt[:, 4:5], lamt[:, 2:3], LAM_INIT, lamt[:, 3:4], ALU.add, ALU.subtract, ["lamt"], ["lamt"])
    ts("dve", lamt[:, 5:6], lamt[:, 4:5], -1.0, None, ALU.mult, None, ["lamt"], ["lamt"])

    def front_load(src_ap, T):
        xt, kx = XT.next()
        dma(xt[0:T, :], src_ap, w=[kx])
        return xt, kx

    def front_a1(xt, kx, T):
        sa, ks = STT.next()
        act(xn[0:T, :], xt[0:T, :], AF.Square, [kx], ["xn", ks], accum=sa[0:T, 0:1])
        rstd_from_ss(sa, ks, 0, 1, 2, 1, T, 1.0 / 1024)
        return sa, ks

    def front_a2(xt, kx, sa, ks, T):
        ts("dve", xn[0:T, :], xt[0:T, :], sa[0:T, 2:3], None, ALU.mult, None, [kx, ks], ["xn"])

    def front_a(xt, kx, T):
        sa, ks = front_a1(xt, kx, T)
        front_a2(xt, kx, sa, ks, T)
        return sa, ks

    def front_b(xt, kx, sa, ks, T):
        for kc in range(8):
            tr(TB[:, kc * 128:kc * 128 + T], xn[0:T, kc * 128:(kc + 1) * 128], identb[0:T, 0:T],
               ["xn", "identb"], [TBK])
        hT, kh = HT.next()
        cpy("act", hT[:, :, 0:T], TB.rearrange("p (k t) -> p k t", k=8)[:, :, 0:T], [TBK], [kh])
        return xt, kx, hT, kh, sa, ks

    def front_compute(xt, kx, T):
        sa, ks = front_a(xt, kx, T)
        return front_b(xt, kx, sa, ks, T)

    def front(src_ap, T):
        xt, kx = front_load(src_ap, T)
        return front_compute(xt, kx, T)

    def group_norm_rstd(src, ksrc, T, ng, sa, ks, base):
        tt("dve", SQ[0:T, 0:ng * 64], src, src, ALU.mult, [ksrc], ["sq"])
        red(sa[0:T, base:base + ng], SQ[0:T, 0:ng * 64].rearrange("p (g d) -> p g d", d=64), ALU.add,
            ["sq"], [ks])
        rstd_from_ss(sa, ks, base, base + ng, base + 2 * ng, ng, T, 1.0 / 64)
        return sa[0:T, base + 2 * ng:base + 3 * ng]

    def phase1_tile(fr, T, ti, j, k_dst, v_dst, later, hook=None):
        xt, kx, hT, kh, sa, ks = fr
        tok = slice(j * 128, j * 128 + T)
        ktok = slice(ti * 128, ti * 128 + T)
        for g in range(8):
            c0 = g * 512
            n = min(512, NIN - c0)
            ps, kp = mmnext()
            for kc in range(8):
                mm(ps[0:T, 0:n], hT[:, kc, 0:T], w_in_bf[:, kc, c0:c0 + n], kc == 0, kc == 7,
                   [kh] + W_IN_K, [kp])
            for it in later:
                it[0] -= 1
            ready = [it for it in later if it[0] <= 0]
            for it in ready:
                later.remove(it)
            for it in ready:
                it[1]()
            if hook is not None:
                hook(g)
            if g == 0 or g == 1:
                zf, kz = ZF.next()
                cpy("act", zf[0:T, :], ps[0:T, 0:512], [kp], [kz])
                r8 = group_norm_rstd(zf[0:T, :], kz, T, 8, sa, ks, 8 if g == 0 else 32)

                def b01(g=g, zf=zf, kz=kz, r8=r8):
                    nb, kn = NB.next()
                    if g == 0:
                        tt("dve", nb[0:T, :].rearrange("p (g d) -> p g d", d=64),
                           zf[0:T, :].rearrange("p (g d) -> p g d", d=64),
                           r8.unsqueeze(2).to_broadcast([T, 8, 64]), ALU.mult, [kz, ks], [kn])
                    else:
                        tt("dve", zf[0:T, :].rearrange("p (g d) -> p g d", d=64),
                           zf[0:T, :].rearrange("p (g d) -> p g d", d=64),
                           r8.unsqueeze(2).to_broadcast([T, 8, 64]), ALU.mult, [kz, ks], [kz])
                        ko, kk = KOUT.next()
                        tt("dve", ko[0:T, :].rearrange("p (g d) -> p g d", d=64),
                           zf[0:T, :].rearrange("p (g d) -> p g d", d=64),
                           cpc("gka_rep")[0:T, :].unsqueeze(1).to_broadcast([T, 8, 64]), ALU.mult,
                           [kz, "cp"], [kk])
                        dma(k_dst, ko[0:T, :], r=[kk])
                        cpy("dve", nb[0:T, :], ko[0:T, :], [kk], [kn])

                    def d01(nb=nb, kn=kn):
                        for h in range(4):
                            tr(TB[:, h * 128:h * 128 + T], nb[0:T, h * 128:(h + 1) * 128], identb[0:T, 0:T],
                               [kn, "identb"], [TBK])
                        src = TB[:, 0:512].rearrange("p (h t) -> p h t", h=4)[:, :, 0:T]
                        if g == 0:
                            ts("dve", qT[:, :, tok], src, cpc("gqa_pp"), None, ALU.mult, None, [TBK, "cp"],
                               [("qT", j)])
                        else:
                            cpy("act", kT[:, :, ktok], src, [TBK], [("kT", ti)])
                    later.append([DELAY, d01])
                later.append([2, b01])
            elif g == 2:
                vo, kv = VOUT.next()
                cpy("act", vo[0:T, :], ps[0:T, 0:512], [kp], [kv])
                dma(v_dst, vo[0:T, :], r=[kv])
                cpy("pool", v_aug[0:T, ti, :, 0:128], vo[0:T, :].rearrange("p (h d) -> p h d", h=4),
                    [kv, "v_ones"], [("v", ti)])
            elif g == 3:
                th, kt = TH.next()
                act(th[0:T, :], ps[0:T, 0:512], AF.Tanh, [kp], [kt], scale=0.5)
                stt(mix[0:T, j, 0:512], th[0:T, :], 1.0, ps[0:T, 0:512], ALU.add, ALU.mult,
                    [kt, kp], [("mix", j, 0)])
            elif g == 4:
                cpy("act", QKB[0:T, j, :], ps[0:T, 0:512], [kp], [("QKB", j)])

                def d4():
                    for i in range(8):
                        tr(TB[0:64, i * 128:i * 128 + T], QKB[0:T, j, i * 64:(i + 1) * 64], identb[0:T, 0:T],
                           [("QKB", j), "identb"], [TBK])
                    cpy("dve", QKT[:, :, tok], TB[0:64, :].rearrange("p (h t) -> p h t", h=8)[:, :, 0:T],
                        [TBK], [("QKT", j)])
                later.append([DELAY, d4])
            elif g == 5:
                cpy("dve", vb_aug[0:T, j, :, 0:64], ps[0:T, 0:256].rearrange("p (h d) -> p h d", h=4),
                    [kp, "vb_ones"], [("vb", j)])
                act(OBG[0:T, j, :], ps[0:T, 256:512], AF.Tanh, [kp], [("OBG", j)], scale=0.5)
            elif g == 6:
                th, kt = TH.next()
                act(th[0:T, 0:256], ps[0:T, 0:256], AF.Tanh, [kp], [kt], scale=0.5)
                stt(th[0:T, 256:512], th[0:T, 0:256], 1.0, ps[0:T, 0:256], ALU.add, ALU.mult, [kt, kp], [kt])
                stt(mix[0:T, j, 512:768], OBG[0:T, j, :], 1.0, th[0:T, 256:512], ALU.add, ALU.mult,
                    [("OBG", j), kt], [("mix", j, 1)])
                zf, kz = ZF.next()
                cpy("act", zf[0:T, 0:256], ps[0:T, 256:512], [kp], [kz])
                r4 = group_norm_rstd(zf[0:T, 0:256], kz, T, 4, sa, ks, 56)

                def b6(zf=zf, kz=kz, r4=r4):
                    nb, kn = NB.next()
                    tt("dve", nb[0:T, 0:256].rearrange("p (g d) -> p g d", d=64),
                       zf[0:T, 0:256].rearrange("p (g d) -> p g d", d=64),
                       r4.unsqueeze(2).to_broadcast([T, 4, 64]), ALU.mult, [kz, ks], [kn])

                    def d6(nb=nb, kn=kn):
                        for hp in range(2):
                            tr(TB[:, hp * 128:hp * 128 + T], nb[0:T, hp * 128:(hp + 1) * 128], identb[0:T, 0:T],
                               [kn, "identb"], [TBK])
                        ts("dve", qmT[:, :, tok], TB[:, 0:256].rearrange("p (h t) -> p h t", h=2)[:, :, 0:T],
                           cpc("gqm_pp"), None, ALU.mult, None, [TBK, "cp"], [("qmT", j)])
                    later.append([DELAY, d6])
                later.append([2, b6])
            else:
                th, kt = TH.next()
                act(th[0:T, 0:256], ps[0:T, 0:256], AF.Tanh, [kp], [kt], scale=0.5)
                stt(mix[0:T, j, 768:1024], th[0:T, 0:256], 1.0, ps[0:T, 0:256], ALU.add, ALU.mult,
                    [kt, kp], [("mix", j, 2)])
                cpy("dve", GIF[0:T, j, :], ps[0:T, 256:264], [kp], [("GIF", j)])

    def phase1_block(tiles, T):
        later = []
        xt, kx = front_load(tiles[0][0], T)
        fr = front_compute(xt, kx, T)
        for idx, (src_ap, ti, j, k_dst, v_dst) in enumerate(tiles):
            nxt = {}
            hook = None
            if idx + 1 < len(tiles):
                nx = front_load(tiles[idx + 1][0], T)

                def hook(g, nx=nx, nxt=nxt):
                    if g == 0:
                        nxt["a"] = front_a1(nx[0], nx[1], T)
                    elif g == 2:
                        front_a2(nx[0], nx[1], nxt["a"][0], nxt["a"][1], T)
                    elif g == 5:
                        nxt["fr"] = front_b(nx[0], nx[1], nxt["a"][0], nxt["a"][1], T)
            phase1_tile(fr, T, ti, j, k_dst, v_dst, later, hook)
            fr = nxt.get("fr")
        while later:
            later.pop(0)[1]()

    def gates_block(T, ntile):
        NQ = ntile * 128
        ibT = TF[0:4, 0:NQ]
        fbT = TBf[0:4, 0:NQ]
        for j in range(ntile):
            tr(TF[0:4, j * 128:j * 128 + T], GIF[0:T, j, 0:4], identf[0:T, 0:T], [("GIF", j), "cp"], [TFK])
            tr(TBf[0:4, j * 128:j * 128 + T], GIF[0:T, j, 4:8], identf[0:T, 0:T], [("GIF", j), "cp"], [TBK])
        if T < 128:
            memset("dve", GA[:, 0:NQ], 0.0, [KGA])
        cs = slice(0, T) if ntile == 1 else slice(0, NQ)
        act(GA[:, cs], fbT[:, cs], AF.Exp, [TBK, "nbf"], [KGA], bias=nbf[0:4, 0:1], scale=-1.0)
        act(GA[:, cs], GA[:, cs], AF.Ln, [KGA], [KGA], bias=1.0)
        scan(GB[:, cs], zerob[0:4, cs], GA[:, cs], Bprev[:, 0:1], ALU.add, ALU.subtract,
             ["zerob", KGA, "Bprev"], [KGB])
        stt(GU[:, cs], ibT[:, cs], cpc("bi_pp")[0:4, :], GB[:, cs], ALU.add, ALU.subtract,
            [TFK, "cp", KGB], [KGU])
        if ntile == 1:
            red(UM[:, 0:1], GU[:, cs], ALU.max, [KGU], ["UM"])
        else:
            red(UM[:, 0:ntile], GU[:, cs].rearrange("p (c t) -> p c t", c=ntile), ALU.max, [KGU], ["UM"])
        scan(MUALL[:, 1:1 + ntile], UM[:, 0:ntile], UM[:, 0:ntile], MUALL[:, 0:1], ALU.max, ALU.max,
             ["UM", "MUALL"], ["MUALL"])
        tt("dve", C0[:, 0:ntile], MUALL[:, 0:ntile], MUALL[:, 1:1 + ntile], ALU.subtract, ["MUALL"], ["C0"])
        act(C0[:, 0:ntile], C0[:, 0:ntile], AF.Exp, ["C0"], ["C0"])
        if ntile == 1:
            mub = MUALL[:, 1:2].to_broadcast([4, T])
            tt("dve", GW[:, cs], GU[:, cs], mub, ALU.subtract, [KGU, "MUALL"], [KGW])
            tt("dve", GE[:, cs], GB[:, cs], mub, ALU.add, [KGB, "MUALL"], [KGE])
        else:
            mub = MUALL[:, 1:1 + ntile].unsqueeze(2).to_broadcast([4, ntile, 128])
            tt("dve", GW[:, cs].rearrange("p (c t) -> p c t", c=ntile),
               GU[:, cs].rearrange("p (c t) -> p c t", c=ntile), mub, ALU.subtract, [KGU, "MUALL"], [KGW])
            tt("dve", GE[:, cs].rearrange("p (c t) -> p c t", c=ntile),
               GB[:, cs].rearrange("p (c t) -> p c t", c=ntile), mub, ALU.add, [KGB, "MUALL"], [KGE])
        act(GW[:, cs], GW[:, cs], AF.Exp, [KGW], [KGW])
        act(GE[:, cs], GE[:, cs], AF.Exp, [KGE], [KGE], scale=-1.0)
        for j in range(ntile):
            tr(TF[0:T, j * 8:j * 8 + 4], GW[0:4, j * 128:j * 128 + T], identf[0:4, 0:4], [KGW, "cp"], [TFK])
            tr(TF[0:T, j * 8 + 4:j * 8 + 8], GE[0:4, j * 128:j * 128 + T], identf[0:4, 0:4], [KGE, "cp"], [TFK])
        cpy("dve", WE[0:T, 0:ntile, :], TF[0:T, 0:ntile * 8].rearrange("p (c e) -> p c e", e=8), [TFK], ["WE"])
        tt("dve", CD[:, 0:ntile, :], C0[:, 0:ntile].unsqueeze(2).to_broadcast([4, ntile, 4]),
           cpc("SEL2")[0:4, :].unsqueeze(1).to_broadcast([4, ntile, 4]), ALU.mult, ["C0", "cp"], ["CD"])
        mm(TBf[0:64, 0:ntile * 4], cpc("SEL")[0:4, 0:64], CD[:, 0:ntile, :].rearrange("p c e -> p (c e)"), True, True,
           ["cp", "CD"], [TBK])
        cpy("dve", C0B[:, 0:ntile, :], TBf[0:64, 0:ntile * 4].rearrange("p (c e) -> p c e", e=4), [TBK], ["C0B"])
        last = T - 1 if ntile == 1 else NQ - 1
        cpy("dve", Bprev[:, 0:1], GB[:, last:last + 1], [KGB], ["Bprev"])
        cpy("dve", MUALL[:, 0:1], MUALL[:, ntile:ntile + 1], ["MUALL"], ["MUALL"])

    def attention_block(T, ntile, ktiles, sample):
        NQ = ntile * T
        ABK = [4, 5, 6]
        qkeys = [("qT", jj) for jj in range(ntile)]
        steps = [(h, t, nk, dsub) for h in range(4) for (t, nk, dsub) in ktiles]
        state = {"n": 0}

        def emit_S(st_):
            h, t, nk, dsub = st_
            c0 = 0 if dsub is None else dsub * T
            ncol = NQ - c0
            b0 = 2 * (state["n"] % 2)
            state["n"] += 1
            kps = [("pb", b0), ("pb", b0 + 1)]
            for m in range(2):
                pr = slice(64 * m, 64 * m + 64)
                mm(PB[b0 + m][0:nk, 0:ncol], kT[pr, h, t * 128:t * 128 + nk], qT[pr, h, c0:NQ],
                   True, dsub is None, [("kT", t)] + qkeys, [kps[m]])
            if dsub is not None:
                for m in range(2):
                    mm(PB[b0 + m][0:nk, 0:T], identb[0:nk, 0:nk], corrb[0:nk, h, 0:T], False, True,
                       ["identb", "corrb"], [kps[m]])
            pt, kpt = PT.next()
            if sample or h != 0:
                calls = [(c0, NQ)]
            else:
                calls = [(c, c + 128) for c in range(c0, NQ, 128)]
            pair = PBIG[0:nk, b0 * 512:(b0 + 2) * 512].rearrange("p (m c) -> p m c", m=2)
            for (ca, cb) in calls:
                if sample:
                    bias = cpc("ABS")[0:nk, h * 17 + t:h * 17 + t + 1]
                else:
                    tref = ktiles[-1][0] - (ntile - 1) + (cb - 1) // 128
                    bias = cpc("AB")[0:nk, h * 16 + (tref - t):h * 16 + (tref - t) + 1]
                act(pt[0:nk, :, ca - c0:cb - c0], pair[:, :, ca - c0:cb - c0], AF.Exp, kps + ["cp"], [kpt],
                    bias=bias, scale=0.125)
            return pt, kpt, c0

        def emit_PV(st_, pt, kpt, c0):
            h, t, nk, dsub = st_
            for m in range(2):
                for i in range(ntile):
                    if i * T < c0:
                        continue
                    a_ = m * 4 + i
                    bnk = ABK[a_ // 3]
                    o = (a_ % 3) * 129
                    mm(PB[bnk][0:T, o:o + 129], pt[0:nk, m, i * T - c0:(i + 1) * T - c0],
                       v_aug[0:nk, t, h, 0:129], False, True, [kpt, ("v", t), "v_ones"], [("pb", bnk)])

        def evac(h):
            bset = ABK
            sa, ks = STT.next()
            if ntile == 4:
                for b_ in range(3):
                    nacc = 3 if b_ < 2 else 2
                    recip(sa[0:T, 3 * b_:3 * b_ + nacc], PB[bset[b_]][0:T, 128:129 * nacc:129],
                          [("pb", bset[b_])], [ks])
            else:
                recip(sa[0:T, 0:1], PB[bset[0]][0:T, 128:129], [("pb", bset[0])], [ks])
                recip(sa[0:T, 4:5], PB[bset[1]][0:T, 129 + 128:129 + 129], [("pb", bset[1])], [ks])
            tt("dve", sa[0:T, 4:4 + ntile], sa[0:T, 4:4 + ntile], lamt[0:T, 5:6].to_broadcast([T, ntile]),
               ALU.mult, [ks, "lamt"], [ks])
            oos = []
            for i in range(ntile):
                a1, a2 = i, 4 + i
                o1, k1 = O1.next()
                b1, b2 = bset[a1 // 3], bset[a2 // 3]
                ts("dve", o1[0:T, :], PB[b1][0:T, (a1 % 3) * 129:(a1 % 3) * 129 + 128], sa[0:T, a1:a1 + 1], None,
                   ALU.mult, None, [("pb", b1), ks], [k1])
                oo, ko = OO.next()
                stt(oo[0:T, :], PB[b2][0:T, (a2 % 3) * 129:(a2 % 3) * 129 + 128], sa[0:T, a2:a2 + 1],
                    o1[0:T, :], ALU.mult, ALU.add, [("pb", b2), ks, k1], [ko])
                oos.append((oo, ko, o1, k1))
            for i in range(ntile):
                oo, ko, o1, k1 = oos[i]
                S.add("dve", (lambda o1=o1, oo=oo, sa=sa, i=i: nc.vector.scalar_tensor_tensor(
                    out=o1[0:T, :], in0=oo[0:T, :], scalar=1.0, in1=oo[0:T, :], op0=ALU.mult, op1=ALU.mult,
                    accum_out=sa[0:T, 8 + i:9 + i])), [ko], [k1, ks])
            ts("dve", sa[0:T, 12:12 + ntile], sa[0:T, 8:8 + ntile], 1.0 / 128, EPS, ALU.mult, ALU.add, [ks], [ks])
            tt("pool", sa[0:T, 16:16 + ntile], sa[0:T, 12:12 + ntile], cm05[0:T, 0:ntile], ALU.pow,
               [ks, "cm05"], [ks])

            def fin(h=h, sa=sa, ks=ks, oos=oos):
                for ii in range(ntile):
                    oo2, ko2 = oos[ii][0], oos[ii][1]
                    stt(mix[0:T, ii, h * 128:(h + 1) * 128], oo2[0:T, :], sa[0:T, 16 + ii:17 + ii],
                        mix[0:T, ii, h * 128:(h + 1) * 128], ALU.mult, ALU.mult,
                        [ko2, ks, ("mix", ii, 0)], [("mix", ii, 0)])
            return fin

        prev = None
        fins = []
        for k, st_ in enumerate(steps):
            h = st_[0]
            new_head = (k == 0 or steps[k - 1][0] != h)
            pt, kpt, c0 = emit_S(st_)
            if prev is not None:
                emit_PV(*prev)
                if new_head:
                    fins.append(evac(prev[0][0]))
            if new_head:
                for b_ in range(3):
                    mm(PB[ABK[b_]][0:T, :], zerob[:, 0:T], zerob[:, :], True, True, ["zerob"], [("pb", ABK[b_])])
            if len(fins) > 0 and not new_head and (k == 0 or steps[k - 2][0] == h):
                for f_ in fins:
                    f_()
                del fins[:]
            prev = (st_, pt, kpt, c0)
        emit_PV(*prev)
        fins.append(evac(prev[0][0]))
        for f_ in fins:
            f_()

    def mem_block(T, ntile):
        NQ = ntile * T
        banks = [(ACC[0], ACCK[0]), (ACC[1], ACCK[1]), (ACC[2], ACCK[2]), (TF, TFK)]
        for i in range(ntile):
            mm(banks[i][0][0:T, :], zerob[:, 0:T], zerob[:, :], True, True, ["zerob"], [banks[i][1]])
        for h in range(4):
            pr = slice(64 * (h % 2), 64 * (h % 2) + 64)
            for nt in range(2):
                ps, kp = mmnext()
                mm(ps[:, 0:NQ], mkT[pr, h // 2, nt * 128:(nt + 1) * 128], qmT[pr, h // 2, 0:NQ], True, True,
                   ["mkT"] + [("qmT", jj) for jj in range(ntile)], [kp])
                pt, kpt = PT.next()
                act(pt[:, 0, 0:NQ], ps[:, 0:NQ], AF.Exp, [kp], [kpt], scale=0.125)
                for i in range(ntile):
                    mm(banks[i][0][0:T, h * 65:(h + 1) * 65], pt[:, 0, i * T:(i + 1) * T], mv_aug[:, nt, h, 0:65],
                       False, True, [kpt, "mv", "mv_ones"], [banks[i][1]])
        for i in range(ntile):
            bk, kb = banks[i]
            sa, ks = STT.next()
            recip(sa[0:T, 0:4], bk[0:T, 64:260:65], [kb], [ks])
            tt("dve", OM[0:T, :, :], bk[0:T, 0:260].rearrange("p (h e) -> p h e", e=65)[:, :, 0:64],
               sa[0:T, 0:4].unsqueeze(2).to_broadcast([T, 4, 64]), ALU.mult, [kb, ks], ["HN"])
            tt("dve", mix[0:T, i, 768:1024], OM[0:T, :, :].rearrange("p h d -> p (h d)"),
               mix[0:T, i, 768:1024], ALU.mult, ["HN", ("mix", i, 2)], [("mix", i, 2)])

    def mlstm_block(T, ntile):
        for j in range(ntile):
            tok = slice(j * 128, j * 128 + T)
            tt("dve", Sd[:, :, 0:65], Sst[:, :, 0:65], C0B[:, j, :].unsqueeze(2).to_broadcast([64, 4, 65]), ALU.mult,
               ["Sst", "C0B"], ["Sd"])
            cpy("pool", Cs[:, :, 0:65], Sd[:, :, 0:65], ["Sd"], ["Cs"])
            tt("pool", vw[0:T, :, 0:65], vb_aug[0:T, j, :, 0:65], WE[0:T, j, 0:4].unsqueeze(2).to_broadcast([T, 4, 65]),
               ALU.mult, [("vb", j), "vb_ones", "WE"], ["vw"])
            chk(80)
            psA, kA = mmnext()
            for h in range(4):
                mm(psA[0:T, h * 128:h * 128 + T], QKT[:, 4 + h, tok], QKT[:, h, tok], True, True,
                   [("QKT", j)], [kA])
            tt("dve", AT[0:T, :, 0:T], psA[0:T, :].rearrange("p (h t) -> p h t", h=4)[:, :, 0:T],
               cpc("maskU")[0:T, 0:T].unsqueeze(1).to_broadcast([T, 4, T]), ALU.mult, [kA, "cp"], ["AT"])
            chk(81)
            psO, kO = ACC[j % 2], ACCK[j % 2]
            for h in range(4):
                mm(psO[0:T, h * 65:(h + 1) * 65], AT[0:T, h, 0:T], vw[0:T, h, 0:65], True, False, ["AT", "vw"], [kO])
                mm(psO[0:T, h * 65:(h + 1) * 65], QKT[:, h, tok], Cs[:, h, 0:65], False, True,
                   [("QKT", j), "Cs"], [kO])
            chk(82)
            psS, kS = ACC[2], ACCK[2]
            for h in range(4):
                mm(psS[0:64, h * 65:(h + 1) * 65], QKB[0:T, j, 256 + h * 64:256 + (h + 1) * 64],
                   vw[0:T, h, 0:65], True, True, [("QKB", j), "vw"], [kS])
            chk(83)
            tt("dve", Sst[:, :, 0:65], Sd[:, :, 0:65], psS[0:64, 0:260].rearrange("p (c e) -> p c e", e=65), ALU.add,
               ["Sd", kS], ["Sst"])
            chk(84)
            sa, ks = STT.next()
            cpy("dve", sa[0:T, 20:24], psO[0:T, 64:260:65], [kO], [ks])
            stt(sa[0:T, 24:28], sa[0:T, 20:24], -1.0, sa[0:T, 20:24], ALU.mult, ALU.max, [ks], [ks])
            tt("dve", sa[0:T, 0:4], sa[0:T, 24:28], WE[0:T, j, 4:8], ALU.max, [ks, "WE"], [ks])
            recip(sa[0:T, 4:8], sa[0:T, 0:4], [ks], [ks])
            tt("dve", HN[0:T, :, :], psO[0:T, 0:260].rearrange("p (h e) -> p h e", e=65)[:, :, 0:64],
               sa[0:T, 4:8].unsqueeze(2).to_broadcast([T, 4, 64]), ALU.mult, [kO, ks], ["HN"])
            r4 = group_norm_rstd(HN[0:T, :, :].rearrange("p h d -> p (h d)"), "HN", T, 4, sa, ks, 8)
            tt("dve", HN[0:T, :, :], HN[0:T, :, :], r4.unsqueeze(2).to_broadcast([T, 4, 64]), ALU.mult,
               ["HN", ks], ["HN"])
            tt("dve", mix[0:T, j, 512:768], HN[0:T, :, :].rearrange("p h d -> p (h d)"), mix[0:T, j, 512:768],
               ALU.mult, ["HN", ("mix", j, 1)], [("mix", j, 1)])

    def phase3_tile(src_ap, dst_ap, T, j):
        for kc in range(8):
            tr(TB[:, kc * 128:kc * 128 + T], mix[0:T, j, kc * 128:(kc + 1) * 128], identb[0:T, 0:T],
               [("mix", j, 0), ("mix", j, 1), ("mix", j, 2), "identb"], [TBK])
        cpy("act", mixT[:, :, 0:T], TB.rearrange("p (k t) -> p k t", k=8)[:, :, 0:T], [TBK], ["mixT"])
        xt, kx = XT.next()
        dma(xt[0:T, :], src_ap, w=[kx])
        for half in range(2):
            ps, kp = ACC[half], ACCK[half]
            for kc in range(8):
                mm(ps[0:T, :], mixT[:, kc, 0:T], w_out_bf[:, kc, half * 512:(half + 1) * 512], kc == 0, kc == 7,
                   ["mixT"] + W_OUT_K, [kp])
            tt("dve", xt[0:T, half * 512:(half + 1) * 512], ps[0:T, :], xt[0:T, half * 512:(half + 1) * 512],
               ALU.add, [kp, kx], [kx])
        dma(dst_ap, xt[0:T, :], r=[kx])

    def memkv_seq(s):
        for nt in range(2):
            xt, kx, hT, kh, sa, ks = front(memp[s, nt * 128:(nt + 1) * 128, :], 128)
            ps, kp = mmnext()
            for kc in range(8):
                mm(ps[:, :], hT[:, kc, :], w_mkv_bf[:, kc, :], kc == 0, kc == 7, [kh] + W_MKV_K, [kp])
            zf, kz = ZF.next()
            cpy("act", zf[:, :], ps[:, :], [kp], [kz])
            dma(pmv[s, nt * 128:(nt + 1) * 128, :], zf[:, 256:512], r=[kz])
            cpy("pool", mv_aug[:, nt, :, 0:64], zf[:, 256:512].rearrange("p (h d) -> p h d", h=4),
                [kz, "mv_ones"], ["mv"])
            r4 = group_norm_rstd(zf[:, 0:256], kz, 128, 4, sa, ks, 8)
            tt("dve", zf[:, 0:256].rearrange("p (g d) -> p g d", d=64),
               zf[:, 0:256].rearrange("p (g d) -> p g d", d=64),
               r4.unsqueeze(2).to_broadcast([128, 4, 64]), ALU.mult, [kz, ks], [kz])
            ko, kk = KOUT.next()
            tt("dve", ko[:, 0:256].rearrange("p (g d) -> p g d", d=64),
               zf[:, 0:256].rearrange("p (g d) -> p g d", d=64),
               cpc("gkm_rep").unsqueeze(1).to_broadcast([128, 4, 64]), ALU.mult, [kz, "cp"], [kk])
            dma(pmk[s, nt * 128:(nt + 1) * 128, :], ko[:, 0:256], r=[kk])
            nb, kn = NB.next()
            cpy("pool", nb[:, 0:256], ko[:, 0:256], [kk], [kn])
            for hp in range(2):
                tr(TB[:, hp * 128:(hp + 1) * 128], nb[:, hp * 128:(hp + 1) * 128], identb[:, :],
                   [kn, "identb"], [TBK])
            cpy("act", mkT[:, :, nt * 128:(nt + 1) * 128], TB[:, 0:256].rearrange("p (h t) -> p h t", h=2),
                [TBK], ["mkT"])

    def state_out(dC, dn, dm):
        for h in range(4):
            dma(dC[h], Sst[:, h, 0:64], r=["Sst"])
            dma(dn[h].unsqueeze(1), Sst[:, h, 64:65], r=["Sst"])
        tt("dve", mfin[:, :], Bprev[:, 0:1], MUALL[:, 0:1], ALU.add, ["Bprev", "MUALL"], ["mfin"])
        dma(dm, mfin[:, :], r=["mfin"])

    try:
        chk(1)
        for s in range(NSEQ):
            memkv_seq(s)
            memset("dve", Sst[:, :, :], 0.0, ["Sst"])
            memset("dve", Bprev[:, :], 0.0, ["Bprev"])
            memset("dve", MUALL[:, 0:1], 0.0, ["MUALL"])
            for b in range(NBLK):
                tl = []
                for j in range(4):
                    ti = 4 * b + j
                    rows = slice(ti * 128, (ti + 1) * 128)
                    tl.append((xp[s, rows, :], ti, j, pk[s, rows, :], pv[s, rows, :]))
                phase1_block(tl, 128)
                gates_block(128, 4)
                ktiles = [(t, 128, None) for t in range(4 * b)] + [(4 * b + i, 128, i) for i in range(4)]
                attention_block(128, 4, ktiles, False)
                mem_block(128, 4)
                mlstm_block(128, 4)
                for j in range(4):
                    ti = 4 * b + j
                    rows = slice(ti * 128, (ti + 1) * 128)
                    phase3_tile(xp[s, rows, :], yp[s, rows, :], 128, j)
            state_out(pC[s], pn[s], pm[s].unsqueeze(1))

        if DO_SAMPLE:
            for t in range(16):
                xt, kx = XT.next()
                dma(xt[:, 0:512], ck[t * 128:(t + 1) * 128, :], w=[kx])
                dma(xt[:, 512:1024], cv[t * 128:(t + 1) * 128, :], w=[kx])
                nb, kn = NB.next()
                cpy("dve", nb[:, :], xt[:, 0:512], [kx], [kn])
                cpy("pool", v_aug[:, t, :, 0:128], xt[:, 512:1024].rearrange("p (h d) -> p h d", h=4),
                    [kx, "v_ones"], [("v", t)])
                for h in range(4):
                    tr(TB[:, h * 128:(h + 1) * 128], nb[:, h * 128:(h + 1) * 128], identb[:, :], [kn, "identb"], [TBK])
                cpy("act", kT[:, :, t * 128:(t + 1) * 128], TB[:, 0:512].rearrange("p (h t) -> p h t", h=4),
                    [TBK], [("kT", t)])
            for nt in range(2):
                xt, kx = XT.next()
                dma(xt[:, 0:256], cmk[nt * 128:(nt + 1) * 128, :], w=[kx])
                dma(xt[:, 256:512], cmv[nt * 128:(nt + 1) * 128, :], w=[kx])
                nb, kn = NB.next()
                cpy("dve", nb[:, 0:256], xt[:, 0:256], [kx], [kn])
                cpy("pool", mv_aug[:, nt, :, 0:64], xt[:, 256:512].rearrange("p (h d) -> p h d", h=4),
                    [kx, "mv_ones"], ["mv"])
                for hp in range(2):
                    tr(TB[:, hp * 128:(hp + 1) * 128], nb[:, hp * 128:(hp + 1) * 128], identb[:, :],
                       [kn, "identb"], [TBK])
                cpy("act", mkT[:, :, nt * 128:(nt + 1) * 128], TB[:, 0:256].rearrange("p (h t) -> p h t", h=2),
                    [TBK], ["mkT"])
            chk(2)
            for h in range(4):
                dma(Sst[:, h, 0:64], sC[h], w=["Sst"])
                dma(Sst[:, h, 64:65], sn[h].unsqueeze(1), w=["Sst"])
            memset("dve", Bprev[:, :], 0.0, ["Bprev"])
            cpy("dve", MUALL[:, 0:1], cpc("sm_pp")[0:4, :], ["cp"], ["MUALL"])
            chk(3)
            phase1_block([(xs[:, :], 16, 0, sk[:, :], sv[:, :])], 16)
            chk(4)
            gates_block(16, 1)
            chk(5)
            attention_block(16, 1, [(t, 128, None) for t in range(16)] + [(16, 16, 0)], True)
            chk(6)
            mem_block(16, 1)
            chk(7)
            mlstm_block(16, 1)
            chk(8)
            phase3_tile(xs[:, :], ys[:, :], 16, 0)
            chk(9)
            state_out(sCo, sno, smo.rearrange("o h -> h o"))

    except _Stop:
        pass

    print('sbuf bytes remaining', nc.sbuf_bytes_remaining)
    S.emit(st)
    st.close()
    return nc


_NC_CACHE = {}


def _get_nc(key=(4, 4, True)):
    if key not in _NC_CACHE:
        _NC_CACHE[key] = build_program(*key)
    return _NC_CACHE[key]


def _in_maps(inp, NSEQ=4):
    maps = []
    c32 = lambda a: np.ascontiguousarray(a, dtype=np.float32)
    for c in range(8):
        m = {
            "xp": c32(inp["x_prompt"][4 * c:4 * c + max(NSEQ, 1)]),
            "memp": c32(inp["mem_prompt"][4 * c:4 * c + max(NSEQ, 1)]),
            "xs": c32(inp["x_sample"][c]),
            "ck": c32(inp["cache_attn_k"][0, c].reshape(2048, 512)),
            "cv": c32(inp["cache_attn_v"][0, c].reshape(2048, 512)),
            "sC": c32(inp["state_mlstm_C"][0, c]),
            "sn": c32(inp["state_mlstm_n"][0, c]),
            "cmk": c32(inp["cache_mem_k"][0, c].reshape(256, 256)),
            "cmv": c32(inp["cache_mem_v"][0, c].reshape(256, 256)),
            "w_in": c32(inp["w_in"][0]),
            "w_out": c32(inp["w_out"][0]),
            "w_mk": c32(inp["w_mk"][0]),
            "w_mv": c32(inp["w_mv"][0]),
        }
        cpa, extra = _make_cp(inp, c)
        m["cp"] = cpa
        m.update(extra)
        maps.append(m)
    return maps


def kernel(**inputs):
    inp = {k: np.asarray(v) for k, v in inputs.items()}
    nc = _get_nc()
    res = run_bass_kernel_spmd(nc, _in_maps(inp), core_ids=list(range(8)))
    R = res.results
    cat = lambda name: np.concatenate([np.asarray(r[name]) for r in R], axis=0)
    stk = lambda name: np.stack([np.asarray(r[name]) for r in R], axis=0)
    y_prompt = cat("yp")
    y_sample = stk("ys")
    p_attn_k = cat("pk").reshape(1, 32, 2048, 4, 128)
    p_attn_v = cat("pv").reshape(1, 32, 2048, 4, 128)
    p_C = cat("pC").reshape(1, 32, 4, 64, 64)
    p_n = cat("pn").reshape(1, 32, 4, 64)
    p_m = cat("pm").reshape(1, 32, 4)
    p_mk = cat("pmk").reshape(1, 32, 256, 4, 64)
    p_mv = cat("pmv").reshape(1, 32, 256, 4, 64)
    s_k = stk("sk").reshape(1, 8, 16, 4, 128)
    s_v = stk("sv").reshape(1, 8, 16, 4, 128)
    s_C = stk("sCo").reshape(1, 8, 4, 64, 64)
    s_n = stk("sno").reshape(1, 8, 4, 64)
    s_m = stk("smo").reshape(1, 8, 4)
    return (y_prompt, y_sample, p_attn_k, p_attn_v, p_C, p_n, p_m, p_mk, p_mv,
            s_k, s_v, s_C, s_n, s_m)
```

```python
import math
from contextlib import ExitStack

import numpy as np
import concourse.bass as bass
import concourse.mybir as mybir
from concourse.bass_utils import run_bass_kernel_spmd

F32 = mybir.dt.float32
BF16 = mybir.dt.bfloat16
AF = mybir.ActivationFunctionType
ALU = mybir.AluOpType
AX = mybir.AxisListType

N_DMA_SEMS = 40
DELAY = 4
EPS = 1e-6
SLOPES = [2.0 ** (-8.0 * (h + 1) / 4) for h in range(4)]
LAM_INIT = 0.8 - 0.6 * math.exp(0.0)
NIN = 3848
BIG = 240000.0


class _Op:
    __slots__ = ("eng", "fn", "deps", "has_dep", "is_dma", "semkey", "val", "ie")


class Sched:
    def __init__(self, nc):
        self.nc = nc
        self.ops = []
        self.lw = {}
        self.rd = {}
        self.eng_n = {"pe": 0, "act": 0, "dve": 0, "pool": 0, "sp": 0}

    def add(self, eng, fn, reads=(), writes=(), dma=False):
        i = len(self.ops)
        deps = set()
        for k in reads:
            w = self.lw.get(k)
            if w is not None:
                deps.add(w)
            if isinstance(k, tuple) and k[0] == "pb":
                for r in self.rd.get(k, ()):
                    if self.ops[r].eng != eng:
                        deps.add(r)
        for k in writes:
            w = self.lw.get(k)
            if w is not None:
                deps.add(w)
            for r in self.rd.get(k, ()):
                deps.add(r)
        op = _Op()
        op.eng = eng
        op.fn = fn
        op.is_dma = dma
        op.has_dep = False
        op.semkey = None
        op.val = 0
        op.ie = self.eng_n[eng]
        self.eng_n[eng] += 1
        keep = []
        for d in deps:
            p = self.ops[d]
            if p.eng == eng and not p.is_dma:
                if eng == "pe" or eng == "sp":
                    continue
                if eng != "pool" and op.ie - p.ie > 2:
                    continue
            p.has_dep = True
            keep.append(d)
        op.deps = keep
        for k in reads:
            self.rd.setdefault(k, []).append(i)
        for k in writes:
            self.lw[k] = i
            self.rd[k] = []
        self.ops.append(op)
        return i

    def emit(self, stack):
        nc = self.nc
        engobj = {"pe": nc.tensor, "act": nc.scalar, "dve": nc.vector,
                  "pool": nc.gpsimd, "sp": nc.sync}
        esem = {e: stack.enter_context(nc.semaphore("s_" + e))
                for e in ("pe", "act", "dve", "pool")}
        dsem = [stack.enter_context(nc.semaphore("d%d" % i)) for i in range(N_DMA_SEMS)]
        waited = {e: {} for e in engobj}
        cnt = {e: 0 for e in engobj}
        dcnt = [0] * N_DMA_SEMS
        rr = 0
        rrp = 0
        for op in self.ops:
            E = engobj[op.eng]
            need = {}
            for d in op.deps:
                p = self.ops[d]
                if need.get(p.semkey, 0) < p.val:
                    need[p.semkey] = p.val
            s = None
            if op.is_dma:
                if op.eng == "pool":
                    s = N_DMA_SEMS - 8 + (rrp % 8)
                    rrp += 1
                else:
                    s = rr % (N_DMA_SEMS - 8)
                    rr += 1
                if dcnt[s] > 0 and need.get(("d", s), 0) < dcnt[s]:
                    need[("d", s)] = dcnt[s]
                dcnt[s] += 16
                op.semkey = ("d", s)
                op.val = dcnt[s]
            w = waited[op.eng]
            for key, val in need.items():
                if w.get(key, 0) >= val:
                    continue
                so = dsem[key[1]] if key[0] == "d" else esem[key[1]]
                E.wait_ge(so, val)
                w[key] = val
            inst = op.fn()
            if op.is_dma:
                inst.then_inc(dsem[s], 16)
            elif op.has_dep:
                cnt[op.eng] += 1
                op.semkey = ("e", op.eng)
                op.val = cnt[op.eng]
                inst.then_inc(esem[op.eng], 1)
        for s in range(N_DMA_SEMS):
            if dcnt[s] > 0:
                nc.sync.wait_ge(dsem[s], dcnt[s])
        return cnt


class Rot:
    def __init__(self, aps, name):
        self.aps = aps
        self.name = name
        self.i = 0

    def next(self):
        k = self.i % len(self.aps)
        self.i += 1
        return self.aps[k], (self.name, k)


def _cp_layout():
    off = {}
    c = 0
    for name, n in [("ident", 128), ("maskU", 128), ("AB", 64), ("ABS", 68),
                    ("gqa_pp", 1), ("gqm_pp", 1), ("gkm_rep", 64), ("gka_rep", 64),
                    ("gnorm_pp", 8), ("gmem_pp", 8), ("rows_pp", 8),
                    ("bi_pp", 1), ("bf_pp", 1), ("sm_pp", 1), ("SEL", 128), ("SEL2", 4)]:
        off[name] = (c, c + n)
        c += n
    return off, c


CP_OFF, NCP = _cp_layout()


def _make_cp(inp, core):
    cp = np.zeros((128, NCP), np.float32)

    def put(name, arr):
        a, b = CP_OFF[name]
        cp[:, a:b] = arr

    p = np.arange(128)
    put("ident", np.eye(128, dtype=np.float32))
    put("maskU", (p[:, None] <= p[None, :]).astype(np.float32))
    corr = np.zeros((128, 4, 128), np.float32)
    k = p[:, None]
    q = p[None, :]
    for h in range(4):
        c_ = np.where(k > q, -16.0 * SLOPES[h] * (k - q), 0.0)
        c_ = np.where((k // 64) > (q // 64), -BIG, c_)
        corr[:, h, :] = c_
    extra = {"corr": corr.reshape(128, 512)}
    AB = np.zeros((128, 4, 16), np.float32)
    for h in range(4):
        for r in range(16):
            AB[:, h, r] = SLOPES[h] * (p - 127 - 128 * r)
    put("AB", AB.reshape(128, 64))
    ABS = np.zeros((128, 4, 17), np.float32)
    for h in range(4):
        for t in range(17):
            ABS[:, h, t] = SLOPES[h] * np.minimum(128 * t + p - 2063, 0)
    put("ABS", ABS.reshape(128, 68))
    put("gqa_pp", inp["g_qa"][0][p % 64][:, None])
    put("gqm_pp", inp["g_qm"][0][p % 64][:, None])
    put("gkm_rep", np.broadcast_to(inp["g_km"][0][None, :], (128, 64)))
    put("gka_rep", np.broadcast_to(inp["g_ka"][0][None, :], (128, 64)))
    put("gnorm_pp", inp["g_norm"][0].reshape(8, 128).T)
    put("gmem_pp", inp["g_mem"][0].reshape(8, 128).T)
    rows = np.ones((128, 8), np.float32)
    rows[:, 0:4] = inp["g_subln"][0][:, None]
    rows[:, 4:6] = inp["g_mh"][0][p % 64][:, None]
    put("rows_pp", rows)
    lam4 = np.concatenate([inp["lam_q1"][0], inp["lam_k1"][0], inp["lam_q2"][0], inp["lam_k2"][0]])
    extra["lam4"] = np.ascontiguousarray(np.broadcast_to(lam4[None, :], (128, 256)))
    put("bi_pp", inp["b_i"][0][p % 4][:, None])
    put("bf_pp", inp["b_f"][0][p % 4][:, None])
    put("sm_pp", inp["state_mlstm_m"][0, core][p % 4][:, None])
    SEL = np.zeros((128, 128), np.float32)
    SEL[0:4, :] = 1.0
    put("SEL", SEL)
    SEL2 = np.zeros((128, 4), np.float32)
    SEL2[0:4, 0:4] = np.eye(4, dtype=np.float32)
    put("SEL2", SEL2)
    return cp, extra


class _Stop(Exception):
    pass


def build_program(NSEQ=4, NBLK=4, DO_SAMPLE=True, STAGE=99):
    nc = bass.Bass("TRN2", target_bir_lowering=False)
    S = Sched(nc)

    def din(name, shape):
        return nc.dram_tensor(name, shape, F32, kind="ExternalInput").ap()

    def dout(name, shape):
        return nc.dram_tensor(name, shape, F32, kind="ExternalOutput").ap()

    NS = max(NSEQ, 1)
    xp = din("xp", [NS, 2048, 1024])
    memp = din("memp", [NS, 256, 1024])
    xs = din("xs", [16, 1024])
    ck = din("ck", [2048, 512])
    cv = din("cv", [2048, 512])
    sC = din("sC", [4, 64, 64])
    sn = din("sn", [4, 64])
    cmk = din("cmk", [256, 256])
    cmv = din("cmv", [256, 256])
    w_in = din("w_in", [1024, NIN])
    w_out = din("w_out", [1024, 1024])
    w_mk = din("w_mk", [1024, 256])
    w_mv = din("w_mv", [1024, 256])
    cpd = din("cp", [128, NCP])
    corrd = din("corr", [128, 512])
    lam4d = din("lam4", [128, 256])

    yp = dout("yp", [NS, 2048, 1024])
    pk = dout("pk", [NS, 2048, 512])
    pv = dout("pv", [NS, 2048, 512])
    pC = dout("pC", [NS, 4, 64, 64])
    pn = dout("pn", [NS, 4, 64])
    pm = dout("pm", [NS, 4])
    pmk = dout("pmk", [NS, 256, 256])
    pmv = dout("pmv", [NS, 256, 256])
    ys = dout("ys", [16, 1024])
    sk = dout("sk", [16, 512])
    sv = dout("sv", [16, 512])
    sCo = dout("sCo", [4, 64, 64])
    sno = dout("sno", [4, 64])
    smo = dout("smo", [1, 4])

    st = ExitStack()

    def chk(n):
        if STAGE == n:
            raise _Stop()

    def sb(name, shape, dt=F32):
        return st.enter_context(nc.sbuf_tensor("sb_" + name, shape, dt))

    cp = sb("cp", [128, NCP])
    w_in_bf = sb("w_in_bf", [128, 8, NIN], BF16)
    w_out_bf = sb("w_out_bf", [128, 8, 1024], BF16)
    w_mkv_bf = sb("w_mkv_bf", [128, 8, 512], BF16)
    identb = sb("identb", [128, 128], BF16)
    corrb = sb("corrb", [128, 4, 128], BF16)
    zerob = sb("zerob", [128, 512], BF16)
    cm05 = sb("cm05", [128, 8])
    lamt = sb("lamt", [128, 8])
    rs8 = sb("rs8", [128, 8])
    nbf = sb("nbf", [128, 1])

    kT = sb("kT", [128, 4, 2064], BF16)
    v_aug = sb("v_aug", [128, 17, 4, 130], BF16)
    mkT = sb("mkT", [128, 2, 256], BF16)
    mv_aug = sb("mv_aug", [128, 2, 4, 66], BF16)

    qT = sb("qT", [128, 4, 512], BF16)
    mix = sb("mix", [128, 4, 1024], BF16)
    OBG = sb("OBG", [128, 4, 256], BF16)
    QKB = sb("QKB", [128, 4, 512], BF16)
    QKT = sb("QKT", [64, 8, 512], BF16)
    vb_aug = sb("vb_aug", [128, 4, 4, 66], BF16)
    qmT = sb("qmT", [128, 2, 512], BF16)
    GIF = sb("GIF", [128, 4, 8])
    WE = sb("WE", [128, 4, 8])
    C0B = sb("C0B", [64, 4, 4])
    Sst = sb("Sst", [64, 4, 66])
    Sd = sb("Sd", [64, 4, 66])
    Cs = sb("Cs", [64, 4, 66], BF16)
    Bprev = sb("Bprev", [4, 1])
    MUALL = sb("MUALL", [4, 8])
    UM = sb("UM", [4, 4])
    C0 = sb("C0", [4, 4])
    CD = sb("CD", [4, 4, 4])
    mfin = sb("mfin", [4, 1])

    XT = Rot([sb("xt%d" % i, [128, 1024])[:] for i in range(2)], "xt")
    xn = sb("xn", [128, 1024], BF16)
    HT = Rot([sb("hT%d" % i, [128, 8, 128], BF16)[:] for i in range(2)], "hT")
    ZF = Rot([sb("zf%d" % i, [128, 512])[:] for i in range(2)], "zf")
    SQ = sb("sq", [128, 512])
    KOUT = Rot([sb("kvout%d" % i, [128, 512])[:] for i in range(3)], "kvout")
    VOUT = KOUT
    TH = Rot([sb("th%d" % i, [128, 512])[:] for i in range(2)], "th")
    NB = Rot([sb("nb%d" % i, [128, 512], BF16)[:] for i in range(4)], "nb")
    PT = Rot([sb("pt%d" % i, [128, 2, 512], BF16)[:] for i in range(3)], "pt")
    STT = Rot([sb("stat%d" % i, [128, 72])[:] for i in range(4)], "stat")
    O1 = Rot([sb("o1s%d" % i, [128, 128])[:] for i in range(4)], "o1s")
    OO = Rot([sb("oo%d" % i, [128, 128])[:] for i in range(4)], "oo")
    mixT = sb("mixT", [128, 8, 128], BF16)
    AT = sb("AT", [128, 4, 128], BF16)
    vw = sb("vw", [128, 4, 66], BF16)
    HN = sb("HN", [128, 4, 64])
    OM = HN
    GA, KGA = SQ[0:4, :], "sq"
    GB, KGB = TH.aps[0][0:4, :], ("th", 0)
    GU, KGU = TH.aps[1][0:4, :], ("th", 1)
    GW, KGW = ZF.aps[0][0:4, :], ("zf", 0)
    GE, KGE = ZF.aps[1][0:4, :], ("zf", 1)

    PBIG = st.enter_context(nc.psum_tensor("pbig", [128, 4096], F32))
    PB = [PBIG[:, i * 512:(i + 1) * 512] for i in range(8)]
    MM = Rot([PB[0], PB[1], PB[2]], "pb_mm")
    MM.keys = [("pb", 0), ("pb", 1), ("pb", 2)]
    ACC = [PB[3], PB[4], PB[5]]
    ACCK = [("pb", 3), ("pb", 4), ("pb", 5)]
    TBf = PB[6]
    TB = PB[6].bitcast(BF16)
    TBK = ("pb", 6)
    TF = PB[7]
    TFK = ("pb", 7)

    def mmnext():
        k = MM.i % 3
        MM.i += 1
        return PB[k], ("pb", k)

    def cpc(name, a=None, b=None):
        o, e = CP_OFF[name]
        if a is None:
            return cp[:, o:e]
        return cp[:, o + a:o + b]

    def dma(out, in_, r=(), w=(), q="sp", **kw):
        if q == "sp":
            S.add("sp", lambda: nc.sync.dma_start(out=out, in_=in_, **kw), r, w, dma=True)
        else:
            S.add("pool", lambda: nc.gpsimd.dma_start(out=out, in_=in_, **kw), r, w, dma=True)

    def mm(out, lhsT, rhs, start, stop, r, w):
        S.add("pe", lambda: nc.tensor.matmul(out, lhsT=lhsT, rhs=rhs, start=start, stop=stop,
                                             skip_group_check=True), r, w)

    def tr(out, in_, ident, r, w):
        S.add("pe", lambda: nc.tensor.transpose(out=out, in_=in_, identity=ident), r, w)

    def act(out, in_, func, r, w, bias=0.0, scale=1.0, accum=None):
        if accum is None:
            S.add("act", lambda: nc.scalar.activation(out=out, in_=in_, func=func, bias=bias, scale=scale), r, w)
        else:
            S.add("act", lambda: nc.scalar.activation(out=out, in_=in_, func=func, bias=bias, scale=scale,
                                                      accum_out=accum), r, w)

    def engo(e):
        return nc.vector if e == "dve" else nc.gpsimd

    def ts(e, out, in0, s1, s2, op0, op1, r, w):
        if s2 is None:
            S.add(e, lambda: engo(e).tensor_scalar(out=out, in0=in0, scalar1=s1, scalar2=None, op0=op0), r, w)
        else:
            S.add(e, lambda: engo(e).tensor_scalar(out=out, in0=in0, scalar1=s1, scalar2=s2, op0=op0, op1=op1), r, w)

    def tt(e, out, in0, in1, op, r, w):
        S.add(e, lambda: engo(e).tensor_tensor(out=out, in0=in0, in1=in1, op=op), r, w)

    def stt(out, in0, scalar, in1, op0, op1, r, w):
        S.add("dve", lambda: nc.vector.scalar_tensor_tensor(out=out, in0=in0, scalar=scalar, in1=in1,
                                                            op0=op0, op1=op1), r, w)

    def cpy(e, out, in_, r, w):
        if e == "act":
            S.add("act", lambda: nc.scalar.copy(out=out, in_=in_), r, w)
        else:
            S.add(e, lambda: engo(e).tensor_copy(out=out, in_=in_), r, w)

    def memset(e, ap, val, w):
        S.add(e, lambda: engo(e).memset(ap, val), (), w)

    def red(out, in_, op, r, w):
        S.add("dve", lambda: nc.vector.tensor_reduce(out=out, in_=in_, axis=AX.X, op=op), r, w)

    def recip(out, in_, r, w):
        S.add("dve", lambda: nc.vector.reciprocal(out=out, in_=in_), r, w)

    def scan(out, d0, d1, init, op0, op1, r, w):
        S.add("dve", lambda: nc.vector.tensor_tensor_scan(out=out, data0=d0, data1=d1, initial=init,
                                                          op0=op0, op1=op1), r, w)

    def rstd_from_ss(stt_ap, kst, c_ss, c_tmp, c_out, n, T, inv):
        ts("dve", stt_ap[0:T, c_tmp:c_tmp + n], stt_ap[0:T, c_ss:c_ss + n], inv, EPS, ALU.mult, ALU.add,
           [kst], [kst])
        tt("pool", stt_ap[0:T, c_out:c_out + n], stt_ap[0:T, c_tmp:c_tmp + n], cm05[0:T, 0:n], ALU.pow,
           [kst, "cm05"], [kst])

    dma(cp[:], cpd, w=["cp"])
    cpy("dve", identb[:], cpc("ident"), ["cp"], ["identb"])
    identf = cpc("ident")
    dma(TH.aps[0][:, :], corrd, w=[("th", 0)])
    cpy("dve", corrb[:].rearrange("p h q -> p (h q)"), TH.aps[0][:, :], [("th", 0)], ["corrb"])
    dma(TH.aps[1][:, 0:256], lam4d, w=[("th", 1)])
    memset("pool", zerob[:], 0.0, ["zerob"])
    memset("pool", cm05[:], -0.5, ["cm05"])
    memset("pool", v_aug[:, :, :, 128:129], 1.0, ["v_ones"])
    memset("pool", vb_aug[:, :, :, 64:65], 1.0, ["vb_ones"])
    memset("pool", mv_aug[:, :, :, 64:65], 1.0, ["mv_ones"])
    ts("dve", rs8[:, 0:4], cpc("rows_pp", 0, 4), 0.5 * (1.0 - LAM_INIT), None, ALU.mult, None, ["cp"], ["rs8"])
    ts("dve", rs8[:, 4:6], cpc("rows_pp", 4, 6), 0.25, None, ALU.mult, None, ["cp"], ["rs8"])
    ts("dve", rs8[:, 6:8], cpc("rows_pp", 6, 8), 0.5, None, ALU.mult, None, ["cp"], ["rs8"])
    ts("dve", nbf[:], cpc("bf_pp"), -1.0, None, ALU.mult, None, ["cp"], ["nbf"])
    wv = w_in.rearrange("(k p) n -> p k n", p=128)
    wov = w_out.rearrange("(k p) n -> p k n", p=128)
    wkv = w_mk.rearrange("(k p) n -> p k n", p=128)
    wvv = w_mv.rearrange("(k p) n -> p k n", p=128)

    def wload(dst, src, n, scal, key, extra=None):
        xt, kx = XT.next()
        dma(xt[:, 0:n], src, w=[kx])
        if extra is None:
            ts("dve", dst, xt[:, 0:n], scal, None, ALU.mult, None, [kx, "cp", "rs8"], [key])
        else:
            ts("dve", dst, xt[:, 0:n], scal, extra, ALU.mult, ALU.mult, [kx, "cp", "rs8"], [key])

    for kc in range(8):
        wload(w_mkv_bf[:, kc, 0:256], wkv[:, kc, :], 256, cpc("gmem_pp", kc, kc + 1), ("w_mkv", kc, 0))
        wload(w_mkv_bf[:, kc, 256:512], wvv[:, kc, :], 256, cpc("gmem_pp", kc, kc + 1), ("w_mkv", kc, 1))
    for kc in range(8):
        g = cpc("gnorm_pp", kc, kc + 1)
        wload(w_in_bf[:, kc, 0:1024], wv[:, kc, 0:1024], 1024, g, ("w_in", kc, 0))
        wload(w_in_bf[:, kc, 1024:2048], wv[:, kc, 1024:2048], 1024, g, ("w_in", kc, 1))
        wload(w_in_bf[:, kc, 2048:2304], wv[:, kc, 2048:2304], 256, g, ("w_in", kc, 2))
        wload(w_in_bf[:, kc, 2304:2560], wv[:, kc, 2304:2560], 256, g, ("w_in", kc, 2), 0.125)
        wload(w_in_bf[:, kc, 2560:3072], wv[:, kc, 2560:3072], 512, g, ("w_in", kc, 2))
        wload(w_in_bf[:, kc, 3072:3840], wv[:, kc, 3080:3848], 768, g, ("w_in", kc, 3))
        wload(w_in_bf[:, kc, 3840:3848], wv[:, kc, 3072:3080], 8, g, ("w_in", kc, 3))
    for kc in range(8):
        wload(w_out_bf[:, kc, :], wov[:, kc, :], 1024, rs8[:, kc:kc + 1], ("w_out", kc))
    W_IN_K = [("w_in", kc, i) for kc in range(8) for i in range(4)]
    W_OUT_K = [("w_out", kc) for kc in range(8)]
    W_MKV_K = [("w_mkv", kc, i) for kc in range(8) for i in range(2)]
    l4 = TH.aps[1]
    tt("dve", SQ[:, 0:64], l4[:, 0:64], l4[:, 64:128], ALU.mult, [("th", 1)], ["sq"])
    red(lamt[:, 0:1], SQ[:, 0:64], ALU.add, ["sq"], ["lamt"])
    tt("dve", SQ[:, 64:128], l4[:, 128:192], l4[:, 192:256], ALU.mult, [("th", 1)], ["sq"])
    red(lamt[:, 1:2], SQ[:, 64:128], ALU.add, ["sq"], ["lamt"])
    act(lamt[:, 2:4], lamt[:, 0:2], AF.Exp, ["lamt"], ["lamt"])
    stt(lamt[:, 4:5], lamt[:, 2:3], LAM_INIT, lamt[:, 3:4], ALU.add, ALU.subtract, ["lamt"], ["lamt"])
    ts("dve", lamt[:, 5:6], lamt[:, 4:5], -1.0, None, ALU.mult, None, ["lamt"], ["lamt"])

    def front_load(src_ap, T):
        xt, kx = XT.next()
        dma(xt[0:T, :], src_ap, w=[kx])
        return xt, kx

    def front_a1(xt, kx, T):
        sa, ks = STT.next()
        act(xn[0:T, :], xt[0:T, :], AF.Square, [kx], ["xn", ks], accum=sa[0:T, 0:1])
        rstd_from_ss(sa, ks, 0, 1, 2, 1, T, 1.0 / 1024)
        return sa, ks

    def front_a2(xt, kx, sa, ks, T):
        ts("dve", xn[0:T, :], xt[0:T, :], sa[0:T, 2:3], None, ALU.mult, None, [kx, ks], ["xn"])

    def front_a(xt, kx, T):
        sa, ks = front_a1(xt, kx, T)
        front_a2(xt, kx, sa, ks, T)
        return sa, ks

    def front_b(xt, kx, sa, ks, T):
        for kc in range(8):
            tr(TB[:, kc * 128:kc * 128 + T], xn[0:T, kc * 128:(kc + 1) * 128], identb[0:T, 0:T],
               ["xn", "identb"], [TBK])
        hT, kh = HT.next()
        cpy("act", hT[:, :, 0:T], TB.rearrange("p (k t) -> p k t", k=8)[:, :, 0:T], [TBK], [kh])
        return xt, kx, hT, kh, sa, ks

    def front_compute(xt, kx, T):
        sa, ks = front_a(xt, kx, T)
        return front_b(xt, kx, sa, ks, T)

    def front(src_ap, T):
        xt, kx = front_load(src_ap, T)
        return front_compute(xt, kx, T)

    def group_norm_rstd(src, ksrc, T, ng, sa, ks, base):
        tt("dve", SQ[0:T, 0:ng * 64], src, src, ALU.mult, [ksrc], ["sq"])
        red(sa[0:T, base:base + ng], SQ[0:T, 0:ng * 64].rearrange("p (g d) -> p g d", d=64), ALU.add,
            ["sq"], [ks])
        rstd_from_ss(sa, ks, base, base + ng, base + 2 * ng, ng, T, 1.0 / 64)
        return sa[0:T, base + 2 * ng:base + 3 * ng]

    def phase1_tile(fr, T, ti, j, k_dst, v_dst, later, hook=None):
        xt, kx, hT, kh, sa, ks = fr
        tok = slice(j * 128, j * 128 + T)
        ktok = slice(ti * 128, ti * 128 + T)
        for g in range(8):
            c0 = g * 512
            n = min(512, NIN - c0)
            ps, kp = mmnext()
            for kc in range(8):
                mm(ps[0:T, 0:n], hT[:, kc, 0:T], w_in_bf[:, kc, c0:c0 + n], kc == 0, kc == 7,
                   [kh] + W_IN_K, [kp])
            for it in later:
                it[0] -= 1
            ready = [it for it in later if it[0] <= 0]
            for it in ready:
                later.remove(it)
            for it in ready:
                it[1]()
            if hook is not None:
                hook(g)
            if g == 0 or g == 1:
                zf, kz = ZF.next()
                cpy("act", zf[0:T, :], ps[0:T, 0:512], [kp], [kz])
                r8 = group_norm_rstd(zf[0:T, :], kz, T, 8, sa, ks, 8 if g == 0 else 32)

                def b01(g=g, zf=zf, kz=kz, r8=r8):
                    nb, kn = NB.next()
                    if g == 0:
                        tt("dve", nb[0:T, :].rearrange("p (g d) -> p g d", d=64),
                           zf[0:T, :].rearrange("p (g d) -> p g d", d=64),
                           r8.unsqueeze(2).to_broadcast([T, 8, 64]), ALU.mult, [kz, ks], [kn])
                    else:
                        tt("dve", zf[0:T, :].rearrange("p (g d) -> p g d", d=64),
                           zf[0:T, :].rearrange("p (g d) -> p g d", d=64),
                           r8.unsqueeze(2).to_broadcast([T, 8, 64]), ALU.mult, [kz, ks], [kz])
                        ko, kk = KOUT.next()
                        tt("dve", ko[0:T, :].rearrange("p (g d) -> p g d", d=64),
                           zf[0:T, :].rearrange("p (g d) -> p g d", d=64),
                           cpc("gka_rep")[0:T, :].unsqueeze(1).to_broadcast([T, 8, 64]), ALU.mult,
                           [kz, "cp"], [kk])
                        dma(k_dst, ko[0:T, :], r=[kk])
                        cpy("dve", nb[0:T, :], ko[0:T, :], [kk], [kn])

                    def d01(nb=nb, kn=kn):
                        for h in range(4):
                            tr(TB[:, h * 128:h * 128 + T], nb[0:T, h * 128:(h + 1) * 128], identb[0:T, 0:T],
                               [kn, "identb"], [TBK])
                        src = TB[:, 0:512].rearrange("p (h t) -> p h t", h=4)[:, :, 0:T]
                        if g == 0:
                            ts("dve", qT[:, :, tok], src, cpc("gqa_pp"), None, ALU.mult, None, [TBK, "cp"],
                               [("qT", j)])
                        else:
                            cpy("act", kT[:, :, ktok], src, [TBK], [("kT", ti)])
                    later.append([DELAY, d01])
                later.append([2, b01])
            elif g == 2:
                vo, kv = VOUT.next()
                cpy("act", vo[0:T, :], ps[0:T, 0:512], [kp], [kv])
                dma(v_dst, vo[0:T, :], r=[kv])
                cpy("pool", v_aug[0:T, ti, :, 0:128], vo[0:T, :].rearrange("p (h d) -> p h d", h=4),
                    [kv, "v_ones"], [("v", ti)])
            elif g == 3:
                th, kt = TH.next()
                act(th[0:T, :], ps[0:T, 0:512], AF.Tanh, [kp], [kt], scale=0.5)
                stt(mix[0:T, j, 0:512], th[0:T, :], 1.0, ps[0:T, 0:512], ALU.add, ALU.mult,
                    [kt, kp], [("mix", j, 0)])
            elif g == 4:
                cpy("act", QKB[0:T, j, :], ps[0:T, 0:512], [kp], [("QKB", j)])

                def d4():
                    for i in range(8):
                        tr(TB[0:64, i * 128:i * 128 + T], QKB[0:T, j, i * 64:(i + 1) * 64], identb[0:T, 0:T],
                           [("QKB", j), "identb"], [TBK])
                    cpy("dve", QKT[:, :, tok], TB[0:64, :].rearrange("p (h t) -> p h t", h=8)[:, :, 0:T],
                        [TBK], [("QKT", j)])
                later.append([DELAY, d4])
            elif g == 5:
                cpy("dve", vb_aug[0:T, j, :, 0:64], ps[0:T, 0:256].rearrange("p (h d) -> p h d", h=4),
                    [kp, "vb_ones"], [("vb", j)])
                act(OBG[0:T, j, :], ps[0:T, 256:512], AF.Tanh, [kp], [("OBG", j)], scale=0.5)
            elif g == 6:
                th, kt = TH.next()
                act(th[0:T, 0:256], ps[0:T, 0:256], AF.Tanh, [kp], [kt], scale=0.5)
                stt(th[0:T, 256:512], th[0:T, 0:256], 1.0, ps[0:T, 0:256], ALU.add, ALU.mult, [kt, kp], [kt])
                stt(mix[0:T, j, 512:768], OBG[0:T, j, :], 1.0, th[0:T, 256:512], ALU.add, ALU.mult,
                    [("OBG", j), kt], [("mix", j, 1)])
                zf, kz = ZF.next()
                cpy("act", zf[0:T, 0:256], ps[0:T, 256:512], [kp], [kz])
                r4 = group_norm_rstd(zf[0:T, 0:256], kz, T, 4, sa, ks, 56)

                def b6(zf=zf, kz=kz, r4=r4):
                    nb, kn = NB.next()
                    tt("dve", nb[0:T, 0:256].rearrange("p (g d) -> p g d", d=64),
                       zf[0:T, 0:256].rearrange("p (g d) -> p g d", d=64),
                       r4.unsqueeze(2).to_broadcast([T, 4, 64]), ALU.mult, [kz, ks], [kn])

                    def d6(nb=nb, kn=kn):
                        for hp in range(2):
                            tr(TB[:, hp * 128:hp * 128 + T], nb[0:T, hp * 128:(hp + 1) * 128], identb[0:T, 0:T],
                               [kn, "identb"], [TBK])
                        ts("dve", qmT[:, :, tok], TB[:, 0:256].rearrange("p (h t) -> p h t", h=2)[:, :, 0:T],
                           cpc("gqm_pp"), None, ALU.mult, None, [TBK, "cp"], [("qmT", j)])
                    later.append([DELAY, d6])
                later.append([2, b6])
            else:
                th, kt = TH.next()
                act(th[0:T, 0:256], ps[0:T, 0:256], AF.Tanh, [kp], [kt], scale=0.5)
                stt(mix[0:T, j, 768:1024], th[0:T, 0:256], 1.0, ps[0:T, 0:256], ALU.add, ALU.mult,
                    [kt, kp], [("mix", j, 2)])
                cpy("dve", GIF[0:T, j, :], ps[0:T, 256:264], [kp], [("GIF", j)])

    def phase1_block(tiles, T):
        later = []
        xt, kx = front_load(tiles[0][0], T)
        fr = front_compute(xt, kx, T)
        for idx, (src_ap, ti, j, k_dst, v_dst) in enumerate(tiles):
            nxt = {}
            hook = None
            if idx + 1 < len(tiles):
                nx = front_load(tiles[idx + 1][0], T)

                def hook(g, nx=nx, nxt=nxt):
                    if g == 0:
                        nxt["a"] = front_a1(nx[0], nx[1], T)
                    elif g == 2:
                        front_a2(nx[0], nx[1], nxt["a"][0], nxt["a"][1], T)
                    elif g == 5:
                        nxt["fr"] = front_b(nx[0], nx[1], nxt["a"][0], nxt["a"][1], T)
            phase1_tile(fr, T, ti, j, k_dst, v_dst, later, hook)
            fr = nxt.get("fr")
        while later:
            later.pop(0)[1]()

    def gates_block(T, ntile):
        NQ = ntile * 128
        ibT = TF[0:4, 0:NQ]
        fbT = TBf[0:4, 0:NQ]
        for j in range(ntile):
            tr(TF[0:4, j * 128:j * 128 + T], GIF[0:T, j, 0:4], identf[0:T, 0:T], [("GIF", j), "cp"], [TFK])
            tr(TBf[0:4, j * 128:j * 128 + T], GIF[0:T, j, 4:8], identf[0:T, 0:T], [("GIF", j), "cp"], [TBK])
        if T < 128:
            memset("dve", GA[:, 0:NQ], 0.0, [KGA])
        cs = slice(0, T) if ntile == 1 else slice(0, NQ)
        act(GA[:, cs], fbT[:, cs], AF.Exp, [TBK, "nbf"], [KGA], bias=nbf[0:4, 0:1], scale=-1.0)
        act(GA[:, cs], GA[:, cs], AF.Ln, [KGA], [KGA], bias=1.0)
        scan(GB[:, cs], zerob[0:4, cs], GA[:, cs], Bprev[:, 0:1], ALU.add, ALU.subtract,
             ["zerob", KGA, "Bprev"], [KGB])
        stt(GU[:, cs], ibT[:, cs], cpc("bi_pp")[0:4, :], GB[:, cs], ALU.add, ALU.subtract,
            [TFK, "cp", KGB], [KGU])
        if ntile == 1:
            red(UM[:, 0:1], GU[:, cs], ALU.max, [KGU], ["UM"])
        else:
            red(UM[:, 0:ntile], GU[:, cs].rearrange("p (c t) -> p c t", c=ntile), ALU.max, [KGU], ["UM"])
        scan(MUALL[:, 1:1 + ntile], UM[:, 0:ntile], UM[:, 0:ntile], MUALL[:, 0:1], ALU.max, ALU.max,
             ["UM", "MUALL"], ["MUALL"])
        tt("dve", C0[:, 0:ntile], MUALL[:, 0:ntile], MUALL[:, 1:1 + ntile], ALU.subtract, ["MUALL"], ["C0"])
        act(C0[:, 0:ntile], C0[:, 0:ntile], AF.Exp, ["C0"], ["C0"])
        if ntile == 1:
            mub = MUALL[:, 1:2].to_broadcast([4, T])
            tt("dve", GW[:, cs], GU[:, cs], mub, ALU.subtract, [KGU, "MUALL"], [KGW])
            tt("dve", GE[:, cs], GB[:, cs], mub, ALU.add, [KGB, "MUALL"], [KGE])
        else:
            mub = MUALL[:, 1:1 + ntile].unsqueeze(2).to_broadcast([4, ntile, 128])
            tt("dve", GW[:, cs].rearrange("p (c t) -> p c t", c=ntile),
               GU[:, cs].rearrange("p (c t) -> p c t", c=ntile), mub, ALU.subtract, [KGU, "MUALL"], [KGW])
            tt("dve", GE[:, cs].rearrange("p (c t) -> p c t", c=ntile),
               GB[:, cs].rearrange("p (c t) -> p c t", c=ntile), mub, ALU.add, [KGB, "MUALL"], [KGE])
        act(GW[:, cs], GW[:, cs], AF.Exp, [KGW], [KGW])
        act(GE[:, cs], GE[:, cs], AF.Exp, [KGE], [KGE], scale=-1.0)
        for j in range(ntile):
            tr(TF[0:T, j * 8:j * 8 + 4], GW[0:4, j * 128:j * 128 + T], identf[0:4, 0:4], [KGW, "cp"], [TFK])
            tr(TF[0:T, j * 8 + 4:j * 8 + 8], GE[0:4, j * 128:j * 128 + T], identf[0:4, 0:4], [KGE, "cp"], [TFK])
        cpy("dve", WE[0:T, 0:ntile, :], TF[0:T, 0:ntile * 8].rearrange("p (c e) -> p c e", e=8), [TFK], ["WE"])
        tt("dve", CD[:, 0:ntile, :], C0[:, 0:ntile].unsqueeze(2).to_broadcast([4, ntile, 4]),
           cpc("SEL2")[0:4, :].unsqueeze(1).to_broadcast([4, ntile, 4]), ALU.mult, ["C0", "cp"], ["CD"])
        mm(TBf[0:64, 0:ntile * 4], cpc("SEL")[0:4, 0:64], CD[:, 0:ntile, :].rearrange("p c e -> p (c e)"), True, True,
           ["cp", "CD"], [TBK])
        cpy("dve", C0B[:, 0:ntile, :], TBf[0:64, 0:ntile * 4].rearrange("p (c e) -> p c e", e=4), [TBK], ["C0B"])
        last = T - 1 if ntile == 1 else NQ - 1
        cpy("dve", Bprev[:, 0:1], GB[:, last:last + 1], [KGB], ["Bprev"])
        cpy("dve", MUALL[:, 0:1], MUALL[:, ntile:ntile + 1], ["MUALL"], ["MUALL"])

    def attention_block(T, ntile, ktiles, sample):
        NQ = ntile * T
        ABK = [4, 5, 6]
        qkeys = [("qT", jj) for jj in range(ntile)]
        steps = [(h, t, nk, dsub) for h in range(4) for (t, nk, dsub) in ktiles]
        state = {"n": 0}

        def emit_S(st_):
            h, t, nk, dsub = st_
            c0 = 0 if dsub is None else dsub * T
            ncol = NQ - c0
            b0 = 2 * (state["n"] % 2)
            state["n"] += 1
            kps = [("pb", b0), ("pb", b0 + 1)]
            for m in range(2):
                pr = slice(64 * m, 64 * m + 64)
                mm(PB[b0 + m][0:nk, 0:ncol], kT[pr, h, t * 128:t * 128 + nk], qT[pr, h, c0:NQ],
                   True, dsub is None, [("kT", t)] + qkeys, [kps[m]])
            if dsub is not None:
                for m in range(2):
                    mm(PB[b0 + m][0:nk, 0:T], identb[0:nk, 0:nk], corrb[0:nk, h, 0:T], False, True,
                       ["identb", "corrb"], [kps[m]])
            pt, kpt = PT.next()
            if sample or h != 0:
                calls = [(c0, NQ)]
            else:
                calls = [(c, c + 128) for c in range(c0, NQ, 128)]
            pair = PBIG[0:nk, b0 * 512:(b0 + 2) * 512].rearrange("p (m c) -> p m c", m=2)
            for (ca, cb) in calls:
                if sample:
                    bias = cpc("ABS")[0:nk, h * 17 + t:h * 17 + t + 1]
                else:
                    tref = ktiles[-1][0] - (ntile - 1) + (cb - 1) // 128
                    bias = cpc("AB")[0:nk, h * 16 + (tref - t):h * 16 + (tref - t) + 1]
                act(pt[0:nk, :, ca - c0:cb - c0], pair[:, :, ca - c0:cb - c0], AF.Exp, kps + ["cp"], [kpt],
                    bias=bias, scale=0.125)
            return pt, kpt, c0

        def emit_PV(st_, pt, kpt, c0):
            h, t, nk, dsub = st_
            for m in range(2):
                for i in range(ntile):
                    if i * T < c0:
                        continue
                    a_ = m * 4 + i
                    bnk = ABK[a_ // 3]
                    o = (a_ % 3) * 129
                    mm(PB[bnk][0:T, o:o + 129], pt[0:nk, m, i * T - c0:(i + 1) * T - c0],
                       v_aug[0:nk, t, h, 0:129], False, True, [kpt, ("v", t), "v_ones"], [("pb", bnk)])

        def evac(h):
            bset = ABK
            sa, ks = STT.next()
            if ntile == 4:
                for b_ in range(3):
                    nacc = 3 if b_ < 2 else 2
                    recip(sa[0:T, 3 * b_:3 * b_ + nacc], PB[bset[b_]][0:T, 128:129 * nacc:129],
                          [("pb", bset[b_])], [ks])
            else:
                recip(sa[0:T, 0:1], PB[bset[0]][0:T, 128:129], [("pb", bset[0])], [ks])
                recip(sa[0:T, 4:5], PB[bset[1]][0:T, 129 + 128:129 + 129], [("pb", bset[1])], [ks])
            tt("dve", sa[0:T, 4:4 + ntile], sa[0:T, 4:4 + ntile], lamt[0:T, 5:6].to_broadcast([T, ntile]),
               ALU.mult, [ks, "lamt"], [ks])
            oos = []
            for i in range(ntile):
                a1, a2 = i, 4 + i
                o1, k1 = O1.next()
                b1, b2 = bset[a1 // 3], bset[a2 // 3]
                ts("dve", o1[0:T, :], PB[b1][0:T, (a1 % 3) * 129:(a1 % 3) * 129 + 128], sa[0:T, a1:a1 + 1], None,
                   ALU.mult, None, [("pb", b1), ks], [k1])
                oo, ko = OO.next()
                stt(oo[0:T, :], PB[b2][0:T, (a2 % 3) * 129:(a2 % 3) * 129 + 128], sa[0:T, a2:a2 + 1],
                    o1[0:T, :], ALU.mult, ALU.add, [("pb", b2), ks, k1], [ko])
                oos.append((oo, ko, o1, k1))
            for i in range(ntile):
                oo, ko, o1, k1 = oos[i]
                S.add("dve", (lambda o1=o1, oo=oo, sa=sa, i=i: nc.vector.scalar_tensor_tensor(
                    out=o1[0:T, :], in0=oo[0:T, :], scalar=1.0, in1=oo[0:T, :], op0=ALU.mult, op1=ALU.mult,
                    accum_out=sa[0:T, 8 + i:9 + i])), [ko], [k1, ks])
            ts("dve", sa[0:T, 12:12 + ntile], sa[0:T, 8:8 + ntile], 1.0 / 128, EPS, ALU.mult, ALU.add, [ks], [ks])
            tt("pool", sa[0:T, 16:16 + ntile], sa[0:T, 12:12 + ntile], cm05[0:T, 0:ntile], ALU.pow,
               [ks, "cm05"], [ks])

            def fin(h=h, sa=sa, ks=ks, oos=oos):
                for ii in range(ntile):
                    oo2, ko2 = oos[ii][0], oos[ii][1]
                    stt(mix[0:T, ii, h * 128:(h + 1) * 128], oo2[0:T, :], sa[0:T, 16 + ii:17 + ii],
                        mix[0:T, ii, h * 128:(h + 1) * 128], ALU.mult, ALU.mult,
                        [ko2, ks, ("mix", ii, 0)], [("mix", ii, 0)])
            return fin

        prev = None
        fins = []
        for k, st_ in enumerate(steps):
            h = st_[0]
            new_head = (k == 0 or steps[k - 1][0] != h)
            pt, kpt, c0 = emit_S(st_)
            if prev is not None:
                emit_PV(*prev)
                if new_head:
                    fins.append(evac(prev[0][0]))
            if new_head:
                for b_ in range(3):
                    mm(PB[ABK[b_]][0:T, :], zerob[:, 0:T], zerob[:, :], True, True, ["zerob"], [("pb", ABK[b_])])
            if len(fins) > 0 and not new_head and (k == 0 or steps[k - 2][0] == h):
                for f_ in fins:
                    f_()
                del fins[:]
            prev = (st_, pt, kpt, c0)
        emit_PV(*prev)
        fins.append(evac(prev[0][0]))
        for f_ in fins:
            f_()

    def mem_block(T, ntile):
        NQ = ntile * T
        banks = [(ACC[0], ACCK[0]), (ACC[1], ACCK[1]), (ACC[2], ACCK[2]), (TF, TFK)]
        for i in range(ntile):
            mm(banks[i][0][0:T, :], zerob[:, 0:T], zerob[:, :], True, True, ["zerob"], [banks[i][1]])
        prevm = None
        for h in range(4):
            pr = slice(64 * (h % 2), 64 * (h % 2) + 64)
            for nt in range(2):
                ps, kp = mmnext()
                mm(ps[:, 0:NQ], mkT[pr, h // 2, nt * 128:(nt + 1) * 128], qmT[pr, h // 2, 0:NQ], True, True,
                   ["mkT"] + [("qmT", jj) for jj in range(ntile)], [kp])
                pt, kpt = PT.next()
                act(pt[:, 0, 0:NQ], ps[:, 0:NQ], AF.Exp, [kp], [kpt], scale=0.125)
                if prevm is not None:
                    ph, pnt, ppt, pkpt = prevm
                    for i in range(ntile):
                        mm(banks[i][0][0:T, ph * 65:(ph + 1) * 65], ppt[:, 0, i * T:(i + 1) * T],
                           mv_aug[:, pnt, ph, 0:65], False, True, [pkpt, "mv", "mv_ones"], [banks[i][1]])
                prevm = (h, nt, pt, kpt)
        ph, pnt, ppt, pkpt = prevm
        for i in range(ntile):
            mm(banks[i][0][0:T, ph * 65:(ph + 1) * 65], ppt[:, 0, i * T:(i + 1) * T],
               mv_aug[:, pnt, ph, 0:65], False, True, [pkpt, "mv", "mv_ones"], [banks[i][1]])
        for i in range(ntile):
            bk, kb = banks[i]
            sa, ks = STT.next()
            recip(sa[0:T, 0:4], bk[0:T, 64:260:65], [kb], [ks])
            tt("dve", OM[0:T, :, :], bk[0:T, 0:260].rearrange("p (h e) -> p h e", e=65)[:, :, 0:64],
               sa[0:T, 0:4].unsqueeze(2).to_broadcast([T, 4, 64]), ALU.mult, [kb, ks], ["HN"])
            tt("dve", mix[0:T, i, 768:1024], OM[0:T, :, :].rearrange("p h d -> p (h d)"),
               mix[0:T, i, 768:1024], ALU.mult, ["HN", ("mix", i, 2)], [("mix", i, 2)])

    def mlstm_block(T, ntile):
        pend_a = []
        pend_b = []
        for j in range(ntile):
            tok = slice(j * 128, j * 128 + T)
            tt("dve", Sd[:, :, 0:65], Sst[:, :, 0:65], C0B[:, j, :].unsqueeze(2).to_broadcast([64, 4, 65]), ALU.mult,
               ["Sst", "C0B"], ["Sd"])
            cpy("pool", Cs[:, :, 0:65], Sd[:, :, 0:65], ["Sd"], ["Cs"])
            tt("pool", vw[0:T, :, 0:65], vb_aug[0:T, j, :, 0:65], WE[0:T, j, 0:4].unsqueeze(2).to_broadcast([T, 4, 65]),
               ALU.mult, [("vb", j), "vb_ones", "WE"], ["vw"])
            chk(80)
            psA, kA = mmnext()
            for h in range(4):
                mm(psA[0:T, h * 128:h * 128 + T], QKT[:, 4 + h, tok], QKT[:, h, tok], True, True,
                   [("QKT", j)], [kA])
            tt("dve", AT[0:T, :, 0:T], psA[0:T, :].rearrange("p (h t) -> p h t", h=4)[:, :, 0:T],
               cpc("maskU")[0:T, 0:T].unsqueeze(1).to_broadcast([T, 4, T]), ALU.mult, [kA, "cp"], ["AT"])
            chk(81)
            psO, kO = ACC[j % 2], ACCK[j % 2]
            for h in range(4):
                mm(psO[0:T, h * 65:(h + 1) * 65], AT[0:T, h, 0:T], vw[0:T, h, 0:65], True, False, ["AT", "vw"], [kO])
                mm(psO[0:T, h * 65:(h + 1) * 65], QKT[:, h, tok], Cs[:, h, 0:65], False, True,
                   [("QKT", j), "Cs"], [kO])
            chk(82)
            psS, kS = ACC[2], ACCK[2]
            for h in range(4):
                mm(psS[0:64, h * 65:(h + 1) * 65], QKB[0:T, j, 256 + h * 64:256 + (h + 1) * 64],
                   vw[0:T, h, 0:65], True, True, [("QKB", j), "vw"], [kS])
            chk(83)
            tt("dve", Sst[:, :, 0:65], Sd[:, :, 0:65], psS[0:64, 0:260].rearrange("p (c e) -> p c e", e=65), ALU.add,
               ["Sd", kS], ["Sst"])

            def evac_a(j=j, psO=psO, kO=kO):
                sa, ks = STT.next()
                cpy("dve", sa[0:T, 20:24], psO[0:T, 64:260:65], [kO], [ks])
                stt(sa[0:T, 24:28], sa[0:T, 20:24], -1.0, sa[0:T, 20:24], ALU.mult, ALU.max, [ks], [ks])
                tt("dve", sa[0:T, 0:4], sa[0:T, 24:28], WE[0:T, j, 4:8], ALU.max, [ks, "WE"], [ks])
                recip(sa[0:T, 4:8], sa[0:T, 0:4], [ks], [ks])
                tt("dve", HN[0:T, :, :], psO[0:T, 0:260].rearrange("p (h e) -> p h e", e=65)[:, :, 0:64],
                   sa[0:T, 4:8].unsqueeze(2).to_broadcast([T, 4, 64]), ALU.mult, [kO, ks], ["HN"])
                r4 = group_norm_rstd(HN[0:T, :, :].rearrange("p h d -> p (h d)"), "HN", T, 4, sa, ks, 8)

                def evac_b():
                    tt("dve", HN[0:T, :, :], HN[0:T, :, :], r4.unsqueeze(2).to_broadcast([T, 4, 64]), ALU.mult,
                       ["HN", ks], ["HN"])
                    tt("dve", mix[0:T, j, 512:768], HN[0:T, :, :].rearrange("p h d -> p (h d)"),
                       mix[0:T, j, 512:768], ALU.mult, ["HN", ("mix", j, 1)], [("mix", j, 1)])
                return evac_b
            pend_a.append(evac_a)
            if len(pend_a) == 2:
                for f_ in pend_b:
                    f_()
                del pend_b[:]
                pend_b.append(pend_a.pop(0)())
        for f_ in pend_b:
            f_()
        del pend_b[:]
        while pend_a:
            pend_a.pop(0)()()

    def phase3_tile(src_ap, dst_ap, T, j):
        for kc in range(8):
            tr(TB[:, kc * 128:kc * 128 + T], mix[0:T, j, kc * 128:(kc + 1) * 128], identb[0:T, 0:T],
               [("mix", j, 0), ("mix", j, 1), ("mix", j, 2), "identb"], [TBK])
        cpy("act", mixT[:, :, 0:T], TB.rearrange("p (k t) -> p k t", k=8)[:, :, 0:T], [TBK], ["mixT"])
        xt, kx = XT.next()
        dma(xt[0:T, :], src_ap, w=[kx])
        for half in range(2):
            ps, kp = ACC[half], ACCK[half]
            for kc in range(8):
                mm(ps[0:T, :], mixT[:, kc, 0:T], w_out_bf[:, kc, half * 512:(half + 1) * 512], kc == 0, kc == 7,
                   ["mixT"] + W_OUT_K, [kp])
            tt("dve", xt[0:T, half * 512:(half + 1) * 512], ps[0:T, :], xt[0:T, half * 512:(half + 1) * 512],
               ALU.add, [kp, kx], [kx])
        dma(dst_ap, xt[0:T, :], r=[kx])

    def memkv_seq(s):
        for nt in range(2):
            xt, kx, hT, kh, sa, ks = front(memp[s, nt * 128:(nt + 1) * 128, :], 128)
            ps, kp = mmnext()
            for kc in range(8):
                mm(ps[:, :], hT[:, kc, :], w_mkv_bf[:, kc, :], kc == 0, kc == 7, [kh] + W_MKV_K, [kp])
            zf, kz = ZF.next()
            cpy("act", zf[:, :], ps[:, :], [kp], [kz])
            dma(pmv[s, nt * 128:(nt + 1) * 128, :], zf[:, 256:512], r=[kz])
            cpy("pool", mv_aug[:, nt, :, 0:64], zf[:, 256:512].rearrange("p (h d) -> p h d", h=4),
                [kz, "mv_ones"], ["mv"])
            r4 = group_norm_rstd(zf[:, 0:256], kz, 128, 4, sa, ks, 8)
            tt("dve", zf[:, 0:256].rearrange("p (g d) -> p g d", d=64),
               zf[:, 0:256].rearrange("p (g d) -> p g d", d=64),
               r4.unsqueeze(2).to_broadcast([128, 4, 64]), ALU.mult, [kz, ks], [kz])
            ko, kk = KOUT.next()
            tt("dve", ko[:, 0:256].rearrange("p (g d) -> p g d", d=64),
               zf[:, 0:256].rearrange("p (g d) -> p g d", d=64),
               cpc("gkm_rep").unsqueeze(1).to_broadcast([128, 4, 64]), ALU.mult, [kz, "cp"], [kk])
            dma(pmk[s, nt * 128:(nt + 1) * 128, :], ko[:, 0:256], r=[kk])
            nb, kn = NB.next()
            cpy("pool", nb[:, 0:256], ko[:, 0:256], [kk], [kn])
            for hp in range(2):
                tr(TB[:, hp * 128:(hp + 1) * 128], nb[:, hp * 128:(hp + 1) * 128], identb[:, :],
                   [kn, "identb"], [TBK])
            cpy("act", mkT[:, :, nt * 128:(nt + 1) * 128], TB[:, 0:256].rearrange("p (h t) -> p h t", h=2),
                [TBK], ["mkT"])

    def state_out(dC, dn, dm):
        for h in range(4):
            dma(dC[h], Sst[:, h, 0:64], r=["Sst"])
            dma(dn[h].unsqueeze(1), Sst[:, h, 64:65], r=["Sst"])
        tt("dve", mfin[:, :], Bprev[:, 0:1], MUALL[:, 0:1], ALU.add, ["Bprev", "MUALL"], ["mfin"])
        dma(dm, mfin[:, :], r=["mfin"])

    try:
        chk(1)
        if DO_SAMPLE:
            for t in range(16):
                xt, kx = XT.next()
                dma(xt[:, 0:512], ck[t * 128:(t + 1) * 128, :], w=[kx])
                dma(xt[:, 512:1024], cv[t * 128:(t + 1) * 128, :], w=[kx])
                nb, kn = NB.next()
                cpy("dve", nb[:, :], xt[:, 0:512], [kx], [kn])
                cpy("pool", v_aug[:, t, :, 0:128], xt[:, 512:1024].rearrange("p (h d) -> p h d", h=4),
                    [kx, "v_ones"], [("v", t)])
                for h in range(4):
                    tr(TB[:, h * 128:(h + 1) * 128], nb[:, h * 128:(h + 1) * 128], identb[:, :], [kn, "identb"], [TBK])
                cpy("act", kT[:, :, t * 128:(t + 1) * 128], TB[:, 0:512].rearrange("p (h t) -> p h t", h=4),
                    [TBK], [("kT", t)])
            for nt in range(2):
                xt, kx = XT.next()
                dma(xt[:, 0:256], cmk[nt * 128:(nt + 1) * 128, :], w=[kx])
                dma(xt[:, 256:512], cmv[nt * 128:(nt + 1) * 128, :], w=[kx])
                nb, kn = NB.next()
                cpy("dve", nb[:, 0:256], xt[:, 0:256], [kx], [kn])
                cpy("pool", mv_aug[:, nt, :, 0:64], xt[:, 256:512].rearrange("p (h d) -> p h d", h=4),
                    [kx, "mv_ones"], ["mv"])
                for hp in range(2):
                    tr(TB[:, hp * 128:(hp + 1) * 128], nb[:, hp * 128:(hp + 1) * 128], identb[:, :],
                       [kn, "identb"], [TBK])
                cpy("act", mkT[:, :, nt * 128:(nt + 1) * 128], TB[:, 0:256].rearrange("p (h t) -> p h t", h=2),
                    [TBK], ["mkT"])
            chk(2)
            for h in range(4):
                dma(Sst[:, h, 0:64], sC[h], w=["Sst"])
                dma(Sst[:, h, 64:65], sn[h].unsqueeze(1), w=["Sst"])
            memset("dve", Bprev[:, :], 0.0, ["Bprev"])
            cpy("dve", MUALL[:, 0:1], cpc("sm_pp")[0:4, :], ["cp"], ["MUALL"])
            chk(3)
            phase1_block([(xs[:, :], 16, 0, sk[:, :], sv[:, :])], 16)
            chk(4)
            gates_block(16, 1)
            chk(5)
            attention_block(16, 1, [(t, 128, None) for t in range(16)] + [(16, 16, 0)], True)
            chk(6)
            mem_block(16, 1)
            chk(7)
            mlstm_block(16, 1)
            chk(8)
            phase3_tile(xs[:, :], ys[:, :], 16, 0)
            chk(9)
            state_out(sCo, sno, smo.rearrange("o h -> h o"))

        for s in range(NSEQ):
            memkv_seq(s)
            memset("dve", Sst[:, :, :], 0.0, ["Sst"])
            memset("dve", Bprev[:, :], 0.0, ["Bprev"])
            memset("dve", MUALL[:, 0:1], 0.0, ["MUALL"])
            for b in range(NBLK):
                tl = []
                for j in range(4):
                    ti = 4 * b + j
                    rows = slice(ti * 128, (ti + 1) * 128)
                    tl.append((xp[s, rows, :], ti, j, pk[s, rows, :], pv[s, rows, :]))
                phase1_block(tl, 128)
                gates_block(128, 4)
                ktiles = [(t, 128, None) for t in range(4 * b)] + [(4 * b + i, 128, i) for i in range(4)]
                attention_block(128, 4, ktiles, False)
                mem_block(128, 4)
                mlstm_block(128, 4)
                for j in range(4):
                    ti = 4 * b + j
                    rows = slice(ti * 128, (ti + 1) * 128)
                    phase3_tile(xp[s, rows, :], yp[s, rows, :], 128, j)
            state_out(pC[s], pn[s], pm[s].unsqueeze(1))

    except _Stop:
        pass

    print('sbuf bytes remaining', nc.sbuf_bytes_remaining)
    S.emit(st)
    st.close()
    return nc


_NC_CACHE = {}


def _get_nc(key=(4, 4, True)):
    if key not in _NC_CACHE:
        _NC_CACHE[key] = build_program(*key)
    return _NC_CACHE[key]


def _in_maps(inp, NSEQ=4):
    maps = []
    c32 = lambda a: np.ascontiguousarray(a, dtype=np.float32)
    for c in range(8):
        m = {
            "xp": c32(inp["x_prompt"][4 * c:4 * c + max(NSEQ, 1)]),
            "memp": c32(inp["mem_prompt"][4 * c:4 * c + max(NSEQ, 1)]),
            "xs": c32(inp["x_sample"][c]),
            "ck": c32(inp["cache_attn_k"][0, c].reshape(2048, 512)),
            "cv": c32(inp["cache_attn_v"][0, c].reshape(2048, 512)),
            "sC": c32(inp["state_mlstm_C"][0, c]),
            "sn": c32(inp["state_mlstm_n"][0, c]),
            "cmk": c32(inp["cache_mem_k"][0, c].reshape(256, 256)),
            "cmv": c32(inp["cache_mem_v"][0, c].reshape(256, 256)),
            "w_in": c32(inp["w_in"][0]),
            "w_out": c32(inp["w_out"][0]),
            "w_mk": c32(inp["w_mk"][0]),
            "w_mv": c32(inp["w_mv"][0]),
        }
        cpa, extra = _make_cp(inp, c)
        m["cp"] = cpa
        m.update(extra)
        maps.append(m)
    return maps


def kernel(**inputs):
    inp = {k: np.asarray(v) for k, v in inputs.items()}
    nc = _get_nc()
    res = run_bass_kernel_spmd(nc, _in_maps(inp), core_ids=list(range(8)))
    R = res.results
    cat = lambda name: np.concatenate([np.asarray(r[name]) for r in R], axis=0)
    stk = lambda name: np.stack([np.asarray(r[name]) for r in R], axis=0)
    y_prompt = cat("yp")
    y_sample = stk("ys")
    p_attn_k = cat("pk").reshape(1, 32, 2048, 4, 128)
    p_attn_v = cat("pv").reshape(1, 32, 2048, 4, 128)
    p_C = cat("pC").reshape(1, 32, 4, 64, 64)
    p_n = cat("pn").reshape(1, 32, 4, 64)
    p_m = cat("pm").reshape(1, 32, 4)
    p_mk = cat("pmk").reshape(1, 32, 256, 4, 64)
    p_mv = cat("pmv").reshape(1, 32, 256, 4, 64)
    s_k = stk("sk").reshape(1, 8, 16, 4, 128)
    s_v = stk("sv").reshape(1, 8, 16, 4, 128)
    s_C = stk("sCo").reshape(1, 8, 4, 64, 64)
    s_n = stk("sno").reshape(1, 8, 4, 64)
    s_m = stk("smo").reshape(1, 8, 4)
    return (y_prompt, y_sample, p_attn_k, p_attn_v, p_C, p_n, p_m, p_mk, p_mv,
            s_k, s_v, s_C, s_n, s_m)
```

```python
import math
from contextlib import ExitStack

import numpy as np
import concourse.bass as bass
import concourse.mybir as mybir
from concourse.bass_utils import run_bass_kernel_spmd

F32 = mybir.dt.float32
BF16 = mybir.dt.bfloat16
AF = mybir.ActivationFunctionType
ALU = mybir.AluOpType
AX = mybir.AxisListType

N_DMA_SEMS = 40
DELAY = 4
EPS = 1e-6
SLOPES = [2.0 ** (-8.0 * (h + 1) / 4) for h in range(4)]
LAM_INIT = 0.8 - 0.6 * math.exp(0.0)
NIN = 3848
BIG = 240000.0


class _Op:
    __slots__ = ("eng", "fn", "deps", "has_dep", "is_dma", "semkey", "val", "ie")


class Sched:
    def __init__(self, nc):
        self.nc = nc
        self.ops = []
        self.lw = {}
        self.rd = {}
        self.eng_n = {"pe": 0, "act": 0, "dve": 0, "pool": 0, "sp": 0}

    def add(self, eng, fn, reads=(), writes=(), dma=False):
        i = len(self.ops)
        deps = set()
        for k in reads:
            w = self.lw.get(k)
            if w is not None:
                deps.add(w)
            if isinstance(k, tuple) and k[0] == "pb":
                for r in self.rd.get(k, ()):
                    if self.ops[r].eng != eng:
                        deps.add(r)
        for k in writes:
            w = self.lw.get(k)
            if w is not None:
                deps.add(w)
            for r in self.rd.get(k, ()):
                deps.add(r)
        op = _Op()
        op.eng = eng
        op.fn = fn
        op.is_dma = dma
        op.has_dep = False
        op.semkey = None
        op.val = 0
        op.ie = self.eng_n[eng]
        self.eng_n[eng] += 1
        keep = []
        for d in deps:
            p = self.ops[d]
            if p.eng == eng and not p.is_dma:
                if eng == "pe" or eng == "sp":
                    continue
                if eng != "pool" and op.ie - p.ie > 2:
                    continue
            p.has_dep = True
            keep.append(d)
        op.deps = keep
        for k in reads:
            self.rd.setdefault(k, []).append(i)
        for k in writes:
            self.lw[k] = i
            self.rd[k] = []
        self.ops.append(op)
        return i

    def emit(self, stack):
        nc = self.nc
        engobj = {"pe": nc.tensor, "act": nc.scalar, "dve": nc.vector,
                  "pool": nc.gpsimd, "sp": nc.sync}
        esem = {e: stack.enter_context(nc.semaphore("s_" + e))
                for e in ("pe", "act", "dve", "pool")}
        dsem = [stack.enter_context(nc.semaphore("d%d" % i)) for i in range(N_DMA_SEMS)]
        waited = {e: {} for e in engobj}
        cnt = {e: 0 for e in engobj}
        dcnt = [0] * N_DMA_SEMS
        rr = 0
        rrp = 0
        for op in self.ops:
            E = engobj[op.eng]
            need = {}
            for d in op.deps:
                p = self.ops[d]
                if need.get(p.semkey, 0) < p.val:
                    need[p.semkey] = p.val
            s = None
            if op.is_dma:
                if op.eng == "pool":
                    s = N_DMA_SEMS - 8 + (rrp % 8)
                    rrp += 1
                else:
                    s = rr % (N_DMA_SEMS - 8)
                    rr += 1
                if dcnt[s] > 0 and need.get(("d", s), 0) < dcnt[s]:
                    need[("d", s)] = dcnt[s]
                dcnt[s] += 16
                op.semkey = ("d", s)
                op.val = dcnt[s]
            w = waited[op.eng]
            for key, val in need.items():
                if w.get(key, 0) >= val:
                    continue
                so = dsem[key[1]] if key[0] == "d" else esem[key[1]]
                E.wait_ge(so, val)
                w[key] = val
            inst = op.fn()
            if op.is_dma:
                inst.then_inc(dsem[s], 16)
            elif op.has_dep:
                cnt[op.eng] += 1
                op.semkey = ("e", op.eng)
                op.val = cnt[op.eng]
                inst.then_inc(esem[op.eng], 1)
        for s in range(N_DMA_SEMS):
            if dcnt[s] > 0:
                nc.sync.wait_ge(dsem[s], dcnt[s])
        return cnt


class Rot:
    def __init__(self, aps, name):
        self.aps = aps
        self.name = name
        self.i = 0

    def next(self):
        k = self.i % len(self.aps)
        self.i += 1
        return self.aps[k], (self.name, k)


def _cp_layout():
    off = {}
    c = 0
    for name, n in [("ident", 128), ("maskU", 128), ("AB", 64), ("ABS", 68),
                    ("gqa_pp", 1), ("gqm_pp", 1), ("gkm_rep", 64), ("gka_rep", 64),
                    ("gnorm_pp", 8), ("gmem_pp", 8), ("rows_pp", 8),
                    ("bi_pp", 1), ("bf_pp", 1), ("sm_pp", 1), ("SEL", 128), ("SEL2", 4)]:
        off[name] = (c, c + n)
        c += n
    return off, c


CP_OFF, NCP = _cp_layout()


def _make_cp(inp, core):
    cp = np.zeros((128, NCP), np.float32)

    def put(name, arr):
        a, b = CP_OFF[name]
        cp[:, a:b] = arr

    p = np.arange(128)
    put("ident", np.eye(128, dtype=np.float32))
    put("maskU", (p[:, None] <= p[None, :]).astype(np.float32))
    corr = np.zeros((128, 4, 128), np.float32)
    k = p[:, None]
    q = p[None, :]
    for h in range(4):
        c_ = np.where(k > q, -16.0 * SLOPES[h] * (k - q), 0.0)
        c_ = np.where((k // 64) > (q // 64), -BIG, c_)
        corr[:, h, :] = c_
    extra = {"corr": corr.reshape(128, 512)}
    AB = np.zeros((128, 4, 16), np.float32)
    for h in range(4):
        for r in range(16):
            AB[:, h, r] = SLOPES[h] * (p - 127 - 128 * r)
    put("AB", AB.reshape(128, 64))
    ABS = np.zeros((128, 4, 17), np.float32)
    for h in range(4):
        for t in range(17):
            ABS[:, h, t] = SLOPES[h] * np.minimum(128 * t + p - 2063, 0)
    put("ABS", ABS.reshape(128, 68))
    put("gqa_pp", inp["g_qa"][0][p % 64][:, None])
    put("gqm_pp", inp["g_qm"][0][p % 64][:, None])
    put("gkm_rep", np.broadcast_to(inp["g_km"][0][None, :], (128, 64)))
    put("gka_rep", np.broadcast_to(inp["g_ka"][0][None, :], (128, 64)))
    put("gnorm_pp", inp["g_norm"][0].reshape(8, 128).T)
    put("gmem_pp", inp["g_mem"][0].reshape(8, 128).T)
    rows = np.ones((128, 8), np.float32)
    rows[:, 0:4] = inp["g_subln"][0][:, None]
    rows[:, 4:6] = inp["g_mh"][0][p % 64][:, None]
    put("rows_pp", rows)
    lam4 = np.concatenate([inp["lam_q1"][0], inp["lam_k1"][0], inp["lam_q2"][0], inp["lam_k2"][0]])
    extra["lam4"] = np.ascontiguousarray(np.broadcast_to(lam4[None, :], (128, 256)))
    put("bi_pp", inp["b_i"][0][p % 4][:, None])
    put("bf_pp", inp["b_f"][0][p % 4][:, None])
    put("sm_pp", inp["state_mlstm_m"][0, core][p % 4][:, None])
    SEL = np.zeros((128, 128), np.float32)
    SEL[0:4, :] = 1.0
    put("SEL", SEL)
    SEL2 = np.zeros((128, 4), np.float32)
    SEL2[0:4, 0:4] = np.eye(4, dtype=np.float32)
    put("SEL2", SEL2)
    return cp, extra


class _Stop(Exception):
    pass


def build_program(NSEQ=4, NBLK=4, DO_SAMPLE=True, STAGE=99):
    nc = bass.Bass("TRN2", target_bir_lowering=False)
    S = Sched(nc)

    def din(name, shape):
        return nc.dram_tensor(name, shape, F32, kind="ExternalInput").ap()

    def dout(name, shape):
        return nc.dram_tensor(name, shape, F32, kind="ExternalOutput").ap()

    NS = max(NSEQ, 1)
    xp = din("xp", [NS, 2048, 1024])
    memp = din("memp", [NS, 256, 1024])
    xs = din("xs", [16, 1024])
    ck = din("ck", [2048, 512])
    cv = din("cv", [2048, 512])
    sC = din("sC", [4, 64, 64])
    sn = din("sn", [4, 64])
    cmk = din("cmk", [256, 256])
    cmv = din("cmv", [256, 256])
    w_in = din("w_in", [1024, NIN])
    w_out = din("w_out", [1024, 1024])
    w_mk = din("w_mk", [1024, 256])
    w_mv = din("w_mv", [1024, 256])
    cpd = din("cp", [128, NCP])
    corrd = din("corr", [128, 512])
    lam4d = din("lam4", [128, 256])

    yp = dout("yp", [NS, 2048, 1024])
    pk = dout("pk", [NS, 2048, 512])
    pv = dout("pv", [NS, 2048, 512])
    pC = dout("pC", [NS, 4, 64, 64])
    pn = dout("pn", [NS, 4, 64])
    pm = dout("pm", [NS, 4])
    pmk = dout("pmk", [NS, 256, 256])
    pmv = dout("pmv", [NS, 256, 256])
    ys = dout("ys", [16, 1024])
    sk = dout("sk", [16, 512])
    sv = dout("sv", [16, 512])
    sCo = dout("sCo", [4, 64, 64])
    sno = dout("sno", [4, 64])
    smo = dout("smo", [1, 4])

    st = ExitStack()

    def chk(n):
        if STAGE == n:
            raise _Stop()

    def sb(name, shape, dt=F32):
        return st.enter_context(nc.sbuf_tensor("sb_" + name, shape, dt))

    cp = sb("cp", [128, NCP])
    w_in_bf = sb("w_in_bf", [128, 8, NIN], BF16)
    w_out_bf = sb("w_out_bf", [128, 8, 1024], BF16)
    w_mkv_bf = sb("w_mkv_bf", [128, 8, 512], BF16)
    identb = sb("identb", [128, 128], BF16)
    corrb = sb("corrb", [128, 4, 128], BF16)
    zerob = sb("zerob", [128, 512], BF16)
    cm05 = sb("cm05", [128, 8])
    lamt = sb("lamt", [128, 8])
    rs8 = sb("rs8", [128, 8])
    nbf = sb("nbf", [128, 1])

    kT = sb("kT", [128, 4, 2064], BF16)
    v_aug = sb("v_aug", [128, 17, 4, 130], BF16)
    mkT = sb("mkT", [128, 2, 256], BF16)
    mv_aug = sb("mv_aug", [128, 2, 4, 66], BF16)

    qT = sb("qT", [128, 4, 512], BF16)
    mix = sb("mix", [128, 4, 1024], BF16)
    OBG = sb("OBG", [128, 4, 256], BF16)
    QKB = sb("QKB", [128, 4, 512], BF16)
    QKT = sb("QKT", [64, 8, 512], BF16)
    vb_aug = sb("vb_aug", [128, 4, 4, 66], BF16)
    qmT = sb("qmT", [128, 2, 512], BF16)
    GIF = sb("GIF", [128, 4, 8])
    WE = sb("WE", [128, 4, 8])
    C0B = sb("C0B", [64, 4, 4])
    Sst = sb("Sst", [64, 4, 66])
    Sd = sb("Sd", [64, 4, 66])
    Cs = sb("Cs", [64, 4, 66], BF16)
    Bprev = sb("Bprev", [4, 1])
    MUALL = sb("MUALL", [4, 8])
    UM = sb("UM", [4, 4])
    C0 = sb("C0", [4, 4])
    CD = sb("CD", [4, 4, 4])
    mfin = sb("mfin", [4, 1])

    XT = Rot([sb("xt%d" % i, [128, 1024])[:] for i in range(2)], "xt")
    xn = sb("xn", [128, 1024], BF16)
    HT = Rot([sb("hT%d" % i, [128, 8, 128], BF16)[:] for i in range(2)], "hT")
    ZF = Rot([sb("zf%d" % i, [128, 512])[:] for i in range(2)], "zf")
    SQ = sb("sq", [128, 512])
    KOUT = Rot([sb("kvout%d" % i, [128, 512])[:] for i in range(3)], "kvout")
    VOUT = KOUT
    TH = Rot([sb("th%d" % i, [128, 512])[:] for i in range(2)], "th")
    NB = Rot([sb("nb%d" % i, [128, 512], BF16)[:] for i in range(4)], "nb")
    PT = Rot([sb("pt%d" % i, [128, 2, 512], BF16)[:] for i in range(3)], "pt")
    STT = Rot([sb("stat%d" % i, [128, 72])[:] for i in range(4)], "stat")
    O1 = Rot([sb("o1s%d" % i, [128, 128])[:] for i in range(4)], "o1s")
    OO = Rot([sb("oo%d" % i, [128, 128])[:] for i in range(4)], "oo")
    mixT = sb("mixT", [128, 8, 128], BF16)
    AT = sb("AT", [128, 4, 128], BF16)
    vw = sb("vw", [128, 4, 66], BF16)
    HN = sb("HN", [128, 4, 64])
    OM = HN
    GA, KGA = SQ[0:4, :], "sq"
    GB, KGB = TH.aps[0][0:4, :], ("th", 0)
    GU, KGU = TH.aps[1][0:4, :], ("th", 1)
    GW, KGW = ZF.aps[0][0:4, :], ("zf", 0)
    GE, KGE = ZF.aps[1][0:4, :], ("zf", 1)

    PBIG = st.enter_context(nc.psum_tensor("pbig", [128, 4096], F32))
    PB = [PBIG[:, i * 512:(i + 1) * 512] for i in range(8)]
    MM = Rot([PB[0], PB[1], PB[2]], "pb_mm")
    MM.keys = [("pb", 0), ("pb", 1), ("pb", 2)]
    ACC = [PB[3], PB[4], PB[5]]
    ACCK = [("pb", 3), ("pb", 4), ("pb", 5)]
    TBf = PB[6]
    TB = PB[6].bitcast(BF16)
    TBK = ("pb", 6)
    TF = PB[7]
    TFK = ("pb", 7)

    def mmnext():
        k = MM.i % 3
        MM.i += 1
        return PB[k], ("pb", k)

    def cpc(name, a=None, b=None):
        o, e = CP_OFF[name]
        if a is None:
            return cp[:, o:e]
        return cp[:, o + a:o + b]

    def dma(out, in_, r=(), w=(), q="sp", **kw):
        if q == "sp":
            S.add("sp", lambda: nc.sync.dma_start(out=out, in_=in_, **kw), r, w, dma=True)
        else:
            S.add("pool", lambda: nc.gpsimd.dma_start(out=out, in_=in_, **kw), r, w, dma=True)

    def mm(out, lhsT, rhs, start, stop, r, w):
        S.add("pe", lambda: nc.tensor.matmul(out, lhsT=lhsT, rhs=rhs, start=start, stop=stop,
                                             skip_group_check=True), r, w)

    def tr(out, in_, ident, r, w):
        S.add("pe", lambda: nc.tensor.transpose(out=out, in_=in_, identity=ident), r, w)

    def act(out, in_, func, r, w, bias=0.0, scale=1.0, accum=None):
        if accum is None:
            S.add("act", lambda: nc.scalar.activation(out=out, in_=in_, func=func, bias=bias, scale=scale), r, w)
        else:
            S.add("act", lambda: nc.scalar.activation(out=out, in_=in_, func=func, bias=bias, scale=scale,
                                                      accum_out=accum), r, w)

    def engo(e):
        return nc.vector if e == "dve" else nc.gpsimd

    def ts(e, out, in0, s1, s2, op0, op1, r, w):
        if s2 is None:
            S.add(e, lambda: engo(e).tensor_scalar(out=out, in0=in0, scalar1=s1, scalar2=None, op0=op0), r, w)
        else:
            S.add(e, lambda: engo(e).tensor_scalar(out=out, in0=in0, scalar1=s1, scalar2=s2, op0=op0, op1=op1), r, w)

    def tt(e, out, in0, in1, op, r, w):
        S.add(e, lambda: engo(e).tensor_tensor(out=out, in0=in0, in1=in1, op=op), r, w)

    def stt(out, in0, scalar, in1, op0, op1, r, w):
        S.add("dve", lambda: nc.vector.scalar_tensor_tensor(out=out, in0=in0, scalar=scalar, in1=in1,
                                                            op0=op0, op1=op1), r, w)

    def cpy(e, out, in_, r, w):
        if e == "act":
            S.add("act", lambda: nc.scalar.copy(out=out, in_=in_), r, w)
        else:
            S.add(e, lambda: engo(e).tensor_copy(out=out, in_=in_), r, w)

    def memset(e, ap, val, w):
        S.add(e, lambda: engo(e).memset(ap, val), (), w)

    def red(out, in_, op, r, w):
        S.add("dve", lambda: nc.vector.tensor_reduce(out=out, in_=in_, axis=AX.X, op=op), r, w)

    def recip(out, in_, r, w):
        S.add("dve", lambda: nc.vector.reciprocal(out=out, in_=in_), r, w)

    def scan(out, d0, d1, init, op0, op1, r, w):
        S.add("dve", lambda: nc.vector.tensor_tensor_scan(out=out, data0=d0, data1=d1, initial=init,
                                                          op0=op0, op1=op1), r, w)

    def rstd_from_ss(stt_ap, kst, c_ss, c_tmp, c_out, n, T, inv):
        ts("dve", stt_ap[0:T, c_tmp:c_tmp + n], stt_ap[0:T, c_ss:c_ss + n], inv, EPS, ALU.mult, ALU.add,
           [kst], [kst])
        tt("pool", stt_ap[0:T, c_out:c_out + n], stt_ap[0:T, c_tmp:c_tmp + n], cm05[0:T, 0:n], ALU.pow,
           [kst, "cm05"], [kst])

    dma(cp[:], cpd, w=["cp"])
    cpy("dve", identb[:], cpc("ident"), ["cp"], ["identb"])
    identf = cpc("ident")
    dma(TH.aps[0][:, :], corrd, w=[("th", 0)])
    cpy("dve", corrb[:].rearrange("p h q -> p (h q)"), TH.aps[0][:, :], [("th", 0)], ["corrb"])
    dma(TH.aps[1][:, 0:256], lam4d, w=[("th", 1)])
    memset("pool", zerob[:], 0.0, ["zerob"])
    memset("pool", cm05[:], -0.5, ["cm05"])
    memset("pool", v_aug[:, :, :, 128:129], 1.0, ["v_ones"])
    memset("pool", vb_aug[:, :, :, 64:65], 1.0, ["vb_ones"])
    memset("pool", mv_aug[:, :, :, 64:65], 1.0, ["mv_ones"])
    ts("dve", rs8[:, 0:4], cpc("rows_pp", 0, 4), 0.5 * (1.0 - LAM_INIT), None, ALU.mult, None, ["cp"], ["rs8"])
    ts("dve", rs8[:, 4:6], cpc("rows_pp", 4, 6), 0.25, None, ALU.mult, None, ["cp"], ["rs8"])
    ts("dve", rs8[:, 6:8], cpc("rows_pp", 6, 8), 0.5, None, ALU.mult, None, ["cp"], ["rs8"])
    ts("dve", nbf[:], cpc("bf_pp"), -1.0, None, ALU.mult, None, ["cp"], ["nbf"])
    wv = w_in.rearrange("(k p) n -> p k n", p=128)
    wov = w_out.rearrange("(k p) n -> p k n", p=128)
    wkv = w_mk.rearrange("(k p) n -> p k n", p=128)
    wvv = w_mv.rearrange("(k p) n -> p k n", p=128)

    WST = Rot([XT.aps[0][:, 0:512], XT.aps[1][:, 0:512], ZF.aps[0], ZF.aps[1], KOUT.aps[0], KOUT.aps[1],
               KOUT.aps[2]], "wst")
    WSTK = [("xt", 0), ("xt", 1), ("zf", 0), ("zf", 1), ("kvout", 0), ("kvout", 1), ("kvout", 2)]

    def wload(dst, src, n, scal, key, extra=None):
        if n > 512:
            h_ = n // 2
            wload(dst[:, 0:h_], src[:, 0:h_], h_, scal, key, extra)
            wload(dst[:, h_:n], src[:, h_:n], n - h_, scal, key, extra)
            return
        i_ = WST.i % len(WST.aps)
        xt, _ = WST.next()
        kx = WSTK[i_]
        dma(xt[:, 0:n], src, w=[kx])
        if extra is None:
            ts("dve", dst, xt[:, 0:n], scal, None, ALU.mult, None, [kx, "cp", "rs8"], [key])
        else:
            ts("dve", dst, xt[:, 0:n], scal, extra, ALU.mult, ALU.mult, [kx, "cp", "rs8"], [key])

    for kc in range(8):
        wload(w_mkv_bf[:, kc, 0:256], wkv[:, kc, :], 256, cpc("gmem_pp", kc, kc + 1), ("w_mkv", kc, 0))
        wload(w_mkv_bf[:, kc, 256:512], wvv[:, kc, :], 256, cpc("gmem_pp", kc, kc + 1), ("w_mkv", kc, 1))
    for kc in range(8):
        g = cpc("gnorm_pp", kc, kc + 1)
        wload(w_in_bf[:, kc, 0:1024], wv[:, kc, 0:1024], 1024, g, ("w_in", kc, 0))
        wload(w_in_bf[:, kc, 1024:2048], wv[:, kc, 1024:2048], 1024, g, ("w_in", kc, 1))
        wload(w_in_bf[:, kc, 2048:2304], wv[:, kc, 2048:2304], 256, g, ("w_in", kc, 2))
        wload(w_in_bf[:, kc, 2304:2560], wv[:, kc, 2304:2560], 256, g, ("w_in", kc, 2), 0.125)
        wload(w_in_bf[:, kc, 2560:3072], wv[:, kc, 2560:3072], 512, g, ("w_in", kc, 2))
        wload(w_in_bf[:, kc, 3072:3840], wv[:, kc, 3080:3848], 768, g, ("w_in", kc, 3))
        wload(w_in_bf[:, kc, 3840:3848], wv[:, kc, 3072:3080], 8, g, ("w_in", kc, 3))
    for kc in range(8):
        wload(w_out_bf[:, kc, :], wov[:, kc, :], 1024, rs8[:, kc:kc + 1], ("w_out", kc))
    W_IN_K = [("w_in", kc, i) for kc in range(8) for i in range(4)]
    W_OUT_K = [("w_out", kc) for kc in range(8)]
    W_MKV_K = [("w_mkv", kc, i) for kc in range(8) for i in range(2)]
    l4 = TH.aps[1]
    tt("dve", SQ[:, 0:64], l4[:, 0:64], l4[:, 64:128], ALU.mult, [("th", 1)], ["sq"])
    red(lamt[:, 0:1], SQ[:, 0:64], ALU.add, ["sq"], ["lamt"])
    tt("dve", SQ[:, 64:128], l4[:, 128:192], l4[:, 192:256], ALU.mult, [("th", 1)], ["sq"])
    red(lamt[:, 1:2], SQ[:, 64:128], ALU.add, ["sq"], ["lamt"])
    act(lamt[:, 2:4], lamt[:, 0:2], AF.Exp, ["lamt"], ["lamt"])
    stt(lamt[:, 4:5], lamt[:, 2:3], LAM_INIT, lamt[:, 3:4], ALU.add, ALU.subtract, ["lamt"], ["lamt"])
    ts("dve", lamt[:, 5:6], lamt[:, 4:5], -1.0, None, ALU.mult, None, ["lamt"], ["lamt"])

    def front_load(src_ap, T):
        xt, kx = XT.next()
        dma(xt[0:T, :], src_ap, w=[kx])
        return xt, kx

    def front_a1(xt, kx, T):
        sa, ks = STT.next()
        act(xn[0:T, :], xt[0:T, :], AF.Square, [kx], ["xn", ks], accum=sa[0:T, 0:1])
        rstd_from_ss(sa, ks, 0, 1, 2, 1, T, 1.0 / 1024)
        return sa, ks

    def front_a2(xt, kx, sa, ks, T):
        ts("dve", xn[0:T, :], xt[0:T, :], sa[0:T, 2:3], None, ALU.mult, None, [kx, ks], ["xn"])

    def front_a(xt, kx, T):
        sa, ks = front_a1(xt, kx, T)
        front_a2(xt, kx, sa, ks, T)
        return sa, ks

    def front_b(xt, kx, sa, ks, T):
        for kc in range(8):
            tr(TB[:, kc * 128:kc * 128 + T], xn[0:T, kc * 128:(kc + 1) * 128], identb[0:T, 0:T],
               ["xn", "identb"], [TBK])
        hT, kh = HT.next()
        cpy("act", hT[:, :, 0:T], TB.rearrange("p (k t) -> p k t", k=8)[:, :, 0:T], [TBK], [kh])
        return xt, kx, hT, kh, sa, ks

    def front_compute(xt, kx, T):
        sa, ks = front_a(xt, kx, T)
        return front_b(xt, kx, sa, ks, T)

    def front(src_ap, T):
        xt, kx = front_load(src_ap, T)
        return front_compute(xt, kx, T)

    def group_norm_rstd(src, ksrc, T, ng, sa, ks, base):
        tt("dve", SQ[0:T, 0:ng * 64], src, src, ALU.mult, [ksrc], ["sq"])
        red(sa[0:T, base:base + ng], SQ[0:T, 0:ng * 64].rearrange("p (g d) -> p g d", d=64), ALU.add,
            ["sq"], [ks])
        rstd_from_ss(sa, ks, base, base + ng, base + 2 * ng, ng, T, 1.0 / 64)
        return sa[0:T, base + 2 * ng:base + 3 * ng]

    def phase1_tile(fr, T, ti, j, k_dst, v_dst, later, hook=None):
        xt, kx, hT, kh, sa, ks = fr
        tok = slice(j * 128, j * 128 + T)
        ktok = slice(ti * 128, ti * 128 + T)
        for g in range(8):
            c0 = g * 512
            n = min(512, NIN - c0)
            ps, kp = mmnext()
            for kc in range(8):
                mm(ps[0:T, 0:n], hT[:, kc, 0:T], w_in_bf[:, kc, c0:c0 + n], kc == 0, kc == 7,
                   [kh] + W_IN_K, [kp])
            for it in later:
                it[0] -= 1
            ready = [it for it in later if it[0] <= 0]
            for it in ready:
                later.remove(it)
            for it in ready:
                it[1]()
            if hook is not None:
                hook(g)
            if g == 0 or g == 1:
                zf, kz = ZF.next()
                cpy("act", zf[0:T, :], ps[0:T, 0:512], [kp], [kz])
                r8 = group_norm_rstd(zf[0:T, :], kz, T, 8, sa, ks, 8 if g == 0 else 32)

                def b01(g=g, zf=zf, kz=kz, r8=r8):
                    nb, kn = NB.next()
                    if g == 0:
                        tt("pool", nb[0:T, :].rearrange("p (g d) -> p g d", d=64),
                           zf[0:T, :].rearrange("p (g d) -> p g d", d=64),
                           r8.unsqueeze(2).to_broadcast([T, 8, 64]), ALU.mult, [kz, ks], [kn])
                    else:
                        tt("dve", zf[0:T, :].rearrange("p (g d) -> p g d", d=64),
                           zf[0:T, :].rearrange("p (g d) -> p g d", d=64),
                           r8.unsqueeze(2).to_broadcast([T, 8, 64]), ALU.mult, [kz, ks], [kz])
                        ko, kk = KOUT.next()
                        tt("dve", ko[0:T, :].rearrange("p (g d) -> p g d", d=64),
                           zf[0:T, :].rearrange("p (g d) -> p g d", d=64),
                           cpc("gka_rep")[0:T, :].unsqueeze(1).to_broadcast([T, 8, 64]), ALU.mult,
                           [kz, "cp"], [kk])
                        dma(k_dst, ko[0:T, :], r=[kk])
                        cpy("dve", nb[0:T, :], ko[0:T, :], [kk], [kn])

                    def d01(nb=nb, kn=kn):
                        for h in range(4):
                            tr(TB[:, h * 128:h * 128 + T], nb[0:T, h * 128:(h + 1) * 128], identb[0:T, 0:T],
                               [kn, "identb"], [TBK])
                        src = TB[:, 0:512].rearrange("p (h t) -> p h t", h=4)[:, :, 0:T]
                        if g == 0:
                            ts("dve", qT[:, :, tok], src, cpc("gqa_pp"), None, ALU.mult, None, [TBK, "cp"],
                               [("qT", j)])
                        else:
                            cpy("act", kT[:, :, ktok], src, [TBK], [("kT", ti)])
                    later.append([DELAY, d01])
                later.append([2, b01])
            elif g == 2:
                vo, kv = VOUT.next()
                cpy("act", vo[0:T, :], ps[0:T, 0:512], [kp], [kv])
                dma(v_dst, vo[0:T, :], r=[kv])
                cpy("dve", v_aug[0:T, ti, :, 0:128], vo[0:T, :].rearrange("p (h d) -> p h d", h=4),
                    [kv, "v_ones"], [("v", ti)])
            elif g == 3:
                th, kt = TH.next()
                act(th[0:T, :], ps[0:T, 0:512], AF.Tanh, [kp], [kt], scale=0.5)
                stt(mix[0:T, j, 0:512], th[0:T, :], 1.0, ps[0:T, 0:512], ALU.add, ALU.mult,
                    [kt, kp], [("mix", j, 0)])
            elif g == 4:
                cpy("act", QKB[0:T, j, :], ps[0:T, 0:512], [kp], [("QKB", j)])

                def d4():
                    for i in range(8):
                        tr(TB[0:64, i * 128:i * 128 + T], QKB[0:T, j, i * 64:(i + 1) * 64], identb[0:T, 0:T],
                           [("QKB", j), "identb"], [TBK])
                    cpy("dve", QKT[:, :, tok], TB[0:64, :].rearrange("p (h t) -> p h t", h=8)[:, :, 0:T],
                        [TBK], [("QKT", j)])
                later.append([DELAY, d4])
            elif g == 5:
                cpy("dve", vb_aug[0:T, j, :, 0:64], ps[0:T, 0:256].rearrange("p (h d) -> p h d", h=4),
                    [kp, "vb_ones"], [("vb", j)])
                act(OBG[0:T, j, :], ps[0:T, 256:512], AF.Tanh, [kp], [("OBG", j)], scale=0.5)
            elif g == 6:
                th, kt = TH.next()
                act(th[0:T, 0:256], ps[0:T, 0:256], AF.Tanh, [kp], [kt], scale=0.5)
                stt(th[0:T, 256:512], th[0:T, 0:256], 1.0, ps[0:T, 0:256], ALU.add, ALU.mult, [kt, kp], [kt])
                stt(mix[0:T, j, 512:768], OBG[0:T, j, :], 1.0, th[0:T, 256:512], ALU.add, ALU.mult,
                    [("OBG", j), kt], [("mix", j, 1)])
                zf, kz = ZF.next()
                cpy("act", zf[0:T, 0:256], ps[0:T, 256:512], [kp], [kz])
                r4 = group_norm_rstd(zf[0:T, 0:256], kz, T, 4, sa, ks, 56)

                def b6(zf=zf, kz=kz, r4=r4):
                    nb, kn = NB.next()
                    tt("pool", nb[0:T, 0:256].rearrange("p (g d) -> p g d", d=64),
                       zf[0:T, 0:256].rearrange("p (g d) -> p g d", d=64),
                       r4.unsqueeze(2).to_broadcast([T, 4, 64]), ALU.mult, [kz, ks], [kn])

                    def d6(nb=nb, kn=kn):
                        for hp in range(2):
                            tr(TB[:, hp * 128:hp * 128 + T], nb[0:T, hp * 128:(hp + 1) * 128], identb[0:T, 0:T],
                               [kn, "identb"], [TBK])
                        ts("dve", qmT[:, :, tok], TB[:, 0:256].rearrange("p (h t) -> p h t", h=2)[:, :, 0:T],
                           cpc("gqm_pp"), None, ALU.mult, None, [TBK, "cp"], [("qmT", j)])
                    later.append([DELAY, d6])
                later.append([2, b6])
            else:
                th, kt = TH.next()
                act(th[0:T, 0:256], ps[0:T, 0:256], AF.Tanh, [kp], [kt], scale=0.5)
                stt(mix[0:T, j, 768:1024], th[0:T, 0:256], 1.0, ps[0:T, 0:256], ALU.add, ALU.mult,
                    [kt, kp], [("mix", j, 2)])
                cpy("dve", GIF[0:T, j, :], ps[0:T, 256:264], [kp], [("GIF", j)])

    def phase1_block(tiles, T):
        later = []
        xt, kx = front_load(tiles[0][0], T)
        fr = front_compute(xt, kx, T)
        for idx, (src_ap, ti, j, k_dst, v_dst) in enumerate(tiles):
            nxt = {}
            hook = None
            if idx + 1 < len(tiles):
                nx = front_load(tiles[idx + 1][0], T)

                def hook(g, nx=nx, nxt=nxt):
                    if g == 0:
                        nxt["a"] = front_a1(nx[0], nx[1], T)
                    elif g == 2:
                        front_a2(nx[0], nx[1], nxt["a"][0], nxt["a"][1], T)
                    elif g == 5:
                        nxt["fr"] = front_b(nx[0], nx[1], nxt["a"][0], nxt["a"][1], T)
            phase1_tile(fr, T, ti, j, k_dst, v_dst, later, hook)
            fr = nxt.get("fr")
        while later:
            later.pop(0)[1]()

    def gates_block(T, ntile):
        NQ = ntile * 128
        ibT = TF[0:4, 0:NQ]
        fbT = TBf[0:4, 0:NQ]
        for j in range(ntile):
            tr(TF[0:4, j * 128:j * 128 + T], GIF[0:T, j, 0:4], identf[0:T, 0:T], [("GIF", j), "cp"], [TFK])
            tr(TBf[0:4, j * 128:j * 128 + T], GIF[0:T, j, 4:8], identf[0:T, 0:T], [("GIF", j), "cp"], [TBK])
        if T < 128:
            memset("dve", GA[:, 0:NQ], 0.0, [KGA])
        cs = slice(0, T) if ntile == 1 else slice(0, NQ)
        act(GA[:, cs], fbT[:, cs], AF.Exp, [TBK, "nbf"], [KGA], bias=nbf[0:4, 0:1], scale=-1.0)
        act(GA[:, cs], GA[:, cs], AF.Ln, [KGA], [KGA], bias=1.0)
        scan(GB[:, cs], zerob[0:4, cs], GA[:, cs], Bprev[:, 0:1], ALU.add, ALU.subtract,
             ["zerob", KGA, "Bprev"], [KGB])
        stt(GU[:, cs], ibT[:, cs], cpc("bi_pp")[0:4, :], GB[:, cs], ALU.add, ALU.subtract,
            [TFK, "cp", KGB], [KGU])
        if ntile == 1:
            red(UM[:, 0:1], GU[:, cs], ALU.max, [KGU], ["UM"])
        else:
            red(UM[:, 0:ntile], GU[:, cs].rearrange("p (c t) -> p c t", c=ntile), ALU.max, [KGU], ["UM"])
        scan(MUALL[:, 1:1 + ntile], UM[:, 0:ntile], UM[:, 0:ntile], MUALL[:, 0:1], ALU.max, ALU.max,
             ["UM", "MUALL"], ["MUALL"])
        tt("dve", C0[:, 0:ntile], MUALL[:, 0:ntile], MUALL[:, 1:1 + ntile], ALU.subtract, ["MUALL"], ["C0"])
        act(C0[:, 0:ntile], C0[:, 0:ntile], AF.Exp, ["C0"], ["C0"])
        if ntile == 1:
            mub = MUALL[:, 1:2].to_broadcast([4, T])
            tt("dve", GW[:, cs], GU[:, cs], mub, ALU.subtract, [KGU, "MUALL"], [KGW])
            tt("dve", GE[:, cs], GB[:, cs], mub, ALU.add, [KGB, "MUALL"], [KGE])
        else:
            mub = MUALL[:, 1:1 + ntile].unsqueeze(2).to_broadcast([4, ntile, 128])
            tt("dve", GW[:, cs].rearrange("p (c t) -> p c t", c=ntile),
               GU[:, cs].rearrange("p (c t) -> p c t", c=ntile), mub, ALU.subtract, [KGU, "MUALL"], [KGW])
            tt("dve", GE[:, cs].rearrange("p (c t) -> p c t", c=ntile),
               GB[:, cs].rearrange("p (c t) -> p c t", c=ntile), mub, ALU.add, [KGB, "MUALL"], [KGE])
        act(GW[:, cs], GW[:, cs], AF.Exp, [KGW], [KGW])
        act(GE[:, cs], GE[:, cs], AF.Exp, [KGE], [KGE], scale=-1.0)
        for j in range(ntile):
            tr(TF[0:T, j * 8:j * 8 + 4], GW[0:4, j * 128:j * 128 + T], identf[0:4, 0:4], [KGW, "cp"], [TFK])
            tr(TF[0:T, j * 8 + 4:j * 8 + 8], GE[0:4, j * 128:j * 128 + T], identf[0:4, 0:4], [KGE, "cp"], [TFK])
        cpy("dve", WE[0:T, 0:ntile, :], TF[0:T, 0:ntile * 8].rearrange("p (c e) -> p c e", e=8), [TFK], ["WE"])
        tt("dve", CD[:, 0:ntile, :], C0[:, 0:ntile].unsqueeze(2).to_broadcast([4, ntile, 4]),
           cpc("SEL2")[0:4, :].unsqueeze(1).to_broadcast([4, ntile, 4]), ALU.mult, ["C0", "cp"], ["CD"])
        mm(TBf[0:64, 0:ntile * 4], cpc("SEL")[0:4, 0:64], CD[:, 0:ntile, :].rearrange("p c e -> p (c e)"), True, True,
           ["cp", "CD"], [TBK])
        cpy("dve", C0B[:, 0:ntile, :], TBf[0:64, 0:ntile * 4].rearrange("p (c e) -> p c e", e=4), [TBK], ["C0B"])
        last = T - 1 if ntile == 1 else NQ - 1
        cpy("dve", Bprev[:, 0:1], GB[:, last:last + 1], [KGB], ["Bprev"])
        cpy("dve", MUALL[:, 0:1], MUALL[:, ntile:ntile + 1], ["MUALL"], ["MUALL"])

    def attention_block(T, ntile, ktiles, sample):
        NQ = ntile * T
        ABK = [4, 5, 6]
        qkeys = [("qT", jj) for jj in range(ntile)]
        steps = [(h, t, nk, dsub) for h in range(4) for (t, nk, dsub) in ktiles]
        state = {"n": 0}

        def emit_S(st_):
            h, t, nk, dsub = st_
            c0 = 0 if dsub is None else dsub * T
            ncol = NQ - c0
            b0 = 2 * (state["n"] % 2)
            state["n"] += 1
            kps = [("pb", b0), ("pb", b0 + 1)]
            for m in range(2):
                pr = slice(64 * m, 64 * m + 64)
                mm(PB[b0 + m][0:nk, 0:ncol], kT[pr, h, t * 128:t * 128 + nk], qT[pr, h, c0:NQ],
                   True, dsub is None, [("kT", t)] + qkeys, [kps[m]])
            if dsub is not None:
                for m in range(2):
                    mm(PB[b0 + m][0:nk, 0:T], identb[0:nk, 0:nk], corrb[0:nk, h, 0:T], False, True,
                       ["identb", "corrb"], [kps[m]])
            pt, kpt = PT.next()
            if sample or h != 0:
                calls = [(c0, NQ)]
            else:
                calls = [(c, c + 128) for c in range(c0, NQ, 128)]
            pair = PBIG[0:nk, b0 * 512:(b0 + 2) * 512].rearrange("p (m c) -> p m c", m=2)
            for (ca, cb) in calls:
                if sample:
                    bias = cpc("ABS")[0:nk, h * 17 + t:h * 17 + t + 1]
                else:
                    tref = ktiles[-1][0] - (ntile - 1) + (cb - 1) // 128
                    bias = cpc("AB")[0:nk, h * 16 + (tref - t):h * 16 + (tref - t) + 1]
                act(pt[0:nk, :, ca - c0:cb - c0], pair[:, :, ca - c0:cb - c0], AF.Exp, kps + ["cp"], [kpt],
                    bias=bias, scale=0.125)
            return pt, kpt, c0

        def emit_PV(st_, pt, kpt, c0):
            h, t, nk, dsub = st_
            for m in range(2):
                for i in range(ntile):
                    if i * T < c0:
                        continue
                    a_ = m * 4 + i
                    bnk = ABK[a_ // 3]
                    o = (a_ % 3) * 129
                    mm(PB[bnk][0:T, o:o + 129], pt[0:nk, m, i * T - c0:(i + 1) * T - c0],
                       v_aug[0:nk, t, h, 0:129], False, True, [kpt, ("v", t), "v_ones"], [("pb", bnk)])

        def evac(h):
            bset = ABK
            sa, ks = STT.next()
            if ntile == 4:
                for b_ in range(3):
                    nacc = 3 if b_ < 2 else 2
                    recip(sa[0:T, 3 * b_:3 * b_ + nacc], PB[bset[b_]][0:T, 128:129 * nacc:129],
                          [("pb", bset[b_])], [ks])
            else:
                recip(sa[0:T, 0:1], PB[bset[0]][0:T, 128:129], [("pb", bset[0])], [ks])
                recip(sa[0:T, 4:5], PB[bset[1]][0:T, 129 + 128:129 + 129], [("pb", bset[1])], [ks])
            tt("dve", sa[0:T, 4:4 + ntile], sa[0:T, 4:4 + ntile], lamt[0:T, 5:6].to_broadcast([T, ntile]),
               ALU.mult, [ks, "lamt"], [ks])
            oos = []
            for i in range(ntile):
                a1, a2 = i, 4 + i
                o1, k1 = O1.next()
                b1, b2 = bset[a1 // 3], bset[a2 // 3]
                ts("dve", o1[0:T, :], PB[b1][0:T, (a1 % 3) * 129:(a1 % 3) * 129 + 128], sa[0:T, a1:a1 + 1], None,
                   ALU.mult, None, [("pb", b1), ks], [k1])
                oo, ko = OO.next()
                stt(oo[0:T, :], PB[b2][0:T, (a2 % 3) * 129:(a2 % 3) * 129 + 128], sa[0:T, a2:a2 + 1],
                    o1[0:T, :], ALU.mult, ALU.add, [("pb", b2), ks, k1], [ko])
                oos.append((oo, ko, o1, k1))
            for i in range(ntile):
                oo, ko, o1, k1 = oos[i]
                S.add("dve", (lambda o1=o1, oo=oo, sa=sa, i=i: nc.vector.scalar_tensor_tensor(
                    out=o1[0:T, :], in0=oo[0:T, :], scalar=1.0, in1=oo[0:T, :], op0=ALU.mult, op1=ALU.mult,
                    accum_out=sa[0:T, 8 + i:9 + i])), [ko], [k1, ks])
            ts("dve", sa[0:T, 12:12 + ntile], sa[0:T, 8:8 + ntile], 1.0 / 128, EPS, ALU.mult, ALU.add, [ks], [ks])
            tt("pool", sa[0:T, 16:16 + ntile], sa[0:T, 12:12 + ntile], cm05[0:T, 0:ntile], ALU.pow,
               [ks, "cm05"], [ks])

            def fin(h=h, sa=sa, ks=ks, oos=oos):
                for ii in range(ntile):
                    oo2, ko2 = oos[ii][0], oos[ii][1]
                    stt(mix[0:T, ii, h * 128:(h + 1) * 128], oo2[0:T, :], sa[0:T, 16 + ii:17 + ii],
                        mix[0:T, ii, h * 128:(h + 1) * 128], ALU.mult, ALU.mult,
                        [ko2, ks, ("mix", ii, 0)], [("mix", ii, 0)])
            return fin

        prev = None
        fins = []
        for k, st_ in enumerate(steps):
            h = st_[0]
            new_head = (k == 0 or steps[k - 1][0] != h)
            pt, kpt, c0 = emit_S(st_)
            if prev is not None:
                emit_PV(*prev)
                if new_head:
                    fins.append(evac(prev[0][0]))
            if new_head:
                for b_ in range(3):
                    mm(PB[ABK[b_]][0:T, :], zerob[:, 0:T], zerob[:, :], True, True, ["zerob"], [("pb", ABK[b_])])
            if len(fins) > 0 and not new_head and (k == 0 or steps[k - 2][0] == h):
                for f_ in fins:
                    f_()
                del fins[:]
            prev = (st_, pt, kpt, c0)
        emit_PV(*prev)
        fins.append(evac(prev[0][0]))
        for f_ in fins:
            f_()

    def mem_block(T, ntile):
        NQ = ntile * T
        banks = [(ACC[0], ACCK[0]), (ACC[1], ACCK[1]), (ACC[2], ACCK[2]), (TF, TFK)]
        for i in range(ntile):
            mm(banks[i][0][0:T, :], zerob[:, 0:T], zerob[:, :], True, True, ["zerob"], [banks[i][1]])
        prevm = None
        for h in range(4):
            pr = slice(64 * (h % 2), 64 * (h % 2) + 64)
            for nt in range(2):
                ps, kp = mmnext()
                mm(ps[:, 0:NQ], mkT[pr, h // 2, nt * 128:(nt + 1) * 128], qmT[pr, h // 2, 0:NQ], True, True,
                   ["mkT"] + [("qmT", jj) for jj in range(ntile)], [kp])
                pt, kpt = PT.next()
                act(pt[:, 0, 0:NQ], ps[:, 0:NQ], AF.Exp, [kp], [kpt], scale=0.125)
                if prevm is not None:
                    ph, pnt, ppt, pkpt = prevm
                    for i in range(ntile):
                        mm(banks[i][0][0:T, ph * 65:(ph + 1) * 65], ppt[:, 0, i * T:(i + 1) * T],
                           mv_aug[:, pnt, ph, 0:65], False, True, [pkpt, "mv", "mv_ones"], [banks[i][1]])
                prevm = (h, nt, pt, kpt)
        ph, pnt, ppt, pkpt = prevm
        for i in range(ntile):
            mm(banks[i][0][0:T, ph * 65:(ph + 1) * 65], ppt[:, 0, i * T:(i + 1) * T],
               mv_aug[:, pnt, ph, 0:65], False, True, [pkpt, "mv", "mv_ones"], [banks[i][1]])
        for i in range(ntile):
            bk, kb = banks[i]
            sa, ks = STT.next()
            recip(sa[0:T, 0:4], bk[0:T, 64:260:65], [kb], [ks])
            tt("dve", OM[0:T, :, :], bk[0:T, 0:260].rearrange("p (h e) -> p h e", e=65)[:, :, 0:64],
               sa[0:T, 0:4].unsqueeze(2).to_broadcast([T, 4, 64]), ALU.mult, [kb, ks], ["HN"])
            tt("dve", mix[0:T, i, 768:1024], OM[0:T, :, :].rearrange("p h d -> p (h d)"),
               mix[0:T, i, 768:1024], ALU.mult, ["HN", ("mix", i, 2)], [("mix", i, 2)])

    def mlstm_block(T, ntile):
        pend_a = []
        pend_b = []
        for j in range(ntile):
            tok = slice(j * 128, j * 128 + T)
            tt("dve", Sd[:, :, 0:65], Sst[:, :, 0:65], C0B[:, j, :].unsqueeze(2).to_broadcast([64, 4, 65]), ALU.mult,
               ["Sst", "C0B"], ["Sd"])
            cpy("pool", Cs[:, :, 0:65], Sd[:, :, 0:65], ["Sd"], ["Cs"])
            tt("pool", vw[0:T, :, 0:65], vb_aug[0:T, j, :, 0:65], WE[0:T, j, 0:4].unsqueeze(2).to_broadcast([T, 4, 65]),
               ALU.mult, [("vb", j), "vb_ones", "WE"], ["vw"])
            chk(80)
            psA, kA = mmnext()
            for h in range(4):
                mm(psA[0:T, h * 128:h * 128 + T], QKT[:, 4 + h, tok], QKT[:, h, tok], True, True,
                   [("QKT", j)], [kA])
            tt("dve", AT[0:T, :, 0:T], psA[0:T, :].rearrange("p (h t) -> p h t", h=4)[:, :, 0:T],
               cpc("maskU")[0:T, 0:T].unsqueeze(1).to_broadcast([T, 4, T]), ALU.mult, [kA, "cp"], ["AT"])
            chk(81)
            psO, kO = ACC[j % 2], ACCK[j % 2]
            for h in range(4):
                mm(psO[0:T, h * 65:(h + 1) * 65], AT[0:T, h, 0:T], vw[0:T, h, 0:65], True, False, ["AT", "vw"], [kO])
                mm(psO[0:T, h * 65:(h + 1) * 65], QKT[:, h, tok], Cs[:, h, 0:65], False, True,
                   [("QKT", j), "Cs"], [kO])
            chk(82)
            psS, kS = ACC[2], ACCK[2]
            for h in range(4):
                mm(psS[0:64, h * 65:(h + 1) * 65], QKB[0:T, j, 256 + h * 64:256 + (h + 1) * 64],
                   vw[0:T, h, 0:65], True, True, [("QKB", j), "vw"], [kS])
            chk(83)
            tt("dve", Sst[:, :, 0:65], Sd[:, :, 0:65], psS[0:64, 0:260].rearrange("p (c e) -> p c e", e=65), ALU.add,
               ["Sd", kS], ["Sst"])

            def evac_a(j=j, psO=psO, kO=kO):
                sa, ks = STT.next()
                cpy("dve", sa[0:T, 20:24], psO[0:T, 64:260:65], [kO], [ks])
                stt(sa[0:T, 24:28], sa[0:T, 20:24], -1.0, sa[0:T, 20:24], ALU.mult, ALU.max, [ks], [ks])
                tt("dve", sa[0:T, 0:4], sa[0:T, 24:28], WE[0:T, j, 4:8], ALU.max, [ks, "WE"], [ks])
                recip(sa[0:T, 4:8], sa[0:T, 0:4], [ks], [ks])
                tt("dve", HN[0:T, :, :], psO[0:T, 0:260].rearrange("p (h e) -> p h e", e=65)[:, :, 0:64],
                   sa[0:T, 4:8].unsqueeze(2).to_broadcast([T, 4, 64]), ALU.mult, [kO, ks], ["HN"])
                r4 = group_norm_rstd(HN[0:T, :, :].rearrange("p h d -> p (h d)"), "HN", T, 4, sa, ks, 8)

                def evac_b():
                    tt("dve", HN[0:T, :, :], HN[0:T, :, :], r4.unsqueeze(2).to_broadcast([T, 4, 64]), ALU.mult,
                       ["HN", ks], ["HN"])
                    tt("dve", mix[0:T, j, 512:768], HN[0:T, :, :].rearrange("p h d -> p (h d)"),
                       mix[0:T, j, 512:768], ALU.mult, ["HN", ("mix", j, 1)], [("mix", j, 1)])
                return evac_b
            pend_a.append(evac_a)
            if len(pend_a) == 2:
                for f_ in pend_b:
                    f_()
                del pend_b[:]
                pend_b.append(pend_a.pop(0)())
        for f_ in pend_b:
            f_()
        del pend_b[:]
        while pend_a:
            pend_a.pop(0)()()

    def phase3_tile(src_ap, dst_ap, T, j):
        for kc in range(8):
            tr(TB[:, kc * 128:kc * 128 + T], mix[0:T, j, kc * 128:(kc + 1) * 128], identb[0:T, 0:T],
               [("mix", j, 0), ("mix", j, 1), ("mix", j, 2), "identb"], [TBK])
        cpy("act", mixT[:, :, 0:T], TB.rearrange("p (k t) -> p k t", k=8)[:, :, 0:T], [TBK], ["mixT"])
        xt, kx = XT.next()
        dma(xt[0:T, :], src_ap, w=[kx])
        for half in range(2):
            ps, kp = ACC[half], ACCK[half]
            for kc in range(8):
                mm(ps[0:T, :], mixT[:, kc, 0:T], w_out_bf[:, kc, half * 512:(half + 1) * 512], kc == 0, kc == 7,
                   ["mixT"] + W_OUT_K, [kp])
            tt("dve", xt[0:T, half * 512:(half + 1) * 512], ps[0:T, :], xt[0:T, half * 512:(half + 1) * 512],
               ALU.add, [kp, kx], [kx])
        dma(dst_ap, xt[0:T, :], r=[kx])

    def memkv_seq(s):
        for nt in range(2):
            xt, kx, hT, kh, sa, ks = front(memp[s, nt * 128:(nt + 1) * 128, :], 128)
            ps, kp = mmnext()
            for kc in range(8):
                mm(ps[:, :], hT[:, kc, :], w_mkv_bf[:, kc, :], kc == 0, kc == 7, [kh] + W_MKV_K, [kp])
            zf, kz = ZF.next()
            cpy("act", zf[:, :], ps[:, :], [kp], [kz])
            dma(pmv[s, nt * 128:(nt + 1) * 128, :], zf[:, 256:512], r=[kz])
            cpy("pool", mv_aug[:, nt, :, 0:64], zf[:, 256:512].rearrange("p (h d) -> p h d", h=4),
                [kz, "mv_ones"], ["mv"])
            r4 = group_norm_rstd(zf[:, 0:256], kz, 128, 4, sa, ks, 8)
            tt("dve", zf[:, 0:256].rearrange("p (g d) -> p g d", d=64),
               zf[:, 0:256].rearrange("p (g d) -> p g d", d=64),
               r4.unsqueeze(2).to_broadcast([128, 4, 64]), ALU.mult, [kz, ks], [kz])
            ko, kk = KOUT.next()
            tt("dve", ko[:, 0:256].rearrange("p (g d) -> p g d", d=64),
               zf[:, 0:256].rearrange("p (g d) -> p g d", d=64),
               cpc("gkm_rep").unsqueeze(1).to_broadcast([128, 4, 64]), ALU.mult, [kz, "cp"], [kk])
            dma(pmk[s, nt * 128:(nt + 1) * 128, :], ko[:, 0:256], r=[kk])
            nb, kn = NB.next()
            cpy("pool", nb[:, 0:256], ko[:, 0:256], [kk], [kn])
            for hp in range(2):
                tr(TB[:, hp * 128:(hp + 1) * 128], nb[:, hp * 128:(hp + 1) * 128], identb[:, :],
                   [kn, "identb"], [TBK])
            cpy("act", mkT[:, :, nt * 128:(nt + 1) * 128], TB[:, 0:256].rearrange("p (h t) -> p h t", h=2),
                [TBK], ["mkT"])

    def state_out(dC, dn, dm):
        for h in range(4):
            dma(dC[h], Sst[:, h, 0:64], r=["Sst"])
            dma(dn[h].unsqueeze(1), Sst[:, h, 64:65], r=["Sst"])
        tt("dve", mfin[:, :], Bprev[:, 0:1], MUALL[:, 0:1], ALU.add, ["Bprev", "MUALL"], ["mfin"])
        dma(dm, mfin[:, :], r=["mfin"])

    try:
        chk(1)
        if DO_SAMPLE:
            for t in range(16):
                xt, kx = XT.next()
                dma(xt[:, 0:512], ck[t * 128:(t + 1) * 128, :], w=[kx])
                dma(xt[:, 512:1024], cv[t * 128:(t + 1) * 128, :], w=[kx])
                nb, kn = NB.next()
                cpy("dve", nb[:, :], xt[:, 0:512], [kx], [kn])
                cpy("pool", v_aug[:, t, :, 0:128], xt[:, 512:1024].rearrange("p (h d) -> p h d", h=4),
                    [kx, "v_ones"], [("v", t)])
                for h in range(4):
                    tr(TB[:, h * 128:(h + 1) * 128], nb[:, h * 128:(h + 1) * 128], identb[:, :], [kn, "identb"], [TBK])
                cpy("act", kT[:, :, t * 128:(t + 1) * 128], TB[:, 0:512].rearrange("p (h t) -> p h t", h=4),
                    [TBK], [("kT", t)])
            for nt in range(2):
                xt, kx = XT.next()
                dma(xt[:, 0:256], cmk[nt * 128:(nt + 1) * 128, :], w=[kx])
                dma(xt[:, 256:512], cmv[nt * 128:(nt + 1) * 128, :], w=[kx])
                nb, kn = NB.next()
                cpy("dve", nb[:, 0:256], xt[:, 0:256], [kx], [kn])
                cpy("pool", mv_aug[:, nt, :, 0:64], xt[:, 256:512].rearrange("p (h d) -> p h d", h=4),
                    [kx, "mv_ones"], ["mv"])
                for hp in range(2):
                    tr(TB[:, hp * 128:(hp + 1) * 128], nb[:, hp * 128:(hp + 1) * 128], identb[:, :],
                       [kn, "identb"], [TBK])
                cpy("act", mkT[:, :, nt * 128:(nt + 1) * 128], TB[:, 0:256].rearrange("p (h t) -> p h t", h=2),
                    [TBK], ["mkT"])
            chk(2)
            for h in range(4):
                dma(Sst[:, h, 0:64], sC[h], w=["Sst"])
                dma(Sst[:, h, 64:65], sn[h].unsqueeze(1), w=["Sst"])
            memset("dve", Bprev[:, :], 0.0, ["Bprev"])
            cpy("dve", MUALL[:, 0:1], cpc("sm_pp")[0:4, :], ["cp"], ["MUALL"])
            chk(3)
            phase1_block([(xs[:, :], 16, 0, sk[:, :], sv[:, :])], 16)
            chk(4)
            gates_block(16, 1)
            chk(5)
            attention_block(16, 1, [(t, 128, None) for t in range(16)] + [(16, 16, 0)], True)
            chk(6)
            mem_block(16, 1)
            chk(7)
            mlstm_block(16, 1)
            chk(8)
            phase3_tile(xs[:, :], ys[:, :], 16, 0)
            chk(9)
            state_out(sCo, sno, smo.rearrange("o h -> h o"))

        for s in range(NSEQ):
            memkv_seq(s)
            memset("dve", Sst[:, :, :], 0.0, ["Sst"])
            memset("dve", Bprev[:, :], 0.0, ["Bprev"])
            memset("dve", MUALL[:, 0:1], 0.0, ["MUALL"])
            for b in range(NBLK):
                tl = []
                for j in range(4):
                    ti = 4 * b + j
                    rows = slice(ti * 128, (ti + 1) * 128)
                    tl.append((xp[s, rows, :], ti, j, pk[s, rows, :], pv[s, rows, :]))
                phase1_block(tl, 128)
                gates_block(128, 4)
                ktiles = [(t, 128, None) for t in range(4 * b)] + [(4 * b + i, 128, i) for i in range(4)]
                attention_block(128, 4, ktiles, False)
                mem_block(128, 4)
                mlstm_block(128, 4)
                for j in range(4):
                    ti = 4 * b + j
                    rows = slice(ti * 128, (ti + 1) * 128)
                    phase3_tile(xp[s, rows, :], yp[s, rows, :], 128, j)
            state_out(pC[s], pn[s], pm[s].unsqueeze(1))

    except _Stop:
        pass

    print('sbuf bytes remaining', nc.sbuf_bytes_remaining)
    S.emit(st)
    st.close()
    return nc


_NC_CACHE = {}


def _get_nc(key=(4, 4, True)):
    if key not in _NC_CACHE:
        _NC_CACHE[key] = build_program(*key)
    return _NC_CACHE[key]


def _in_maps(inp, NSEQ=4):
    maps = []
    c32 = lambda a: np.ascontiguousarray(a, dtype=np.float32)
    for c in range(8):
        m = {
            "xp": c32(inp["x_prompt"][4 * c:4 * c + max(NSEQ, 1)]),
            "memp": c32(inp["mem_prompt"][4 * c:4 * c + max(NSEQ, 1)]),
            "xs": c32(inp["x_sample"][c]),
            "ck": c32(inp["cache_attn_k"][0, c].reshape(2048, 512)),
            "cv": c32(inp["cache_attn_v"][0, c].reshape(2048, 512)),
            "sC": c32(inp["state_mlstm_C"][0, c]),
            "sn": c32(inp["state_mlstm_n"][0, c]),
            "cmk": c32(inp["cache_mem_k"][0, c].reshape(256, 256)),
            "cmv": c32(inp["cache_mem_v"][0, c].reshape(256, 256)),
            "w_in": c32(inp["w_in"][0]),
            "w_out": c32(inp["w_out"][0]),
            "w_mk": c32(inp["w_mk"][0]),
            "w_mv": c32(inp["w_mv"][0]),
        }
        cpa, extra = _make_cp(inp, c)
        m["cp"] = cpa
        m.update(extra)
        maps.append(m)
    return maps


def kernel(**inputs):
    inp = {k: np.asarray(v) for k, v in inputs.items()}
    nc = _get_nc()
    res = run_bass_kernel_spmd(nc, _in_maps(inp), core_ids=list(range(8)))
    R = res.results
    cat = lambda name: np.concatenate([np.asarray(r[name]) for r in R], axis=0)
    stk = lambda name: np.stack([np.asarray(r[name]) for r in R], axis=0)
    y_prompt = cat("yp")
    y_sample = stk("ys")
    p_attn_k = cat("pk").reshape(1, 32, 2048, 4, 128)
    p_attn_v = cat("pv").reshape(1, 32, 2048, 4, 128)
    p_C = cat("pC").reshape(1, 32, 4, 64, 64)
    p_n = cat("pn").reshape(1, 32, 4, 64)
    p_m = cat("pm").reshape(1, 32, 4)
    p_mk = cat("pmk").reshape(1, 32, 256, 4, 64)
    p_mv = cat("pmv").reshape(1, 32, 256, 4, 64)
    s_k = stk("sk").reshape(1, 8, 16, 4, 128)
    s_v = stk("sv").reshape(1, 8, 16, 4, 128)
    s_C = stk("sCo").reshape(1, 8, 4, 64, 64)
    s_n = stk("sno").reshape(1, 8, 4, 64)
    s_m = stk("smo").reshape(1, 8, 4)
    return (y_prompt, y_sample, p_attn_k, p_attn_v, p_C, p_n, p_m, p_mk, p_mv,
            s_k, s_v, s_C, s_n, s_m)
```

```python
import math
from contextlib import ExitStack

import numpy as np
import concourse.bass as bass
import concourse.mybir as mybir
from concourse.bass_utils import run_bass_kernel_spmd

F32 = mybir.dt.float32
BF16 = mybir.dt.bfloat16
AF = mybir.ActivationFunctionType
ALU = mybir.AluOpType
AX = mybir.AxisListType

N_DMA_SEMS = 40
DELAY = 4
EPS = 1e-6
SLOPES = [2.0 ** (-8.0 * (h + 1) / 4) for h in range(4)]
LAM_INIT = 0.8 - 0.6 * math.exp(0.0)
NIN = 3848
BIG = 240000.0


class _Op:
    __slots__ = ("eng", "fn", "deps", "has_dep", "is_dma", "semkey", "val", "ie")


class Sched:
    def __init__(self, nc):
        self.nc = nc
        self.ops = []
        self.lw = {}
        self.rd = {}
        self.eng_n = {"pe": 0, "act": 0, "dve": 0, "pool": 0, "sp": 0}

    def add(self, eng, fn, reads=(), writes=(), dma=False):
        i = len(self.ops)
        deps = set()
        for k in reads:
            w = self.lw.get(k)
            if w is not None:
                deps.add(w)
            if isinstance(k, tuple) and k[0] == "pb":
                for r in self.rd.get(k, ()):
                    if self.ops[r].eng != eng:
                        deps.add(r)
        for k in writes:
            w = self.lw.get(k)
            if w is not None:
                deps.add(w)
            for r in self.rd.get(k, ()):
                deps.add(r)
        op = _Op()
        op.eng = eng
        op.fn = fn
        op.is_dma = dma
        op.has_dep = False
        op.semkey = None
        op.val = 0
        op.ie = self.eng_n[eng]
        self.eng_n[eng] += 1
        keep = []
        for d in deps:
            p = self.ops[d]
            if p.eng == eng and not p.is_dma:
                if eng == "pe" or eng == "sp":
                    continue
                if eng != "pool" and op.ie - p.ie > 2:
                    continue
            p.has_dep = True
            keep.append(d)
        op.deps = keep
        for k in reads:
            self.rd.setdefault(k, []).append(i)
        for k in writes:
            self.lw[k] = i
            self.rd[k] = []
        self.ops.append(op)
        return i

    def emit(self, stack):
        nc = self.nc
        engobj = {"pe": nc.tensor, "act": nc.scalar, "dve": nc.vector,
                  "pool": nc.gpsimd, "sp": nc.sync}
        esem = {e: stack.enter_context(nc.semaphore("s_" + e))
                for e in ("pe", "act", "dve", "pool")}
        dsem = [stack.enter_context(nc.semaphore("d%d" % i)) for i in range(N_DMA_SEMS)]
        waited = {e: {} for e in engobj}
        cnt = {e: 0 for e in engobj}
        dcnt = [0] * N_DMA_SEMS
        rr = 0
        rrp = 0
        for op in self.ops:
            E = engobj[op.eng]
            need = {}
            for d in op.deps:
                p = self.ops[d]
                if need.get(p.semkey, 0) < p.val:
                    need[p.semkey] = p.val
            s = None
            if op.is_dma:
                if op.eng == "pool":
                    s = N_DMA_SEMS - 8 + (rrp % 8)
                    rrp += 1
                else:
                    s = rr % (N_DMA_SEMS - 8)
                    rr += 1
                if dcnt[s] > 0 and need.get(("d", s), 0) < dcnt[s]:
                    need[("d", s)] = dcnt[s]
                dcnt[s] += 16
                op.semkey = ("d", s)
                op.val = dcnt[s]
            w = waited[op.eng]
            for key, val in need.items():
                if w.get(key, 0) >= val:
                    continue
                so = dsem[key[1]] if key[0] == "d" else esem[key[1]]
                E.wait_ge(so, val)
                w[key] = val
            inst = op.fn()
            if op.is_dma:
                inst.then_inc(dsem[s], 16)
            elif op.has_dep:
                cnt[op.eng] += 1
                op.semkey = ("e", op.eng)
                op.val = cnt[op.eng]
                inst.then_inc(esem[op.eng], 1)
        for s in range(N_DMA_SEMS):
            if dcnt[s] > 0:
                nc.sync.wait_ge(dsem[s], dcnt[s])
        return cnt


class Rot:
    def __init__(self, aps, name):
        self.aps = aps
        self.name = name
        self.i = 0

    def next(self):
        k = self.i % len(self.aps)
        self.i += 1
        return self.aps[k], (self.name, k)


def _cp_layout():
    off = {}
    c = 0
    for name, n in [("ident", 128), ("maskU", 128), ("AB", 64), ("ABS", 68),
                    ("gqa_pp", 1), ("gqm_pp", 1), ("gkm_rep", 64), ("gka_rep", 64),
                    ("gnorm_pp", 8), ("gmem_pp", 8), ("rows_pp", 8),
                    ("bi_pp", 1), ("bf_pp", 1), ("sm_pp", 1), ("SEL", 128), ("SEL2", 4)]:
        off[name] = (c, c + n)
        c += n
    return off, c


CP_OFF, NCP = _cp_layout()


def _make_cp(inp, core):
    cp = np.zeros((128, NCP), np.float32)

    def put(name, arr):
        a, b = CP_OFF[name]
        cp[:, a:b] = arr

    p = np.arange(128)
    put("ident", np.eye(128, dtype=np.float32))
    put("maskU", (p[:, None] <= p[None, :]).astype(np.float32))
    corr = np.zeros((128, 4, 128), np.float32)
    k = p[:, None]
    q = p[None, :]
    for h in range(4):
        c_ = np.where(k > q, -16.0 * SLOPES[h] * (k - q), 0.0)
        c_ = np.where((k // 64) > (q // 64), -BIG, c_)
        corr[:, h, :] = c_
    extra = {"corr": corr.reshape(128, 512)}
    AB = np.zeros((128, 4, 16), np.float32)
    for h in range(4):
        for r in range(16):
            AB[:, h, r] = SLOPES[h] * (p - 127 - 128 * r)
    put("AB", AB.reshape(128, 64))
    ABS = np.zeros((128, 4, 17), np.float32)
    for h in range(4):
        for t in range(17):
            ABS[:, h, t] = SLOPES[h] * np.minimum(128 * t + p - 2063, 0)
    put("ABS", ABS.reshape(128, 68))
    put("gqa_pp", inp["g_qa"][0][p % 64][:, None])
    put("gqm_pp", inp["g_qm"][0][p % 64][:, None])
    put("gkm_rep", np.broadcast_to(inp["g_km"][0][None, :], (128, 64)))
    put("gka_rep", np.broadcast_to(inp["g_ka"][0][None, :], (128, 64)))
    put("gnorm_pp", inp["g_norm"][0].reshape(8, 128).T)
    put("gmem_pp", inp["g_mem"][0].reshape(8, 128).T)
    rows = np.ones((128, 8), np.float32)
    rows[:, 0:4] = inp["g_subln"][0][:, None]
    rows[:, 4:6] = inp["g_mh"][0][p % 64][:, None]
    put("rows_pp", rows)
    lam4 = np.concatenate([inp["lam_q1"][0], inp["lam_k1"][0], inp["lam_q2"][0], inp["lam_k2"][0]])
    extra["lam4"] = np.ascontiguousarray(np.broadcast_to(lam4[None, :], (128, 256)))
    put("bi_pp", inp["b_i"][0][p % 4][:, None])
    put("bf_pp", inp["b_f"][0][p % 4][:, None])
    put("sm_pp", inp["state_mlstm_m"][0, core][p % 4][:, None])
    SEL = np.zeros((128, 128), np.float32)
    SEL[0:4, :] = 1.0
    put("SEL", SEL)
    SEL2 = np.zeros((128, 4), np.float32)
    SEL2[0:4, 0:4] = np.eye(4, dtype=np.float32)
    put("SEL2", SEL2)
    return cp, extra


class _Stop(Exception):
    pass


def build_program(NSEQ=4, NBLK=4, DO_SAMPLE=True, STAGE=99):
    nc = bass.Bass("TRN2", target_bir_lowering=False)
    S = Sched(nc)

    def din(name, shape):
        return nc.dram_tensor(name, shape, F32, kind="ExternalInput").ap()

    def dout(name, shape):
        return nc.dram_tensor(name, shape, F32, kind="ExternalOutput").ap()

    NS = max(NSEQ, 1)
    xp = din("xp", [NS, 2048, 1024])
    memp = din("memp", [NS, 256, 1024])
    xs = din("xs", [16, 1024])
    ck = din("ck", [2048, 512])
    cv = din("cv", [2048, 512])
    sC = din("sC", [4, 64, 64])
    sn = din("sn", [4, 64])
    cmk = din("cmk", [256, 256])
    cmv = din("cmv", [256, 256])
    w_in = din("w_in", [1024, NIN])
    w_out = din("w_out", [1024, 1024])
    w_mk = din("w_mk", [1024, 256])
    w_mv = din("w_mv", [1024, 256])
    cpd = din("cp", [128, NCP])
    corrd = din("corr", [128, 512])
    lam4d = din("lam4", [128, 256])

    yp = dout("yp", [NS, 2048, 1024])
    pk = dout("pk", [NS, 2048, 512])
    pv = dout("pv", [NS, 2048, 512])
    pC = dout("pC", [NS, 4, 64, 64])
    pn = dout("pn", [NS, 4, 64])
    pm = dout("pm", [NS, 4])
    pmk = dout("pmk", [NS, 256, 256])
    pmv = dout("pmv", [NS, 256, 256])
    ys = dout("ys", [16, 1024])
    sk = dout("sk", [16, 512])
    sv = dout("sv", [16, 512])
    sCo = dout("sCo", [4, 64, 64])
    sno = dout("sno", [4, 64])
    smo = dout("smo", [1, 4])

    st = ExitStack()

    def chk(n):
        if STAGE == n:
            raise _Stop()

    def sb(name, shape, dt=F32):
        return st.enter_context(nc.sbuf_tensor("sb_" + name, shape, dt))

    cp = sb("cp", [128, NCP])
    w_in_bf = sb("w_in_bf", [128, 8, NIN], BF16)
    w_out_bf = sb("w_out_bf", [128, 8, 1024], BF16)
    w_mkv_bf = sb("w_mkv_bf", [128, 8, 512], BF16)
    identb = sb("identb", [128, 128], BF16)
    corrb = sb("corrb", [128, 4, 128], BF16)
    zerob = sb("zerob", [128, 512], BF16)
    cm05 = sb("cm05", [128, 8])
    lamt = sb("lamt", [128, 8])
    rs8 = sb("rs8", [128, 8])
    nbf = sb("nbf", [128, 1])

    kT = sb("kT", [128, 4, 2064], BF16)
    v_aug = sb("v_aug", [128, 17, 4, 130], BF16)
    mkT = sb("mkT", [128, 2, 256], BF16)
    mv_aug = sb("mv_aug", [128, 2, 4, 66], BF16)

    qT = sb("qT", [128, 4, 512], BF16)
    mix = sb("mix", [128, 4, 1024], BF16)
    OBG = sb("OBG", [128, 4, 256], BF16)
    QKB = sb("QKB", [128, 4, 512], BF16)
    QKT = sb("QKT", [64, 8, 512], BF16)
    vb_aug = sb("vb_aug", [128, 4, 4, 66], BF16)
    qmT = sb("qmT", [128, 2, 512], BF16)
    GIF = sb("GIF", [128, 4, 8])
    WE = sb("WE", [128, 4, 8])
    C0B = sb("C0B", [64, 4, 4])
    Sst = sb("Sst", [64, 4, 66])
    Sd = sb("Sd", [64, 4, 66])
    Cs = sb("Cs", [64, 4, 66], BF16)
    Bprev = sb("Bprev", [4, 1])
    MUALL = sb("MUALL", [4, 8])
    UM = sb("UM", [4, 4])
    C0 = sb("C0", [4, 4])
    CD = sb("CD", [4, 4, 4])
    mfin = sb("mfin", [4, 1])

    XT = Rot([sb("xt%d" % i, [128, 1024])[:] for i in range(2)], "xt")
    xn = sb("xn", [128, 1024], BF16)
    HT = Rot([sb("hT%d" % i, [128, 8, 128], BF16)[:] for i in range(2)], "hT")
    ZF = Rot([sb("zf%d" % i, [128, 512])[:] for i in range(2)], "zf")
    SQ = sb("sq", [128, 512])
    KOUT = Rot([sb("kvout%d" % i, [128, 512])[:] for i in range(3)], "kvout")
    VOUT = KOUT
    TH = Rot([sb("th%d" % i, [128, 512])[:] for i in range(2)], "th")
    NB = Rot([sb("nb%d" % i, [128, 512], BF16)[:] for i in range(4)], "nb")
    PT = Rot([sb("pt%d" % i, [128, 2, 512], BF16)[:] for i in range(3)], "pt")
    STT = Rot([sb("stat%d" % i, [128, 72])[:] for i in range(4)], "stat")
    EV = sb("ev", [128, 1024])

    class _RotK(Rot):
        def next(self):
            k = self.i % len(self.aps)
            self.i += 1
            return self.aps[k], "ev"
    O1 = _RotK([EV[:, i * 128:(i + 1) * 128] for i in range(4)], "o1s")
    OO = _RotK([EV[:, 512 + i * 128:512 + (i + 1) * 128] for i in range(4)], "oo")
    AT = sb("AT", [128, 4, 128], BF16)
    VW = [sb("vw%d" % i, [128, 4, 66], BF16) for i in range(4)]
    HN = sb("HN", [128, 4, 64])
    OM = HN
    GA, KGA = SQ[0:4, :], "sq"
    GB, KGB = TH.aps[0][0:4, :], ("th", 0)
    GU, KGU = TH.aps[1][0:4, :], ("th", 1)
    GW, KGW = ZF.aps[0][0:4, :], ("zf", 0)
    GE, KGE = ZF.aps[1][0:4, :], ("zf", 1)

    PBIG = st.enter_context(nc.psum_tensor("pbig", [128, 4096], F32))
    PB = [PBIG[:, i * 512:(i + 1) * 512] for i in range(8)]
    MM = Rot([PB[0], PB[1], PB[2]], "pb_mm")
    MM.keys = [("pb", 0), ("pb", 1), ("pb", 2)]
    ACC = [PB[3], PB[4], PB[5]]
    ACCK = [("pb", 3), ("pb", 4), ("pb", 5)]
    TBf = PB[6]
    TB = PB[6].bitcast(BF16)
    TBK = ("pb", 6)
    TF = PB[7]
    TFK = ("pb", 7)

    def mmnext():
        k = MM.i % 3
        MM.i += 1
        return PB[k], ("pb", k)

    def cpc(name, a=None, b=None):
        o, e = CP_OFF[name]
        if a is None:
            return cp[:, o:e]
        return cp[:, o + a:o + b]

    def dma(out, in_, r=(), w=(), q="sp", **kw):
        if q == "sp":
            S.add("sp", lambda: nc.sync.dma_start(out=out, in_=in_, **kw), r, w, dma=True)
        else:
            S.add("pool", lambda: nc.gpsimd.dma_start(out=out, in_=in_, **kw), r, w, dma=True)

    def mm(out, lhsT, rhs, start, stop, r, w):
        S.add("pe", lambda: nc.tensor.matmul(out, lhsT=lhsT, rhs=rhs, start=start, stop=stop,
                                             skip_group_check=True), r, w)

    def tr(out, in_, ident, r, w):
        S.add("pe", lambda: nc.tensor.transpose(out=out, in_=in_, identity=ident), r, w)

    def act(out, in_, func, r, w, bias=0.0, scale=1.0, accum=None):
        if accum is None:
            S.add("act", lambda: nc.scalar.activation(out=out, in_=in_, func=func, bias=bias, scale=scale), r, w)
        else:
            S.add("act", lambda: nc.scalar.activation(out=out, in_=in_, func=func, bias=bias, scale=scale,
                                                      accum_out=accum), r, w)

    def engo(e):
        return nc.vector if e == "dve" else nc.gpsimd

    def ts(e, out, in0, s1, s2, op0, op1, r, w):
        if s2 is None:
            S.add(e, lambda: engo(e).tensor_scalar(out=out, in0=in0, scalar1=s1, scalar2=None, op0=op0), r, w)
        else:
            S.add(e, lambda: engo(e).tensor_scalar(out=out, in0=in0, scalar1=s1, scalar2=s2, op0=op0, op1=op1), r, w)

    def tt(e, out, in0, in1, op, r, w):
        S.add(e, lambda: engo(e).tensor_tensor(out=out, in0=in0, in1=in1, op=op), r, w)

    def stt(out, in0, scalar, in1, op0, op1, r, w):
        S.add("dve", lambda: nc.vector.scalar_tensor_tensor(out=out, in0=in0, scalar=scalar, in1=in1,
                                                            op0=op0, op1=op1), r, w)

    def cpy(e, out, in_, r, w):
        if e == "act":
            S.add("act", lambda: nc.scalar.copy(out=out, in_=in_), r, w)
        else:
            S.add(e, lambda: engo(e).tensor_copy(out=out, in_=in_), r, w)

    def memset(e, ap, val, w):
        S.add(e, lambda: engo(e).memset(ap, val), (), w)

    def red(out, in_, op, r, w):
        S.add("dve", lambda: nc.vector.tensor_reduce(out=out, in_=in_, axis=AX.X, op=op), r, w)

    def recip(out, in_, r, w):
        S.add("dve", lambda: nc.vector.reciprocal(out=out, in_=in_), r, w)

    def scan(out, d0, d1, init, op0, op1, r, w):
        S.add("dve", lambda: nc.vector.tensor_tensor_scan(out=out, data0=d0, data1=d1, initial=init,
                                                          op0=op0, op1=op1), r, w)

    def rstd_from_ss(stt_ap, kst, c_ss, c_tmp, c_out, n, T, inv):
        ts("dve", stt_ap[0:T, c_tmp:c_tmp + n], stt_ap[0:T, c_ss:c_ss + n], inv, EPS, ALU.mult, ALU.add,
           [kst], [kst])
        tt("pool", stt_ap[0:T, c_out:c_out + n], stt_ap[0:T, c_tmp:c_tmp + n], cm05[0:T, 0:n], ALU.pow,
           [kst, "cm05"], [kst])

    dma(cp[:], cpd, w=["cp"])
    cpy("dve", identb[:], cpc("ident"), ["cp"], ["identb"])
    identf = cpc("ident")
    dma(TH.aps[0][:, :], corrd, w=[("th", 0)])
    cpy("dve", corrb[:].rearrange("p h q -> p (h q)"), TH.aps[0][:, :], [("th", 0)], ["corrb"])
    dma(TH.aps[1][:, 0:256], lam4d, w=[("th", 1)])
    memset("pool", zerob[:], 0.0, ["zerob"])
    memset("pool", cm05[:], -0.5, ["cm05"])
    memset("pool", v_aug[:, :, :, 128:129], 1.0, ["v_ones"])
    memset("pool", vb_aug[:, :, :, 64:65], 1.0, ["vb_ones"])
    memset("pool", mv_aug[:, :, :, 64:65], 1.0, ["mv_ones"])
    ts("dve", rs8[:, 0:4], cpc("rows_pp", 0, 4), 0.5 * (1.0 - LAM_INIT), None, ALU.mult, None, ["cp"], ["rs8"])
    ts("dve", rs8[:, 4:6], cpc("rows_pp", 4, 6), 0.25, None, ALU.mult, None, ["cp"], ["rs8"])
    ts("dve", rs8[:, 6:8], cpc("rows_pp", 6, 8), 0.5, None, ALU.mult, None, ["cp"], ["rs8"])
    ts("dve", nbf[:], cpc("bf_pp"), -1.0, None, ALU.mult, None, ["cp"], ["nbf"])
    wv = w_in.rearrange("(k p) n -> p k n", p=128)
    wov = w_out.rearrange("(k p) n -> p k n", p=128)
    wkv = w_mk.rearrange("(k p) n -> p k n", p=128)
    wvv = w_mv.rearrange("(k p) n -> p k n", p=128)

    WST = Rot([XT.aps[0][:, 0:512], XT.aps[1][:, 0:512], ZF.aps[0], ZF.aps[1], KOUT.aps[0], KOUT.aps[1],
               KOUT.aps[2]], "wst")
    WSTK = [("xt", 0), ("xt", 1), ("zf", 0), ("zf", 1), ("kvout", 0), ("kvout", 1), ("kvout", 2)]

    def wload(dst, src, n, scal, key, extra=None):
        if n > 512:
            h_ = n // 2
            wload(dst[:, 0:h_], src[:, 0:h_], h_, scal, key, extra)
            wload(dst[:, h_:n], src[:, h_:n], n - h_, scal, key, extra)
            return
        i_ = WST.i % len(WST.aps)
        xt, _ = WST.next()
        kx = WSTK[i_]
        dma(xt[:, 0:n], src, w=[kx])
        if extra is None:
            ts("dve", dst, xt[:, 0:n], scal, None, ALU.mult, None, [kx, "cp", "rs8"], [key])
        else:
            ts("dve", dst, xt[:, 0:n], scal, extra, ALU.mult, ALU.mult, [kx, "cp", "rs8"], [key])

    for kc in range(8):
        wload(w_mkv_bf[:, kc, 0:256], wkv[:, kc, :], 256, cpc("gmem_pp", kc, kc + 1), ("w_mkv", kc, 0))
        wload(w_mkv_bf[:, kc, 256:512], wvv[:, kc, :], 256, cpc("gmem_pp", kc, kc + 1), ("w_mkv", kc, 1))
    for kc in range(8):
        g = cpc("gnorm_pp", kc, kc + 1)
        wload(w_in_bf[:, kc, 0:1024], wv[:, kc, 0:1024], 1024, g, ("w_in", kc, 0))
        wload(w_in_bf[:, kc, 1024:2048], wv[:, kc, 1024:2048], 1024, g, ("w_in", kc, 1))
        wload(w_in_bf[:, kc, 2048:2304], wv[:, kc, 2048:2304], 256, g, ("w_in", kc, 2))
        wload(w_in_bf[:, kc, 2304:2560], wv[:, kc, 2304:2560], 256, g, ("w_in", kc, 2), 0.125)
        wload(w_in_bf[:, kc, 2560:3072], wv[:, kc, 2560:3072], 512, g, ("w_in", kc, 2))
        wload(w_in_bf[:, kc, 3072:3840], wv[:, kc, 3080:3848], 768, g, ("w_in", kc, 3))
        wload(w_in_bf[:, kc, 3840:3848], wv[:, kc, 3072:3080], 8, g, ("w_in", kc, 3))
    for kc in range(8):
        wload(w_out_bf[:, kc, :], wov[:, kc, :], 1024, rs8[:, kc:kc + 1], ("w_out", kc))
    W_IN_K = [("w_in", kc, i) for kc in range(8) for i in range(4)]
    W_OUT_K = [("w_out", kc) for kc in range(8)]
    W_MKV_K = [("w_mkv", kc, i) for kc in range(8) for i in range(2)]
    l4 = TH.aps[1]
    tt("dve", SQ[:, 0:64], l4[:, 0:64], l4[:, 64:128], ALU.mult, [("th", 1)], ["sq"])
    red(lamt[:, 0:1], SQ[:, 0:64], ALU.add, ["sq"], ["lamt"])
    tt("dve", SQ[:, 64:128], l4[:, 128:192], l4[:, 192:256], ALU.mult, [("th", 1)], ["sq"])
    red(lamt[:, 1:2], SQ[:, 64:128], ALU.add, ["sq"], ["lamt"])
    act(lamt[:, 2:4], lamt[:, 0:2], AF.Exp, ["lamt"], ["lamt"])
    stt(lamt[:, 4:5], lamt[:, 2:3], LAM_INIT, lamt[:, 3:4], ALU.add, ALU.subtract, ["lamt"], ["lamt"])
    ts("dve", lamt[:, 5:6], lamt[:, 4:5], -1.0, None, ALU.mult, None, ["lamt"], ["lamt"])

    def front_load(src_ap, T):
        xt, kx = XT.next()
        dma(xt[0:T, :], src_ap, w=[kx])
        return xt, kx

    def front_a1(xt, kx, T):
        sa, ks = STT.next()
        act(xn[0:T, :], xt[0:T, :], AF.Square, [kx], ["xn", ks], accum=sa[0:T, 0:1])
        rstd_from_ss(sa, ks, 0, 1, 2, 1, T, 1.0 / 1024)
        return sa, ks

    def front_a2(xt, kx, sa, ks, T):
        ts("dve", xn[0:T, :], xt[0:T, :], sa[0:T, 2:3], None, ALU.mult, None, [kx, ks], ["xn"])

    def front_a(xt, kx, T):
        sa, ks = front_a1(xt, kx, T)
        front_a2(xt, kx, sa, ks, T)
        return sa, ks

    def front_b(xt, kx, sa, ks, T):
        for kc in range(8):
            tr(TB[:, kc * 128:kc * 128 + T], xn[0:T, kc * 128:(kc + 1) * 128], identb[0:T, 0:T],
               ["xn", "identb"], [TBK])
        hT, kh = HT.next()
        cpy("act", hT[:, :, 0:T], TB.rearrange("p (k t) -> p k t", k=8)[:, :, 0:T], [TBK], [kh])
        return xt, kx, hT, kh, sa, ks

    def front_compute(xt, kx, T):
        sa, ks = front_a(xt, kx, T)
        return front_b(xt, kx, sa, ks, T)

    def front(src_ap, T):
        xt, kx = front_load(src_ap, T)
        return front_compute(xt, kx, T)

    def group_norm_rstd(src, ksrc, T, ng, sa, ks, base):
        tt("dve", SQ[0:T, 0:ng * 64], src, src, ALU.mult, [ksrc], ["sq"])
        red(sa[0:T, base:base + ng], SQ[0:T, 0:ng * 64].rearrange("p (g d) -> p g d", d=64), ALU.add,
            ["sq"], [ks])
        rstd_from_ss(sa, ks, base, base + ng, base + 2 * ng, ng, T, 1.0 / 64)
        return sa[0:T, base + 2 * ng:base + 3 * ng]

    def phase1_tile(fr, T, ti, j, k_dst, v_dst, later, hook=None):
        xt, kx, hT, kh, sa, ks = fr
        tok = slice(j * 128, j * 128 + T)
        ktok = slice(ti * 128, ti * 128 + T)
        for g in range(8):
            c0 = g * 512
            n = min(512, NIN - c0)
            ps, kp = mmnext()
            for kc in range(8):
                mm(ps[0:T, 0:n], hT[:, kc, 0:T], w_in_bf[:, kc, c0:c0 + n], kc == 0, kc == 7,
                   [kh] + W_IN_K, [kp])
            for it in later:
                it[0] -= 1
            ready = [it for it in later if it[0] <= 0]
            for it in ready:
                later.remove(it)
            for it in ready:
                it[1]()
            if hook is not None:
                hook(g)
            if g == 0 or g == 1:
                zf, kz = ZF.next()
                cpy("act", zf[0:T, :], ps[0:T, 0:512], [kp], [kz])
                r8 = group_norm_rstd(zf[0:T, :], kz, T, 8, sa, ks, 8 if g == 0 else 32)

                def b01(g=g, zf=zf, kz=kz, r8=r8):
                    nb, kn = NB.next()
                    if g == 0:
                        tt("pool", nb[0:T, :].rearrange("p (g d) -> p g d", d=64),
                           zf[0:T, :].rearrange("p (g d) -> p g d", d=64),
                           r8.unsqueeze(2).to_broadcast([T, 8, 64]), ALU.mult, [kz, ks], [kn])
                    else:
                        tt("dve", zf[0:T, :].rearrange("p (g d) -> p g d", d=64),
                           zf[0:T, :].rearrange("p (g d) -> p g d", d=64),
                           r8.unsqueeze(2).to_broadcast([T, 8, 64]), ALU.mult, [kz, ks], [kz])
                        ko, kk = KOUT.next()
                        tt("dve", ko[0:T, :].rearrange("p (g d) -> p g d", d=64),
                           zf[0:T, :].rearrange("p (g d) -> p g d", d=64),
                           cpc("gka_rep")[0:T, :].unsqueeze(1).to_broadcast([T, 8, 64]), ALU.mult,
                           [kz, "cp"], [kk])
                        dma(k_dst, ko[0:T, :], r=[kk])
                        cpy("dve", nb[0:T, :], ko[0:T, :], [kk], [kn])

                    def d01(nb=nb, kn=kn):
                        for h in range(4):
                            tr(TB[:, h * 128:h * 128 + T], nb[0:T, h * 128:(h + 1) * 128], identb[0:T, 0:T],
                               [kn, "identb"], [TBK])
                        src = TB[:, 0:512].rearrange("p (h t) -> p h t", h=4)[:, :, 0:T]
                        if g == 0:
                            ts("dve", qT[:, :, tok], src, cpc("gqa_pp"), None, ALU.mult, None, [TBK, "cp"],
                               [("qT", j)])
                        else:
                            cpy("act", kT[:, :, ktok], src, [TBK], [("kT", ti)])
                    later.append([DELAY, d01])
                later.append([2, b01])
            elif g == 2:
                vo, kv = VOUT.next()
                cpy("act", vo[0:T, :], ps[0:T, 0:512], [kp], [kv])
                dma(v_dst, vo[0:T, :], r=[kv])
                cpy("dve", v_aug[0:T, ti, :, 0:128], vo[0:T, :].rearrange("p (h d) -> p h d", h=4),
                    [kv, "v_ones"], [("v", ti)])
            elif g == 3:
                th, kt = TH.next()
                act(th[0:T, :], ps[0:T, 0:512], AF.Tanh, [kp], [kt], scale=0.5)
                stt(mix[0:T, j, 0:512], th[0:T, :], 1.0, ps[0:T, 0:512], ALU.add, ALU.mult,
                    [kt, kp], [("mix", j, 0)])
            elif g == 4:
                cpy("act", QKB[0:T, j, :], ps[0:T, 0:512], [kp], [("QKB", j)])

                def d4():
                    for i in range(8):
                        tr(TB[0:64, i * 128:i * 128 + T], QKB[0:T, j, i * 64:(i + 1) * 64], identb[0:T, 0:T],
                           [("QKB", j), "identb"], [TBK])
                    cpy("dve", QKT[:, :, tok], TB[0:64, :].rearrange("p (h t) -> p h t", h=8)[:, :, 0:T],
                        [TBK], [("QKT", j)])
                later.append([DELAY, d4])
            elif g == 5:
                cpy("dve", vb_aug[0:T, j, :, 0:64], ps[0:T, 0:256].rearrange("p (h d) -> p h d", h=4),
                    [kp, "vb_ones"], [("vb", j)])
                act(OBG[0:T, j, :], ps[0:T, 256:512], AF.Tanh, [kp], [("OBG", j)], scale=0.5)
            elif g == 6:
                th, kt = TH.next()
                act(th[0:T, 0:256], ps[0:T, 0:256], AF.Tanh, [kp], [kt], scale=0.5)
                stt(th[0:T, 256:512], th[0:T, 0:256], 1.0, ps[0:T, 0:256], ALU.add, ALU.mult, [kt, kp], [kt])
                stt(mix[0:T, j, 512:768], OBG[0:T, j, :], 1.0, th[0:T, 256:512], ALU.add, ALU.mult,
                    [("OBG", j), kt], [("mix", j, 1)])
                zf, kz = ZF.next()
                cpy("act", zf[0:T, 0:256], ps[0:T, 256:512], [kp], [kz])
                r4 = group_norm_rstd(zf[0:T, 0:256], kz, T, 4, sa, ks, 56)

                def b6(zf=zf, kz=kz, r4=r4):
                    nb, kn = NB.next()
                    tt("pool", nb[0:T, 0:256].rearrange("p (g d) -> p g d", d=64),
                       zf[0:T, 0:256].rearrange("p (g d) -> p g d", d=64),
                       r4.unsqueeze(2).to_broadcast([T, 4, 64]), ALU.mult, [kz, ks], [kn])

                    def d6(nb=nb, kn=kn):
                        for hp in range(2):
                            tr(TB[:, hp * 128:hp * 128 + T], nb[0:T, hp * 128:(hp + 1) * 128], identb[0:T, 0:T],
                               [kn, "identb"], [TBK])
                        ts("dve", qmT[:, :, tok], TB[:, 0:256].rearrange("p (h t) -> p h t", h=2)[:, :, 0:T],
                           cpc("gqm_pp"), None, ALU.mult, None, [TBK, "cp"], [("qmT", j)])
                    later.append([DELAY, d6])
                later.append([2, b6])
            else:
                th, kt = TH.next()
                act(th[0:T, 0:256], ps[0:T, 0:256], AF.Tanh, [kp], [kt], scale=0.5)
                stt(mix[0:T, j, 768:1024], th[0:T, 0:256], 1.0, ps[0:T, 0:256], ALU.add, ALU.mult,
                    [kt, kp], [("mix", j, 2)])
                cpy("dve", GIF[0:T, j, :], ps[0:T, 256:264], [kp], [("GIF", j)])

    def phase1_block(tiles, T, pre=None):
        later = []
        if pre is None:
            xt, kx = front_load(tiles[0][0], T)
            fr = front_compute(xt, kx, T)
        else:
            fr = front_b(pre[0], pre[1], pre[2], pre[3], T)
        for idx, (src_ap, ti, j, k_dst, v_dst) in enumerate(tiles):
            nxt = {}
            hook = None
            if idx + 1 < len(tiles):
                nx = front_load(tiles[idx + 1][0], T)

                def hook(g, nx=nx, nxt=nxt):
                    if g == 0:
                        nxt["a"] = front_a1(nx[0], nx[1], T)
                    elif g == 2:
                        front_a2(nx[0], nx[1], nxt["a"][0], nxt["a"][1], T)
                    elif g == 5:
                        nxt["fr"] = front_b(nx[0], nx[1], nxt["a"][0], nxt["a"][1], T)
            phase1_tile(fr, T, ti, j, k_dst, v_dst, later, hook)
            fr = nxt.get("fr")
        while later:
            later.pop(0)[1]()

    def gates_block(T, ntile):
        NQ = ntile * 128
        ibT = TF[0:4, 0:NQ]
        fbT = TBf[0:4, 0:NQ]
        for j in range(ntile):
            tr(TF[0:4, j * 128:j * 128 + T], GIF[0:T, j, 0:4], identf[0:T, 0:T], [("GIF", j), "cp"], [TFK])
            tr(TBf[0:4, j * 128:j * 128 + T], GIF[0:T, j, 4:8], identf[0:T, 0:T], [("GIF", j), "cp"], [TBK])
        if T < 128:
            memset("dve", GA[:, 0:NQ], 0.0, [KGA])
        cs = slice(0, T) if ntile == 1 else slice(0, NQ)
        act(GA[:, cs], fbT[:, cs], AF.Exp, [TBK, "nbf"], [KGA], bias=nbf[0:4, 0:1], scale=-1.0)
        act(GA[:, cs], GA[:, cs], AF.Ln, [KGA], [KGA], bias=1.0)
        scan(GB[:, cs], zerob[0:4, cs], GA[:, cs], Bprev[:, 0:1], ALU.add, ALU.subtract,
             ["zerob", KGA, "Bprev"], [KGB])
        stt(GU[:, cs], ibT[:, cs], cpc("bi_pp")[0:4, :], GB[:, cs], ALU.add, ALU.subtract,
            [TFK, "cp", KGB], [KGU])
        if ntile == 1:
            red(UM[:, 0:1], GU[:, cs], ALU.max, [KGU], ["UM"])
        else:
            red(UM[:, 0:ntile], GU[:, cs].rearrange("p (c t) -> p c t", c=ntile), ALU.max, [KGU], ["UM"])
        scan(MUALL[:, 1:1 + ntile], UM[:, 0:ntile], UM[:, 0:ntile], MUALL[:, 0:1], ALU.max, ALU.max,
             ["UM", "MUALL"], ["MUALL"])
        tt("dve", C0[:, 0:ntile], MUALL[:, 0:ntile], MUALL[:, 1:1 + ntile], ALU.subtract, ["MUALL"], ["C0"])
        act(C0[:, 0:ntile], C0[:, 0:ntile], AF.Exp, ["C0"], ["C0"])
        if ntile == 1:
            mub = MUALL[:, 1:2].to_broadcast([4, T])
            tt("dve", GW[:, cs], GU[:, cs], mub, ALU.subtract, [KGU, "MUALL"], [KGW])
            tt("dve", GE[:, cs], GB[:, cs], mub, ALU.add, [KGB, "MUALL"], [KGE])
        else:
            mub = MUALL[:, 1:1 + ntile].unsqueeze(2).to_broadcast([4, ntile, 128])
            tt("dve", GW[:, cs].rearrange("p (c t) -> p c t", c=ntile),
               GU[:, cs].rearrange("p (c t) -> p c t", c=ntile), mub, ALU.subtract, [KGU, "MUALL"], [KGW])
            tt("dve", GE[:, cs].rearrange("p (c t) -> p c t", c=ntile),
               GB[:, cs].rearrange("p (c t) -> p c t", c=ntile), mub, ALU.add, [KGB, "MUALL"], [KGE])
        act(GW[:, cs], GW[:, cs], AF.Exp, [KGW], [KGW])
        act(GE[:, cs], GE[:, cs], AF.Exp, [KGE], [KGE], scale=-1.0)
        for j in range(ntile):
            tr(TF[0:T, j * 8:j * 8 + 4], GW[0:4, j * 128:j * 128 + T], identf[0:4, 0:4], [KGW, "cp"], [TFK])
            tr(TF[0:T, j * 8 + 4:j * 8 + 8], GE[0:4, j * 128:j * 128 + T], identf[0:4, 0:4], [KGE, "cp"], [TFK])
        cpy("dve", WE[0:T, 0:ntile, :], TF[0:T, 0:ntile * 8].rearrange("p (c e) -> p c e", e=8), [TFK], ["WE"])
        tt("dve", CD[:, 0:ntile, :], C0[:, 0:ntile].unsqueeze(2).to_broadcast([4, ntile, 4]),
           cpc("SEL2")[0:4, :].unsqueeze(1).to_broadcast([4, ntile, 4]), ALU.mult, ["C0", "cp"], ["CD"])
        mm(TBf[0:64, 0:ntile * 4], cpc("SEL")[0:4, 0:64], CD[:, 0:ntile, :].rearrange("p c e -> p (c e)"), True, True,
           ["cp", "CD"], [TBK])
        cpy("dve", C0B[:, 0:ntile, :], TBf[0:64, 0:ntile * 4].rearrange("p (c e) -> p c e", e=4), [TBK], ["C0B"])
        last = T - 1 if ntile == 1 else NQ - 1
        cpy("dve", Bprev[:, 0:1], GB[:, last:last + 1], [KGB], ["Bprev"])
        cpy("dve", MUALL[:, 0:1], MUALL[:, ntile:ntile + 1], ["MUALL"], ["MUALL"])

    def attention_block(T, ntile, ktiles, sample):
        NQ = ntile * T
        ABK = [4, 5, 6]
        qkeys = [("qT", jj) for jj in range(ntile)]
        steps = [(h, t, nk, dsub) for h in range(4) for (t, nk, dsub) in ktiles]
        state = {"n": 0}

        def emit_S(st_):
            h, t, nk, dsub = st_
            c0 = 0 if dsub is None else dsub * T
            ncol = NQ - c0
            b0 = 2 * (state["n"] % 2)
            state["n"] += 1
            kps = [("pb", b0), ("pb", b0 + 1)]
            for m in range(2):
                pr = slice(64 * m, 64 * m + 64)
                mm(PB[b0 + m][0:nk, 0:ncol], kT[pr, h, t * 128:t * 128 + nk], qT[pr, h, c0:NQ],
                   True, dsub is None, [("kT", t)] + qkeys, [kps[m]])
            if dsub is not None:
                for m in range(2):
                    mm(PB[b0 + m][0:nk, 0:T], identb[0:nk, 0:nk], corrb[0:nk, h, 0:T], False, True,
                       ["identb", "corrb"], [kps[m]])
            pt, kpt = PT.next()
            if sample or h != 0:
                calls = [(c0, NQ)]
            else:
                calls = [(c, c + 128) for c in range(c0, NQ, 128)]
            pair = PBIG[0:nk, b0 * 512:(b0 + 2) * 512].rearrange("p (m c) -> p m c", m=2)
            for (ca, cb) in calls:
                if sample:
                    bias = cpc("ABS")[0:nk, h * 17 + t:h * 17 + t + 1]
                else:
                    tref = ktiles[-1][0] - (ntile - 1) + (cb - 1) // 128
                    bias = cpc("AB")[0:nk, h * 16 + (tref - t):h * 16 + (tref - t) + 1]
                act(pt[0:nk, :, ca - c0:cb - c0], pair[:, :, ca - c0:cb - c0], AF.Exp, kps + ["cp"], [kpt],
                    bias=bias, scale=0.125)
            return pt, kpt, c0

        def emit_PV(st_, pt, kpt, c0):
            h, t, nk, dsub = st_
            for m in range(2):
                for i in range(ntile):
                    if i * T < c0:
                        continue
                    a_ = m * 4 + i
                    bnk = ABK[a_ // 3]
                    o = (a_ % 3) * 129
                    mm(PB[bnk][0:T, o:o + 129], pt[0:nk, m, i * T - c0:(i + 1) * T - c0],
                       v_aug[0:nk, t, h, 0:129], False, True, [kpt, ("v", t), "v_ones"], [("pb", bnk)])

        def evac(h):
            bset = ABK
            sa, ks = STT.next()
            if ntile == 4:
                for b_ in range(3):
                    nacc = 3 if b_ < 2 else 2
                    recip(sa[0:T, 3 * b_:3 * b_ + nacc], PB[bset[b_]][0:T, 128:129 * nacc:129],
                          [("pb", bset[b_])], [ks])
            else:
                recip(sa[0:T, 0:1], PB[bset[0]][0:T, 128:129], [("pb", bset[0])], [ks])
                recip(sa[0:T, 4:5], PB[bset[1]][0:T, 129 + 128:129 + 129], [("pb", bset[1])], [ks])
            tt("dve", sa[0:T, 4:4 + ntile], sa[0:T, 4:4 + ntile], lamt[0:T, 5:6].to_broadcast([T, ntile]),
               ALU.mult, [ks, "lamt"], [ks])
            oos = []
            for i in range(ntile):
                a1, a2 = i, 4 + i
                o1, k1 = O1.next()
                b1, b2 = bset[a1 // 3], bset[a2 // 3]
                ts("dve", o1[0:T, :], PB[b1][0:T, (a1 % 3) * 129:(a1 % 3) * 129 + 128], sa[0:T, a1:a1 + 1], None,
                   ALU.mult, None, [("pb", b1), ks], [k1])
                oo, ko = OO.next()
                stt(oo[0:T, :], PB[b2][0:T, (a2 % 3) * 129:(a2 % 3) * 129 + 128], sa[0:T, a2:a2 + 1],
                    o1[0:T, :], ALU.mult, ALU.add, [("pb", b2), ks, k1], [ko])
                oos.append((oo, ko, o1, k1))
            for i in range(ntile):
                oo, ko, o1, k1 = oos[i]
                S.add("dve", (lambda o1=o1, oo=oo, sa=sa, i=i: nc.vector.scalar_tensor_tensor(
                    out=o1[0:T, :], in0=oo[0:T, :], scalar=1.0, in1=oo[0:T, :], op0=ALU.mult, op1=ALU.mult,
                    accum_out=sa[0:T, 8 + i:9 + i])), [ko], [k1, ks])
            ts("dve", sa[0:T, 12:12 + ntile], sa[0:T, 8:8 + ntile], 1.0 / 128, EPS, ALU.mult, ALU.add, [ks], [ks])
            tt("pool", sa[0:T, 16:16 + ntile], sa[0:T, 12:12 + ntile], cm05[0:T, 0:ntile], ALU.pow,
               [ks, "cm05"], [ks])

            def fin(h=h, sa=sa, ks=ks, oos=oos):
                for ii in range(ntile):
                    oo2, ko2 = oos[ii][0], oos[ii][1]
                    stt(mix[0:T, ii, h * 128:(h + 1) * 128], oo2[0:T, :], sa[0:T, 16 + ii:17 + ii],
                        mix[0:T, ii, h * 128:(h + 1) * 128], ALU.mult, ALU.mult,
                        [ko2, ks, ("mix", ii, 0)], [("mix", ii, 0)])
            return fin

        prev = None
        fins = []
        for k, st_ in enumerate(steps):
            h = st_[0]
            new_head = (k == 0 or steps[k - 1][0] != h)
            pt, kpt, c0 = emit_S(st_)
            if prev is not None:
                emit_PV(*prev)
                if new_head:
                    fins.append(evac(prev[0][0]))
            if new_head:
                for b_ in range(3):
                    mm(PB[ABK[b_]][0:T, :], zerob[:, 0:T], zerob[:, :], True, True, ["zerob"], [("pb", ABK[b_])])
            if len(fins) > 0 and not new_head and (k == 0 or steps[k - 2][0] == h):
                for f_ in fins:
                    f_()
                del fins[:]
            prev = (st_, pt, kpt, c0)
        emit_PV(*prev)
        fins.append(evac(prev[0][0]))
        for f_ in fins:
            f_()

    def mem_block(T, ntile):
        NQ = ntile * T
        banks = [(ACC[0], ACCK[0]), (ACC[1], ACCK[1]), (ACC[2], ACCK[2]), (TF, TFK)]
        for i in range(ntile):
            mm(banks[i][0][0:T, :], zerob[:, 0:T], zerob[:, :], True, True, ["zerob"], [banks[i][1]])
        prevm = None
        for h in range(4):
            pr = slice(64 * (h % 2), 64 * (h % 2) + 64)
            for nt in range(2):
                ps, kp = mmnext()
                mm(ps[:, 0:NQ], mkT[pr, h // 2, nt * 128:(nt + 1) * 128], qmT[pr, h // 2, 0:NQ], True, True,
                   ["mkT"] + [("qmT", jj) for jj in range(ntile)], [kp])
                pt, kpt = PT.next()
                act(pt[:, 0, 0:NQ], ps[:, 0:NQ], AF.Exp, [kp], [kpt], scale=0.125)
                if prevm is not None:
                    ph, pnt, ppt, pkpt = prevm
                    for i in range(ntile):
                        mm(banks[i][0][0:T, ph * 65:(ph + 1) * 65], ppt[:, 0, i * T:(i + 1) * T],
                           mv_aug[:, pnt, ph, 0:65], False, True, [pkpt, "mv", "mv_ones"], [banks[i][1]])
                prevm = (h, nt, pt, kpt)
        ph, pnt, ppt, pkpt = prevm
        for i in range(ntile):
            mm(banks[i][0][0:T, ph * 65:(ph + 1) * 65], ppt[:, 0, i * T:(i + 1) * T],
               mv_aug[:, pnt, ph, 0:65], False, True, [pkpt, "mv", "mv_ones"], [banks[i][1]])
        for i in range(ntile):
            bk, kb = banks[i]
            sa, ks = STT.next()
            recip(sa[0:T, 0:4], bk[0:T, 64:260:65], [kb], [ks])
            tt("dve", OM[0:T, :, :], bk[0:T, 0:260].rearrange("p (h e) -> p h e", e=65)[:, :, 0:64],
               sa[0:T, 0:4].unsqueeze(2).to_broadcast([T, 4, 64]), ALU.mult, [kb, ks], ["HN"])
            tt("dve", mix[0:T, i, 768:1024], OM[0:T, :, :].rearrange("p h d -> p (h d)"),
               mix[0:T, i, 768:1024], ALU.mult, ["HN", ("mix", i, 2)], [("mix", i, 2)])

    def mlstm_block(T, ntile):
        pend_a = []
        pend_b = []
        for j in range(ntile):
            tt("pool", VW[j][0:T, :, 0:65], vb_aug[0:T, j, :, 0:65],
               WE[0:T, j, 0:4].unsqueeze(2).to_broadcast([T, 4, 65]), ALU.mult,
               [("vb", j), "vb_ones", "WE"], [("vw", j)])
        for j in range(ntile):
            tok = slice(j * 128, j * 128 + T)
            vw = VW[j]
            kvw = ("vw", j)
            tt("dve", Sd[:, :, 0:65], Sst[:, :, 0:65], C0B[:, j, :].unsqueeze(2).to_broadcast([64, 4, 65]), ALU.mult,
               ["Sst", "C0B"], ["Sd"])
            cpy("act", Cs[:, :, 0:65], Sd[:, :, 0:65], ["Sd"], ["Cs"])
            psA, kA = mmnext()
            for h in range(4):
                mm(psA[0:T, h * 128:h * 128 + T], QKT[:, 4 + h, tok], QKT[:, h, tok], True, True,
                   [("QKT", j)], [kA])
            psS, kS = ACC[2], ACCK[2]
            for h in range(4):
                mm(psS[0:64, h * 65:(h + 1) * 65], QKB[0:T, j, 256 + h * 64:256 + (h + 1) * 64],
                   vw[0:T, h, 0:65], True, True, [("QKB", j), kvw], [kS])
            tt("dve", Sst[:, :, 0:65], Sd[:, :, 0:65], psS[0:64, 0:260].rearrange("p (c e) -> p c e", e=65), ALU.add,
               ["Sd", kS], ["Sst"])
            tt("dve", AT[0:T, :, 0:T], psA[0:T, :].rearrange("p (h t) -> p h t", h=4)[:, :, 0:T],
               cpc("maskU")[0:T, 0:T].unsqueeze(1).to_broadcast([T, 4, T]), ALU.mult, [kA, "cp"], ["AT"])
            psO, kO = ACC[j % 2], ACCK[j % 2]
            for h in range(4):
                mm(psO[0:T, h * 65:(h + 1) * 65], AT[0:T, h, 0:T], vw[0:T, h, 0:65], True, False, ["AT", kvw], [kO])
                mm(psO[0:T, h * 65:(h + 1) * 65], QKT[:, h, tok], Cs[:, h, 0:65], False, True,
                   [("QKT", j), "Cs"], [kO])

            def evac_a(j=j, psO=psO, kO=kO):
                sa, ks = STT.next()
                cpy("dve", sa[0:T, 20:24], psO[0:T, 64:260:65], [kO], [ks])
                stt(sa[0:T, 24:28], sa[0:T, 20:24], -1.0, sa[0:T, 20:24], ALU.mult, ALU.max, [ks], [ks])
                tt("dve", sa[0:T, 0:4], sa[0:T, 24:28], WE[0:T, j, 4:8], ALU.max, [ks, "WE"], [ks])
                recip(sa[0:T, 4:8], sa[0:T, 0:4], [ks], [ks])
                tt("dve", HN[0:T, :, :], psO[0:T, 0:260].rearrange("p (h e) -> p h e", e=65)[:, :, 0:64],
                   sa[0:T, 4:8].unsqueeze(2).to_broadcast([T, 4, 64]), ALU.mult, [kO, ks], ["HN"])
                r4 = group_norm_rstd(HN[0:T, :, :].rearrange("p h d -> p (h d)"), "HN", T, 4, sa, ks, 8)

                def evac_b():
                    tt("dve", HN[0:T, :, :], HN[0:T, :, :], r4.unsqueeze(2).to_broadcast([T, 4, 64]), ALU.mult,
                       ["HN", ks], ["HN"])
                    tt("dve", mix[0:T, j, 512:768], HN[0:T, :, :].rearrange("p h d -> p (h d)"),
                       mix[0:T, j, 512:768], ALU.mult, ["HN", ("mix", j, 1)], [("mix", j, 1)])
                return evac_b
            pend_a.append(evac_a)
            if len(pend_a) == 2:
                for f_ in pend_b:
                    f_()
                del pend_b[:]
                pend_b.append(pend_a.pop(0)())
        for f_ in pend_b:
            f_()
        del pend_b[:]
        while pend_a:
            pend_a.pop(0)()()

    def phase3_block(tiles, T):
        YB = [[(PB[3], ("pb", 3)), (PB[4], ("pb", 4))], [(PB[5], ("pb", 5)), (PB[0], ("pb", 0))]]
        xts = {}

        def stage_t(idx):
            src_ap, dst_ap, j = tiles[idx]
            mT, kmT = HT.next()
            for kc in range(8):
                tr(TB[:, kc * 128:kc * 128 + T], mix[0:T, j, kc * 128:(kc + 1) * 128], identb[0:T, 0:T],
                   [("mix", j, 0), ("mix", j, 1), ("mix", j, 2), "identb"], [TBK])
            cpy("act", mT[:, :, 0:T], TB.rearrange("p (k t) -> p k t", k=8)[:, :, 0:T], [TBK], [kmT])
            xt, kx = XT.next()
            dma(xt[0:T, :], src_ap, w=[kx])
            xts[idx] = (mT, kmT, xt, kx)

        def stage_m(idx):
            src_ap, dst_ap, j = tiles[idx]
            mT, kmT, xt, kx = xts[idx]
            for half in range(2):
                ps, kp = YB[idx % 2][half]
                for kc in range(8):
                    mm(ps[0:T, :], mT[:, kc, 0:T], w_out_bf[:, kc, half * 512:(half + 1) * 512], kc == 0, kc == 7,
                       [kmT] + W_OUT_K, [kp])
                tt("dve", xt[0:T, half * 512:(half + 1) * 512], ps[0:T, :], xt[0:T, half * 512:(half + 1) * 512],
                   ALU.add, [kp, kx], [kx])
            dma(dst_ap, xt[0:T, :], r=[kx])

        stage_t(0)
        for idx in range(len(tiles)):
            if idx + 1 < len(tiles):
                stage_t(idx + 1)
            stage_m(idx)

    def memkv_seq(s):
        for nt in range(2):
            xt, kx, hT, kh, sa, ks = front(memp[s, nt * 128:(nt + 1) * 128, :], 128)
            ps, kp = mmnext()
            for kc in range(8):
                mm(ps[:, :], hT[:, kc, :], w_mkv_bf[:, kc, :], kc == 0, kc == 7, [kh] + W_MKV_K, [kp])
            zf, kz = ZF.next()
            cpy("act", zf[:, :], ps[:, :], [kp], [kz])
            dma(pmv[s, nt * 128:(nt + 1) * 128, :], zf[:, 256:512], r=[kz])
            cpy("pool", mv_aug[:, nt, :, 0:64], zf[:, 256:512].rearrange("p (h d) -> p h d", h=4),
                [kz, "mv_ones"], ["mv"])
            r4 = group_norm_rstd(zf[:, 0:256], kz, 128, 4, sa, ks, 8)
            tt("dve", zf[:, 0:256].rearrange("p (g d) -> p g d", d=64),
               zf[:, 0:256].rearrange("p (g d) -> p g d", d=64),
               r4.unsqueeze(2).to_broadcast([128, 4, 64]), ALU.mult, [kz, ks], [kz])
            ko, kk = KOUT.next()
            tt("dve", ko[:, 0:256].rearrange("p (g d) -> p g d", d=64),
               zf[:, 0:256].rearrange("p (g d) -> p g d", d=64),
               cpc("gkm_rep").unsqueeze(1).to_broadcast([128, 4, 64]), ALU.mult, [kz, "cp"], [kk])
            dma(pmk[s, nt * 128:(nt + 1) * 128, :], ko[:, 0:256], r=[kk])
            nb, kn = NB.next()
            cpy("pool", nb[:, 0:256], ko[:, 0:256], [kk], [kn])
            for hp in range(2):
                tr(TB[:, hp * 128:(hp + 1) * 128], nb[:, hp * 128:(hp + 1) * 128], identb[:, :],
                   [kn, "identb"], [TBK])
            cpy("act", mkT[:, :, nt * 128:(nt + 1) * 128], TB[:, 0:256].rearrange("p (h t) -> p h t", h=2),
                [TBK], ["mkT"])

    def state_out(dC, dn, dm):
        for h in range(4):
            dma(dC[h], Sst[:, h, 0:64], r=["Sst"])
            dma(dn[h].unsqueeze(1), Sst[:, h, 64:65], r=["Sst"])
        tt("dve", mfin[:, :], Bprev[:, 0:1], MUALL[:, 0:1], ALU.add, ["Bprev", "MUALL"], ["mfin"])
        dma(dm, mfin[:, :], r=["mfin"])

    try:
        chk(1)
        if DO_SAMPLE:
            for t in range(16):
                i1 = WST.i % len(WST.aps)
                xk, _ = WST.next()
                i2 = WST.i % len(WST.aps)
                xv, _ = WST.next()
                kx, kx2 = WSTK[i1], WSTK[i2]
                dma(xk[:, 0:512], ck[t * 128:(t + 1) * 128, :], w=[kx])
                dma(xv[:, 0:512], cv[t * 128:(t + 1) * 128, :], w=[kx2])
                nb, kn = NB.next()
                cpy("dve", nb[:, :], xk[:, 0:512], [kx], [kn])
                cpy("dve" if t % 2 == 0 else "pool", v_aug[:, t, :, 0:128],
                    xv[:, 0:512].rearrange("p (h d) -> p h d", h=4), [kx2, "v_ones"], [("v", t)])
                for h in range(4):
                    tr(TB[:, h * 128:(h + 1) * 128], nb[:, h * 128:(h + 1) * 128], identb[:, :], [kn, "identb"], [TBK])
                cpy("act", kT[:, :, t * 128:(t + 1) * 128], TB[:, 0:512].rearrange("p (h t) -> p h t", h=4),
                    [TBK], [("kT", t)])
            for nt in range(2):
                xt, kx = XT.next()
                dma(xt[:, 0:256], cmk[nt * 128:(nt + 1) * 128, :], w=[kx])
                dma(xt[:, 256:512], cmv[nt * 128:(nt + 1) * 128, :], w=[kx])
                nb, kn = NB.next()
                cpy("dve", nb[:, 0:256], xt[:, 0:256], [kx], [kn])
                cpy("pool", mv_aug[:, nt, :, 0:64], xt[:, 256:512].rearrange("p (h d) -> p h d", h=4),
                    [kx, "mv_ones"], ["mv"])
                for hp in range(2):
                    tr(TB[:, hp * 128:(hp + 1) * 128], nb[:, hp * 128:(hp + 1) * 128], identb[:, :],
                       [kn, "identb"], [TBK])
                cpy("act", mkT[:, :, nt * 128:(nt + 1) * 128], TB[:, 0:256].rearrange("p (h t) -> p h t", h=2),
                    [TBK], ["mkT"])
            chk(2)
            for h in range(4):
                dma(Sst[:, h, 0:64], sC[h], w=["Sst"])
                dma(Sst[:, h, 64:65], sn[h].unsqueeze(1), w=["Sst"])
            memset("dve", Bprev[:, :], 0.0, ["Bprev"])
            cpy("dve", MUALL[:, 0:1], cpc("sm_pp")[0:4, :], ["cp"], ["MUALL"])
            chk(3)
            phase1_block([(xs[:, :], 16, 0, sk[:, :], sv[:, :])], 16)
            chk(4)
            gates_block(16, 1)
            chk(5)
            attention_block(16, 1, [(t, 128, None) for t in range(16)] + [(16, 16, 0)], True)
            chk(6)
            mem_block(16, 1)
            chk(7)
            mlstm_block(16, 1)
            chk(8)
            phase3_block([(xs[:, :], ys[:, :], 0)], 16)
            chk(9)
            state_out(sCo, sno, smo.rearrange("o h -> h o"))

        for s in range(NSEQ):
            memkv_seq(s)
            memset("dve", Sst[:, :, :], 0.0, ["Sst"])
            memset("dve", Bprev[:, :], 0.0, ["Bprev"])
            memset("dve", MUALL[:, 0:1], 0.0, ["MUALL"])
            pre = None
            for b in range(NBLK):
                tl = []
                for j in range(4):
                    ti = 4 * b + j
                    rows = slice(ti * 128, (ti + 1) * 128)
                    tl.append((xp[s, rows, :], ti, j, pk[s, rows, :], pv[s, rows, :]))
                phase1_block(tl, 128, pre)
                pre = None
                gates_block(128, 4)
                ktiles = [(t, 128, None) for t in range(4 * b)] + [(4 * b + i, 128, i) for i in range(4)]
                attention_block(128, 4, ktiles, False)
                mem_block(128, 4)
                mlstm_block(128, 4)
                if b + 1 < NBLK:
                    nrows = slice((4 * b + 4) * 128, (4 * b + 5) * 128)
                    dma(EV[:, :], xp[s, nrows, :], w=["ev"])
                    sa_, ks_ = front_a(EV[:, :], "ev", 128)
                    pre = (EV[:, :], "ev", sa_, ks_)
                tl3 = []
                for j in range(4):
                    ti = 4 * b + j
                    rows = slice(ti * 128, (ti + 1) * 128)
                    tl3.append((xp[s, rows, :], yp[s, rows, :], j))
                phase3_block(tl3, 128)
            state_out(pC[s], pn[s], pm[s].unsqueeze(1))

    except _Stop:
        pass

    print('sbuf bytes remaining', nc.sbuf_bytes_remaining)
    S.emit(st)
    st.close()
    return nc


_NC_CACHE = {}


def _get_nc(key=(4, 4, True)):
    if key not in _NC_CACHE:
        _NC_CACHE[key] = build_program(*key)
    return _NC_CACHE[key]


def _in_maps(inp, NSEQ=4):
    maps = []
    c32 = lambda a: np.ascontiguousarray(a, dtype=np.float32)
    for c in range(8):
        m = {
            "xp": c32(inp["x_prompt"][4 * c:4 * c + max(NSEQ, 1)]),
            "memp": c32(inp["mem_prompt"][4 * c:4 * c + max(NSEQ, 1)]),
            "xs": c32(inp["x_sample"][c]),
            "ck": c32(inp["cache_attn_k"][0, c].reshape(2048, 512)),
            "cv": c32(inp["cache_attn_v"][0, c].reshape(2048, 512)),
            "sC": c32(inp["state_mlstm_C"][0, c]),
            "sn": c32(inp["state_mlstm_n"][0, c]),
            "cmk": c32(inp["cache_mem_k"][0, c].reshape(256, 256)),
            "cmv": c32(inp["cache_mem_v"][0, c].reshape(256, 256)),
            "w_in": c32(inp["w_in"][0]),
            "w_out": c32(inp["w_out"][0]),
            "w_mk": c32(inp["w_mk"][0]),
            "w_mv": c32(inp["w_mv"][0]),
        }
        cpa, extra = _make_cp(inp, c)
        m["cp"] = cpa
        m.update(extra)
        maps.append(m)
    return maps


def kernel(**inputs):
    inp = {k: np.asarray(v) for k, v in inputs.items()}
    nc = _get_nc()
    res = run_bass_kernel_spmd(nc, _in_maps(inp), core_ids=list(range(8)))
    R = res.results
    cat = lambda name: np.concatenate([np.asarray(r[name]) for r in R], axis=0)
    stk = lambda name: np.stack([np.asarray(r[name]) for r in R], axis=0)
    y_prompt = cat("yp")
    y_sample = stk("ys")
    p_attn_k = cat("pk").reshape(1, 32, 2048, 4, 128)
    p_attn_v = cat("pv").reshape(1, 32, 2048, 4, 128)
    p_C = cat("pC").reshape(1, 32, 4, 64, 64)
    p_n = cat("pn").reshape(1, 32, 4, 64)
    p_m = cat("pm").reshape(1, 32, 4)
    p_mk = cat("pmk").reshape(1, 32, 256, 4, 64)
    p_mv = cat("pmv").reshape(1, 32, 256, 4, 64)
    s_k = stk("sk").reshape(1, 8, 16, 4, 128)
    s_v = stk("sv").reshape(1, 8, 16, 4, 128)
    s_C = stk("sCo").reshape(1, 8, 4, 64, 64)
    s_n = stk("sno").reshape(1, 8, 4, 64)
    s_m = stk("smo").reshape(1, 8, 4)
    return (y_prompt, y_sample, p_attn_k, p_attn_v, p_C, p_n, p_m, p_mk, p_mv,
            s_k, s_v, s_C, s_n, s_m)
```

```python
import math
from contextlib import ExitStack

import numpy as np
import concourse.bass as bass
import concourse.mybir as mybir
from concourse.bass_utils import run_bass_kernel_spmd

F32 = mybir.dt.float32
BF16 = mybir.dt.bfloat16
AF = mybir.ActivationFunctionType
ALU = mybir.AluOpType
AX = mybir.AxisListType

N_DMA_SEMS = 40
DELAY = 4
KEEPWARM = False
EPS = 1e-6
SLOPES = [2.0 ** (-8.0 * (h + 1) / 4) for h in range(4)]
LAM_INIT = 0.8 - 0.6 * math.exp(0.0)
NIN = 3848
BIG = 240000.0


class _Op:
    __slots__ = ("eng", "fn", "deps", "has_dep", "is_dma", "semkey", "val", "ie")


class Sched:
    def __init__(self, nc):
        self.nc = nc
        self.ops = []
        self.lw = {}
        self.rd = {}
        self.eng_n = {"pe": 0, "act": 0, "dve": 0, "pool": 0, "sp": 0}

    def add(self, eng, fn, reads=(), writes=(), dma=False):
        i = len(self.ops)
        deps = set()
        for k in reads:
            w = self.lw.get(k)
            if w is not None:
                deps.add(w)
            if isinstance(k, tuple) and k[0] == "pb":
                for r in self.rd.get(k, ()):
                    if self.ops[r].eng != eng:
                        deps.add(r)
        for k in writes:
            w = self.lw.get(k)
            if w is not None:
                deps.add(w)
            for r in self.rd.get(k, ()):
                deps.add(r)
        op = _Op()
        op.eng = eng
        op.fn = fn
        op.is_dma = dma
        op.has_dep = False
        op.semkey = None
        op.val = 0
        op.ie = self.eng_n[eng]
        self.eng_n[eng] += 1
        keep = []
        for d in deps:
            p = self.ops[d]
            if p.eng == eng and not p.is_dma:
                if eng == "pe" or eng == "sp":
                    continue
                if eng != "pool" and op.ie - p.ie > 2:
                    continue
            p.has_dep = True
            keep.append(d)
        op.deps = keep
        for k in reads:
            self.rd.setdefault(k, []).append(i)
        for k in writes:
            self.lw[k] = i
            self.rd[k] = []
        self.ops.append(op)
        return i

    def emit(self, stack):
        nc = self.nc
        engobj = {"pe": nc.tensor, "act": nc.scalar, "dve": nc.vector,
                  "pool": nc.gpsimd, "sp": nc.sync}
        esem = {e: stack.enter_context(nc.semaphore("s_" + e))
                for e in ("pe", "act", "dve", "pool")}
        dsem = [stack.enter_context(nc.semaphore("d%d" % i)) for i in range(N_DMA_SEMS)]
        waited = {e: {} for e in engobj}
        cnt = {e: 0 for e in engobj}
        dcnt = [0] * N_DMA_SEMS
        rr = 0
        rrp = 0
        for op in self.ops:
            E = engobj[op.eng]
            need = {}
            for d in op.deps:
                p = self.ops[d]
                if need.get(p.semkey, 0) < p.val:
                    need[p.semkey] = p.val
            s = None
            if op.is_dma:
                if op.eng == "pool":
                    s = N_DMA_SEMS - 8 + (rrp % 8)
                    rrp += 1
                else:
                    s = rr % (N_DMA_SEMS - 8)
                    rr += 1
                if dcnt[s] > 0 and need.get(("d", s), 0) < dcnt[s]:
                    need[("d", s)] = dcnt[s]
                dcnt[s] += 16
                op.semkey = ("d", s)
                op.val = dcnt[s]
            w = waited[op.eng]
            for key, val in need.items():
                if w.get(key, 0) >= val:
                    continue
                so = dsem[key[1]] if key[0] == "d" else esem[key[1]]
                E.wait_ge(so, val)
                w[key] = val
            inst = op.fn()
            if op.is_dma:
                inst.then_inc(dsem[s], 16)
            elif op.has_dep:
                cnt[op.eng] += 1
                op.semkey = ("e", op.eng)
                op.val = cnt[op.eng]
                inst.then_inc(esem[op.eng], 1)
        for s in range(N_DMA_SEMS):
            if dcnt[s] > 0:
                nc.sync.wait_ge(dsem[s], dcnt[s])
        return cnt


class Rot:
    def __init__(self, aps, name):
        self.aps = aps
        self.name = name
        self.i = 0

    def next(self):
        k = self.i % len(self.aps)
        self.i += 1
        return self.aps[k], (self.name, k)


def _cp_layout():
    off = {}
    c = 0
    for name, n in [("ident", 128), ("maskU", 128), ("AB", 64), ("ABS", 68),
                    ("gqa_pp", 1), ("gqm_pp", 1), ("gkm_rep", 64), ("gka_rep", 64),
                    ("gnorm_pp", 8), ("gmem_pp", 8), ("rows_pp", 8),
                    ("bi_pp", 1), ("bf_pp", 1), ("sm_pp", 1), ("SEL", 128), ("SEL2", 4)]:
        off[name] = (c, c + n)
        c += n
    return off, c


CP_OFF, NCP = _cp_layout()


def _make_cp(inp, core):
    cp = np.zeros((128, NCP), np.float32)

    def put(name, arr):
        a, b = CP_OFF[name]
        cp[:, a:b] = arr

    p = np.arange(128)
    put("ident", np.eye(128, dtype=np.float32))
    put("maskU", (p[:, None] <= p[None, :]).astype(np.float32))
    corr = np.zeros((128, 4, 128), np.float32)
    k = p[:, None]
    q = p[None, :]
    for h in range(4):
        c_ = np.where(k > q, -16.0 * SLOPES[h] * (k - q), 0.0)
        c_ = np.where((k // 64) > (q // 64), -BIG, c_)
        corr[:, h, :] = c_
    extra = {"corr": corr.reshape(128, 512)}
    AB = np.zeros((128, 4, 16), np.float32)
    for h in range(4):
        for r in range(16):
            AB[:, h, r] = SLOPES[h] * (p - 127 - 128 * r)
    put("AB", AB.reshape(128, 64))
    ABS = np.zeros((128, 4, 17), np.float32)
    for h in range(4):
        for t in range(17):
            ABS[:, h, t] = SLOPES[h] * np.minimum(128 * t + p - 2063, 0)
    put("ABS", ABS.reshape(128, 68))
    put("gqa_pp", inp["g_qa"][0][p % 64][:, None])
    put("gqm_pp", inp["g_qm"][0][p % 64][:, None])
    put("gkm_rep", np.broadcast_to(inp["g_km"][0][None, :], (128, 64)))
    put("gka_rep", np.broadcast_to(inp["g_ka"][0][None, :], (128, 64)))
    put("gnorm_pp", inp["g_norm"][0].reshape(8, 128).T)
    put("gmem_pp", inp["g_mem"][0].reshape(8, 128).T)
    rows = np.ones((128, 8), np.float32)
    rows[:, 0:4] = inp["g_subln"][0][:, None]
    rows[:, 4:6] = inp["g_mh"][0][p % 64][:, None]
    put("rows_pp", rows)
    lam4 = np.concatenate([inp["lam_q1"][0], inp["lam_k1"][0], inp["lam_q2"][0], inp["lam_k2"][0]])
    extra["lam4"] = np.ascontiguousarray(np.broadcast_to(lam4[None, :], (128, 256)))
    put("bi_pp", inp["b_i"][0][p % 4][:, None])
    put("bf_pp", inp["b_f"][0][p % 4][:, None])
    put("sm_pp", inp["state_mlstm_m"][0, core][p % 4][:, None])
    SEL = np.zeros((128, 128), np.float32)
    SEL[0:4, :] = 1.0
    put("SEL", SEL)
    SEL2 = np.zeros((128, 4), np.float32)
    SEL2[0:4, 0:4] = np.eye(4, dtype=np.float32)
    put("SEL2", SEL2)
    return cp, extra


class _Stop(Exception):
    pass


def build_program(NSEQ=4, NBLK=4, DO_SAMPLE=True, STAGE=99):
    nc = bass.Bass("TRN2", target_bir_lowering=False)
    S = Sched(nc)

    def din(name, shape):
        return nc.dram_tensor(name, shape, F32, kind="ExternalInput").ap()

    def dout(name, shape):
        return nc.dram_tensor(name, shape, F32, kind="ExternalOutput").ap()

    NS = max(NSEQ, 1)
    xp = din("xp", [NS, 2048, 1024])
    memp = din("memp", [NS, 256, 1024])
    xs = din("xs", [16, 1024])
    ck = din("ck", [2048, 512])
    cv = din("cv", [2048, 512])
    sC = din("sC", [4, 64, 64])
    sn = din("sn", [4, 64])
    cmk = din("cmk", [256, 256])
    cmv = din("cmv", [256, 256])
    w_in = din("w_in", [1024, NIN])
    w_out = din("w_out", [1024, 1024])
    w_mk = din("w_mk", [1024, 256])
    w_mv = din("w_mv", [1024, 256])
    cpd = din("cp", [128, NCP])
    corrd = din("corr", [128, 512])
    lam4d = din("lam4", [128, 256])

    yp = dout("yp", [NS, 2048, 1024])
    pk = dout("pk", [NS, 2048, 512])
    pv = dout("pv", [NS, 2048, 512])
    pC = dout("pC", [NS, 4, 64, 64])
    pn = dout("pn", [NS, 4, 64])
    pm = dout("pm", [NS, 4])
    pmk = dout("pmk", [NS, 256, 256])
    pmv = dout("pmv", [NS, 256, 256])
    ys = dout("ys", [16, 1024])
    sk = dout("sk", [16, 512])
    sv = dout("sv", [16, 512])
    sCo = dout("sCo", [4, 64, 64])
    sno = dout("sno", [4, 64])
    smo = dout("smo", [1, 4])

    st = ExitStack()

    def chk(n):
        if STAGE == n:
            raise _Stop()

    def sb(name, shape, dt=F32):
        return st.enter_context(nc.sbuf_tensor("sb_" + name, shape, dt))

    cp = sb("cp", [128, NCP])
    w_in_bf = sb("w_in_bf", [128, 8, NIN], BF16)
    w_out_bf = sb("w_out_bf", [128, 8, 1024], BF16)
    w_mkv_bf = sb("w_mkv_bf", [128, 8, 512], BF16)
    identb = sb("identb", [128, 128], BF16)
    corrb = sb("corrb", [128, 4, 128], BF16)
    zerob = sb("zerob", [128, 512], BF16)
    cm05 = sb("cm05", [128, 8])
    lamt = sb("lamt", [128, 8])
    rs8 = sb("rs8", [128, 8])
    nbf = sb("nbf", [128, 1])

    kT = sb("kT", [128, 4, 2064], BF16)
    v_aug = sb("v_aug", [128, 17, 4, 130], BF16)
    mkT = sb("mkT", [128, 2, 256], BF16)
    mv_aug = sb("mv_aug", [128, 2, 4, 66], BF16)

    qT = sb("qT", [128, 4, 512], BF16)
    mix = sb("mix", [128, 4, 1024], BF16)
    OBG = sb("OBG", [128, 4, 256], BF16)
    QKB = sb("QKB", [128, 4, 512], BF16)
    QKT = sb("QKT", [64, 8, 512], BF16)
    vb_aug = sb("vb_aug", [128, 4, 4, 66], BF16)
    qmT = sb("qmT", [128, 2, 512], BF16)
    GIF = sb("GIF", [128, 4, 8])
    WE = sb("WE", [128, 4, 8])
    C0B = sb("C0B", [64, 4, 4])
    Sst = sb("Sst", [64, 4, 66])
    Sd = sb("Sd", [64, 4, 66])
    Cs = sb("Cs", [64, 4, 66], BF16)
    Bprev = sb("Bprev", [4, 1])
    MUALL = sb("MUALL", [4, 8])
    UM = sb("UM", [4, 4])
    C0 = sb("C0", [4, 4])
    CD = sb("CD", [4, 4, 4])
    mfin = sb("mfin", [4, 1])

    XT = Rot([sb("xt%d" % i, [128, 1024])[:] for i in range(2)], "xt")
    xn = sb("xn", [128, 1024], BF16)
    HT = Rot([sb("hT%d" % i, [128, 8, 128], BF16)[:] for i in range(2)], "hT")
    ZF = Rot([sb("zf%d" % i, [128, 512])[:] for i in range(2)], "zf")
    SQ = sb("sq", [128, 512])
    KOUT = Rot([sb("kvout%d" % i, [128, 512])[:] for i in range(3)], "kvout")
    VOUT = KOUT
    TH = Rot([sb("th%d" % i, [128, 512])[:] for i in range(2)], "th")
    NB = Rot([sb("nb%d" % i, [128, 512], BF16)[:] for i in range(4)], "nb")
    PT = Rot([sb("pt%d" % i, [128, 2, 512], BF16)[:] for i in range(3)], "pt")
    STT = Rot([sb("stat%d" % i, [128, 72])[:] for i in range(4)], "stat")
    EV = sb("ev", [128, 1024])

    class _RotK(Rot):
        def next(self):
            k = self.i % len(self.aps)
            self.i += 1
            return self.aps[k], "ev"
    O1 = _RotK([EV[:, i * 128:(i + 1) * 128] for i in range(4)], "o1s")
    OO = _RotK([EV[:, 512 + i * 128:512 + (i + 1) * 128] for i in range(4)], "oo")
    AT = sb("AT", [128, 4, 128], BF16)
    VW = [sb("vw%d" % i, [128, 4, 66], BF16) for i in range(4)]
    HN = sb("HN", [128, 4, 64])
    OM = HN
    GA, KGA = SQ[0:4, :], "sq"
    GB, KGB = TH.aps[0][0:4, :], ("th", 0)
    GU, KGU = TH.aps[1][0:4, :], ("th", 1)
    GW, KGW = ZF.aps[0][0:4, :], ("zf", 0)
    GE, KGE = ZF.aps[1][0:4, :], ("zf", 1)

    PBIG = st.enter_context(nc.psum_tensor("pbig", [128, 4096], F32))
    PB = [PBIG[:, i * 512:(i + 1) * 512] for i in range(8)]
    MM = Rot([PB[0], PB[1], PB[2]], "pb_mm")
    MM.keys = [("pb", 0), ("pb", 1), ("pb", 2)]
    ACC = [PB[3], PB[4], PB[5]]
    ACCK = [("pb", 3), ("pb", 4), ("pb", 5)]
    TBf = PB[6]
    TB = PB[6].bitcast(BF16)
    TBK = ("pb", 6)
    TF = PB[7]
    TFK = ("pb", 7)

    def mmnext():
        k = MM.i % 3
        MM.i += 1
        return PB[k], ("pb", k)

    def cpc(name, a=None, b=None):
        o, e = CP_OFF[name]
        if a is None:
            return cp[:, o:e]
        return cp[:, o + a:o + b]

    def dma(out, in_, r=(), w=(), q="sp", **kw):
        if q == "sp":
            S.add("sp", lambda: nc.sync.dma_start(out=out, in_=in_, **kw), r, w, dma=True)
        else:
            S.add("pool", lambda: nc.gpsimd.dma_start(out=out, in_=in_, **kw), r, w, dma=True)

    def mm(out, lhsT, rhs, start, stop, r, w):
        S.add("pe", lambda: nc.tensor.matmul(out, lhsT=lhsT, rhs=rhs, start=start, stop=stop,
                                             skip_group_check=True), r, w)

    def tr(out, in_, ident, r, w):
        S.add("pe", lambda: nc.tensor.transpose(out=out, in_=in_, identity=ident), r, w)

    def act(out, in_, func, r, w, bias=0.0, scale=1.0, accum=None):
        if accum is None:
            S.add("act", lambda: nc.scalar.activation(out=out, in_=in_, func=func, bias=bias, scale=scale), r, w)
        else:
            S.add("act", lambda: nc.scalar.activation(out=out, in_=in_, func=func, bias=bias, scale=scale,
                                                      accum_out=accum), r, w)

    def engo(e):
        return nc.vector if e == "dve" else nc.gpsimd

    def ts(e, out, in0, s1, s2, op0, op1, r, w):
        if s2 is None:
            S.add(e, lambda: engo(e).tensor_scalar(out=out, in0=in0, scalar1=s1, scalar2=None, op0=op0), r, w)
        else:
            S.add(e, lambda: engo(e).tensor_scalar(out=out, in0=in0, scalar1=s1, scalar2=s2, op0=op0, op1=op1), r, w)

    def tt(e, out, in0, in1, op, r, w):
        S.add(e, lambda: engo(e).tensor_tensor(out=out, in0=in0, in1=in1, op=op), r, w)

    def stt(out, in0, scalar, in1, op0, op1, r, w):
        S.add("dve", lambda: nc.vector.scalar_tensor_tensor(out=out, in0=in0, scalar=scalar, in1=in1,
                                                            op0=op0, op1=op1), r, w)

    def cpy(e, out, in_, r, w):
        if e == "act":
            S.add("act", lambda: nc.scalar.copy(out=out, in_=in_), r, w)
        else:
            S.add(e, lambda: engo(e).tensor_copy(out=out, in_=in_), r, w)

    def memset(e, ap, val, w):
        S.add(e, lambda: engo(e).memset(ap, val), (), w)

    def red(out, in_, op, r, w):
        S.add("dve", lambda: nc.vector.tensor_reduce(out=out, in_=in_, axis=AX.X, op=op), r, w)

    def recip(out, in_, r, w):
        S.add("dve", lambda: nc.vector.reciprocal(out=out, in_=in_), r, w)

    def scan(out, d0, d1, init, op0, op1, r, w):
        S.add("dve", lambda: nc.vector.tensor_tensor_scan(out=out, data0=d0, data1=d1, initial=init,
                                                          op0=op0, op1=op1), r, w)

    def rstd_from_ss(stt_ap, kst, c_ss, c_tmp, c_out, n, T, inv):
        ts("dve", stt_ap[0:T, c_tmp:c_tmp + n], stt_ap[0:T, c_ss:c_ss + n], inv, EPS, ALU.mult, ALU.add,
           [kst], [kst])
        tt("pool", stt_ap[0:T, c_out:c_out + n], stt_ap[0:T, c_tmp:c_tmp + n], cm05[0:T, 0:n], ALU.pow,
           [kst, "cm05"], [kst])

    dma(cp[:], cpd, w=["cp"])
    cpy("dve", identb[:], cpc("ident"), ["cp"], ["identb"])
    identf = cpc("ident")
    dma(TH.aps[0][:, :], corrd, w=[("th", 0)])
    cpy("dve", corrb[:].rearrange("p h q -> p (h q)"), TH.aps[0][:, :], [("th", 0)], ["corrb"])
    dma(TH.aps[1][:, 0:256], lam4d, w=[("th", 1)])
    memset("pool", zerob[:], 0.0, ["zerob"])
    memset("pool", cm05[:], -0.5, ["cm05"])
    memset("pool", v_aug[:, :, :, 128:129], 1.0, ["v_ones"])
    memset("pool", vb_aug[:, :, :, 64:65], 1.0, ["vb_ones"])
    memset("pool", mv_aug[:, :, :, 64:65], 1.0, ["mv_ones"])
    ts("dve", rs8[:, 0:4], cpc("rows_pp", 0, 4), 0.5 * (1.0 - LAM_INIT), None, ALU.mult, None, ["cp"], ["rs8"])
    ts("dve", rs8[:, 4:6], cpc("rows_pp", 4, 6), 0.25, None, ALU.mult, None, ["cp"], ["rs8"])
    ts("dve", rs8[:, 6:8], cpc("rows_pp", 6, 8), 0.5, None, ALU.mult, None, ["cp"], ["rs8"])
    ts("dve", nbf[:], cpc("bf_pp"), -1.0, None, ALU.mult, None, ["cp"], ["nbf"])
    wv = w_in.rearrange("(k p) n -> p k n", p=128)
    wov = w_out.rearrange("(k p) n -> p k n", p=128)
    wkv = w_mk.rearrange("(k p) n -> p k n", p=128)
    wvv = w_mv.rearrange("(k p) n -> p k n", p=128)

    WST = Rot([XT.aps[0][:, 0:512], XT.aps[1][:, 0:512], ZF.aps[0], ZF.aps[1], KOUT.aps[0], KOUT.aps[1],
               KOUT.aps[2]], "wst")
    WSTK = [("xt", 0), ("xt", 1), ("zf", 0), ("zf", 1), ("kvout", 0), ("kvout", 1), ("kvout", 2)]

    def wload(dst, src, n, scal, key, extra=None):
        if n > 512:
            h_ = n // 2
            wload(dst[:, 0:h_], src[:, 0:h_], h_, scal, key, extra)
            wload(dst[:, h_:n], src[:, h_:n], n - h_, scal, key, extra)
            return
        i_ = WST.i % len(WST.aps)
        xt, _ = WST.next()
        kx = WSTK[i_]
        dma(xt[:, 0:n], src, w=[kx])
        if extra is None:
            ts("dve", dst, xt[:, 0:n], scal, None, ALU.mult, None, [kx, "cp", "rs8"], [key])
        else:
            ts("dve", dst, xt[:, 0:n], scal, extra, ALU.mult, ALU.mult, [kx, "cp", "rs8"], [key])

    for kc in range(8):
        wload(w_mkv_bf[:, kc, 0:256], wkv[:, kc, :], 256, cpc("gmem_pp", kc, kc + 1), ("w_mkv", kc, 0))
        wload(w_mkv_bf[:, kc, 256:512], wvv[:, kc, :], 256, cpc("gmem_pp", kc, kc + 1), ("w_mkv", kc, 1))
    for kc in range(8):
        g = cpc("gnorm_pp", kc, kc + 1)
        wload(w_in_bf[:, kc, 0:1024], wv[:, kc, 0:1024], 1024, g, ("w_in", kc, 0))
        wload(w_in_bf[:, kc, 1024:2048], wv[:, kc, 1024:2048], 1024, g, ("w_in", kc, 1))
        wload(w_in_bf[:, kc, 2048:2304], wv[:, kc, 2048:2304], 256, g, ("w_in", kc, 2))
        wload(w_in_bf[:, kc, 2304:2560], wv[:, kc, 2304:2560], 256, g, ("w_in", kc, 2), 0.125)
        wload(w_in_bf[:, kc, 2560:3072], wv[:, kc, 2560:3072], 512, g, ("w_in", kc, 2))
        wload(w_in_bf[:, kc, 3072:3840], wv[:, kc, 3080:3848], 768, g, ("w_in", kc, 3))
        wload(w_in_bf[:, kc, 3840:3848], wv[:, kc, 3072:3080], 8, g, ("w_in", kc, 3))
    for kc in range(8):
        wload(w_out_bf[:, kc, :], wov[:, kc, :], 1024, rs8[:, kc:kc + 1], ("w_out", kc))
    W_IN_K = [("w_in", kc, i) for kc in range(8) for i in range(4)]
    W_OUT_K = [("w_out", kc) for kc in range(8)]
    W_MKV_K = [("w_mkv", kc, i) for kc in range(8) for i in range(2)]
    l4 = TH.aps[1]
    tt("dve", SQ[:, 0:64], l4[:, 0:64], l4[:, 64:128], ALU.mult, [("th", 1)], ["sq"])
    red(lamt[:, 0:1], SQ[:, 0:64], ALU.add, ["sq"], ["lamt"])
    tt("dve", SQ[:, 64:128], l4[:, 128:192], l4[:, 192:256], ALU.mult, [("th", 1)], ["sq"])
    red(lamt[:, 1:2], SQ[:, 64:128], ALU.add, ["sq"], ["lamt"])
    act(lamt[:, 2:4], lamt[:, 0:2], AF.Exp, ["lamt"], ["lamt"])
    stt(lamt[:, 4:5], lamt[:, 2:3], LAM_INIT, lamt[:, 3:4], ALU.add, ALU.subtract, ["lamt"], ["lamt"])
    ts("dve", lamt[:, 5:6], lamt[:, 4:5], -1.0, None, ALU.mult, None, ["lamt"], ["lamt"])

    def front_load(src_ap, T):
        xt, kx = XT.next()
        dma(xt[0:T, :], src_ap, w=[kx])
        return xt, kx

    def front_a1(xt, kx, T):
        sa, ks = STT.next()
        act(xn[0:T, :], xt[0:T, :], AF.Square, [kx], ["xn", ks], accum=sa[0:T, 0:1])
        rstd_from_ss(sa, ks, 0, 1, 2, 1, T, 1.0 / 1024)
        return sa, ks

    def front_a2(xt, kx, sa, ks, T):
        ts("dve", xn[0:T, :], xt[0:T, :], sa[0:T, 2:3], None, ALU.mult, None, [kx, ks], ["xn"])

    def front_a(xt, kx, T):
        sa, ks = front_a1(xt, kx, T)
        front_a2(xt, kx, sa, ks, T)
        return sa, ks

    def front_b(xt, kx, sa, ks, T):
        for kc in range(8):
            tr(TB[:, kc * 128:kc * 128 + T], xn[0:T, kc * 128:(kc + 1) * 128], identb[0:T, 0:T],
               ["xn", "identb"], [TBK])
        hT, kh = HT.next()
        cpy("act", hT[:, :, 0:T], TB.rearrange("p (k t) -> p k t", k=8)[:, :, 0:T], [TBK], [kh])
        return xt, kx, hT, kh, sa, ks

    def front_compute(xt, kx, T):
        sa, ks = front_a(xt, kx, T)
        return front_b(xt, kx, sa, ks, T)

    def front(src_ap, T):
        xt, kx = front_load(src_ap, T)
        return front_compute(xt, kx, T)

    def group_norm_rstd(src, ksrc, T, ng, sa, ks, base):
        tt("dve", SQ[0:T, 0:ng * 64], src, src, ALU.mult, [ksrc], ["sq"])
        red(sa[0:T, base:base + ng], SQ[0:T, 0:ng * 64].rearrange("p (g d) -> p g d", d=64), ALU.add,
            ["sq"], [ks])
        rstd_from_ss(sa, ks, base, base + ng, base + 2 * ng, ng, T, 1.0 / 64)
        return sa[0:T, base + 2 * ng:base + 3 * ng]

    def phase1_tile(fr, T, ti, j, k_dst, v_dst, later, hook=None):
        xt, kx, hT, kh, sa, ks = fr
        tok = slice(j * 128, j * 128 + T)
        ktok = slice(ti * 128, ti * 128 + T)
        for g in range(8):
            c0 = g * 512
            n = min(512, NIN - c0)
            ps, kp = mmnext()
            for kc in range(8):
                mm(ps[0:T, 0:n], hT[:, kc, 0:T], w_in_bf[:, kc, c0:c0 + n], kc == 0, kc == 7,
                   [kh] + W_IN_K, [kp])
            for it in later:
                it[0] -= 1
            ready = [it for it in later if it[0] <= 0]
            for it in ready:
                later.remove(it)
            for it in ready:
                it[1]()
            if hook is not None:
                hook(g)
            if g == 0 or g == 1:
                zf, kz = ZF.next()
                cpy("act", zf[0:T, :], ps[0:T, 0:512], [kp], [kz])
                r8 = group_norm_rstd(zf[0:T, :], kz, T, 8, sa, ks, 8 if g == 0 else 32)

                def b01(g=g, zf=zf, kz=kz, r8=r8):
                    nb, kn = NB.next()
                    if g == 0:
                        tt("pool", nb[0:T, :].rearrange("p (g d) -> p g d", d=64),
                           zf[0:T, :].rearrange("p (g d) -> p g d", d=64),
                           r8.unsqueeze(2).to_broadcast([T, 8, 64]), ALU.mult, [kz, ks], [kn])
                    else:
                        tt("dve", zf[0:T, :].rearrange("p (g d) -> p g d", d=64),
                           zf[0:T, :].rearrange("p (g d) -> p g d", d=64),
                           r8.unsqueeze(2).to_broadcast([T, 8, 64]), ALU.mult, [kz, ks], [kz])
                        ko, kk = KOUT.next()
                        tt("dve", ko[0:T, :].rearrange("p (g d) -> p g d", d=64),
                           zf[0:T, :].rearrange("p (g d) -> p g d", d=64),
                           cpc("gka_rep")[0:T, :].unsqueeze(1).to_broadcast([T, 8, 64]), ALU.mult,
                           [kz, "cp"], [kk])
                        dma(k_dst, ko[0:T, :], r=[kk])
                        cpy("dve", nb[0:T, :], ko[0:T, :], [kk], [kn])

                    def d01(nb=nb, kn=kn):
                        for h in range(4):
                            tr(TB[:, h * 128:h * 128 + T], nb[0:T, h * 128:(h + 1) * 128], identb[0:T, 0:T],
                               [kn, "identb"], [TBK])
                        src = TB[:, 0:512].rearrange("p (h t) -> p h t", h=4)[:, :, 0:T]
                        if g == 0:
                            ts("dve", qT[:, :, tok], src, cpc("gqa_pp"), None, ALU.mult, None, [TBK, "cp"],
                               [("qT", j)])
                        else:
                            cpy("act", kT[:, :, ktok], src, [TBK], [("kT", ti)])
                    later.append([DELAY, d01])
                later.append([2, b01])
            elif g == 2:
                vo, kv = VOUT.next()
                cpy("act", vo[0:T, :], ps[0:T, 0:512], [kp], [kv])
                dma(v_dst, vo[0:T, :], r=[kv])
                cpy("dve", v_aug[0:T, ti, :, 0:128], vo[0:T, :].rearrange("p (h d) -> p h d", h=4),
                    [kv, "v_ones"], [("v", ti)])
            elif g == 3:
                th, kt = TH.next()
                act(th[0:T, :], ps[0:T, 0:512], AF.Tanh, [kp], [kt], scale=0.5)
                stt(mix[0:T, j, 0:512], th[0:T, :], 1.0, ps[0:T, 0:512], ALU.add, ALU.mult,
                    [kt, kp], [("mix", j, 0)])
            elif g == 4:
                cpy("act", QKB[0:T, j, :], ps[0:T, 0:512], [kp], [("QKB", j)])

                def d4():
                    for i in range(8):
                        tr(TB[0:64, i * 128:i * 128 + T], QKB[0:T, j, i * 64:(i + 1) * 64], identb[0:T, 0:T],
                           [("QKB", j), "identb"], [TBK])
                    cpy("dve", QKT[:, :, tok], TB[0:64, :].rearrange("p (h t) -> p h t", h=8)[:, :, 0:T],
                        [TBK], [("QKT", j)])
                later.append([DELAY, d4])
            elif g == 5:
                cpy("dve", vb_aug[0:T, j, :, 0:64], ps[0:T, 0:256].rearrange("p (h d) -> p h d", h=4),
                    [kp, "vb_ones"], [("vb", j)])
                act(OBG[0:T, j, :], ps[0:T, 256:512], AF.Tanh, [kp], [("OBG", j)], scale=0.5)
            elif g == 6:
                th, kt = TH.next()
                act(th[0:T, 0:256], ps[0:T, 0:256], AF.Tanh, [kp], [kt], scale=0.5)
                stt(th[0:T, 256:512], th[0:T, 0:256], 1.0, ps[0:T, 0:256], ALU.add, ALU.mult, [kt, kp], [kt])
                stt(mix[0:T, j, 512:768], OBG[0:T, j, :], 1.0, th[0:T, 256:512], ALU.add, ALU.mult,
                    [("OBG", j), kt], [("mix", j, 1)])
                zf, kz = ZF.next()
                cpy("act", zf[0:T, 0:256], ps[0:T, 256:512], [kp], [kz])
                r4 = group_norm_rstd(zf[0:T, 0:256], kz, T, 4, sa, ks, 56)

                def b6(zf=zf, kz=kz, r4=r4):
                    nb, kn = NB.next()
                    tt("pool", nb[0:T, 0:256].rearrange("p (g d) -> p g d", d=64),
                       zf[0:T, 0:256].rearrange("p (g d) -> p g d", d=64),
                       r4.unsqueeze(2).to_broadcast([T, 4, 64]), ALU.mult, [kz, ks], [kn])

                    def d6(nb=nb, kn=kn):
                        for hp in range(2):
                            tr(TB[:, hp * 128:hp * 128 + T], nb[0:T, hp * 128:(hp + 1) * 128], identb[0:T, 0:T],
                               [kn, "identb"], [TBK])
                        ts("dve", qmT[:, :, tok], TB[:, 0:256].rearrange("p (h t) -> p h t", h=2)[:, :, 0:T],
                           cpc("gqm_pp"), None, ALU.mult, None, [TBK, "cp"], [("qmT", j)])
                    later.append([DELAY, d6])
                later.append([2, b6])
            else:
                th, kt = TH.next()
                act(th[0:T, 0:256], ps[0:T, 0:256], AF.Tanh, [kp], [kt], scale=0.5)
                stt(mix[0:T, j, 768:1024], th[0:T, 0:256], 1.0, ps[0:T, 0:256], ALU.add, ALU.mult,
                    [kt, kp], [("mix", j, 2)])
                cpy("dve", GIF[0:T, j, :], ps[0:T, 256:264], [kp], [("GIF", j)])

    def phase1_block(tiles, T, pre=None):
        later = []
        if pre is None:
            xt, kx = front_load(tiles[0][0], T)
            fr = front_compute(xt, kx, T)
        else:
            fr = front_b(pre[0], pre[1], pre[2], pre[3], T)
        for idx, (src_ap, ti, j, k_dst, v_dst) in enumerate(tiles):
            nxt = {}
            hook = None
            if idx + 1 < len(tiles):
                nx = front_load(tiles[idx + 1][0], T)

                def hook(g, nx=nx, nxt=nxt):
                    if g == 0:
                        nxt["a"] = front_a1(nx[0], nx[1], T)
                    elif g == 2:
                        front_a2(nx[0], nx[1], nxt["a"][0], nxt["a"][1], T)
                    elif g == 5:
                        nxt["fr"] = front_b(nx[0], nx[1], nxt["a"][0], nxt["a"][1], T)
            phase1_tile(fr, T, ti, j, k_dst, v_dst, later, hook)
            fr = nxt.get("fr")
        while later:
            later.pop(0)[1]()

    def gates_block(T, ntile):
        NQ = ntile * 128
        ibT = TF[0:4, 0:NQ]
        fbT = TBf[0:4, 0:NQ]
        for j in range(ntile):
            tr(TF[0:4, j * 128:j * 128 + T], GIF[0:T, j, 0:4], identf[0:T, 0:T], [("GIF", j), "cp"], [TFK])
            tr(TBf[0:4, j * 128:j * 128 + T], GIF[0:T, j, 4:8], identf[0:T, 0:T], [("GIF", j), "cp"], [TBK])
        if T < 128:
            memset("dve", GA[:, 0:NQ], 0.0, [KGA])
        cs = slice(0, T) if ntile == 1 else slice(0, NQ)
        act(GA[:, cs], fbT[:, cs], AF.Exp, [TBK, "nbf"], [KGA], bias=nbf[0:4, 0:1], scale=-1.0)
        act(GA[:, cs], GA[:, cs], AF.Ln, [KGA], [KGA], bias=1.0)
        scan(GB[:, cs], zerob[0:4, cs], GA[:, cs], Bprev[:, 0:1], ALU.add, ALU.subtract,
             ["zerob", KGA, "Bprev"], [KGB])
        stt(GU[:, cs], ibT[:, cs], cpc("bi_pp")[0:4, :], GB[:, cs], ALU.add, ALU.subtract,
            [TFK, "cp", KGB], [KGU])
        if ntile == 1:
            red(UM[:, 0:1], GU[:, cs], ALU.max, [KGU], ["UM"])
        else:
            red(UM[:, 0:ntile], GU[:, cs].rearrange("p (c t) -> p c t", c=ntile), ALU.max, [KGU], ["UM"])
        scan(MUALL[:, 1:1 + ntile], UM[:, 0:ntile], UM[:, 0:ntile], MUALL[:, 0:1], ALU.max, ALU.max,
             ["UM", "MUALL"], ["MUALL"])
        tt("dve", C0[:, 0:ntile], MUALL[:, 0:ntile], MUALL[:, 1:1 + ntile], ALU.subtract, ["MUALL"], ["C0"])
        if ntile == 1:
            mub = MUALL[:, 1:2].to_broadcast([4, T])
            tt("dve", GW[:, cs], GU[:, cs], mub, ALU.subtract, [KGU, "MUALL"], [KGW])
            tt("dve", GE[:, cs], GB[:, cs], mub, ALU.add, [KGB, "MUALL"], [KGE])
        else:
            mub = MUALL[:, 1:1 + ntile].unsqueeze(2).to_broadcast([4, ntile, 128])
            tt("dve", GW[:, cs].rearrange("p (c t) -> p c t", c=ntile),
               GU[:, cs].rearrange("p (c t) -> p c t", c=ntile), mub, ALU.subtract, [KGU, "MUALL"], [KGW])
            tt("dve", GE[:, cs].rearrange("p (c t) -> p c t", c=ntile),
               GB[:, cs].rearrange("p (c t) -> p c t", c=ntile), mub, ALU.add, [KGB, "MUALL"], [KGE])
        return lambda: gates_part2(T, ntile, cs, NQ)

    def gates_part2(T, ntile, cs, NQ):
        act(C0[:, 0:ntile], C0[:, 0:ntile], AF.Exp, ["C0"], ["C0"])
        act(GW[:, cs], GW[:, cs], AF.Exp, [KGW], [KGW])
        act(GE[:, cs], GE[:, cs], AF.Exp, [KGE], [KGE], scale=-1.0)
        for j in range(ntile):
            tr(TF[0:T, j * 8:j * 8 + 4], GW[0:4, j * 128:j * 128 + T], identf[0:4, 0:4], [KGW, "cp"], [TFK])
            tr(TF[0:T, j * 8 + 4:j * 8 + 8], GE[0:4, j * 128:j * 128 + T], identf[0:4, 0:4], [KGE, "cp"], [TFK])
        cpy("dve", WE[0:T, 0:ntile, :], TF[0:T, 0:ntile * 8].rearrange("p (c e) -> p c e", e=8), [TFK], ["WE"])
        tt("dve", CD[:, 0:ntile, :], C0[:, 0:ntile].unsqueeze(2).to_broadcast([4, ntile, 4]),
           cpc("SEL2")[0:4, :].unsqueeze(1).to_broadcast([4, ntile, 4]), ALU.mult, ["C0", "cp"], ["CD"])
        mm(TBf[0:64, 0:ntile * 4], cpc("SEL")[0:4, 0:64], CD[:, 0:ntile, :].rearrange("p c e -> p (c e)"), True, True,
           ["cp", "CD"], [TBK])
        cpy("dve", C0B[:, 0:ntile, :], TBf[0:64, 0:ntile * 4].rearrange("p (c e) -> p c e", e=4), [TBK], ["C0B"])
        last = T - 1 if ntile == 1 else NQ - 1
        cpy("dve", Bprev[:, 0:1], GB[:, last:last + 1], [KGB], ["Bprev"])
        cpy("dve", MUALL[:, 0:1], MUALL[:, ntile:ntile + 1], ["MUALL"], ["MUALL"])

    def attention_block(T, ntile, ktiles, sample):
        NQ = ntile * T
        ABK = [4, 5, 6]
        qkeys = [("qT", jj) for jj in range(ntile)]
        steps = [(h, t, nk, dsub) for h in range(4) for (t, nk, dsub) in ktiles]
        state = {"n": 0}

        def emit_S(st_):
            h, t, nk, dsub = st_
            c0 = 0 if dsub is None else dsub * T
            ncol = NQ - c0
            b0 = 2 * (state["n"] % 2)
            state["n"] += 1
            kps = [("pb", b0), ("pb", b0 + 1)]
            for m in range(2):
                pr = slice(64 * m, 64 * m + 64)
                mm(PB[b0 + m][0:nk, 0:ncol], kT[pr, h, t * 128:t * 128 + nk], qT[pr, h, c0:NQ],
                   True, dsub is None, [("kT", t)] + qkeys, [kps[m]])
            if dsub is not None:
                for m in range(2):
                    mm(PB[b0 + m][0:nk, 0:T], identb[0:nk, 0:nk], corrb[0:nk, h, 0:T], False, True,
                       ["identb", "corrb"], [kps[m]])
            if KEEPWARM and not sample:
                mm(PB[7][:, :], zerob[:, 0:128], zerob[:, :], True, True, ["zerob"], [("pb", 7)])
            pt, kpt = PT.next()
            if sample or h != 0:
                calls = [(c0, NQ)]
            else:
                calls = [(c, c + 128) for c in range(c0, NQ, 128)]
            pair = PBIG[0:nk, b0 * 512:(b0 + 2) * 512].rearrange("p (m c) -> p m c", m=2)
            for (ca, cb) in calls:
                if sample:
                    bias = cpc("ABS")[0:nk, h * 17 + t:h * 17 + t + 1]
                else:
                    tref = ktiles[-1][0] - (ntile - 1) + (cb - 1) // 128
                    bias = cpc("AB")[0:nk, h * 16 + (tref - t):h * 16 + (tref - t) + 1]
                act(pt[0:nk, :, ca - c0:cb - c0], pair[:, :, ca - c0:cb - c0], AF.Exp, kps + ["cp"], [kpt],
                    bias=bias, scale=0.125)
            return pt, kpt, c0

        def emit_PV(st_, pt, kpt, c0):
            h, t, nk, dsub = st_
            for m in range(2):
                for i in range(ntile):
                    if i * T < c0:
                        continue
                    a_ = m * 4 + i
                    bnk = ABK[a_ // 3]
                    o = (a_ % 3) * 129
                    mm(PB[bnk][0:T, o:o + 129], pt[0:nk, m, i * T - c0:(i + 1) * T - c0],
                       v_aug[0:nk, t, h, 0:129], False, True, [kpt, ("v", t), "v_ones"], [("pb", bnk)])

        def evac(h):
            bset = ABK
            sa, ks = STT.next()
            if ntile == 4:
                for b_ in range(3):
                    nacc = 3 if b_ < 2 else 2
                    recip(sa[0:T, 3 * b_:3 * b_ + nacc], PB[bset[b_]][0:T, 128:129 * nacc:129],
                          [("pb", bset[b_])], [ks])
            else:
                recip(sa[0:T, 0:1], PB[bset[0]][0:T, 128:129], [("pb", bset[0])], [ks])
                recip(sa[0:T, 4:5], PB[bset[1]][0:T, 129 + 128:129 + 129], [("pb", bset[1])], [ks])
            tt("dve", sa[0:T, 4:4 + ntile], sa[0:T, 4:4 + ntile], lamt[0:T, 5:6].to_broadcast([T, ntile]),
               ALU.mult, [ks, "lamt"], [ks])
            oos = []
            for i in range(ntile):
                a1, a2 = i, 4 + i
                o1, k1 = O1.next()
                b1, b2 = bset[a1 // 3], bset[a2 // 3]
                ts("dve", o1[0:T, :], PB[b1][0:T, (a1 % 3) * 129:(a1 % 3) * 129 + 128], sa[0:T, a1:a1 + 1], None,
                   ALU.mult, None, [("pb", b1), ks], [k1])
                oo, ko = OO.next()
                stt(oo[0:T, :], PB[b2][0:T, (a2 % 3) * 129:(a2 % 3) * 129 + 128], sa[0:T, a2:a2 + 1],
                    o1[0:T, :], ALU.mult, ALU.add, [("pb", b2), ks, k1], [ko])
                oos.append((oo, ko, o1, k1))
            for i in range(ntile):
                oo, ko, o1, k1 = oos[i]
                S.add("dve", (lambda o1=o1, oo=oo, sa=sa, i=i: nc.vector.scalar_tensor_tensor(
                    out=o1[0:T, :], in0=oo[0:T, :], scalar=1.0, in1=oo[0:T, :], op0=ALU.mult, op1=ALU.mult,
                    accum_out=sa[0:T, 8 + i:9 + i])), [ko], [k1, ks])
            ts("dve", sa[0:T, 12:12 + ntile], sa[0:T, 8:8 + ntile], 1.0 / 128, EPS, ALU.mult, ALU.add, [ks], [ks])
            tt("pool", sa[0:T, 16:16 + ntile], sa[0:T, 12:12 + ntile], cm05[0:T, 0:ntile], ALU.pow,
               [ks, "cm05"], [ks])

            def fin(h=h, sa=sa, ks=ks, oos=oos):
                for ii in range(ntile):
                    oo2, ko2 = oos[ii][0], oos[ii][1]
                    stt(mix[0:T, ii, h * 128:(h + 1) * 128], oo2[0:T, :], sa[0:T, 16 + ii:17 + ii],
                        mix[0:T, ii, h * 128:(h + 1) * 128], ALU.mult, ALU.mult,
                        [ko2, ks, ("mix", ii, 0)], [("mix", ii, 0)])
            return fin

        prev = None
        fins = []
        for k, st_ in enumerate(steps):
            h = st_[0]
            new_head = (k == 0 or steps[k - 1][0] != h)
            pt, kpt, c0 = emit_S(st_)
            if prev is not None:
                emit_PV(*prev)
                if new_head:
                    fins.append(evac(prev[0][0]))
            if new_head:
                for b_ in range(3):
                    mm(PB[ABK[b_]][0:T, :], zerob[:, 0:T], zerob[:, :], True, True, ["zerob"], [("pb", ABK[b_])])
            if len(fins) > 0 and not new_head and (k == 0 or steps[k - 2][0] == h):
                for f_ in fins:
                    f_()
                del fins[:]
            prev = (st_, pt, kpt, c0)
        emit_PV(*prev)
        fins.append(evac(prev[0][0]))
        for f_ in fins:
            f_()

    def mem_block(T, ntile):
        NQ = ntile * T
        banks = [(ACC[0], ACCK[0]), (ACC[1], ACCK[1]), (ACC[2], ACCK[2]), (TF, TFK)]
        for i in range(ntile):
            mm(banks[i][0][0:T, :], zerob[:, 0:T], zerob[:, :], True, True, ["zerob"], [banks[i][1]])
        prevm = None
        for h in range(4):
            pr = slice(64 * (h % 2), 64 * (h % 2) + 64)
            for nt in range(2):
                ps, kp = mmnext()
                mm(ps[:, 0:NQ], mkT[pr, h // 2, nt * 128:(nt + 1) * 128], qmT[pr, h // 2, 0:NQ], True, True,
                   ["mkT"] + [("qmT", jj) for jj in range(ntile)], [kp])
                pt, kpt = PT.next()
                act(pt[:, 0, 0:NQ], ps[:, 0:NQ], AF.Exp, [kp], [kpt], scale=0.125)
                if prevm is not None:
                    ph, pnt, ppt, pkpt = prevm
                    for i in range(ntile):
                        mm(banks[i][0][0:T, ph * 65:(ph + 1) * 65], ppt[:, 0, i * T:(i + 1) * T],
                           mv_aug[:, pnt, ph, 0:65], False, True, [pkpt, "mv", "mv_ones"], [banks[i][1]])
                prevm = (h, nt, pt, kpt)
        ph, pnt, ppt, pkpt = prevm
        for i in range(ntile):
            mm(banks[i][0][0:T, ph * 65:(ph + 1) * 65], ppt[:, 0, i * T:(i + 1) * T],
               mv_aug[:, pnt, ph, 0:65], False, True, [pkpt, "mv", "mv_ones"], [banks[i][1]])
        for i in range(ntile):
            bk, kb = banks[i]
            sa, ks = STT.next()
            recip(sa[0:T, 0:4], bk[0:T, 64:260:65], [kb], [ks])
            tt("dve", OM[0:T, :, :], bk[0:T, 0:260].rearrange("p (h e) -> p h e", e=65)[:, :, 0:64],
               sa[0:T, 0:4].unsqueeze(2).to_broadcast([T, 4, 64]), ALU.mult, [kb, ks], ["HN"])
            tt("dve", mix[0:T, i, 768:1024], OM[0:T, :, :].rearrange("p h d -> p (h d)"),
               mix[0:T, i, 768:1024], ALU.mult, ["HN", ("mix", i, 2)], [("mix", i, 2)])

    def mlstm_block(T, ntile):
        pend_a = []
        pend_b = []
        for j in range(ntile):
            tt("pool", VW[j][0:T, :, 0:65], vb_aug[0:T, j, :, 0:65],
               WE[0:T, j, 0:4].unsqueeze(2).to_broadcast([T, 4, 65]), ALU.mult,
               [("vb", j), "vb_ones", "WE"], [("vw", j)])
        for j in range(ntile):
            tok = slice(j * 128, j * 128 + T)
            vw = VW[j]
            kvw = ("vw", j)
            tt("dve", Sd[:, :, 0:65], Sst[:, :, 0:65], C0B[:, j, :].unsqueeze(2).to_broadcast([64, 4, 65]), ALU.mult,
               ["Sst", "C0B"], ["Sd"])
            cpy("act", Cs[:, :, 0:65], Sd[:, :, 0:65], ["Sd"], ["Cs"])
            psA, kA = mmnext()
            for h in range(4):
                mm(psA[0:T, h * 128:h * 128 + T], QKT[:, 4 + h, tok], QKT[:, h, tok], True, True,
                   [("QKT", j)], [kA])
            psS, kS = ACC[2], ACCK[2]
            for h in range(4):
                mm(psS[0:64, h * 65:(h + 1) * 65], QKB[0:T, j, 256 + h * 64:256 + (h + 1) * 64],
                   vw[0:T, h, 0:65], True, True, [("QKB", j), kvw], [kS])
            tt("dve", Sst[:, :, 0:65], Sd[:, :, 0:65], psS[0:64, 0:260].rearrange("p (c e) -> p c e", e=65), ALU.add,
               ["Sd", kS], ["Sst"])
            tt("dve", AT[0:T, :, 0:T], psA[0:T, :].rearrange("p (h t) -> p h t", h=4)[:, :, 0:T],
               cpc("maskU")[0:T, 0:T].unsqueeze(1).to_broadcast([T, 4, T]), ALU.mult, [kA, "cp"], ["AT"])
            psO, kO = ACC[j % 2], ACCK[j % 2]
            for h in range(4):
                mm(psO[0:T, h * 65:(h + 1) * 65], AT[0:T, h, 0:T], vw[0:T, h, 0:65], True, False, ["AT", kvw], [kO])
                mm(psO[0:T, h * 65:(h + 1) * 65], QKT[:, h, tok], Cs[:, h, 0:65], False, True,
                   [("QKT", j), "Cs"], [kO])

            def evac_a(j=j, psO=psO, kO=kO):
                sa, ks = STT.next()
                cpy("dve", sa[0:T, 20:24], psO[0:T, 64:260:65], [kO], [ks])
                stt(sa[0:T, 24:28], sa[0:T, 20:24], -1.0, sa[0:T, 20:24], ALU.mult, ALU.max, [ks], [ks])
                tt("dve", sa[0:T, 0:4], sa[0:T, 24:28], WE[0:T, j, 4:8], ALU.max, [ks, "WE"], [ks])
                recip(sa[0:T, 4:8], sa[0:T, 0:4], [ks], [ks])
                tt("dve", HN[0:T, :, :], psO[0:T, 0:260].rearrange("p (h e) -> p h e", e=65)[:, :, 0:64],
                   sa[0:T, 4:8].unsqueeze(2).to_broadcast([T, 4, 64]), ALU.mult, [kO, ks], ["HN"])
                r4 = group_norm_rstd(HN[0:T, :, :].rearrange("p h d -> p (h d)"), "HN", T, 4, sa, ks, 8)

                def evac_b():
                    tt("dve", HN[0:T, :, :], HN[0:T, :, :], r4.unsqueeze(2).to_broadcast([T, 4, 64]), ALU.mult,
                       ["HN", ks], ["HN"])
                    tt("dve", mix[0:T, j, 512:768], HN[0:T, :, :].rearrange("p h d -> p (h d)"),
                       mix[0:T, j, 512:768], ALU.mult, ["HN", ("mix", j, 1)], [("mix", j, 1)])
                return evac_b
            pend_a.append(evac_a)
            if len(pend_a) == 2:
                for f_ in pend_b:
                    f_()
                del pend_b[:]
                pend_b.append(pend_a.pop(0)())
        for f_ in pend_b:
            f_()
        del pend_b[:]
        while pend_a:
            pend_a.pop(0)()()

    def phase3_block(tiles, T):
        YB = [[(PB[3], ("pb", 3)), (PB[4], ("pb", 4))], [(PB[5], ("pb", 5)), (PB[0], ("pb", 0))]]
        xts = {}

        def stage_t(idx):
            src_ap, dst_ap, j = tiles[idx]
            mT, kmT = HT.next()
            for kc in range(8):
                tr(TB[:, kc * 128:kc * 128 + T], mix[0:T, j, kc * 128:(kc + 1) * 128], identb[0:T, 0:T],
                   [("mix", j, 0), ("mix", j, 1), ("mix", j, 2), "identb"], [TBK])
            cpy("act", mT[:, :, 0:T], TB.rearrange("p (k t) -> p k t", k=8)[:, :, 0:T], [TBK], [kmT])
            xt, kx = XT.next()
            dma(xt[0:T, :], src_ap, w=[kx])
            xts[idx] = (mT, kmT, xt, kx)

        def stage_m(idx):
            src_ap, dst_ap, j = tiles[idx]
            mT, kmT, xt, kx = xts[idx]
            for half in range(2):
                ps, kp = YB[idx % 2][half]
                for kc in range(8):
                    mm(ps[0:T, :], mT[:, kc, 0:T], w_out_bf[:, kc, half * 512:(half + 1) * 512], kc == 0, kc == 7,
                       [kmT] + W_OUT_K, [kp])
                tt("dve", xt[0:T, half * 512:(half + 1) * 512], ps[0:T, :], xt[0:T, half * 512:(half + 1) * 512],
                   ALU.add, [kp, kx], [kx])
            dma(dst_ap, xt[0:T, :], r=[kx])

        stage_t(0)
        for idx in range(len(tiles)):
            if idx + 1 < len(tiles):
                stage_t(idx + 1)
            stage_m(idx)

    def memkv_seq(s):
        for nt in range(2):
            xt, kx, hT, kh, sa, ks = front(memp[s, nt * 128:(nt + 1) * 128, :], 128)
            ps, kp = mmnext()
            for kc in range(8):
                mm(ps[:, :], hT[:, kc, :], w_mkv_bf[:, kc, :], kc == 0, kc == 7, [kh] + W_MKV_K, [kp])
            zf, kz = ZF.next()
            cpy("act", zf[:, :], ps[:, :], [kp], [kz])
            dma(pmv[s, nt * 128:(nt + 1) * 128, :], zf[:, 256:512], r=[kz])
            cpy("pool", mv_aug[:, nt, :, 0:64], zf[:, 256:512].rearrange("p (h d) -> p h d", h=4),
                [kz, "mv_ones"], ["mv"])
            r4 = group_norm_rstd(zf[:, 0:256], kz, 128, 4, sa, ks, 8)
            tt("dve", zf[:, 0:256].rearrange("p (g d) -> p g d", d=64),
               zf[:, 0:256].rearrange("p (g d) -> p g d", d=64),
               r4.unsqueeze(2).to_broadcast([128, 4, 64]), ALU.mult, [kz, ks], [kz])
            ko, kk = KOUT.next()
            tt("dve", ko[:, 0:256].rearrange("p (g d) -> p g d", d=64),
               zf[:, 0:256].rearrange("p (g d) -> p g d", d=64),
               cpc("gkm_rep").unsqueeze(1).to_broadcast([128, 4, 64]), ALU.mult, [kz, "cp"], [kk])
            dma(pmk[s, nt * 128:(nt + 1) * 128, :], ko[:, 0:256], r=[kk])
            nb, kn = NB.next()
            cpy("pool", nb[:, 0:256], ko[:, 0:256], [kk], [kn])
            for hp in range(2):
                tr(TB[:, hp * 128:(hp + 1) * 128], nb[:, hp * 128:(hp + 1) * 128], identb[:, :],
                   [kn, "identb"], [TBK])
            cpy("act", mkT[:, :, nt * 128:(nt + 1) * 128], TB[:, 0:256].rearrange("p (h t) -> p h t", h=2),
                [TBK], ["mkT"])

    def state_out(dC, dn, dm):
        for h in range(4):
            dma(dC[h], Sst[:, h, 0:64], r=["Sst"])
            dma(dn[h].unsqueeze(1), Sst[:, h, 64:65], r=["Sst"])
        tt("dve", mfin[:, :], Bprev[:, 0:1], MUALL[:, 0:1], ALU.add, ["Bprev", "MUALL"], ["mfin"])
        dma(dm, mfin[:, :], r=["mfin"])

    try:
        chk(1)
        if DO_SAMPLE:
            for t in range(16):
                i1 = WST.i % len(WST.aps)
                xk, _ = WST.next()
                i2 = WST.i % len(WST.aps)
                xv, _ = WST.next()
                kx, kx2 = WSTK[i1], WSTK[i2]
                dma(xk[:, 0:512], ck[t * 128:(t + 1) * 128, :], w=[kx])
                dma(xv[:, 0:512], cv[t * 128:(t + 1) * 128, :], w=[kx2])
                nb, kn = NB.next()
                cpy("dve", nb[:, :], xk[:, 0:512], [kx], [kn])
                cpy("dve" if t % 2 == 0 else "pool", v_aug[:, t, :, 0:128],
                    xv[:, 0:512].rearrange("p (h d) -> p h d", h=4), [kx2, "v_ones"], [("v", t)])
                for h in range(4):
                    tr(TB[:, h * 128:(h + 1) * 128], nb[:, h * 128:(h + 1) * 128], identb[:, :], [kn, "identb"], [TBK])
                cpy("act", kT[:, :, t * 128:(t + 1) * 128], TB[:, 0:512].rearrange("p (h t) -> p h t", h=4),
                    [TBK], [("kT", t)])
            for nt in range(2):
                xt, kx = XT.next()
                dma(xt[:, 0:256], cmk[nt * 128:(nt + 1) * 128, :], w=[kx])
                dma(xt[:, 256:512], cmv[nt * 128:(nt + 1) * 128, :], w=[kx])
                nb, kn = NB.next()
                cpy("dve", nb[:, 0:256], xt[:, 0:256], [kx], [kn])
                cpy("pool", mv_aug[:, nt, :, 0:64], xt[:, 256:512].rearrange("p (h d) -> p h d", h=4),
                    [kx, "mv_ones"], ["mv"])
                for hp in range(2):
                    tr(TB[:, hp * 128:(hp + 1) * 128], nb[:, hp * 128:(hp + 1) * 128], identb[:, :],
                       [kn, "identb"], [TBK])
                cpy("act", mkT[:, :, nt * 128:(nt + 1) * 128], TB[:, 0:256].rearrange("p (h t) -> p h t", h=2),
                    [TBK], ["mkT"])
            chk(2)
            for h in range(4):
                dma(Sst[:, h, 0:64], sC[h], w=["Sst"])
                dma(Sst[:, h, 64:65], sn[h].unsqueeze(1), w=["Sst"])
            memset("dve", Bprev[:, :], 0.0, ["Bprev"])
            cpy("dve", MUALL[:, 0:1], cpc("sm_pp")[0:4, :], ["cp"], ["MUALL"])
            chk(3)
            phase1_block([(xs[:, :], 16, 0, sk[:, :], sv[:, :])], 16)
            chk(4)
            g2 = gates_block(16, 1)
            chk(5)
            attention_block(16, 1, [(t, 128, None) for t in range(16)] + [(16, 16, 0)], True)
            g2()
            chk(6)
            mem_block(16, 1)
            chk(7)
            mlstm_block(16, 1)
            chk(8)
            phase3_block([(xs[:, :], ys[:, :], 0)], 16)
            chk(9)
            state_out(sCo, sno, smo.rearrange("o h -> h o"))

        pre = None
        for s in range(NSEQ):
            if s == 0:
                memkv_seq(s)
            memset("dve", Sst[:, :, :], 0.0, ["Sst"])
            memset("dve", Bprev[:, :], 0.0, ["Bprev"])
            memset("dve", MUALL[:, 0:1], 0.0, ["MUALL"])
            for b in range(NBLK):
                tl = []
                for j in range(4):
                    ti = 4 * b + j
                    rows = slice(ti * 128, (ti + 1) * 128)
                    tl.append((xp[s, rows, :], ti, j, pk[s, rows, :], pv[s, rows, :]))
                phase1_block(tl, 128, pre)
                pre = None
                g2 = gates_block(128, 4)
                ktiles = [(t, 128, None) for t in range(4 * b)] + [(4 * b + i, 128, i) for i in range(4)]
                attention_block(128, 4, ktiles, False)
                g2()
                mem_block(128, 4)
                if b + 1 == NBLK and s + 1 < NSEQ:
                    memkv_seq(s + 1)
                mlstm_block(128, 4)
                nsrc = None
                if b + 1 < NBLK:
                    nsrc = xp[s, (4 * b + 4) * 128:(4 * b + 5) * 128, :]
                elif s + 1 < NSEQ:
                    nsrc = xp[s + 1, 0:128, :]
                if nsrc is not None:
                    dma(EV[:, :], nsrc, w=["ev"])
                    sa_, ks_ = front_a(EV[:, :], "ev", 128)
                    pre = (EV[:, :], "ev", sa_, ks_)
                tl3 = []
                for j in range(4):
                    ti = 4 * b + j
                    rows = slice(ti * 128, (ti + 1) * 128)
                    tl3.append((xp[s, rows, :], yp[s, rows, :], j))
                phase3_block(tl3, 128)
            state_out(pC[s], pn[s], pm[s].unsqueeze(1))

    except _Stop:
        pass

    print('sbuf bytes remaining', nc.sbuf_bytes_remaining)
    S.emit(st)
    st.close()
    return nc


_NC_CACHE = {}


def _get_nc(key=(4, 4, True)):
    if key not in _NC_CACHE:
        _NC_CACHE[key] = build_program(*key)
    return _NC_CACHE[key]


def _in_maps(inp, NSEQ=4):
    maps = []
    c32 = lambda a: np.ascontiguousarray(a, dtype=np.float32)
    for c in range(8):
        m = {
            "xp": c32(inp["x_prompt"][4 * c:4 * c + max(NSEQ, 1)]),
            "memp": c32(inp["mem_prompt"][4 * c:4 * c + max(NSEQ, 1)]),
            "xs": c32(inp["x_sample"][c]),
            "ck": c32(inp["cache_attn_k"][0, c].reshape(2048, 512)),
            "cv": c32(inp["cache_attn_v"][0, c].reshape(2048, 512)),
            "sC": c32(inp["state_mlstm_C"][0, c]),
            "sn": c32(inp["state_mlstm_n"][0, c]),
            "cmk": c32(inp["cache_mem_k"][0, c].reshape(256, 256)),
            "cmv": c32(inp["cache_mem_v"][0, c].reshape(256, 256)),
            "w_in": c32(inp["w_in"][0]),
            "w_out": c32(inp["w_out"][0]),
            "w_mk": c32(inp["w_mk"][0]),
            "w_mv": c32(inp["w_mv"][0]),
        }
        cpa, extra = _make_cp(inp, c)
        m["cp"] = cpa
        m.update(extra)
        maps.append(m)
    return maps


def kernel(**inputs):
    inp = {k: np.asarray(v) for k, v in inputs.items()}
    nc = _get_nc()
    res = run_bass_kernel_spmd(nc, _in_maps(inp), core_ids=list(range(8)))
    R = res.results
    cat = lambda name: np.concatenate([np.asarray(r[name]) for r in R], axis=0)
    stk = lambda name: np.stack([np.asarray(r[name]) for r in R], axis=0)
    y_prompt = cat("yp")
    y_sample = stk("ys")
    p_attn_k = cat("pk").reshape(1, 32, 2048, 4, 128)
    p_attn_v = cat("pv").reshape(1, 32, 2048, 4, 128)
    p_C = cat("pC").reshape(1, 32, 4, 64, 64)
    p_n = cat("pn").reshape(1, 32, 4, 64)
    p_m = cat("pm").reshape(1, 32, 4)
    p_mk = cat("pmk").reshape(1, 32, 256, 4, 64)
    p_mv = cat("pmv").reshape(1, 32, 256, 4, 64)
    s_k = stk("sk").reshape(1, 8, 16, 4, 128)
    s_v = stk("sv").reshape(1, 8, 16, 4, 128)
    s_C = stk("sCo").reshape(1, 8, 4, 64, 64)
    s_n = stk("sno").reshape(1, 8, 4, 64)
    s_m = stk("smo").reshape(1, 8, 4)
    return (y_prompt, y_sample, p_attn_k, p_attn_v, p_C, p_n, p_m, p_mk, p_mv,
            s_k, s_v, s_C, s_n, s_m)
```

```python
import math
from contextlib import ExitStack

import numpy as np
import concourse.bass as bass
import concourse.mybir as mybir
from concourse.bass_utils import run_bass_kernel_spmd

F32 = mybir.dt.float32
BF16 = mybir.dt.bfloat16
AF = mybir.ActivationFunctionType
ALU = mybir.AluOpType
AX = mybir.AxisListType

N_DMA_SEMS = 40
DELAY = 4
KEEPWARM = False
NWARM = 12
EPS = 1e-6
SLOPES = [2.0 ** (-8.0 * (h + 1) / 4) for h in range(4)]
LAM_INIT = 0.8 - 0.6 * math.exp(0.0)
NIN = 3848
BIG = 240000.0


class _Op:
    __slots__ = ("eng", "fn", "deps", "has_dep", "is_dma", "semkey", "val", "ie")


class Sched:
    def __init__(self, nc):
        self.nc = nc
        self.ops = []
        self.lw = {}
        self.rd = {}
        self.eng_n = {"pe": 0, "act": 0, "dve": 0, "pool": 0, "sp": 0}

    def add(self, eng, fn, reads=(), writes=(), dma=False):
        i = len(self.ops)
        deps = set()
        for k in reads:
            w = self.lw.get(k)
            if w is not None:
                deps.add(w)
            if isinstance(k, tuple) and k[0] == "pb":
                for r in self.rd.get(k, ()):
                    if self.ops[r].eng != eng:
                        deps.add(r)
        for k in writes:
            w = self.lw.get(k)
            if w is not None:
                deps.add(w)
            for r in self.rd.get(k, ()):
                deps.add(r)
        op = _Op()
        op.eng = eng
        op.fn = fn
        op.is_dma = dma
        op.has_dep = False
        op.semkey = None
        op.val = 0
        op.ie = self.eng_n[eng]
        self.eng_n[eng] += 1
        keep = []
        for d in deps:
            p = self.ops[d]
            if p.eng == eng and not p.is_dma:
                if eng == "pe" or eng == "sp":
                    continue
                if eng != "pool" and op.ie - p.ie > 2:
                    continue
            p.has_dep = True
            keep.append(d)
        op.deps = keep
        for k in reads:
            self.rd.setdefault(k, []).append(i)
        for k in writes:
            self.lw[k] = i
            self.rd[k] = []
        self.ops.append(op)
        return i

    def emit(self, stack):
        nc = self.nc
        engobj = {"pe": nc.tensor, "act": nc.scalar, "dve": nc.vector,
                  "pool": nc.gpsimd, "sp": nc.sync}
        esem = {e: stack.enter_context(nc.semaphore("s_" + e))
                for e in ("pe", "act", "dve", "pool")}
        dsem = [stack.enter_context(nc.semaphore("d%d" % i)) for i in range(N_DMA_SEMS)]
        waited = {e: {} for e in engobj}
        cnt = {e: 0 for e in engobj}
        dcnt = [0] * N_DMA_SEMS
        rr = 0
        rrp = 0
        for op in self.ops:
            E = engobj[op.eng]
            need = {}
            for d in op.deps:
                p = self.ops[d]
                if need.get(p.semkey, 0) < p.val:
                    need[p.semkey] = p.val
            s = None
            if op.is_dma:
                if op.eng == "pool":
                    s = N_DMA_SEMS - 8 + (rrp % 8)
                    rrp += 1
                else:
                    s = rr % (N_DMA_SEMS - 8)
                    rr += 1
                if dcnt[s] > 0 and need.get(("d", s), 0) < dcnt[s]:
                    need[("d", s)] = dcnt[s]
                dcnt[s] += 16
                op.semkey = ("d", s)
                op.val = dcnt[s]
            w = waited[op.eng]
            for key, val in need.items():
                if w.get(key, 0) >= val:
                    continue
                so = dsem[key[1]] if key[0] == "d" else esem[key[1]]
                E.wait_ge(so, val)
                w[key] = val
            inst = op.fn()
            if op.is_dma:
                inst.then_inc(dsem[s], 16)
            elif op.has_dep:
                cnt[op.eng] += 1
                op.semkey = ("e", op.eng)
                op.val = cnt[op.eng]
                inst.then_inc(esem[op.eng], 1)
        for s in range(N_DMA_SEMS):
            if dcnt[s] > 0:
                nc.sync.wait_ge(dsem[s], dcnt[s])
        return cnt


class Rot:
    def __init__(self, aps, name):
        self.aps = aps
        self.name = name
        self.i = 0

    def next(self):
        k = self.i % len(self.aps)
        self.i += 1
        return self.aps[k], (self.name, k)


def _cp_layout():
    off = {}
    c = 0
    for name, n in [("ident", 128), ("maskU", 128), ("AB", 64), ("ABS", 68),
                    ("gqa_pp", 1), ("gqm_pp", 1), ("gkm_rep", 64), ("gka_rep", 64),
                    ("gnorm_pp", 8), ("gmem_pp", 8), ("rows_pp", 8),
                    ("bi_pp", 1), ("bf_pp", 1), ("sm_pp", 1), ("SEL", 128), ("SEL2", 4)]:
        off[name] = (c, c + n)
        c += n
    return off, c


CP_OFF, NCP = _cp_layout()


def _make_cp(inp, core):
    cp = np.zeros((128, NCP), np.float32)

    def put(name, arr):
        a, b = CP_OFF[name]
        cp[:, a:b] = arr

    p = np.arange(128)
    put("ident", np.eye(128, dtype=np.float32))
    put("maskU", (p[:, None] <= p[None, :]).astype(np.float32))
    corr = np.zeros((128, 4, 128), np.float32)
    k = p[:, None]
    q = p[None, :]
    for h in range(4):
        c_ = np.where(k > q, -16.0 * SLOPES[h] * (k - q), 0.0)
        c_ = np.where((k // 64) > (q // 64), -BIG, c_)
        corr[:, h, :] = c_
    extra = {"corr": corr.reshape(128, 512)}
    AB = np.zeros((128, 4, 16), np.float32)
    for h in range(4):
        for r in range(16):
            AB[:, h, r] = SLOPES[h] * (p - 127 - 128 * r)
    put("AB", AB.reshape(128, 64))
    ABS = np.zeros((128, 4, 17), np.float32)
    for h in range(4):
        for t in range(17):
            ABS[:, h, t] = SLOPES[h] * np.minimum(128 * t + p - 2063, 0)
    put("ABS", ABS.reshape(128, 68))
    put("gqa_pp", inp["g_qa"][0][p % 64][:, None])
    put("gqm_pp", inp["g_qm"][0][p % 64][:, None])
    put("gkm_rep", np.broadcast_to(inp["g_km"][0][None, :], (128, 64)))
    put("gka_rep", np.broadcast_to(inp["g_ka"][0][None, :], (128, 64)))
    put("gnorm_pp", inp["g_norm"][0].reshape(8, 128).T)
    put("gmem_pp", inp["g_mem"][0].reshape(8, 128).T)
    rows = np.ones((128, 8), np.float32)
    rows[:, 0:4] = inp["g_subln"][0][:, None]
    rows[:, 4:6] = inp["g_mh"][0][p % 64][:, None]
    put("rows_pp", rows)
    lam4 = np.concatenate([inp["lam_q1"][0], inp["lam_k1"][0], inp["lam_q2"][0], inp["lam_k2"][0]])
    extra["lam4"] = np.ascontiguousarray(np.broadcast_to(lam4[None, :], (128, 256)))
    put("bi_pp", inp["b_i"][0][p % 4][:, None])
    put("bf_pp", inp["b_f"][0][p % 4][:, None])
    put("sm_pp", inp["state_mlstm_m"][0, core][p % 4][:, None])
    SEL = np.zeros((128, 128), np.float32)
    SEL[0:4, :] = 1.0
    put("SEL", SEL)
    SEL2 = np.zeros((128, 4), np.float32)
    SEL2[0:4, 0:4] = np.eye(4, dtype=np.float32)
    put("SEL2", SEL2)
    return cp, extra


class _Stop(Exception):
    pass


def build_program(NSEQ=4, NBLK=4, DO_SAMPLE=True, STAGE=99):
    nc = bass.Bass("TRN2", target_bir_lowering=False)
    S = Sched(nc)

    def din(name, shape):
        return nc.dram_tensor(name, shape, F32, kind="ExternalInput").ap()

    def dout(name, shape):
        return nc.dram_tensor(name, shape, F32, kind="ExternalOutput").ap()

    NS = max(NSEQ, 1)
    xp = din("xp", [NS, 2048, 1024])
    memp = din("memp", [NS, 256, 1024])
    xs = din("xs", [16, 1024])
    ck = din("ck", [2048, 512])
    cv = din("cv", [2048, 512])
    sC = din("sC", [4, 64, 64])
    sn = din("sn", [4, 64])
    cmk = din("cmk", [256, 256])
    cmv = din("cmv", [256, 256])
    w_in = din("w_in", [1024, NIN])
    w_out = din("w_out", [1024, 1024])
    w_mk = din("w_mk", [1024, 256])
    w_mv = din("w_mv", [1024, 256])
    cpd = din("cp", [128, NCP])
    corrd = din("corr", [128, 512])
    lam4d = din("lam4", [128, 256])

    yp = dout("yp", [NS, 2048, 1024])
    pk = dout("pk", [NS, 2048, 512])
    pv = dout("pv", [NS, 2048, 512])
    pC = dout("pC", [NS, 4, 64, 64])
    pn = dout("pn", [NS, 4, 64])
    pm = dout("pm", [NS, 4])
    pmk = dout("pmk", [NS, 256, 256])
    pmv = dout("pmv", [NS, 256, 256])
    ys = dout("ys", [16, 1024])
    sk = dout("sk", [16, 512])
    sv = dout("sv", [16, 512])
    sCo = dout("sCo", [4, 64, 64])
    sno = dout("sno", [4, 64])
    smo = dout("smo", [1, 4])

    st = ExitStack()

    def chk(n):
        if STAGE == n:
            raise _Stop()

    def sb(name, shape, dt=F32):
        return st.enter_context(nc.sbuf_tensor("sb_" + name, shape, dt))

    cp = sb("cp", [128, NCP])
    w_in_bf = sb("w_in_bf", [128, 8, NIN], BF16)
    w_out_bf = sb("w_out_bf", [128, 8, 1024], BF16)
    w_mkv_bf = sb("w_mkv_bf", [128, 8, 512], BF16)
    identb = sb("identb", [128, 128], BF16)
    corrb = sb("corrb", [128, 4, 128], BF16)
    zerob = sb("zerob", [128, 512], BF16)
    cm05 = sb("cm05", [128, 8])
    lamt = sb("lamt", [128, 8])
    rs8 = sb("rs8", [128, 8])
    nbf = sb("nbf", [128, 1])

    kT = sb("kT", [128, 4, 2064], BF16)
    v_aug = sb("v_aug", [128, 17, 4, 130], BF16)
    mkT = sb("mkT", [128, 2, 256], BF16)
    mv_aug = sb("mv_aug", [128, 2, 4, 66], BF16)

    qT = sb("qT", [128, 4, 512], BF16)
    mix = sb("mix", [128, 4, 1024], BF16)
    OBG = sb("OBG", [128, 4, 256], BF16)
    QKB = sb("QKB", [128, 4, 512], BF16)
    QKT = sb("QKT", [64, 8, 512], BF16)
    vb_aug = sb("vb_aug", [128, 4, 4, 66], BF16)
    qmT = sb("qmT", [128, 2, 512], BF16)
    GIF = sb("GIF", [128, 4, 8])
    WE = sb("WE", [128, 4, 8])
    C0B = sb("C0B", [64, 4, 4])
    Sst = sb("Sst", [64, 4, 66])
    Sd = sb("Sd", [64, 4, 66])
    Cs = sb("Cs", [64, 4, 66], BF16)
    Bprev = sb("Bprev", [4, 1])
    MUALL = sb("MUALL", [4, 8])
    UM = sb("UM", [4, 4])
    C0 = sb("C0", [4, 4])
    CD = sb("CD", [4, 4, 4])
    mfin = sb("mfin", [4, 1])

    XT = Rot([sb("xt%d" % i, [128, 1024])[:] for i in range(2)], "xt")
    xn = sb("xn", [128, 1024], BF16)
    HT = Rot([sb("hT%d" % i, [128, 8, 128], BF16)[:] for i in range(2)], "hT")
    ZF = Rot([sb("zf%d" % i, [128, 512])[:] for i in range(2)], "zf")
    SQ = sb("sq", [128, 512])
    KOUT = Rot([sb("kvout%d" % i, [128, 512])[:] for i in range(3)], "kvout")
    VOUT = KOUT
    TH = Rot([sb("th%d" % i, [128, 512])[:] for i in range(2)], "th")
    NB = Rot([sb("nb%d" % i, [128, 512], BF16)[:] for i in range(4)], "nb")
    PT = Rot([sb("pt%d" % i, [128, 2, 512], BF16)[:] for i in range(3)], "pt")
    STT = Rot([sb("stat%d" % i, [128, 72])[:] for i in range(4)], "stat")
    EV = sb("ev", [128, 1024])

    class _RotK(Rot):
        def next(self):
            k = self.i % len(self.aps)
            self.i += 1
            return self.aps[k], "ev"
    O1 = _RotK([EV[:, i * 128:(i + 1) * 128] for i in range(4)], "o1s")
    OO = _RotK([EV[:, 512 + i * 128:512 + (i + 1) * 128] for i in range(4)], "oo")
    AT = sb("AT", [128, 4, 128], BF16)
    VW = [sb("vw%d" % i, [128, 4, 66], BF16) for i in range(4)]
    HN = sb("HN", [128, 4, 64])
    OM = HN
    GA, KGA = SQ[0:4, :], "sq"
    GB, KGB = TH.aps[0][0:4, :], ("th", 0)
    GU, KGU = TH.aps[1][0:4, :], ("th", 1)
    GW, KGW = ZF.aps[0][0:4, :], ("zf", 0)
    GE, KGE = ZF.aps[1][0:4, :], ("zf", 1)

    PBIG = st.enter_context(nc.psum_tensor("pbig", [128, 4096], F32))
    PB = [PBIG[:, i * 512:(i + 1) * 512] for i in range(8)]
    MM = Rot([PB[0], PB[1], PB[2]], "pb_mm")
    MM.keys = [("pb", 0), ("pb", 1), ("pb", 2)]
    ACC = [PB[3], PB[4], PB[5]]
    ACCK = [("pb", 3), ("pb", 4), ("pb", 5)]
    TBf = PB[6]
    TB = PB[6].bitcast(BF16)
    TBK = ("pb", 6)
    TF = PB[7]
    TFK = ("pb", 7)

    def mmnext():
        k = MM.i % 3
        MM.i += 1
        return PB[k], ("pb", k)

    def cpc(name, a=None, b=None):
        o, e = CP_OFF[name]
        if a is None:
            return cp[:, o:e]
        return cp[:, o + a:o + b]

    def dma(out, in_, r=(), w=(), q="sp", **kw):
        if q == "sp":
            S.add("sp", lambda: nc.sync.dma_start(out=out, in_=in_, **kw), r, w, dma=True)
        else:
            S.add("pool", lambda: nc.gpsimd.dma_start(out=out, in_=in_, **kw), r, w, dma=True)

    def mm(out, lhsT, rhs, start, stop, r, w):
        S.add("pe", lambda: nc.tensor.matmul(out, lhsT=lhsT, rhs=rhs, start=start, stop=stop,
                                             skip_group_check=True), r, w)

    def tr(out, in_, ident, r, w):
        S.add("pe", lambda: nc.tensor.transpose(out=out, in_=in_, identity=ident), r, w)

    def act(out, in_, func, r, w, bias=0.0, scale=1.0, accum=None):
        if accum is None:
            S.add("act", lambda: nc.scalar.activation(out=out, in_=in_, func=func, bias=bias, scale=scale), r, w)
        else:
            S.add("act", lambda: nc.scalar.activation(out=out, in_=in_, func=func, bias=bias, scale=scale,
                                                      accum_out=accum), r, w)

    def engo(e):
        return nc.vector if e == "dve" else nc.gpsimd

    def ts(e, out, in0, s1, s2, op0, op1, r, w):
        if s2 is None:
            S.add(e, lambda: engo(e).tensor_scalar(out=out, in0=in0, scalar1=s1, scalar2=None, op0=op0), r, w)
        else:
            S.add(e, lambda: engo(e).tensor_scalar(out=out, in0=in0, scalar1=s1, scalar2=s2, op0=op0, op1=op1), r, w)

    def tt(e, out, in0, in1, op, r, w):
        S.add(e, lambda: engo(e).tensor_tensor(out=out, in0=in0, in1=in1, op=op), r, w)

    def stt(out, in0, scalar, in1, op0, op1, r, w):
        S.add("dve", lambda: nc.vector.scalar_tensor_tensor(out=out, in0=in0, scalar=scalar, in1=in1,
                                                            op0=op0, op1=op1), r, w)

    def cpy(e, out, in_, r, w):
        if e == "act":
            S.add("act", lambda: nc.scalar.copy(out=out, in_=in_), r, w)
        else:
            S.add(e, lambda: engo(e).tensor_copy(out=out, in_=in_), r, w)

    def memset(e, ap, val, w):
        S.add(e, lambda: engo(e).memset(ap, val), (), w)

    def red(out, in_, op, r, w):
        S.add("dve", lambda: nc.vector.tensor_reduce(out=out, in_=in_, axis=AX.X, op=op), r, w)

    def recip(out, in_, r, w):
        S.add("dve", lambda: nc.vector.reciprocal(out=out, in_=in_), r, w)

    def scan(out, d0, d1, init, op0, op1, r, w):
        S.add("dve", lambda: nc.vector.tensor_tensor_scan(out=out, data0=d0, data1=d1, initial=init,
                                                          op0=op0, op1=op1), r, w)

    def rstd_from_ss(stt_ap, kst, c_ss, c_tmp, c_out, n, T, inv):
        ts("dve", stt_ap[0:T, c_tmp:c_tmp + n], stt_ap[0:T, c_ss:c_ss + n], inv, EPS, ALU.mult, ALU.add,
           [kst], [kst])
        tt("pool", stt_ap[0:T, c_out:c_out + n], stt_ap[0:T, c_tmp:c_tmp + n], cm05[0:T, 0:n], ALU.pow,
           [kst, "cm05"], [kst])

    dma(cp[:], cpd, w=["cp"])
    cpy("dve", identb[:], cpc("ident"), ["cp"], ["identb"])
    identf = cpc("ident")
    dma(TH.aps[0][:, :], corrd, w=[("th", 0)])
    cpy("dve", corrb[:].rearrange("p h q -> p (h q)"), TH.aps[0][:, :], [("th", 0)], ["corrb"])
    dma(TH.aps[1][:, 0:256], lam4d, w=[("th", 1)])
    memset("pool", zerob[:], 0.0, ["zerob"])
    memset("pool", cm05[:], -0.5, ["cm05"])
    memset("pool", v_aug[:, :, :, 128:129], 1.0, ["v_ones"])
    memset("pool", vb_aug[:, :, :, 64:65], 1.0, ["vb_ones"])
    memset("pool", mv_aug[:, :, :, 64:65], 1.0, ["mv_ones"])
    ts("dve", rs8[:, 0:4], cpc("rows_pp", 0, 4), 0.5 * (1.0 - LAM_INIT), None, ALU.mult, None, ["cp"], ["rs8"])
    ts("dve", rs8[:, 4:6], cpc("rows_pp", 4, 6), 0.25, None, ALU.mult, None, ["cp"], ["rs8"])
    ts("dve", rs8[:, 6:8], cpc("rows_pp", 6, 8), 0.5, None, ALU.mult, None, ["cp"], ["rs8"])
    ts("dve", nbf[:], cpc("bf_pp"), -1.0, None, ALU.mult, None, ["cp"], ["nbf"])
    wv = w_in.rearrange("(k p) n -> p k n", p=128)
    wov = w_out.rearrange("(k p) n -> p k n", p=128)
    wkv = w_mk.rearrange("(k p) n -> p k n", p=128)
    wvv = w_mv.rearrange("(k p) n -> p k n", p=128)

    WST = Rot([XT.aps[0][:, 0:512], XT.aps[1][:, 0:512], ZF.aps[0], ZF.aps[1], KOUT.aps[0], KOUT.aps[1],
               KOUT.aps[2]], "wst")
    WSTK = [("xt", 0), ("xt", 1), ("zf", 0), ("zf", 1), ("kvout", 0), ("kvout", 1), ("kvout", 2)]

    def wload(dst, src, n, scal, key, extra=None):
        if n > 512:
            h_ = n // 2
            wload(dst[:, 0:h_], src[:, 0:h_], h_, scal, key, extra)
            wload(dst[:, h_:n], src[:, h_:n], n - h_, scal, key, extra)
            return
        i_ = WST.i % len(WST.aps)
        xt, _ = WST.next()
        kx = WSTK[i_]
        dma(xt[:, 0:n], src, w=[kx])
        if extra is None:
            ts("dve", dst, xt[:, 0:n], scal, None, ALU.mult, None, [kx, "cp", "rs8"], [key])
        else:
            ts("dve", dst, xt[:, 0:n], scal, extra, ALU.mult, ALU.mult, [kx, "cp", "rs8"], [key])

    for kc in range(8):
        wload(w_mkv_bf[:, kc, 0:256], wkv[:, kc, :], 256, cpc("gmem_pp", kc, kc + 1), ("w_mkv", kc, 0))
        wload(w_mkv_bf[:, kc, 256:512], wvv[:, kc, :], 256, cpc("gmem_pp", kc, kc + 1), ("w_mkv", kc, 1))
    for kc in range(8):
        g = cpc("gnorm_pp", kc, kc + 1)
        wload(w_in_bf[:, kc, 0:1024], wv[:, kc, 0:1024], 1024, g, ("w_in", kc, 0))
        wload(w_in_bf[:, kc, 1024:2048], wv[:, kc, 1024:2048], 1024, g, ("w_in", kc, 1))
        wload(w_in_bf[:, kc, 2048:2304], wv[:, kc, 2048:2304], 256, g, ("w_in", kc, 2))
        wload(w_in_bf[:, kc, 2304:2560], wv[:, kc, 2304:2560], 256, g, ("w_in", kc, 2), 0.125)
        wload(w_in_bf[:, kc, 2560:3072], wv[:, kc, 2560:3072], 512, g, ("w_in", kc, 2))
        wload(w_in_bf[:, kc, 3072:3840], wv[:, kc, 3080:3848], 768, g, ("w_in", kc, 3))
        wload(w_in_bf[:, kc, 3840:3848], wv[:, kc, 3072:3080], 8, g, ("w_in", kc, 3))
    for kc in range(8):
        wload(w_out_bf[:, kc, :], wov[:, kc, :], 1024, rs8[:, kc:kc + 1], ("w_out", kc))
    W_IN_K = [("w_in", kc, i) for kc in range(8) for i in range(4)]
    W_OUT_K = [("w_out", kc) for kc in range(8)]
    W_MKV_K = [("w_mkv", kc, i) for kc in range(8) for i in range(2)]
    l4 = TH.aps[1]
    tt("dve", SQ[:, 0:64], l4[:, 0:64], l4[:, 64:128], ALU.mult, [("th", 1)], ["sq"])
    red(lamt[:, 0:1], SQ[:, 0:64], ALU.add, ["sq"], ["lamt"])
    tt("dve", SQ[:, 64:128], l4[:, 128:192], l4[:, 192:256], ALU.mult, [("th", 1)], ["sq"])
    red(lamt[:, 1:2], SQ[:, 64:128], ALU.add, ["sq"], ["lamt"])
    act(lamt[:, 2:4], lamt[:, 0:2], AF.Exp, ["lamt"], ["lamt"])
    stt(lamt[:, 4:5], lamt[:, 2:3], LAM_INIT, lamt[:, 3:4], ALU.add, ALU.subtract, ["lamt"], ["lamt"])
    ts("dve", lamt[:, 5:6], lamt[:, 4:5], -1.0, None, ALU.mult, None, ["lamt"], ["lamt"])

    def front_load(src_ap, T):
        xt, kx = XT.next()
        dma(xt[0:T, :], src_ap, w=[kx])
        return xt, kx

    def front_a1(xt, kx, T):
        sa, ks = STT.next()
        act(xn[0:T, :], xt[0:T, :], AF.Square, [kx], ["xn", ks], accum=sa[0:T, 0:1])
        rstd_from_ss(sa, ks, 0, 1, 2, 1, T, 1.0 / 1024)
        return sa, ks

    def front_a2(xt, kx, sa, ks, T):
        ts("dve", xn[0:T, :], xt[0:T, :], sa[0:T, 2:3], None, ALU.mult, None, [kx, ks], ["xn"])

    def front_a(xt, kx, T):
        sa, ks = front_a1(xt, kx, T)
        front_a2(xt, kx, sa, ks, T)
        return sa, ks

    def front_b(xt, kx, sa, ks, T):
        for kc in range(8):
            tr(TB[:, kc * 128:kc * 128 + T], xn[0:T, kc * 128:(kc + 1) * 128], identb[0:T, 0:T],
               ["xn", "identb"], [TBK])
        hT, kh = HT.next()
        cpy("act", hT[:, :, 0:T], TB.rearrange("p (k t) -> p k t", k=8)[:, :, 0:T], [TBK], [kh])
        return xt, kx, hT, kh, sa, ks

    def front_compute(xt, kx, T):
        sa, ks = front_a(xt, kx, T)
        return front_b(xt, kx, sa, ks, T)

    def front(src_ap, T):
        xt, kx = front_load(src_ap, T)
        return front_compute(xt, kx, T)

    def group_norm_rstd(src, ksrc, T, ng, sa, ks, base):
        tt("dve", SQ[0:T, 0:ng * 64], src, src, ALU.mult, [ksrc], ["sq"])
        red(sa[0:T, base:base + ng], SQ[0:T, 0:ng * 64].rearrange("p (g d) -> p g d", d=64), ALU.add,
            ["sq"], [ks])
        rstd_from_ss(sa, ks, base, base + ng, base + 2 * ng, ng, T, 1.0 / 64)
        return sa[0:T, base + 2 * ng:base + 3 * ng]

    def phase1_tile(fr, T, ti, j, k_dst, v_dst, later, hook=None):
        xt, kx, hT, kh, sa, ks = fr
        tok = slice(j * 128, j * 128 + T)
        ktok = slice(ti * 128, ti * 128 + T)
        for g in range(8):
            c0 = g * 512
            n = min(512, NIN - c0)
            ps, kp = mmnext()
            for kc in range(8):
                mm(ps[0:T, 0:n], hT[:, kc, 0:T], w_in_bf[:, kc, c0:c0 + n], kc == 0, kc == 7,
                   [kh] + W_IN_K, [kp])
            for it in later:
                it[0] -= 1
            ready = [it for it in later if it[0] <= 0]
            for it in ready:
                later.remove(it)
            for it in ready:
                it[1]()
            if hook is not None:
                hook(g)
            if g == 0 or g == 1:
                zf, kz = ZF.next()
                cpy("act", zf[0:T, :], ps[0:T, 0:512], [kp], [kz])
                r8 = group_norm_rstd(zf[0:T, :], kz, T, 8, sa, ks, 8 if g == 0 else 32)

                def b01(g=g, zf=zf, kz=kz, r8=r8):
                    nb, kn = NB.next()
                    if g == 0:
                        tt("pool", nb[0:T, :].rearrange("p (g d) -> p g d", d=64),
                           zf[0:T, :].rearrange("p (g d) -> p g d", d=64),
                           r8.unsqueeze(2).to_broadcast([T, 8, 64]), ALU.mult, [kz, ks], [kn])
                    else:
                        tt("dve", zf[0:T, :].rearrange("p (g d) -> p g d", d=64),
                           zf[0:T, :].rearrange("p (g d) -> p g d", d=64),
                           r8.unsqueeze(2).to_broadcast([T, 8, 64]), ALU.mult, [kz, ks], [kz])
                        ko, kk = KOUT.next()
                        tt("dve", ko[0:T, :].rearrange("p (g d) -> p g d", d=64),
                           zf[0:T, :].rearrange("p (g d) -> p g d", d=64),
                           cpc("gka_rep")[0:T, :].unsqueeze(1).to_broadcast([T, 8, 64]), ALU.mult,
                           [kz, "cp"], [kk])
                        dma(k_dst, ko[0:T, :], r=[kk])
                        cpy("dve", nb[0:T, :], ko[0:T, :], [kk], [kn])

                    def d01(nb=nb, kn=kn):
                        for h in range(4):
                            tr(TB[:, h * 128:h * 128 + T], nb[0:T, h * 128:(h + 1) * 128], identb[0:T, 0:T],
                               [kn, "identb"], [TBK])
                        src = TB[:, 0:512].rearrange("p (h t) -> p h t", h=4)[:, :, 0:T]
                        if g == 0:
                            ts("dve", qT[:, :, tok], src, cpc("gqa_pp"), None, ALU.mult, None, [TBK, "cp"],
                               [("qT", j)])
                        else:
                            cpy("act", kT[:, :, ktok], src, [TBK], [("kT", ti)])
                    later.append([DELAY, d01])
                later.append([2, b01])
            elif g == 2:
                vo, kv = VOUT.next()
                cpy("act", vo[0:T, :], ps[0:T, 0:512], [kp], [kv])
                dma(v_dst, vo[0:T, :], r=[kv])
                cpy("dve", v_aug[0:T, ti, :, 0:128], vo[0:T, :].rearrange("p (h d) -> p h d", h=4),
                    [kv, "v_ones"], [("v", ti)])
            elif g == 3:
                th, kt = TH.next()
                act(th[0:T, :], ps[0:T, 0:512], AF.Tanh, [kp], [kt], scale=0.5)
                stt(mix[0:T, j, 0:512], th[0:T, :], 1.0, ps[0:T, 0:512], ALU.add, ALU.mult,
                    [kt, kp], [("mix", j, 0)])
            elif g == 4:
                cpy("act", QKB[0:T, j, :], ps[0:T, 0:512], [kp], [("QKB", j)])

                def d4():
                    for i in range(8):
                        tr(TB[0:64, i * 128:i * 128 + T], QKB[0:T, j, i * 64:(i + 1) * 64], identb[0:T, 0:T],
                           [("QKB", j), "identb"], [TBK])
                    cpy("dve", QKT[:, :, tok], TB[0:64, :].rearrange("p (h t) -> p h t", h=8)[:, :, 0:T],
                        [TBK], [("QKT", j)])
                later.append([DELAY, d4])
            elif g == 5:
                cpy("dve", vb_aug[0:T, j, :, 0:64], ps[0:T, 0:256].rearrange("p (h d) -> p h d", h=4),
                    [kp, "vb_ones"], [("vb", j)])
                act(OBG[0:T, j, :], ps[0:T, 256:512], AF.Tanh, [kp], [("OBG", j)], scale=0.5)
            elif g == 6:
                th, kt = TH.next()
                act(th[0:T, 0:256], ps[0:T, 0:256], AF.Tanh, [kp], [kt], scale=0.5)
                stt(th[0:T, 256:512], th[0:T, 0:256], 1.0, ps[0:T, 0:256], ALU.add, ALU.mult, [kt, kp], [kt])
                stt(mix[0:T, j, 512:768], OBG[0:T, j, :], 1.0, th[0:T, 256:512], ALU.add, ALU.mult,
                    [("OBG", j), kt], [("mix", j, 1)])
                zf, kz = ZF.next()
                cpy("act", zf[0:T, 0:256], ps[0:T, 256:512], [kp], [kz])
                r4 = group_norm_rstd(zf[0:T, 0:256], kz, T, 4, sa, ks, 56)

                def b6(zf=zf, kz=kz, r4=r4):
                    nb, kn = NB.next()
                    tt("pool", nb[0:T, 0:256].rearrange("p (g d) -> p g d", d=64),
                       zf[0:T, 0:256].rearrange("p (g d) -> p g d", d=64),
                       r4.unsqueeze(2).to_broadcast([T, 4, 64]), ALU.mult, [kz, ks], [kn])

                    def d6(nb=nb, kn=kn):
                        for hp in range(2):
                            tr(TB[:, hp * 128:hp * 128 + T], nb[0:T, hp * 128:(hp + 1) * 128], identb[0:T, 0:T],
                               [kn, "identb"], [TBK])
                        ts("dve", qmT[:, :, tok], TB[:, 0:256].rearrange("p (h t) -> p h t", h=2)[:, :, 0:T],
                           cpc("gqm_pp"), None, ALU.mult, None, [TBK, "cp"], [("qmT", j)])
                    later.append([DELAY, d6])
                later.append([2, b6])
            else:
                th, kt = TH.next()
                act(th[0:T, 0:256], ps[0:T, 0:256], AF.Tanh, [kp], [kt], scale=0.5)
                stt(mix[0:T, j, 768:1024], th[0:T, 0:256], 1.0, ps[0:T, 0:256], ALU.add, ALU.mult,
                    [kt, kp], [("mix", j, 2)])
                cpy("dve", GIF[0:T, j, :], ps[0:T, 256:264], [kp], [("GIF", j)])

    def phase1_block(tiles, T, pre=None):
        later = []
        if pre is None:
            xt, kx = front_load(tiles[0][0], T)
            fr = front_compute(xt, kx, T)
        else:
            fr = front_b(pre[0], pre[1], pre[2], pre[3], T)
        for idx, (src_ap, ti, j, k_dst, v_dst) in enumerate(tiles):
            nxt = {}
            hook = None
            if idx + 1 < len(tiles):
                nx = front_load(tiles[idx + 1][0], T)

                def hook(g, nx=nx, nxt=nxt):
                    if g == 0:
                        nxt["a"] = front_a1(nx[0], nx[1], T)
                    elif g == 2:
                        front_a2(nx[0], nx[1], nxt["a"][0], nxt["a"][1], T)
                    elif g == 5:
                        nxt["fr"] = front_b(nx[0], nx[1], nxt["a"][0], nxt["a"][1], T)
            phase1_tile(fr, T, ti, j, k_dst, v_dst, later, hook)
            fr = nxt.get("fr")
        while later:
            later.pop(0)[1]()

    def gates_block(T, ntile):
        NQ = ntile * 128
        ibT = TF[0:4, 0:NQ]
        fbT = TBf[0:4, 0:NQ]
        for j in range(ntile):
            tr(TF[0:4, j * 128:j * 128 + T], GIF[0:T, j, 0:4], identf[0:T, 0:T], [("GIF", j), "cp"], [TFK])
            tr(TBf[0:4, j * 128:j * 128 + T], GIF[0:T, j, 4:8], identf[0:T, 0:T], [("GIF", j), "cp"], [TBK])
        if T < 128:
            memset("dve", GA[:, 0:NQ], 0.0, [KGA])
        cs = slice(0, T) if ntile == 1 else slice(0, NQ)
        act(GA[:, cs], fbT[:, cs], AF.Exp, [TBK, "nbf"], [KGA], bias=nbf[0:4, 0:1], scale=-1.0)
        act(GA[:, cs], GA[:, cs], AF.Ln, [KGA], [KGA], bias=1.0)
        scan(GB[:, cs], zerob[0:4, cs], GA[:, cs], Bprev[:, 0:1], ALU.add, ALU.subtract,
             ["zerob", KGA, "Bprev"], [KGB])
        stt(GU[:, cs], ibT[:, cs], cpc("bi_pp")[0:4, :], GB[:, cs], ALU.add, ALU.subtract,
            [TFK, "cp", KGB], [KGU])
        if ntile == 1:
            red(UM[:, 0:1], GU[:, cs], ALU.max, [KGU], ["UM"])
        else:
            red(UM[:, 0:ntile], GU[:, cs].rearrange("p (c t) -> p c t", c=ntile), ALU.max, [KGU], ["UM"])
        scan(MUALL[:, 1:1 + ntile], UM[:, 0:ntile], UM[:, 0:ntile], MUALL[:, 0:1], ALU.max, ALU.max,
             ["UM", "MUALL"], ["MUALL"])
        tt("dve", C0[:, 0:ntile], MUALL[:, 0:ntile], MUALL[:, 1:1 + ntile], ALU.subtract, ["MUALL"], ["C0"])
        if ntile == 1:
            mub = MUALL[:, 1:2].to_broadcast([4, T])
            tt("dve", GW[:, cs], GU[:, cs], mub, ALU.subtract, [KGU, "MUALL"], [KGW])
            tt("dve", GE[:, cs], GB[:, cs], mub, ALU.add, [KGB, "MUALL"], [KGE])
        else:
            mub = MUALL[:, 1:1 + ntile].unsqueeze(2).to_broadcast([4, ntile, 128])
            tt("dve", GW[:, cs].rearrange("p (c t) -> p c t", c=ntile),
               GU[:, cs].rearrange("p (c t) -> p c t", c=ntile), mub, ALU.subtract, [KGU, "MUALL"], [KGW])
            tt("dve", GE[:, cs].rearrange("p (c t) -> p c t", c=ntile),
               GB[:, cs].rearrange("p (c t) -> p c t", c=ntile), mub, ALU.add, [KGB, "MUALL"], [KGE])
        return lambda: gates_part2(T, ntile, cs, NQ)

    def gates_part2(T, ntile, cs, NQ):
        act(C0[:, 0:ntile], C0[:, 0:ntile], AF.Exp, ["C0"], ["C0"])
        act(GW[:, cs], GW[:, cs], AF.Exp, [KGW], [KGW])
        act(GE[:, cs], GE[:, cs], AF.Exp, [KGE], [KGE], scale=-1.0)
        for j in range(ntile):
            tr(TF[0:T, j * 8:j * 8 + 4], GW[0:4, j * 128:j * 128 + T], identf[0:4, 0:4], [KGW, "cp"], [TFK])
            tr(TF[0:T, j * 8 + 4:j * 8 + 8], GE[0:4, j * 128:j * 128 + T], identf[0:4, 0:4], [KGE, "cp"], [TFK])
        cpy("dve", WE[0:T, 0:ntile, :], TF[0:T, 0:ntile * 8].rearrange("p (c e) -> p c e", e=8), [TFK], ["WE"])
        tt("dve", CD[:, 0:ntile, :], C0[:, 0:ntile].unsqueeze(2).to_broadcast([4, ntile, 4]),
           cpc("SEL2")[0:4, :].unsqueeze(1).to_broadcast([4, ntile, 4]), ALU.mult, ["C0", "cp"], ["CD"])
        mm(TBf[0:64, 0:ntile * 4], cpc("SEL")[0:4, 0:64], CD[:, 0:ntile, :].rearrange("p c e -> p (c e)"), True, True,
           ["cp", "CD"], [TBK])
        cpy("dve", C0B[:, 0:ntile, :], TBf[0:64, 0:ntile * 4].rearrange("p (c e) -> p c e", e=4), [TBK], ["C0B"])
        last = T - 1 if ntile == 1 else NQ - 1
        cpy("dve", Bprev[:, 0:1], GB[:, last:last + 1], [KGB], ["Bprev"])
        cpy("dve", MUALL[:, 0:1], MUALL[:, ntile:ntile + 1], ["MUALL"], ["MUALL"])

    def attention_block(T, ntile, ktiles, sample):
        NQ = ntile * T
        ABK = [4, 5, 6]
        qkeys = [("qT", jj) for jj in range(ntile)]
        steps = [(h, t, nk, dsub) for h in range(4) for (t, nk, dsub) in ktiles]
        state = {"n": 0}

        def emit_S(st_):
            h, t, nk, dsub = st_
            c0 = 0 if dsub is None else dsub * T
            ncol = NQ - c0
            b0 = 2 * (state["n"] % 2)
            state["n"] += 1
            kps = [("pb", b0), ("pb", b0 + 1)]
            for m in range(2):
                pr = slice(64 * m, 64 * m + 64)
                mm(PB[b0 + m][0:nk, 0:ncol], kT[pr, h, t * 128:t * 128 + nk], qT[pr, h, c0:NQ],
                   True, dsub is None, [("kT", t)] + qkeys, [kps[m]])
            if dsub is not None:
                for m in range(2):
                    mm(PB[b0 + m][0:nk, 0:T], identb[0:nk, 0:nk], corrb[0:nk, h, 0:T], False, True,
                       ["identb", "corrb"], [kps[m]])
            if KEEPWARM and not sample:
                mm(PB[7][:, :], zerob[:, 0:128], zerob[:, :], True, True, ["zerob"], [("pb", 7)])
            pt, kpt = PT.next()
            if sample or h != 0:
                calls = [(c0, NQ)]
            else:
                calls = [(max(c, c0), c + 256) for c in range(0, NQ, 256) if c + 256 > c0]
            pair = PBIG[0:nk, b0 * 512:(b0 + 2) * 512].rearrange("p (m c) -> p m c", m=2)
            for (ca, cb) in calls:
                if sample:
                    bias = cpc("ABS")[0:nk, h * 17 + t:h * 17 + t + 1]
                else:
                    tref = ktiles[-1][0] - (ntile - 1) + (cb - 1) // 128
                    bias = cpc("AB")[0:nk, h * 16 + (tref - t):h * 16 + (tref - t) + 1]
                act(pt[0:nk, :, ca - c0:cb - c0], pair[:, :, ca - c0:cb - c0], AF.Exp, kps + ["cp"], [kpt],
                    bias=bias, scale=0.125)
            return pt, kpt, c0

        def emit_PV(st_, pt, kpt, c0):
            h, t, nk, dsub = st_
            for m in range(2):
                for i in range(ntile):
                    if i * T < c0:
                        continue
                    a_ = m * 4 + i
                    bnk = ABK[a_ // 3]
                    o = (a_ % 3) * 129
                    mm(PB[bnk][0:T, o:o + 129], pt[0:nk, m, i * T - c0:(i + 1) * T - c0],
                       v_aug[0:nk, t, h, 0:129], False, True, [kpt, ("v", t), "v_ones"], [("pb", bnk)])

        def evac(h):
            bset = ABK
            sa, ks = STT.next()
            if ntile == 4:
                for b_ in range(3):
                    nacc = 3 if b_ < 2 else 2
                    recip(sa[0:T, 3 * b_:3 * b_ + nacc], PB[bset[b_]][0:T, 128:129 * nacc:129],
                          [("pb", bset[b_])], [ks])
                for b_ in range(3):
                    nacc = 3 if b_ < 2 else 2
                    cpy("dve", EV[0:T, 384 * b_:384 * b_ + 128 * nacc].rearrange("p (a d) -> p a d", d=128),
                        PB[bset[b_]][0:T, 0:129 * nacc].rearrange("p (a e) -> p a e", e=129)[:, :, 0:128],
                        [("pb", bset[b_])], ["ev"])
            else:
                recip(sa[0:T, 0:1], PB[bset[0]][0:T, 128:129], [("pb", bset[0])], [ks])
                recip(sa[0:T, 4:5], PB[bset[1]][0:T, 129 + 128:129 + 129], [("pb", bset[1])], [ks])
                cpy("dve", EV[0:T, 0:128], PB[bset[0]][0:T, 0:128], [("pb", bset[0])], ["ev"])
                cpy("dve", EV[0:T, 512:640], PB[bset[1]][0:T, 129:257], [("pb", bset[1])], ["ev"])
            tt("dve", sa[0:T, 4:4 + ntile], sa[0:T, 4:4 + ntile], lamt[0:T, 5:6].to_broadcast([T, ntile]),
               ALU.mult, [ks, "lamt"], [ks])
            for i in range(ntile):
                e1 = EV[0:T, i * 128:(i + 1) * 128]
                e2 = EV[0:T, (4 + i) * 128:(5 + i) * 128]
                ts("dve", e1, e1, sa[0:T, i:i + 1], None, ALU.mult, None, ["ev", ks], ["ev"])
                stt(e2, e2, sa[0:T, 4 + i:5 + i], e1, ALU.mult, ALU.add, ["ev", ks], ["ev"])
                S.add("dve", (lambda e1=e1, e2=e2, sa=sa, i=i: nc.vector.scalar_tensor_tensor(
                    out=e1, in0=e2, scalar=1.0, in1=e2, op0=ALU.mult, op1=ALU.mult,
                    accum_out=sa[0:T, 8 + i:9 + i])), ["ev"], ["ev", ks])
            ts("dve", sa[0:T, 12:12 + ntile], sa[0:T, 8:8 + ntile], 1.0 / 128, EPS, ALU.mult, ALU.add, [ks], [ks])
            tt("pool", sa[0:T, 16:16 + ntile], sa[0:T, 12:12 + ntile], cm05[0:T, 0:ntile], ALU.pow,
               [ks, "cm05"], [ks])

            def fin(h=h, sa=sa, ks=ks):
                for ii in range(ntile):
                    stt(mix[0:T, ii, h * 128:(h + 1) * 128], EV[0:T, (4 + ii) * 128:(5 + ii) * 128],
                        sa[0:T, 16 + ii:17 + ii], mix[0:T, ii, h * 128:(h + 1) * 128], ALU.mult, ALU.mult,
                        ["ev", ks, ("mix", ii, 0)], [("mix", ii, 0)])
            return fin

        prev = None
        fins = []
        for k, st_ in enumerate(steps):
            h = st_[0]
            new_head = (k == 0 or steps[k - 1][0] != h)
            pt, kpt, c0 = emit_S(st_)
            if prev is not None:
                emit_PV(*prev)
                if new_head:
                    fins.append(evac(prev[0][0]))
            if new_head:
                if NWARM and k > 0 and not sample:
                    for _ in range(NWARM):
                        mm(PB[7][:, :], zerob[:, 0:128], zerob[:, :], True, True, ["zerob"], [("pb", 7)])
                for b_ in range(3):
                    mm(PB[ABK[b_]][0:T, :], zerob[:, 0:T], zerob[:, :], True, True, ["zerob"], [("pb", ABK[b_])])
            if len(fins) > 0 and not new_head and (k == 0 or steps[k - 2][0] == h):
                for f_ in fins:
                    f_()
                del fins[:]
            prev = (st_, pt, kpt, c0)
        emit_PV(*prev)
        fins.append(evac(prev[0][0]))
        for f_ in fins:
            f_()

    def mem_block(T, ntile):
        NQ = ntile * T
        banks = [(ACC[0], ACCK[0]), (ACC[1], ACCK[1]), (ACC[2], ACCK[2]), (TF, TFK)]
        for i in range(ntile):
            mm(banks[i][0][0:T, :], zerob[:, 0:T], zerob[:, :], True, True, ["zerob"], [banks[i][1]])
        prevm = None
        for h in range(4):
            pr = slice(64 * (h % 2), 64 * (h % 2) + 64)
            for nt in range(2):
                ps, kp = mmnext()
                mm(ps[:, 0:NQ], mkT[pr, h // 2, nt * 128:(nt + 1) * 128], qmT[pr, h // 2, 0:NQ], True, True,
                   ["mkT"] + [("qmT", jj) for jj in range(ntile)], [kp])
                pt, kpt = PT.next()
                act(pt[:, 0, 0:NQ], ps[:, 0:NQ], AF.Exp, [kp], [kpt], scale=0.125)
                if prevm is not None:
                    ph, pnt, ppt, pkpt = prevm
                    for i in range(ntile):
                        mm(banks[i][0][0:T, ph * 65:(ph + 1) * 65], ppt[:, 0, i * T:(i + 1) * T],
                           mv_aug[:, pnt, ph, 0:65], False, True, [pkpt, "mv", "mv_ones"], [banks[i][1]])
                prevm = (h, nt, pt, kpt)
        ph, pnt, ppt, pkpt = prevm
        for i in range(ntile):
            mm(banks[i][0][0:T, ph * 65:(ph + 1) * 65], ppt[:, 0, i * T:(i + 1) * T],
               mv_aug[:, pnt, ph, 0:65], False, True, [pkpt, "mv", "mv_ones"], [banks[i][1]])
        for i in range(ntile):
            bk, kb = banks[i]
            sa, ks = STT.next()
            recip(sa[0:T, 0:4], bk[0:T, 64:260:65], [kb], [ks])
            tt("dve", OM[0:T, :, :], bk[0:T, 0:260].rearrange("p (h e) -> p h e", e=65)[:, :, 0:64],
               sa[0:T, 0:4].unsqueeze(2).to_broadcast([T, 4, 64]), ALU.mult, [kb, ks], ["HN"])
            tt("dve", mix[0:T, i, 768:1024], OM[0:T, :, :].rearrange("p h d -> p (h d)"),
               mix[0:T, i, 768:1024], ALU.mult, ["HN", ("mix", i, 2)], [("mix", i, 2)])

    def mlstm_block(T, ntile):
        pend_a = []
        pend_b = []
        for j in range(ntile):
            tt("pool", VW[j][0:T, :, 0:65], vb_aug[0:T, j, :, 0:65],
               WE[0:T, j, 0:4].unsqueeze(2).to_broadcast([T, 4, 65]), ALU.mult,
               [("vb", j), "vb_ones", "WE"], [("vw", j)])
        for j in range(ntile):
            tok = slice(j * 128, j * 128 + T)
            vw = VW[j]
            kvw = ("vw", j)
            tt("dve", Sd[:, :, 0:65], Sst[:, :, 0:65], C0B[:, j, :].unsqueeze(2).to_broadcast([64, 4, 65]), ALU.mult,
               ["Sst", "C0B"], ["Sd"])
            cpy("act", Cs[:, :, 0:65], Sd[:, :, 0:65], ["Sd"], ["Cs"])
            psA, kA = mmnext()
            for h in range(4):
                mm(psA[0:T, h * 128:h * 128 + T], QKT[:, 4 + h, tok], QKT[:, h, tok], True, True,
                   [("QKT", j)], [kA])
            psS, kS = ACC[2], ACCK[2]
            for h in range(4):
                mm(psS[0:64, h * 65:(h + 1) * 65], QKB[0:T, j, 256 + h * 64:256 + (h + 1) * 64],
                   vw[0:T, h, 0:65], True, True, [("QKB", j), kvw], [kS])
            tt("dve", Sst[:, :, 0:65], Sd[:, :, 0:65], psS[0:64, 0:260].rearrange("p (c e) -> p c e", e=65), ALU.add,
               ["Sd", kS], ["Sst"])
            tt("dve", AT[0:T, :, 0:T], psA[0:T, :].rearrange("p (h t) -> p h t", h=4)[:, :, 0:T],
               cpc("maskU")[0:T, 0:T].unsqueeze(1).to_broadcast([T, 4, T]), ALU.mult, [kA, "cp"], ["AT"])
            psO, kO = ACC[j % 2], ACCK[j % 2]
            for h in range(4):
                mm(psO[0:T, h * 65:(h + 1) * 65], AT[0:T, h, 0:T], vw[0:T, h, 0:65], True, False, ["AT", kvw], [kO])
                mm(psO[0:T, h * 65:(h + 1) * 65], QKT[:, h, tok], Cs[:, h, 0:65], False, True,
                   [("QKT", j), "Cs"], [kO])

            def evac_a(j=j, psO=psO, kO=kO):
                sa, ks = STT.next()
                cpy("dve", sa[0:T, 20:24], psO[0:T, 64:260:65], [kO], [ks])
                stt(sa[0:T, 24:28], sa[0:T, 20:24], -1.0, sa[0:T, 20:24], ALU.mult, ALU.max, [ks], [ks])
                tt("dve", sa[0:T, 0:4], sa[0:T, 24:28], WE[0:T, j, 4:8], ALU.max, [ks, "WE"], [ks])
                recip(sa[0:T, 4:8], sa[0:T, 0:4], [ks], [ks])
                tt("dve", HN[0:T, :, :], psO[0:T, 0:260].rearrange("p (h e) -> p h e", e=65)[:, :, 0:64],
                   sa[0:T, 4:8].unsqueeze(2).to_broadcast([T, 4, 64]), ALU.mult, [kO, ks], ["HN"])
                r4 = group_norm_rstd(HN[0:T, :, :].rearrange("p h d -> p (h d)"), "HN", T, 4, sa, ks, 8)

                def evac_b():
                    tt("dve", HN[0:T, :, :], HN[0:T, :, :], r4.unsqueeze(2).to_broadcast([T, 4, 64]), ALU.mult,
                       ["HN", ks], ["HN"])
                    tt("dve", mix[0:T, j, 512:768], HN[0:T, :, :].rearrange("p h d -> p (h d)"),
                       mix[0:T, j, 512:768], ALU.mult, ["HN", ("mix", j, 1)], [("mix", j, 1)])
                return evac_b
            pend_a.append(evac_a)
            if len(pend_a) == 2:
                for f_ in pend_b:
                    f_()
                del pend_b[:]
                pend_b.append(pend_a.pop(0)())
        for f_ in pend_b:
            f_()
        del pend_b[:]
        while pend_a:
            pend_a.pop(0)()()

    def phase3_block(tiles, T):
        YB = [[(PB[3], ("pb", 3)), (PB[4], ("pb", 4))], [(PB[5], ("pb", 5)), (PB[0], ("pb", 0))]]
        xts = {}

        def stage_t(idx):
            src_ap, dst_ap, j = tiles[idx]
            mT, kmT = HT.next()
            for kc in range(8):
                tr(TB[:, kc * 128:kc * 128 + T], mix[0:T, j, kc * 128:(kc + 1) * 128], identb[0:T, 0:T],
                   [("mix", j, 0), ("mix", j, 1), ("mix", j, 2), "identb"], [TBK])
            cpy("act", mT[:, :, 0:T], TB.rearrange("p (k t) -> p k t", k=8)[:, :, 0:T], [TBK], [kmT])
            xt, kx = XT.next()
            dma(xt[0:T, :], src_ap, w=[kx])
            xts[idx] = (mT, kmT, xt, kx)

        def stage_m(idx):
            src_ap, dst_ap, j = tiles[idx]
            mT, kmT, xt, kx = xts[idx]
            for half in range(2):
                ps, kp = YB[idx % 2][half]
                for kc in range(8):
                    mm(ps[0:T, :], mT[:, kc, 0:T], w_out_bf[:, kc, half * 512:(half + 1) * 512], kc == 0, kc == 7,
                       [kmT] + W_OUT_K, [kp])
                tt("dve", xt[0:T, half * 512:(half + 1) * 512], ps[0:T, :], xt[0:T, half * 512:(half + 1) * 512],
                   ALU.add, [kp, kx], [kx])
            dma(dst_ap, xt[0:T, :], r=[kx])

        stage_t(0)
        for idx in range(len(tiles)):
            if idx + 1 < len(tiles):
                stage_t(idx + 1)
            stage_m(idx)

    def memkv_seq(s):
        for nt in range(2):
            xt, kx, hT, kh, sa, ks = front(memp[s, nt * 128:(nt + 1) * 128, :], 128)
            ps, kp = mmnext()
            for kc in range(8):
                mm(ps[:, :], hT[:, kc, :], w_mkv_bf[:, kc, :], kc == 0, kc == 7, [kh] + W_MKV_K, [kp])
            zf, kz = ZF.next()
            cpy("act", zf[:, :], ps[:, :], [kp], [kz])
            dma(pmv[s, nt * 128:(nt + 1) * 128, :], zf[:, 256:512], r=[kz])
            cpy("pool", mv_aug[:, nt, :, 0:64], zf[:, 256:512].rearrange("p (h d) -> p h d", h=4),
                [kz, "mv_ones"], ["mv"])
            r4 = group_norm_rstd(zf[:, 0:256], kz, 128, 4, sa, ks, 8)
            tt("dve", zf[:, 0:256].rearrange("p (g d) -> p g d", d=64),
               zf[:, 0:256].rearrange("p (g d) -> p g d", d=64),
               r4.unsqueeze(2).to_broadcast([128, 4, 64]), ALU.mult, [kz, ks], [kz])
            ko, kk = KOUT.next()
            tt("dve", ko[:, 0:256].rearrange("p (g d) -> p g d", d=64),
               zf[:, 0:256].rearrange("p (g d) -> p g d", d=64),
               cpc("gkm_rep").unsqueeze(1).to_broadcast([128, 4, 64]), ALU.mult, [kz, "cp"], [kk])
            dma(pmk[s, nt * 128:(nt + 1) * 128, :], ko[:, 0:256], r=[kk])
            nb, kn = NB.next()
            cpy("pool", nb[:, 0:256], ko[:, 0:256], [kk], [kn])
            for hp in range(2):
                tr(TB[:, hp * 128:(hp + 1) * 128], nb[:, hp * 128:(hp + 1) * 128], identb[:, :],
                   [kn, "identb"], [TBK])
            cpy("act", mkT[:, :, nt * 128:(nt + 1) * 128], TB[:, 0:256].rearrange("p (h t) -> p h t", h=2),
                [TBK], ["mkT"])

    def state_out(dC, dn, dm):
        for h in range(4):
            dma(dC[h], Sst[:, h, 0:64], r=["Sst"])
            dma(dn[h].unsqueeze(1), Sst[:, h, 64:65], r=["Sst"])
        tt("dve", mfin[:, :], Bprev[:, 0:1], MUALL[:, 0:1], ALU.add, ["Bprev", "MUALL"], ["mfin"])
        dma(dm, mfin[:, :], r=["mfin"])

    try:
        chk(1)
        if DO_SAMPLE:
            for t in range(16):
                i1 = WST.i % len(WST.aps)
                xk, _ = WST.next()
                i2 = WST.i % len(WST.aps)
                xv, _ = WST.next()
                kx, kx2 = WSTK[i1], WSTK[i2]
                dma(xk[:, 0:512], ck[t * 128:(t + 1) * 128, :], w=[kx])
                dma(xv[:, 0:512], cv[t * 128:(t + 1) * 128, :], w=[kx2])
                nb, kn = NB.next()
                cpy("dve", nb[:, :], xk[:, 0:512], [kx], [kn])
                cpy("dve" if t % 2 == 0 else "pool", v_aug[:, t, :, 0:128],
                    xv[:, 0:512].rearrange("p (h d) -> p h d", h=4), [kx2, "v_ones"], [("v", t)])
                for h in range(4):
                    tr(TB[:, h * 128:(h + 1) * 128], nb[:, h * 128:(h + 1) * 128], identb[:, :], [kn, "identb"], [TBK])
                cpy("act", kT[:, :, t * 128:(t + 1) * 128], TB[:, 0:512].rearrange("p (h t) -> p h t", h=4),
                    [TBK], [("kT", t)])
            for nt in range(2):
                xt, kx = XT.next()
                dma(xt[:, 0:256], cmk[nt * 128:(nt + 1) * 128, :], w=[kx])
                dma(xt[:, 256:512], cmv[nt * 128:(nt + 1) * 128, :], w=[kx])
                nb, kn = NB.next()
                cpy("dve", nb[:, 0:256], xt[:, 0:256], [kx], [kn])
                cpy("pool", mv_aug[:, nt, :, 0:64], xt[:, 256:512].rearrange("p (h d) -> p h d", h=4),
                    [kx, "mv_ones"], ["mv"])
                for hp in range(2):
                    tr(TB[:, hp * 128:(hp + 1) * 128], nb[:, hp * 128:(hp + 1) * 128], identb[:, :],
                       [kn, "identb"], [TBK])
                cpy("act", mkT[:, :, nt * 128:(nt + 1) * 128], TB[:, 0:256].rearrange("p (h t) -> p h t", h=2),
                    [TBK], ["mkT"])
            chk(2)
            for h in range(4):
                dma(Sst[:, h, 0:64], sC[h], w=["Sst"])
                dma(Sst[:, h, 64:65], sn[h].unsqueeze(1), w=["Sst"])
            memset("dve", Bprev[:, :], 0.0, ["Bprev"])
            cpy("dve", MUALL[:, 0:1], cpc("sm_pp")[0:4, :], ["cp"], ["MUALL"])
            chk(3)
            phase1_block([(xs[:, :], 16, 0, sk[:, :], sv[:, :])], 16)
            chk(4)
            g2 = gates_block(16, 1)
            chk(5)
            attention_block(16, 1, [(t, 128, None) for t in range(16)] + [(16, 16, 0)], True)
            g2()
            chk(6)
            mem_block(16, 1)
            chk(7)
            mlstm_block(16, 1)
            chk(8)
            phase3_block([(xs[:, :], ys[:, :], 0)], 16)
            chk(9)
            state_out(sCo, sno, smo.rearrange("o h -> h o"))

        pre = None
        for s in range(NSEQ):
            if s == 0:
                memkv_seq(s)
            memset("dve", Sst[:, :, :], 0.0, ["Sst"])
            memset("dve", Bprev[:, :], 0.0, ["Bprev"])
            memset("dve", MUALL[:, 0:1], 0.0, ["MUALL"])
            for b in range(NBLK):
                tl = []
                for j in range(4):
                    ti = 4 * b + j
                    rows = slice(ti * 128, (ti + 1) * 128)
                    tl.append((xp[s, rows, :], ti, j, pk[s, rows, :], pv[s, rows, :]))
                phase1_block(tl, 128, pre)
                pre = None
                g2 = gates_block(128, 4)
                ktiles = [(t, 128, None) for t in range(4 * b)] + [(4 * b + i, 128, i) for i in range(4)]
                attention_block(128, 4, ktiles, False)
                g2()
                mem_block(128, 4)
                if b + 1 == NBLK and s + 1 < NSEQ:
                    memkv_seq(s + 1)
                mlstm_block(128, 4)
                nsrc = None
                if b + 1 < NBLK:
                    nsrc = xp[s, (4 * b + 4) * 128:(4 * b + 5) * 128, :]
                elif s + 1 < NSEQ:
                    nsrc = xp[s + 1, 0:128, :]
                if nsrc is not None:
                    dma(EV[:, :], nsrc, w=["ev"])
                    sa_, ks_ = front_a(EV[:, :], "ev", 128)
                    pre = (EV[:, :], "ev", sa_, ks_)
                tl3 = []
                for j in range(4):
                    ti = 4 * b + j
                    rows = slice(ti * 128, (ti + 1) * 128)
                    tl3.append((xp[s, rows, :], yp[s, rows, :], j))
                phase3_block(tl3, 128)
            state_out(pC[s], pn[s], pm[s].unsqueeze(1))

    except _Stop:
        pass

    print('sbuf bytes remaining', nc.sbuf_bytes_remaining)
    S.emit(st)
    st.close()
    return nc


_NC_CACHE = {}


def _get_nc(key=(4, 4, True)):
    if key not in _NC_CACHE:
        _NC_CACHE[key] = build_program(*key)
    return _NC_CACHE[key]


def _in_maps(inp, NSEQ=4):
    maps = []
    c32 = lambda a: np.ascontiguousarray(a, dtype=np.float32)
    for c in range(8):
        m = {
            "xp": c32(inp["x_prompt"][4 * c:4 * c + max(NSEQ, 1)]),
            "memp": c32(inp["mem_prompt"][4 * c:4 * c + max(NSEQ, 1)]),
            "xs": c32(inp["x_sample"][c]),
            "ck": c32(inp["cache_attn_k"][0, c].reshape(2048, 512)),
            "cv": c32(inp["cache_attn_v"][0, c].reshape(2048, 512)),
            "sC": c32(inp["state_mlstm_C"][0, c]),
            "sn": c32(inp["state_mlstm_n"][0, c]),
            "cmk": c32(inp["cache_mem_k"][0, c].reshape(256, 256)),
            "cmv": c32(inp["cache_mem_v"][0, c].reshape(256, 256)),
            "w_in": c32(inp["w_in"][0]),
            "w_out": c32(inp["w_out"][0]),
            "w_mk": c32(inp["w_mk"][0]),
            "w_mv": c32(inp["w_mv"][0]),
        }
        cpa, extra = _make_cp(inp, c)
        m["cp"] = cpa
        m.update(extra)
        maps.append(m)
    return maps


def kernel(**inputs):
    inp = {k: np.asarray(v) for k, v in inputs.items()}
    nc = _get_nc()
    res = run_bass_kernel_spmd(nc, _in_maps(inp), core_ids=list(range(8)))
    R = res.results
    cat = lambda name: np.concatenate([np.asarray(r[name]) for r in R], axis=0)
    stk = lambda name: np.stack([np.asarray(r[name]) for r in R], axis=0)
    y_prompt = cat("yp")
    y_sample = stk("ys")
    p_attn_k = cat("pk").reshape(1, 32, 2048, 4, 128)
    p_attn_v = cat("pv").reshape(1, 32, 2048, 4, 128)
    p_C = cat("pC").reshape(1, 32, 4, 64, 64)
    p_n = cat("pn").reshape(1, 32, 4, 64)
    p_m = cat("pm").reshape(1, 32, 4)
    p_mk = cat("pmk").reshape(1, 32, 256, 4, 64)
    p_mv = cat("pmv").reshape(1, 32, 256, 4, 64)
    s_k = stk("sk").reshape(1, 8, 16, 4, 128)
    s_v = stk("sv").reshape(1, 8, 16, 4, 128)
    s_C = stk("sCo").reshape(1, 8, 4, 64, 64)
    s_n = stk("sno").reshape(1, 8, 4, 64)
    s_m = stk("smo").reshape(1, 8, 4)
    return (y_prompt, y_sample, p_attn_k, p_attn_v, p_C, p_n, p_m, p_mk, p_mv,
            s_k, s_v, s_C, s_n, s_m)
```

```python
import math
from contextlib import ExitStack

import numpy as np
import concourse.bass as bass
import concourse.mybir as mybir
from concourse.bass_utils import run_bass_kernel_spmd

F32 = mybir.dt.float32
BF16 = mybir.dt.bfloat16
AF = mybir.ActivationFunctionType
ALU = mybir.AluOpType
AX = mybir.AxisListType

N_DMA_SEMS = 40
DELAY = 6
KEEPWARM = False
NWARM = 12
EPS = 1e-6
SLOPES = [2.0 ** (-8.0 * (h + 1) / 4) for h in range(4)]
LAM_INIT = 0.8 - 0.6 * math.exp(0.0)
NIN = 3848
BIG = 240000.0


class _Op:
    __slots__ = ("eng", "fn", "deps", "has_dep", "is_dma", "semkey", "val", "ie")


class Sched:
    def __init__(self, nc):
        self.nc = nc
        self.ops = []
        self.lw = {}
        self.rd = {}
        self.eng_n = {"pe": 0, "act": 0, "dve": 0, "pool": 0, "sp": 0}

    def add(self, eng, fn, reads=(), writes=(), dma=False):
        i = len(self.ops)
        deps = set()
        for k in reads:
            w = self.lw.get(k)
            if w is not None:
                deps.add(w)
            if isinstance(k, tuple) and k[0] == "pb":
                for r in self.rd.get(k, ()):
                    if self.ops[r].eng != eng:
                        deps.add(r)
        for k in writes:
            w = self.lw.get(k)
            if w is not None:
                deps.add(w)
            for r in self.rd.get(k, ()):
                deps.add(r)
        op = _Op()
        op.eng = eng
        op.fn = fn
        op.is_dma = dma
        op.has_dep = False
        op.semkey = None
        op.val = 0
        op.ie = self.eng_n[eng]
        self.eng_n[eng] += 1
        keep = []
        for d in deps:
            p = self.ops[d]
            if p.eng == eng and not p.is_dma:
                if eng == "pe" or eng == "sp":
                    continue
                if eng != "pool" and op.ie - p.ie > 2:
                    continue
            p.has_dep = True
            keep.append(d)
        op.deps = keep
        for k in reads:
            self.rd.setdefault(k, []).append(i)
        for k in writes:
            self.lw[k] = i
            self.rd[k] = []
        self.ops.append(op)
        return i

    def emit(self, stack):
        nc = self.nc
        engobj = {"pe": nc.tensor, "act": nc.scalar, "dve": nc.vector,
                  "pool": nc.gpsimd, "sp": nc.sync}
        esem = {e: stack.enter_context(nc.semaphore("s_" + e))
                for e in ("pe", "act", "dve", "pool")}
        dsem = [stack.enter_context(nc.semaphore("d%d" % i)) for i in range(N_DMA_SEMS)]
        waited = {e: {} for e in engobj}
        cnt = {e: 0 for e in engobj}
        dcnt = [0] * N_DMA_SEMS
        rr = 0
        rrp = 0
        for op in self.ops:
            E = engobj[op.eng]
            need = {}
            for d in op.deps:
                p = self.ops[d]
                if need.get(p.semkey, 0) < p.val:
                    need[p.semkey] = p.val
            s = None
            if op.is_dma:
                if op.eng == "pool":
                    s = N_DMA_SEMS - 8 + (rrp % 8)
                    rrp += 1
                else:
                    s = rr % (N_DMA_SEMS - 8)
                    rr += 1
                if dcnt[s] > 0 and need.get(("d", s), 0) < dcnt[s]:
                    need[("d", s)] = dcnt[s]
                dcnt[s] += 16
                op.semkey = ("d", s)
                op.val = dcnt[s]
            w = waited[op.eng]
            for key, val in need.items():
                if w.get(key, 0) >= val:
                    continue
                so = dsem[key[1]] if key[0] == "d" else esem[key[1]]
                E.wait_ge(so, val)
                w[key] = val
            inst = op.fn()
            if op.is_dma:
                inst.then_inc(dsem[s], 16)
            elif op.has_dep:
                cnt[op.eng] += 1
                op.semkey = ("e", op.eng)
                op.val = cnt[op.eng]
                inst.then_inc(esem[op.eng], 1)
        for s in range(N_DMA_SEMS):
            if dcnt[s] > 0:
                nc.sync.wait_ge(dsem[s], dcnt[s])
        return cnt


class Rot:
    def __init__(self, aps, name):
        self.aps = aps
        self.name = name
        self.i = 0

    def next(self):
        k = self.i % len(self.aps)
        self.i += 1
        return self.aps[k], (self.name, k)


def _cp_layout():
    off = {}
    c = 0
    for name, n in [("ident", 128), ("maskU", 128), ("AB", 64), ("ABS", 68),
                    ("gqa_pp", 1), ("gqm_pp", 1), ("gka_pp", 1), ("gkm_rep", 64), ("gka_rep", 64),
                    ("gnorm_pp", 8), ("gmem_pp", 8), ("rows_pp", 8),
                    ("bi_pp", 1), ("bf_pp", 1), ("sm_pp", 1), ("SEL", 128), ("SEL2", 4)]:
        off[name] = (c, c + n)
        c += n
    return off, c


CP_OFF, NCP = _cp_layout()


def _make_cp(inp, core):
    cp = np.zeros((128, NCP), np.float32)

    def put(name, arr):
        a, b = CP_OFF[name]
        cp[:, a:b] = arr

    p = np.arange(128)
    put("ident", np.eye(128, dtype=np.float32))
    put("maskU", (p[:, None] <= p[None, :]).astype(np.float32))
    corr = np.zeros((128, 4, 128), np.float32)
    k = p[:, None]
    q = p[None, :]
    for h in range(4):
        c_ = np.where(k > q, -16.0 * SLOPES[h] * (k - q), 0.0)
        c_ = np.where((k // 64) > (q // 64), -BIG, c_)
        corr[:, h, :] = c_
    extra = {"corr": corr.reshape(128, 512)}
    AB = np.zeros((128, 4, 16), np.float32)
    for h in range(4):
        for r in range(16):
            AB[:, h, r] = SLOPES[h] * (p - 127 - 128 * r)
    put("AB", AB.reshape(128, 64))
    ABS = np.zeros((128, 4, 17), np.float32)
    for h in range(4):
        for t in range(17):
            ABS[:, h, t] = SLOPES[h] * np.minimum(128 * t + p - 2063, 0)
    put("ABS", ABS.reshape(128, 68))
    put("gqa_pp", inp["g_qa"][0][p % 64][:, None])
    put("gqm_pp", inp["g_qm"][0][p % 64][:, None])
    put("gka_pp", inp["g_ka"][0][p % 64][:, None])
    put("gkm_rep", np.broadcast_to(inp["g_km"][0][None, :], (128, 64)))
    put("gka_rep", np.broadcast_to(inp["g_ka"][0][None, :], (128, 64)))
    put("gnorm_pp", inp["g_norm"][0].reshape(8, 128).T)
    put("gmem_pp", inp["g_mem"][0].reshape(8, 128).T)
    rows = np.ones((128, 8), np.float32)
    rows[:, 0:4] = inp["g_subln"][0][:, None]
    rows[:, 4:6] = inp["g_mh"][0][p % 64][:, None]
    put("rows_pp", rows)
    lam4 = np.concatenate([inp["lam_q1"][0], inp["lam_k1"][0], inp["lam_q2"][0], inp["lam_k2"][0]])
    extra["lam4"] = np.ascontiguousarray(np.broadcast_to(lam4[None, :], (128, 256)))
    put("bi_pp", inp["b_i"][0][p % 4][:, None])
    put("bf_pp", inp["b_f"][0][p % 4][:, None])
    put("sm_pp", inp["state_mlstm_m"][0, core][p % 4][:, None])
    SEL = np.zeros((128, 128), np.float32)
    SEL[0:4, :] = 1.0
    put("SEL", SEL)
    SEL2 = np.zeros((128, 4), np.float32)
    SEL2[0:4, 0:4] = np.eye(4, dtype=np.float32)
    put("SEL2", SEL2)
    return cp, extra


class _Stop(Exception):
    pass


def build_program(NSEQ=4, NBLK=4, DO_SAMPLE=True, STAGE=99):
    nc = bass.Bass("TRN2", target_bir_lowering=False)
    S = Sched(nc)

    def din(name, shape):
        return nc.dram_tensor(name, shape, F32, kind="ExternalInput").ap()

    def dout(name, shape):
        return nc.dram_tensor(name, shape, F32, kind="ExternalOutput").ap()

    NS = max(NSEQ, 1)
    xp = din("xp", [NS, 2048, 1024])
    memp = din("memp", [NS, 256, 1024])
    xs = din("xs", [16, 1024])
    ck = din("ck", [2048, 512])
    cv = din("cv", [2048, 512])
    sC = din("sC", [4, 64, 64])
    sn = din("sn", [4, 64])
    cmk = din("cmk", [256, 256])
    cmv = din("cmv", [256, 256])
    w_in = din("w_in", [1024, NIN])
    w_out = din("w_out", [1024, 1024])
    w_mk = din("w_mk", [1024, 256])
    w_mv = din("w_mv", [1024, 256])
    cpd = din("cp", [128, NCP])
    corrd = din("corr", [128, 512])
    lam4d = din("lam4", [128, 256])

    yp = dout("yp", [NS, 2048, 1024])
    pk = dout("pk", [NS, 2048, 512])
    pv = dout("pv", [NS, 2048, 512])
    pC = dout("pC", [NS, 4, 64, 64])
    pn = dout("pn", [NS, 4, 64])
    pm = dout("pm", [NS, 4])
    pmk = dout("pmk", [NS, 256, 256])
    pmv = dout("pmv", [NS, 256, 256])
    ys = dout("ys", [16, 1024])
    sk = dout("sk", [16, 512])
    sv = dout("sv", [16, 512])
    sCo = dout("sCo", [4, 64, 64])
    sno = dout("sno", [4, 64])
    smo = dout("smo", [1, 4])

    st = ExitStack()

    def chk(n):
        if STAGE == n:
            raise _Stop()

    def sb(name, shape, dt=F32):
        return st.enter_context(nc.sbuf_tensor("sb_" + name, shape, dt))

    cp = sb("cp", [128, NCP])
    w_in_bf = sb("w_in_bf", [128, 8, NIN], BF16)
    w_out_bf = sb("w_out_bf", [128, 8, 1024], BF16)
    w_mkv_bf = sb("w_mkv_bf", [128, 8, 512], BF16)
    identb = sb("identb", [128, 128], BF16)
    corrb = sb("corrb", [128, 4, 128], BF16)
    zerob = sb("zerob", [128, 512], BF16)
    cm05 = sb("cm05", [128, 8])
    lamt = sb("lamt", [128, 8])
    rs8 = sb("rs8", [128, 8])
    nbf = sb("nbf", [128, 1])

    kT = sb("kT", [128, 4, 2064], BF16)
    v_aug = sb("v_aug", [128, 17, 4, 130], BF16)
    mkT = sb("mkT", [128, 2, 256], BF16)
    mv_aug = sb("mv_aug", [128, 2, 4, 66], BF16)

    qT = sb("qT", [128, 4, 512], BF16)
    mix = sb("mix", [128, 4, 1024], BF16)
    OBG = sb("OBG", [128, 4, 256], BF16)
    QKB = sb("QKB", [128, 4, 512], BF16)
    QKT = sb("QKT", [64, 8, 512], BF16)
    vb_aug = sb("vb_aug", [128, 4, 4, 66], BF16)
    qmT = sb("qmT", [128, 2, 512], BF16)
    GIF = sb("GIF", [128, 4, 8])
    WE = sb("WE", [128, 4, 8])
    C0B = sb("C0B", [64, 4, 4])
    Sst = sb("Sst", [64, 4, 66])
    Sd = sb("Sd", [64, 4, 66])
    Cs = sb("Cs", [64, 4, 66], BF16)
    Bprev = sb("Bprev", [4, 1])
    MUALL = sb("MUALL", [4, 8])
    UM = sb("UM", [4, 4])
    C0 = sb("C0", [4, 4])
    CD = sb("CD", [4, 4, 4])
    mfin = sb("mfin", [4, 1])

    XT = Rot([sb("xt%d" % i, [128, 1024])[:] for i in range(2)], "xt")
    xn = sb("xn", [128, 1024], BF16)
    HT = Rot([sb("hT%d" % i, [128, 8, 128], BF16)[:] for i in range(2)], "hT")
    ZF = Rot([sb("zf%d" % i, [128, 512])[:] for i in range(2)], "zf")
    SQ = sb("sq", [128, 512])
    KOUT = Rot([sb("kvout%d" % i, [128, 512])[:] for i in range(3)], "kvout")
    VOUT = KOUT
    TH = Rot([sb("th%d" % i, [128, 512])[:] for i in range(2)], "th")
    NB = Rot([sb("nb%d" % i, [128, 512], BF16)[:] for i in range(4)], "nb")
    PT = Rot([sb("pt%d" % i, [128, 2, 512], BF16)[:] for i in range(3)], "pt")
    STT = Rot([sb("stat%d" % i, [128, 72])[:] for i in range(4)], "stat")
    EV = sb("ev", [128, 1024])

    class _RotK(Rot):
        def next(self):
            k = self.i % len(self.aps)
            self.i += 1
            return self.aps[k], "ev"
    O1 = _RotK([EV[:, i * 128:(i + 1) * 128] for i in range(4)], "o1s")
    OO = _RotK([EV[:, 512 + i * 128:512 + (i + 1) * 128] for i in range(4)], "oo")
    AT = sb("AT", [128, 4, 128], BF16)
    VW = [sb("vw%d" % i, [128, 4, 66], BF16) for i in range(4)]
    HN = sb("HN", [128, 4, 64])
    OM = HN
    GA, KGA = SQ[0:4, :], "sq"
    GB, KGB = TH.aps[0][0:4, :], ("th", 0)
    GU, KGU = TH.aps[1][0:4, :], ("th", 1)
    GW, KGW = ZF.aps[0][0:4, :], ("zf", 0)
    GE, KGE = ZF.aps[1][0:4, :], ("zf", 1)

    PBIG = st.enter_context(nc.psum_tensor("pbig", [128, 4096], F32))
    PB = [PBIG[:, i * 512:(i + 1) * 512] for i in range(8)]
    MM = Rot([PB[0], PB[1], PB[2]], "pb_mm")
    MM.keys = [("pb", 0), ("pb", 1), ("pb", 2)]
    ACC = [PB[3], PB[4], PB[5]]
    ACCK = [("pb", 3), ("pb", 4), ("pb", 5)]
    TBf = PB[6]
    TB = PB[6].bitcast(BF16)
    TBK = ("pb", 6)
    TF = PB[7]
    TFK = ("pb", 7)

    def mmnext():
        k = MM.i % 3
        MM.i += 1
        return PB[k], ("pb", k)

    def cpc(name, a=None, b=None):
        o, e = CP_OFF[name]
        if a is None:
            return cp[:, o:e]
        return cp[:, o + a:o + b]

    def dma(out, in_, r=(), w=(), q="sp", **kw):
        if q == "sp":
            S.add("sp", lambda: nc.sync.dma_start(out=out, in_=in_, **kw), r, w, dma=True)
        else:
            S.add("pool", lambda: nc.gpsimd.dma_start(out=out, in_=in_, **kw), r, w, dma=True)

    def mm(out, lhsT, rhs, start, stop, r, w):
        S.add("pe", lambda: nc.tensor.matmul(out, lhsT=lhsT, rhs=rhs, start=start, stop=stop,
                                             skip_group_check=True), r, w)

    def tr(out, in_, ident, r, w):
        S.add("pe", lambda: nc.tensor.transpose(out=out, in_=in_, identity=ident), r, w)

    def act(out, in_, func, r, w, bias=0.0, scale=1.0, accum=None):
        if accum is None:
            S.add("act", lambda: nc.scalar.activation(out=out, in_=in_, func=func, bias=bias, scale=scale), r, w)
        else:
            S.add("act", lambda: nc.scalar.activation(out=out, in_=in_, func=func, bias=bias, scale=scale,
                                                      accum_out=accum), r, w)

    def engo(e):
        return nc.vector if e == "dve" else nc.gpsimd

    def ts(e, out, in0, s1, s2, op0, op1, r, w):
        if s2 is None:
            S.add(e, lambda: engo(e).tensor_scalar(out=out, in0=in0, scalar1=s1, scalar2=None, op0=op0), r, w)
        else:
            S.add(e, lambda: engo(e).tensor_scalar(out=out, in0=in0, scalar1=s1, scalar2=s2, op0=op0, op1=op1), r, w)

    def tt(e, out, in0, in1, op, r, w):
        S.add(e, lambda: engo(e).tensor_tensor(out=out, in0=in0, in1=in1, op=op), r, w)

    def stt(out, in0, scalar, in1, op0, op1, r, w):
        S.add("dve", lambda: nc.vector.scalar_tensor_tensor(out=out, in0=in0, scalar=scalar, in1=in1,
                                                            op0=op0, op1=op1), r, w)

    def cpy(e, out, in_, r, w):
        if e == "act":
            S.add("act", lambda: nc.scalar.copy(out=out, in_=in_), r, w)
        else:
            S.add(e, lambda: engo(e).tensor_copy(out=out, in_=in_), r, w)

    def memset(e, ap, val, w):
        S.add(e, lambda: engo(e).memset(ap, val), (), w)

    def red(out, in_, op, r, w):
        S.add("dve", lambda: nc.vector.tensor_reduce(out=out, in_=in_, axis=AX.X, op=op), r, w)

    def recip(out, in_, r, w):
        S.add("dve", lambda: nc.vector.reciprocal(out=out, in_=in_), r, w)

    def scan(out, d0, d1, init, op0, op1, r, w):
        S.add("dve", lambda: nc.vector.tensor_tensor_scan(out=out, data0=d0, data1=d1, initial=init,
                                                          op0=op0, op1=op1), r, w)

    def rstd_from_ss(stt_ap, kst, c_ss, c_tmp, c_out, n, T, inv):
        ts("dve", stt_ap[0:T, c_tmp:c_tmp + n], stt_ap[0:T, c_ss:c_ss + n], inv, EPS, ALU.mult, ALU.add,
           [kst], [kst])
        tt("pool", stt_ap[0:T, c_out:c_out + n], stt_ap[0:T, c_tmp:c_tmp + n], cm05[0:T, 0:n], ALU.pow,
           [kst, "cm05"], [kst])

    dma(cp[:], cpd, w=["cp"])
    cpy("dve", identb[:], cpc("ident"), ["cp"], ["identb"])
    identf = cpc("ident")
    dma(TH.aps[0][:, :], corrd, w=[("th", 0)])
    cpy("dve", corrb[:].rearrange("p h q -> p (h q)"), TH.aps[0][:, :], [("th", 0)], ["corrb"])
    dma(TH.aps[1][:, 0:256], lam4d, w=[("th", 1)])
    memset("pool", zerob[:], 0.0, ["zerob"])
    memset("pool", cm05[:], -0.5, ["cm05"])
    memset("pool", v_aug[:, :, :, 128:129], 1.0, ["v_ones"])
    memset("pool", vb_aug[:, :, :, 64:65], 1.0, ["vb_ones"])
    memset("pool", mv_aug[:, :, :, 64:65], 1.0, ["mv_ones"])
    ts("dve", rs8[:, 0:4], cpc("rows_pp", 0, 4), 0.5 * (1.0 - LAM_INIT), None, ALU.mult, None, ["cp"], ["rs8"])
    ts("dve", rs8[:, 4:6], cpc("rows_pp", 4, 6), 0.25, None, ALU.mult, None, ["cp"], ["rs8"])
    ts("dve", rs8[:, 6:8], cpc("rows_pp", 6, 8), 0.5, None, ALU.mult, None, ["cp"], ["rs8"])
    ts("dve", nbf[:], cpc("bf_pp"), -1.0, None, ALU.mult, None, ["cp"], ["nbf"])
    wv = w_in.rearrange("(k p) n -> p k n", p=128)
    wov = w_out.rearrange("(k p) n -> p k n", p=128)
    wkv = w_mk.rearrange("(k p) n -> p k n", p=128)
    wvv = w_mv.rearrange("(k p) n -> p k n", p=128)

    WST = Rot([XT.aps[0][:, 0:512], XT.aps[1][:, 0:512], ZF.aps[0], ZF.aps[1], KOUT.aps[0], KOUT.aps[1],
               KOUT.aps[2]], "wst")
    WSTK = [("xt", 0), ("xt", 1), ("zf", 0), ("zf", 1), ("kvout", 0), ("kvout", 1), ("kvout", 2)]

    def wload(dst, src, n, scal, key, extra=None):
        if n > 512:
            h_ = n // 2
            wload(dst[:, 0:h_], src[:, 0:h_], h_, scal, key, extra)
            wload(dst[:, h_:n], src[:, h_:n], n - h_, scal, key, extra)
            return
        i_ = WST.i % len(WST.aps)
        xt, _ = WST.next()
        kx = WSTK[i_]
        dma(xt[:, 0:n], src, w=[kx])
        if extra is None:
            ts("dve", dst, xt[:, 0:n], scal, None, ALU.mult, None, [kx, "cp", "rs8"], [key])
        else:
            ts("dve", dst, xt[:, 0:n], scal, extra, ALU.mult, ALU.mult, [kx, "cp", "rs8"], [key])

    def load_w_late():
        for kc in range(8):
            wload(w_out_bf[:, kc, :], wov[:, kc, :], 1024, rs8[:, kc:kc + 1], ("w_out", kc))
        for kc in range(8):
            wload(w_mkv_bf[:, kc, 0:256], wkv[:, kc, :], 256, cpc("gmem_pp", kc, kc + 1), ("w_mkv", kc, 0))
            wload(w_mkv_bf[:, kc, 256:512], wvv[:, kc, :], 256, cpc("gmem_pp", kc, kc + 1), ("w_mkv", kc, 1))

    for kc in range(8):
        g = cpc("gnorm_pp", kc, kc + 1)
        wload(w_in_bf[:, kc, 0:1024], wv[:, kc, 0:1024], 1024, g, ("w_in", kc, 0))
        wload(w_in_bf[:, kc, 1024:2048], wv[:, kc, 1024:2048], 1024, g, ("w_in", kc, 1))
        wload(w_in_bf[:, kc, 2048:2304], wv[:, kc, 2048:2304], 256, g, ("w_in", kc, 2))
        wload(w_in_bf[:, kc, 2304:2560], wv[:, kc, 2304:2560], 256, g, ("w_in", kc, 2), 0.125)
        wload(w_in_bf[:, kc, 2560:3072], wv[:, kc, 2560:3072], 512, g, ("w_in", kc, 2))
        wload(w_in_bf[:, kc, 3072:3840], wv[:, kc, 3080:3848], 768, g, ("w_in", kc, 3))
        wload(w_in_bf[:, kc, 3840:3848], wv[:, kc, 3072:3080], 8, g, ("w_in", kc, 3))
    if not DO_SAMPLE:
        load_w_late()
    W_IN_K = [("w_in", kc, i) for kc in range(8) for i in range(4)]
    W_OUT_K = [("w_out", kc) for kc in range(8)]
    W_MKV_K = [("w_mkv", kc, i) for kc in range(8) for i in range(2)]
    l4 = TH.aps[1]
    tt("dve", SQ[:, 0:64], l4[:, 0:64], l4[:, 64:128], ALU.mult, [("th", 1)], ["sq"])
    red(lamt[:, 0:1], SQ[:, 0:64], ALU.add, ["sq"], ["lamt"])
    tt("dve", SQ[:, 64:128], l4[:, 128:192], l4[:, 192:256], ALU.mult, [("th", 1)], ["sq"])
    red(lamt[:, 1:2], SQ[:, 64:128], ALU.add, ["sq"], ["lamt"])
    act(lamt[:, 2:4], lamt[:, 0:2], AF.Exp, ["lamt"], ["lamt"])
    stt(lamt[:, 4:5], lamt[:, 2:3], LAM_INIT, lamt[:, 3:4], ALU.add, ALU.subtract, ["lamt"], ["lamt"])
    ts("dve", lamt[:, 5:6], lamt[:, 4:5], -1.0, None, ALU.mult, None, ["lamt"], ["lamt"])

    def front_load(src_ap, T):
        xt, kx = XT.next()
        dma(xt[0:T, :], src_ap, w=[kx])
        return xt, kx

    def front_a1(xt, kx, T):
        sa, ks = STT.next()
        act(xn[0:T, :], xt[0:T, :], AF.Square, [kx], ["xn", ks], accum=sa[0:T, 0:1])
        rstd_from_ss(sa, ks, 0, 1, 2, 1, T, 1.0 / 1024)
        return sa, ks

    def front_a2(xt, kx, sa, ks, T):
        ts("dve", xn[0:T, :], xt[0:T, :], sa[0:T, 2:3], None, ALU.mult, None, [kx, ks], ["xn"])

    def front_a(xt, kx, T):
        sa, ks = front_a1(xt, kx, T)
        front_a2(xt, kx, sa, ks, T)
        return sa, ks

    def front_b(xt, kx, sa, ks, T):
        for kc in range(8):
            tr(TB[:, kc * 128:kc * 128 + T], xn[0:T, kc * 128:(kc + 1) * 128], identb[0:T, 0:T],
               ["xn", "identb"], [TBK])
        hT, kh = HT.next()
        cpy("act", hT[:, :, 0:T], TB.rearrange("p (k t) -> p k t", k=8)[:, :, 0:T], [TBK], [kh])
        return xt, kx, hT, kh, sa, ks

    def front_compute(xt, kx, T):
        sa, ks = front_a(xt, kx, T)
        return front_b(xt, kx, sa, ks, T)

    def front(src_ap, T):
        xt, kx = front_load(src_ap, T)
        return front_compute(xt, kx, T)

    def group_norm_rstd(src, ksrc, T, ng, sa, ks, base):
        tt("dve", SQ[0:T, 0:ng * 64], src, src, ALU.mult, [ksrc], ["sq"])
        red(sa[0:T, base:base + ng], SQ[0:T, 0:ng * 64].rearrange("p (g d) -> p g d", d=64), ALU.add,
            ["sq"], [ks])
        rstd_from_ss(sa, ks, base, base + ng, base + 2 * ng, ng, T, 1.0 / 64)
        return sa[0:T, base + 2 * ng:base + 3 * ng]

    def phase1_tile(fr, T, ti, j, k_dst, v_dst, later, hook=None):
        xt, kx, hT, kh, sa, ks = fr
        tok = slice(j * 128, j * 128 + T)
        ktok = slice(ti * 128, ti * 128 + T)
        for g in range(8):
            c0 = g * 512
            n = min(512, NIN - c0)
            ps, kp = mmnext()
            for kc in range(8):
                mm(ps[0:T, 0:n], hT[:, kc, 0:T], w_in_bf[:, kc, c0:c0 + n], kc == 0, kc == 7,
                   [kh] + W_IN_K, [kp])
            for it in later:
                it[0] -= 1
            ready = [it for it in later if it[0] <= 0]
            for it in ready:
                later.remove(it)
            for it in ready:
                it[1]()
            if hook is not None:
                hook(g)
            if g == 0 or g == 1:
                zf, kz = ZF.next()
                cpy("act", zf[0:T, :], ps[0:T, 0:512], [kp], [kz])
                r8 = group_norm_rstd(zf[0:T, :], kz, T, 8, sa, ks, 8 if g == 0 else 32)

                def b01(g=g, zf=zf, kz=kz, r8=r8):
                    nb, kn = NB.next()
                    if g == 0:
                        tt("pool", nb[0:T, :].rearrange("p (g d) -> p g d", d=64),
                           zf[0:T, :].rearrange("p (g d) -> p g d", d=64),
                           r8.unsqueeze(2).to_broadcast([T, 8, 64]), ALU.mult, [kz, ks], [kn])
                    else:
                        tt("dve", nb[0:T, :].rearrange("p (g d) -> p g d", d=64),
                           zf[0:T, :].rearrange("p (g d) -> p g d", d=64),
                           r8.unsqueeze(2).to_broadcast([T, 8, 64]), ALU.mult, [kz, ks], [kn])
                        ko, kk = KOUT.next()
                        tt("pool", ko[0:T, :].rearrange("p (g d) -> p g d", d=64),
                           zf[0:T, :].rearrange("p (g d) -> p g d", d=64),
                           r8.unsqueeze(2).to_broadcast([T, 8, 64]), ALU.mult, [kz, ks], [kk])
                        tt("pool", ko[0:T, :].rearrange("p (g d) -> p g d", d=64),
                           ko[0:T, :].rearrange("p (g d) -> p g d", d=64),
                           cpc("gka_rep")[0:T, :].unsqueeze(1).to_broadcast([T, 8, 64]), ALU.mult,
                           [kk, "cp"], [kk])
                        dma(k_dst, ko[0:T, :], r=[kk])

                    def d01(nb=nb, kn=kn):
                        for h in range(4):
                            tr(TB[:, h * 128:h * 128 + T], nb[0:T, h * 128:(h + 1) * 128], identb[0:T, 0:T],
                               [kn, "identb"], [TBK])
                        src = TB[:, 0:512].rearrange("p (h t) -> p h t", h=4)[:, :, 0:T]
                        if g == 0:
                            ts("dve", qT[:, :, tok], src, cpc("gqa_pp"), None, ALU.mult, None, [TBK, "cp"],
                               [("qT", j)])
                        else:
                            act(kT[:, :, ktok], src, AF.Copy, [TBK, "cp"], [("kT", ti)], scale=cpc("gka_pp"))
                    later.append([DELAY, d01])
                later.append([2, b01])
            elif g == 2:
                vo, kv = VOUT.next()
                cpy("act", vo[0:T, :], ps[0:T, 0:512], [kp], [kv])
                dma(v_dst, vo[0:T, :], r=[kv])
                cpy("dve", v_aug[0:T, ti, :, 0:128], vo[0:T, :].rearrange("p (h d) -> p h d", h=4),
                    [kv, "v_ones"], [("v", ti)])
            elif g == 3:
                th, kt = TH.next()
                act(th[0:T, :], ps[0:T, 0:512], AF.Tanh, [kp], [kt], scale=0.5)
                stt(mix[0:T, j, 0:512], th[0:T, :], 1.0, ps[0:T, 0:512], ALU.add, ALU.mult,
                    [kt, kp], [("mix", j, 0)])
            elif g == 4:
                cpy("act", QKB[0:T, j, :], ps[0:T, 0:512], [kp], [("QKB", j)])

                def d4():
                    for i in range(8):
                        tr(TB[0:64, i * 128:i * 128 + T], QKB[0:T, j, i * 64:(i + 1) * 64], identb[0:T, 0:T],
                           [("QKB", j), "identb"], [TBK])
                    cpy("dve", QKT[:, :, tok], TB[0:64, :].rearrange("p (h t) -> p h t", h=8)[:, :, 0:T],
                        [TBK], [("QKT", j)])
                later.append([DELAY, d4])
            elif g == 5:
                cpy("dve", vb_aug[0:T, j, :, 0:64], ps[0:T, 0:256].rearrange("p (h d) -> p h d", h=4),
                    [kp, "vb_ones"], [("vb", j)])
                act(OBG[0:T, j, :], ps[0:T, 256:512], AF.Tanh, [kp], [("OBG", j)], scale=0.5)
            elif g == 6:
                th, kt = TH.next()
                act(th[0:T, 0:256], ps[0:T, 0:256], AF.Tanh, [kp], [kt], scale=0.5)
                stt(th[0:T, 256:512], th[0:T, 0:256], 1.0, ps[0:T, 0:256], ALU.add, ALU.mult, [kt, kp], [kt])
                stt(mix[0:T, j, 512:768], OBG[0:T, j, :], 1.0, th[0:T, 256:512], ALU.add, ALU.mult,
                    [("OBG", j), kt], [("mix", j, 1)])
                zf, kz = ZF.next()
                cpy("act", zf[0:T, 0:256], ps[0:T, 256:512], [kp], [kz])
                r4 = group_norm_rstd(zf[0:T, 0:256], kz, T, 4, sa, ks, 56)

                def b6(zf=zf, kz=kz, r4=r4):
                    nb, kn = NB.next()
                    tt("pool", nb[0:T, 0:256].rearrange("p (g d) -> p g d", d=64),
                       zf[0:T, 0:256].rearrange("p (g d) -> p g d", d=64),
                       r4.unsqueeze(2).to_broadcast([T, 4, 64]), ALU.mult, [kz, ks], [kn])

                    def d6(nb=nb, kn=kn):
                        for hp in range(2):
                            tr(TB[:, hp * 128:hp * 128 + T], nb[0:T, hp * 128:(hp + 1) * 128], identb[0:T, 0:T],
                               [kn, "identb"], [TBK])
                        ts("dve", qmT[:, :, tok], TB[:, 0:256].rearrange("p (h t) -> p h t", h=2)[:, :, 0:T],
                           cpc("gqm_pp"), None, ALU.mult, None, [TBK, "cp"], [("qmT", j)])
                    later.append([DELAY, d6])
                later.append([2, b6])
            else:
                th, kt = TH.next()
                act(th[0:T, 0:256], ps[0:T, 0:256], AF.Tanh, [kp], [kt], scale=0.5)
                stt(mix[0:T, j, 768:1024], th[0:T, 0:256], 1.0, ps[0:T, 0:256], ALU.add, ALU.mult,
                    [kt, kp], [("mix", j, 2)])
                cpy("dve", GIF[0:T, j, :], ps[0:T, 256:264], [kp], [("GIF", j)])

    def phase1_block(tiles, T, pre=None):
        later = []
        if pre is None:
            xt, kx = front_load(tiles[0][0], T)
            fr = front_compute(xt, kx, T)
        else:
            fr = front_b(pre[0], pre[1], pre[2], pre[3], T)
        for idx, (src_ap, ti, j, k_dst, v_dst) in enumerate(tiles):
            nxt = {}
            hook = None
            if idx + 1 < len(tiles):
                nx = front_load(tiles[idx + 1][0], T)

                def hook(g, nx=nx, nxt=nxt):
                    if g == 0:
                        nxt["a"] = front_a1(nx[0], nx[1], T)
                    elif g == 2:
                        front_a2(nx[0], nx[1], nxt["a"][0], nxt["a"][1], T)
                    elif g == 5:
                        nxt["fr"] = front_b(nx[0], nx[1], nxt["a"][0], nxt["a"][1], T)
            phase1_tile(fr, T, ti, j, k_dst, v_dst, later, hook)
            fr = nxt.get("fr")
        while later:
            later.pop(0)[1]()

    def gates_block(T, ntile):
        NQ = ntile * 128
        ibT = TF[0:4, 0:NQ]
        fbT = TBf[0:4, 0:NQ]
        for j in range(ntile):
            tr(TF[0:4, j * 128:j * 128 + T], GIF[0:T, j, 0:4], identf[0:T, 0:T], [("GIF", j), "cp"], [TFK])
            tr(TBf[0:4, j * 128:j * 128 + T], GIF[0:T, j, 4:8], identf[0:T, 0:T], [("GIF", j), "cp"], [TBK])
        if T < 128:
            memset("dve", GA[:, 0:NQ], 0.0, [KGA])
        cs = slice(0, T) if ntile == 1 else slice(0, NQ)
        act(GA[:, cs], fbT[:, cs], AF.Exp, [TBK, "nbf"], [KGA], bias=nbf[0:4, 0:1], scale=-1.0)
        act(GA[:, cs], GA[:, cs], AF.Ln, [KGA], [KGA], bias=1.0)
        scan(GB[:, cs], zerob[0:4, cs], GA[:, cs], Bprev[:, 0:1], ALU.add, ALU.subtract,
             ["zerob", KGA, "Bprev"], [KGB])
        stt(GU[:, cs], ibT[:, cs], cpc("bi_pp")[0:4, :], GB[:, cs], ALU.add, ALU.subtract,
            [TFK, "cp", KGB], [KGU])
        if ntile == 1:
            red(UM[:, 0:1], GU[:, cs], ALU.max, [KGU], ["UM"])
        else:
            red(UM[:, 0:ntile], GU[:, cs].rearrange("p (c t) -> p c t", c=ntile), ALU.max, [KGU], ["UM"])
        scan(MUALL[:, 1:1 + ntile], UM[:, 0:ntile], UM[:, 0:ntile], MUALL[:, 0:1], ALU.max, ALU.max,
             ["UM", "MUALL"], ["MUALL"])
        tt("dve", C0[:, 0:ntile], MUALL[:, 0:ntile], MUALL[:, 1:1 + ntile], ALU.subtract, ["MUALL"], ["C0"])
        if ntile == 1:
            mub = MUALL[:, 1:2].to_broadcast([4, T])
            tt("dve", GW[:, cs], GU[:, cs], mub, ALU.subtract, [KGU, "MUALL"], [KGW])
            tt("dve", GE[:, cs], GB[:, cs], mub, ALU.add, [KGB, "MUALL"], [KGE])
        else:
            mub = MUALL[:, 1:1 + ntile].unsqueeze(2).to_broadcast([4, ntile, 128])
            tt("dve", GW[:, cs].rearrange("p (c t) -> p c t", c=ntile),
               GU[:, cs].rearrange("p (c t) -> p c t", c=ntile), mub, ALU.subtract, [KGU, "MUALL"], [KGW])
            tt("dve", GE[:, cs].rearrange("p (c t) -> p c t", c=ntile),
               GB[:, cs].rearrange("p (c t) -> p c t", c=ntile), mub, ALU.add, [KGB, "MUALL"], [KGE])
        return lambda: gates_part2(T, ntile, cs, NQ)

    def gates_part2(T, ntile, cs, NQ):
        act(C0[:, 0:ntile], C0[:, 0:ntile], AF.Exp, ["C0"], ["C0"])
        act(GW[:, cs], GW[:, cs], AF.Exp, [KGW], [KGW])
        act(GE[:, cs], GE[:, cs], AF.Exp, [KGE], [KGE], scale=-1.0)
        for j in range(ntile):
            tr(TF[0:T, j * 8:j * 8 + 4], GW[0:4, j * 128:j * 128 + T], identf[0:4, 0:4], [KGW, "cp"], [TFK])
            tr(TF[0:T, j * 8 + 4:j * 8 + 8], GE[0:4, j * 128:j * 128 + T], identf[0:4, 0:4], [KGE, "cp"], [TFK])
        cpy("dve", WE[0:T, 0:ntile, :], TF[0:T, 0:ntile * 8].rearrange("p (c e) -> p c e", e=8), [TFK], ["WE"])
        tt("dve", CD[:, 0:ntile, :], C0[:, 0:ntile].unsqueeze(2).to_broadcast([4, ntile, 4]),
           cpc("SEL2")[0:4, :].unsqueeze(1).to_broadcast([4, ntile, 4]), ALU.mult, ["C0", "cp"], ["CD"])
        mm(TBf[0:64, 0:ntile * 4], cpc("SEL")[0:4, 0:64], CD[:, 0:ntile, :].rearrange("p c e -> p (c e)"), True, True,
           ["cp", "CD"], [TBK])
        cpy("dve", C0B[:, 0:ntile, :], TBf[0:64, 0:ntile * 4].rearrange("p (c e) -> p c e", e=4), [TBK], ["C0B"])
        last = T - 1 if ntile == 1 else NQ - 1
        cpy("dve", Bprev[:, 0:1], GB[:, last:last + 1], [KGB], ["Bprev"])
        cpy("dve", MUALL[:, 0:1], MUALL[:, ntile:ntile + 1], ["MUALL"], ["MUALL"])

    def attention_block(T, ntile, ktiles, sample):
        NQ = ntile * T
        ABK = [4, 5, 6]
        qkeys = [("qT", jj) for jj in range(ntile)]
        steps = [(h, t, nk, dsub) for h in range(4) for (t, nk, dsub) in ktiles]
        state = {"n": 0}

        def emit_S(st_):
            h, t, nk, dsub = st_
            c0 = 0 if dsub is None else dsub * T
            ncol = NQ - c0
            b0 = 2 * (state["n"] % 2)
            state["n"] += 1
            kps = [("pb", b0), ("pb", b0 + 1)]
            for m in range(2):
                pr = slice(64 * m, 64 * m + 64)
                mm(PB[b0 + m][0:nk, 0:ncol], kT[pr, h, t * 128:t * 128 + nk], qT[pr, h, c0:NQ],
                   True, dsub is None, [("kT", t)] + qkeys, [kps[m]])
            if dsub is not None:
                for m in range(2):
                    mm(PB[b0 + m][0:nk, 0:T], identb[0:nk, 0:nk], corrb[0:nk, h, 0:T], False, True,
                       ["identb", "corrb"], [kps[m]])
            if KEEPWARM and not sample:
                mm(PB[7][:, :], zerob[:, 0:128], zerob[:, :], True, True, ["zerob"], [("pb", 7)])
            pt, kpt = PT.next()
            if sample or h != 0:
                calls = [(c0, NQ)]
            else:
                calls = [(max(c, c0), c + 256) for c in range(0, NQ, 256) if c + 256 > c0]
            pair = PBIG[0:nk, b0 * 512:(b0 + 2) * 512].rearrange("p (m c) -> p m c", m=2)
            for (ca, cb) in calls:
                if sample:
                    bias = cpc("ABS")[0:nk, h * 17 + t:h * 17 + t + 1]
                else:
                    tref = ktiles[-1][0] - (ntile - 1) + (cb - 1) // 128
                    bias = cpc("AB")[0:nk, h * 16 + (tref - t):h * 16 + (tref - t) + 1]
                act(pt[0:nk, :, ca - c0:cb - c0], pair[:, :, ca - c0:cb - c0], AF.Exp, kps + ["cp"], [kpt],
                    bias=bias, scale=0.125)
            return pt, kpt, c0

        def emit_PV(st_, pt, kpt, c0):
            h, t, nk, dsub = st_
            for m in range(2):
                for i in range(ntile):
                    if i * T < c0:
                        continue
                    a_ = m * 4 + i
                    bnk = ABK[a_ // 3]
                    o = (a_ % 3) * 129
                    mm(PB[bnk][0:T, o:o + 129], pt[0:nk, m, i * T - c0:(i + 1) * T - c0],
                       v_aug[0:nk, t, h, 0:129], False, True, [kpt, ("v", t), "v_ones"], [("pb", bnk)])

        def evac(h):
            bset = ABK
            sa, ks = STT.next()
            if ntile == 4:
                for b_ in range(3):
                    nacc = 3 if b_ < 2 else 2
                    recip(sa[0:T, 3 * b_:3 * b_ + nacc], PB[bset[b_]][0:T, 128:129 * nacc:129],
                          [("pb", bset[b_])], [ks])
                for b_ in range(3):
                    nacc = 3 if b_ < 2 else 2
                    cpy("dve", EV[0:T, 384 * b_:384 * b_ + 128 * nacc].rearrange("p (a d) -> p a d", d=128),
                        PB[bset[b_]][0:T, 0:129 * nacc].rearrange("p (a e) -> p a e", e=129)[:, :, 0:128],
                        [("pb", bset[b_])], ["ev"])
            else:
                recip(sa[0:T, 0:1], PB[bset[0]][0:T, 128:129], [("pb", bset[0])], [ks])
                recip(sa[0:T, 4:5], PB[bset[1]][0:T, 129 + 128:129 + 129], [("pb", bset[1])], [ks])
                cpy("dve", EV[0:T, 0:128], PB[bset[0]][0:T, 0:128], [("pb", bset[0])], ["ev"])
                cpy("dve", EV[0:T, 512:640], PB[bset[1]][0:T, 129:257], [("pb", bset[1])], ["ev"])
            tt("dve", sa[0:T, 4:4 + ntile], sa[0:T, 4:4 + ntile], lamt[0:T, 5:6].to_broadcast([T, ntile]),
               ALU.mult, [ks, "lamt"], [ks])
            for i in range(ntile):
                e1 = EV[0:T, i * 128:(i + 1) * 128]
                e2 = EV[0:T, (4 + i) * 128:(5 + i) * 128]
                ts("dve", e1, e1, sa[0:T, i:i + 1], None, ALU.mult, None, ["ev", ks], ["ev"])
                stt(e2, e2, sa[0:T, 4 + i:5 + i], e1, ALU.mult, ALU.add, ["ev", ks], ["ev"])
                S.add("dve", (lambda e1=e1, e2=e2, sa=sa, i=i: nc.vector.scalar_tensor_tensor(
                    out=e1, in0=e2, scalar=1.0, in1=e2, op0=ALU.mult, op1=ALU.mult,
                    accum_out=sa[0:T, 8 + i:9 + i])), ["ev"], ["ev", ks])
            ts("dve", sa[0:T, 12:12 + ntile], sa[0:T, 8:8 + ntile], 1.0 / 128, EPS, ALU.mult, ALU.add, [ks], [ks])
            tt("pool", sa[0:T, 16:16 + ntile], sa[0:T, 12:12 + ntile], cm05[0:T, 0:ntile], ALU.pow,
               [ks, "cm05"], [ks])

            def fin(h=h, sa=sa, ks=ks):
                for ii in range(ntile):
                    stt(mix[0:T, ii, h * 128:(h + 1) * 128], EV[0:T, (4 + ii) * 128:(5 + ii) * 128],
                        sa[0:T, 16 + ii:17 + ii], mix[0:T, ii, h * 128:(h + 1) * 128], ALU.mult, ALU.mult,
                        ["ev", ks, ("mix", ii, 0)], [("mix", ii, 0)])
            return fin

        prev = None
        fins = []
        for k, st_ in enumerate(steps):
            h = st_[0]
            new_head = (k == 0 or steps[k - 1][0] != h)
            pt, kpt, c0 = emit_S(st_)
            if prev is not None:
                emit_PV(*prev)
                if new_head:
                    fins.append(evac(prev[0][0]))
            if new_head:
                if NWARM and k > 0 and not sample:
                    for _ in range(NWARM):
                        mm(PB[7][:, :], zerob[:, 0:128], zerob[:, :], True, True, ["zerob"], [("pb", 7)])
                for b_ in range(3):
                    mm(PB[ABK[b_]][0:T, :], zerob[:, 0:T], zerob[:, :], True, True, ["zerob"], [("pb", ABK[b_])])
            if len(fins) > 0 and not new_head and (k == 0 or steps[k - 2][0] == h):
                for f_ in fins:
                    f_()
                del fins[:]
            prev = (st_, pt, kpt, c0)
        emit_PV(*prev)
        fins.append(evac(prev[0][0]))
        for f_ in fins:
            f_()

    def mem_block(T, ntile):
        NQ = ntile * T
        banks = [(ACC[0], ACCK[0]), (ACC[1], ACCK[1]), (ACC[2], ACCK[2]), (TF, TFK)]
        for i in range(ntile):
            mm(banks[i][0][0:T, :], zerob[:, 0:T], zerob[:, :], True, True, ["zerob"], [banks[i][1]])
        pendm = []
        for h in range(4):
            pr = slice(64 * (h % 2), 64 * (h % 2) + 64)
            for nt in range(2):
                ps, kp = mmnext()
                mm(ps[:, 0:NQ], mkT[pr, h // 2, nt * 128:(nt + 1) * 128], qmT[pr, h // 2, 0:NQ], True, True,
                   ["mkT"] + [("qmT", jj) for jj in range(ntile)], [kp])
                pt, kpt = PT.next()
                act(pt[:, 0, 0:NQ], ps[:, 0:NQ], AF.Exp, [kp], [kpt], scale=0.125)
                pendm.append((h, nt, pt, kpt))
                if len(pendm) > 2:
                    ph, pnt, ppt, pkpt = pendm.pop(0)
                    for i in range(ntile):
                        mm(banks[i][0][0:T, ph * 65:(ph + 1) * 65], ppt[:, 0, i * T:(i + 1) * T],
                           mv_aug[:, pnt, ph, 0:65], False, True, [pkpt, "mv", "mv_ones"], [banks[i][1]])
        while pendm:
            ph, pnt, ppt, pkpt = pendm.pop(0)
            for i in range(ntile):
                mm(banks[i][0][0:T, ph * 65:(ph + 1) * 65], ppt[:, 0, i * T:(i + 1) * T],
                   mv_aug[:, pnt, ph, 0:65], False, True, [pkpt, "mv", "mv_ones"], [banks[i][1]])
        for i in range(ntile):
            bk, kb = banks[i]
            sa, ks = STT.next()
            recip(sa[0:T, 0:4], bk[0:T, 64:260:65], [kb], [ks])
            tt("dve", OM[0:T, :, :], bk[0:T, 0:260].rearrange("p (h e) -> p h e", e=65)[:, :, 0:64],
               sa[0:T, 0:4].unsqueeze(2).to_broadcast([T, 4, 64]), ALU.mult, [kb, ks], ["HN"])
            tt("dve", mix[0:T, i, 768:1024], OM[0:T, :, :].rearrange("p h d -> p (h d)"),
               mix[0:T, i, 768:1024], ALU.mult, ["HN", ("mix", i, 2)], [("mix", i, 2)])

    def mlstm_block(T, ntile):
        pend_a = []
        pend_b = []
        for j in range(ntile):
            tt("pool", VW[j][0:T, :, 0:65], vb_aug[0:T, j, :, 0:65],
               WE[0:T, j, 0:4].unsqueeze(2).to_broadcast([T, 4, 65]), ALU.mult,
               [("vb", j), "vb_ones", "WE"], [("vw", j)])
        for j in range(ntile):
            tok = slice(j * 128, j * 128 + T)
            vw = VW[j]
            kvw = ("vw", j)
            tt("dve", Sd[:, :, 0:65], Sst[:, :, 0:65], C0B[:, j, :].unsqueeze(2).to_broadcast([64, 4, 65]), ALU.mult,
               ["Sst", "C0B"], ["Sd"])
            cpy("act", Cs[:, :, 0:65], Sd[:, :, 0:65], ["Sd"], ["Cs"])
            psA, kA = mmnext()
            for h in range(4):
                mm(psA[0:T, h * 128:h * 128 + T], QKT[:, 4 + h, tok], QKT[:, h, tok], True, True,
                   [("QKT", j)], [kA])
            psS, kS = ACC[2], ACCK[2]
            for h in range(4):
                mm(psS[0:64, h * 65:(h + 1) * 65], QKB[0:T, j, 256 + h * 64:256 + (h + 1) * 64],
                   vw[0:T, h, 0:65], True, True, [("QKB", j), kvw], [kS])
            tt("dve", Sst[:, :, 0:65], Sd[:, :, 0:65], psS[0:64, 0:260].rearrange("p (c e) -> p c e", e=65), ALU.add,
               ["Sd", kS], ["Sst"])
            tt("dve", AT[0:T, :, 0:T], psA[0:T, :].rearrange("p (h t) -> p h t", h=4)[:, :, 0:T],
               cpc("maskU")[0:T, 0:T].unsqueeze(1).to_broadcast([T, 4, T]), ALU.mult, [kA, "cp"], ["AT"])
            psO, kO = ACC[j % 2], ACCK[j % 2]
            for h in range(4):
                mm(psO[0:T, h * 65:(h + 1) * 65], AT[0:T, h, 0:T], vw[0:T, h, 0:65], True, False, ["AT", kvw], [kO])
                mm(psO[0:T, h * 65:(h + 1) * 65], QKT[:, h, tok], Cs[:, h, 0:65], False, True,
                   [("QKT", j), "Cs"], [kO])

            def evac_a(j=j, psO=psO, kO=kO):
                sa, ks = STT.next()
                cpy("dve", sa[0:T, 20:24], psO[0:T, 64:260:65], [kO], [ks])
                stt(sa[0:T, 24:28], sa[0:T, 20:24], -1.0, sa[0:T, 20:24], ALU.mult, ALU.max, [ks], [ks])
                tt("dve", sa[0:T, 0:4], sa[0:T, 24:28], WE[0:T, j, 4:8], ALU.max, [ks, "WE"], [ks])
                recip(sa[0:T, 4:8], sa[0:T, 0:4], [ks], [ks])
                tt("dve", HN[0:T, :, :], psO[0:T, 0:260].rearrange("p (h e) -> p h e", e=65)[:, :, 0:64],
                   sa[0:T, 4:8].unsqueeze(2).to_broadcast([T, 4, 64]), ALU.mult, [kO, ks], ["HN"])
                r4 = group_norm_rstd(HN[0:T, :, :].rearrange("p h d -> p (h d)"), "HN", T, 4, sa, ks, 8)

                def evac_b():
                    tt("dve", HN[0:T, :, :], HN[0:T, :, :], r4.unsqueeze(2).to_broadcast([T, 4, 64]), ALU.mult,
                       ["HN", ks], ["HN"])
                    tt("dve", mix[0:T, j, 512:768], HN[0:T, :, :].rearrange("p h d -> p (h d)"),
                       mix[0:T, j, 512:768], ALU.mult, ["HN", ("mix", j, 1)], [("mix", j, 1)])
                return evac_b
            pend_a.append(evac_a)
            if len(pend_a) == 2:
                for f_ in pend_b:
                    f_()
                del pend_b[:]
                pend_b.append(pend_a.pop(0)())
        for f_ in pend_b:
            f_()
        del pend_b[:]
        while pend_a:
            pend_a.pop(0)()()

    def phase3_block(tiles, T):
        YB = [[(PB[3], ("pb", 3)), (PB[4], ("pb", 4))], [(PB[5], ("pb", 5)), (PB[0], ("pb", 0))]]
        xts = {}

        def stage_t(idx):
            src_ap, dst_ap, j = tiles[idx]
            mT, kmT = HT.next()
            for kc in range(8):
                tr(TB[:, kc * 128:kc * 128 + T], mix[0:T, j, kc * 128:(kc + 1) * 128], identb[0:T, 0:T],
                   [("mix", j, 0), ("mix", j, 1), ("mix", j, 2), "identb"], [TBK])
            cpy("act", mT[:, :, 0:T], TB.rearrange("p (k t) -> p k t", k=8)[:, :, 0:T], [TBK], [kmT])
            xt, kx = XT.next()
            dma(xt[0:T, :], src_ap, w=[kx])
            xts[idx] = (mT, kmT, xt, kx)

        def stage_m(idx):
            src_ap, dst_ap, j = tiles[idx]
            mT, kmT, xt, kx = xts[idx]
            for half in range(2):
                ps, kp = YB[idx % 2][half]
                for kc in range(8):
                    mm(ps[0:T, :], mT[:, kc, 0:T], w_out_bf[:, kc, half * 512:(half + 1) * 512], kc == 0, kc == 7,
                       [kmT] + W_OUT_K, [kp])
                tt("dve", xt[0:T, half * 512:(half + 1) * 512], ps[0:T, :], xt[0:T, half * 512:(half + 1) * 512],
                   ALU.add, [kp, kx], [kx])
            dma(dst_ap, xt[0:T, :], r=[kx])

        stage_t(0)
        for idx in range(len(tiles)):
            if idx + 1 < len(tiles):
                stage_t(idx + 1)
            stage_m(idx)

    def memkv_seq(s):
        for nt in range(2):
            xt, kx, hT, kh, sa, ks = front(memp[s, nt * 128:(nt + 1) * 128, :], 128)
            ps, kp = mmnext()
            for kc in range(8):
                mm(ps[:, :], hT[:, kc, :], w_mkv_bf[:, kc, :], kc == 0, kc == 7, [kh] + W_MKV_K, [kp])
            zf, kz = ZF.next()
            cpy("act", zf[:, :], ps[:, :], [kp], [kz])
            dma(pmv[s, nt * 128:(nt + 1) * 128, :], zf[:, 256:512], r=[kz])
            cpy("pool", mv_aug[:, nt, :, 0:64], zf[:, 256:512].rearrange("p (h d) -> p h d", h=4),
                [kz, "mv_ones"], ["mv"])
            r4 = group_norm_rstd(zf[:, 0:256], kz, 128, 4, sa, ks, 8)
            tt("dve", zf[:, 0:256].rearrange("p (g d) -> p g d", d=64),
               zf[:, 0:256].rearrange("p (g d) -> p g d", d=64),
               r4.unsqueeze(2).to_broadcast([128, 4, 64]), ALU.mult, [kz, ks], [kz])
            ko, kk = KOUT.next()
            tt("dve", ko[:, 0:256].rearrange("p (g d) -> p g d", d=64),
               zf[:, 0:256].rearrange("p (g d) -> p g d", d=64),
               cpc("gkm_rep").unsqueeze(1).to_broadcast([128, 4, 64]), ALU.mult, [kz, "cp"], [kk])
            dma(pmk[s, nt * 128:(nt + 1) * 128, :], ko[:, 0:256], r=[kk])
            nb, kn = NB.next()
            cpy("pool", nb[:, 0:256], ko[:, 0:256], [kk], [kn])
            for hp in range(2):
                tr(TB[:, hp * 128:(hp + 1) * 128], nb[:, hp * 128:(hp + 1) * 128], identb[:, :],
                   [kn, "identb"], [TBK])
            cpy("act", mkT[:, :, nt * 128:(nt + 1) * 128], TB[:, 0:256].rearrange("p (h t) -> p h t", h=2),
                [TBK], ["mkT"])

    def state_out(dC, dn, dm):
        for h in range(4):
            dma(dC[h], Sst[:, h, 0:64], r=["Sst"])
            dma(dn[h].unsqueeze(1), Sst[:, h, 64:65], r=["Sst"])
        tt("dve", mfin[:, :], Bprev[:, 0:1], MUALL[:, 0:1], ALU.add, ["Bprev", "MUALL"], ["mfin"])
        dma(dm, mfin[:, :], r=["mfin"])

    try:
        chk(1)
        if DO_SAMPLE:
            for t in range(16):
                i1 = WST.i % len(WST.aps)
                xk, _ = WST.next()
                i2 = WST.i % len(WST.aps)
                xv, _ = WST.next()
                kx, kx2 = WSTK[i1], WSTK[i2]
                dma(xk[:, 0:512], ck[t * 128:(t + 1) * 128, :], w=[kx])
                dma(xv[:, 0:512], cv[t * 128:(t + 1) * 128, :], w=[kx2])
                nb, kn = NB.next()
                cpy("dve", nb[:, :], xk[:, 0:512], [kx], [kn])
                cpy("dve" if t % 2 == 0 else "pool", v_aug[:, t, :, 0:128],
                    xv[:, 0:512].rearrange("p (h d) -> p h d", h=4), [kx2, "v_ones"], [("v", t)])
                for h in range(4):
                    tr(TB[:, h * 128:(h + 1) * 128], nb[:, h * 128:(h + 1) * 128], identb[:, :], [kn, "identb"], [TBK])
                cpy("act", kT[:, :, t * 128:(t + 1) * 128], TB[:, 0:512].rearrange("p (h t) -> p h t", h=4),
                    [TBK], [("kT", t)])
            for nt in range(2):
                xt, kx = XT.next()
                dma(xt[:, 0:256], cmk[nt * 128:(nt + 1) * 128, :], w=[kx])
                dma(xt[:, 256:512], cmv[nt * 128:(nt + 1) * 128, :], w=[kx])
                nb, kn = NB.next()
                cpy("dve", nb[:, 0:256], xt[:, 0:256], [kx], [kn])
                cpy("pool", mv_aug[:, nt, :, 0:64], xt[:, 256:512].rearrange("p (h d) -> p h d", h=4),
                    [kx, "mv_ones"], ["mv"])
                for hp in range(2):
                    tr(TB[:, hp * 128:(hp + 1) * 128], nb[:, hp * 128:(hp + 1) * 128], identb[:, :],
                       [kn, "identb"], [TBK])
                cpy("act", mkT[:, :, nt * 128:(nt + 1) * 128], TB[:, 0:256].rearrange("p (h t) -> p h t", h=2),
                    [TBK], ["mkT"])
            chk(2)
            load_w_late()
            for h in range(4):
                dma(Sst[:, h, 0:64], sC[h], w=["Sst"])
                dma(Sst[:, h, 64:65], sn[h].unsqueeze(1), w=["Sst"])
            memset("dve", Bprev[:, :], 0.0, ["Bprev"])
            cpy("dve", MUALL[:, 0:1], cpc("sm_pp")[0:4, :], ["cp"], ["MUALL"])
            chk(3)
            phase1_block([(xs[:, :], 16, 0, sk[:, :], sv[:, :])], 16)
            chk(4)
            g2 = gates_block(16, 1)
            chk(5)
            attention_block(16, 1, [(t, 128, None) for t in range(16)] + [(16, 16, 0)], True)
            g2()
            chk(6)
            mem_block(16, 1)
            chk(7)
            mlstm_block(16, 1)
            chk(8)
            phase3_block([(xs[:, :], ys[:, :], 0)], 16)
            chk(9)
            state_out(sCo, sno, smo.rearrange("o h -> h o"))

        pre = None
        for s in range(NSEQ):
            if s == 0:
                memkv_seq(s)
            memset("dve", Sst[:, :, :], 0.0, ["Sst"])
            memset("dve", Bprev[:, :], 0.0, ["Bprev"])
            memset("dve", MUALL[:, 0:1], 0.0, ["MUALL"])
            for b in range(NBLK):
                tl = []
                for j in range(4):
                    ti = 4 * b + j
                    rows = slice(ti * 128, (ti + 1) * 128)
                    tl.append((xp[s, rows, :], ti, j, pk[s, rows, :], pv[s, rows, :]))
                phase1_block(tl, 128, pre)
                pre = None
                g2 = gates_block(128, 4)
                ktiles = [(t, 128, None) for t in range(4 * b)] + [(4 * b + i, 128, i) for i in range(4)]
                attention_block(128, 4, ktiles, False)
                g2()
                mem_block(128, 4)
                if b + 1 == NBLK and s + 1 < NSEQ:
                    memkv_seq(s + 1)
                mlstm_block(128, 4)
                nsrc = None
                if b + 1 < NBLK:
                    nsrc = xp[s, (4 * b + 4) * 128:(4 * b + 5) * 128, :]
                elif s + 1 < NSEQ:
                    nsrc = xp[s + 1, 0:128, :]
                if nsrc is not None:
                    dma(EV[:, :], nsrc, w=["ev"])
                    sa_, ks_ = front_a(EV[:, :], "ev", 128)
                    pre = (EV[:, :], "ev", sa_, ks_)
                tl3 = []
                for j in range(4):
                    ti = 4 * b + j
                    rows = slice(ti * 128, (ti + 1) * 128)
                    tl3.append((xp[s, rows, :], yp[s, rows, :], j))
                phase3_block(tl3, 128)
            state_out(pC[s], pn[s], pm[s].unsqueeze(1))

    except _Stop:
        pass

    print('sbuf bytes remaining', nc.sbuf_bytes_remaining)
    S.emit(st)
    st.close()
    return nc


_NC_CACHE = {}


def _get_nc(key=(4, 4, True)):
    if key not in _NC_CACHE:
        _NC_CACHE[key] = build_program(*key)
    return _NC_CACHE[key]


def _in_maps(inp, NSEQ=4):
    maps = []
    c32 = lambda a: np.ascontiguousarray(a, dtype=np.float32)
    for c in range(8):
        m = {
            "xp": c32(inp["x_prompt"][4 * c:4 * c + max(NSEQ, 1)]),
            "memp": c32(inp["mem_prompt"][4 * c:4 * c + max(NSEQ, 1)]),
            "xs": c32(inp["x_sample"][c]),
            "ck": c32(inp["cache_attn_k"][0, c].reshape(2048, 512)),
            "cv": c32(inp["cache_attn_v"][0, c].reshape(2048, 512)),
            "sC": c32(inp["state_mlstm_C"][0, c]),
            "sn": c32(inp["state_mlstm_n"][0, c]),
            "cmk": c32(inp["cache_mem_k"][0, c].reshape(256, 256)),
            "cmv": c32(inp["cache_mem_v"][0, c].reshape(256, 256)),
            "w_in": c32(inp["w_in"][0]),
            "w_out": c32(inp["w_out"][0]),
            "w_mk": c32(inp["w_mk"][0]),
            "w_mv": c32(inp["w_mv"][0]),
        }
        cpa, extra = _make_cp(inp, c)
        m["cp"] = cpa
        m.update(extra)
        maps.append(m)
    return maps


def kernel(**inputs):
    inp = {k: np.asarray(v) for k, v in inputs.items()}
    nc = _get_nc()
    res = run_bass_kernel_spmd(nc, _in_maps(inp), core_ids=list(range(8)))
    R = res.results
    cat = lambda name: np.concatenate([np.asarray(r[name]) for r in R], axis=0)
    stk = lambda name: np.stack([np.asarray(r[name]) for r in R], axis=0)
    y_prompt = cat("yp")
    y_sample = stk("ys")
    p_attn_k = cat("pk").reshape(1, 32, 2048, 4, 128)
    p_attn_v = cat("pv").reshape(1, 32, 2048, 4, 128)
    p_C = cat("pC").reshape(1, 32, 4, 64, 64)
    p_n = cat("pn").reshape(1, 32, 4, 64)
    p_m = cat("pm").reshape(1, 32, 4)
    p_mk = cat("pmk").reshape(1, 32, 256, 4, 64)
    p_mv = cat("pmv").reshape(1, 32, 256, 4, 64)
    s_k = stk("sk").reshape(1, 8, 16, 4, 128)
    s_v = stk("sv").reshape(1, 8, 16, 4, 128)
    s_C = stk("sCo").reshape(1, 8, 4, 64, 64)
    s_n = stk("sno").reshape(1, 8, 4, 64)
    s_m = stk("smo").reshape(1, 8, 4)
    return (y_prompt, y_sample, p_attn_k, p_attn_v, p_C, p_n, p_m, p_mk, p_mv,
            s_k, s_v, s_C, s_n, s_m)
```

```python
import math
from contextlib import ExitStack

import numpy as np
import concourse.bass as bass
import concourse.mybir as mybir
from concourse.bass_utils import run_bass_kernel_spmd

F32 = mybir.dt.float32
BF16 = mybir.dt.bfloat16
AF = mybir.ActivationFunctionType
ALU = mybir.AluOpType
AX = mybir.AxisListType

N_DMA_SEMS = 40
DELAY = 6
KEEPWARM = False
NWARM = 12
EPS = 1e-6
SLOPES = [2.0 ** (-8.0 * (h + 1) / 4) for h in range(4)]
LAM_INIT = 0.8 - 0.6 * math.exp(0.0)
NIN = 3848
BIG = 240000.0


class _Op:
    __slots__ = ("eng", "fn", "deps", "has_dep", "is_dma", "semkey", "val", "ie")


class Sched:
    def __init__(self, nc):
        self.nc = nc
        self.ops = []
        self.lw = {}
        self.rd = {}
        self.eng_n = {"pe": 0, "act": 0, "dve": 0, "pool": 0, "sp": 0}

    def add(self, eng, fn, reads=(), writes=(), dma=False):
        i = len(self.ops)
        deps = set()
        for k in reads:
            w = self.lw.get(k)
            if w is not None:
                deps.add(w)
            if isinstance(k, tuple) and k[0] == "pb":
                for r in self.rd.get(k, ()):
                    if self.ops[r].eng != eng:
                        deps.add(r)
        for k in writes:
            w = self.lw.get(k)
            if w is not None:
                deps.add(w)
            for r in self.rd.get(k, ()):
                deps.add(r)
        op = _Op()
        op.eng = eng
        op.fn = fn
        op.is_dma = dma
        op.has_dep = False
        op.semkey = None
        op.val = 0
        op.ie = self.eng_n[eng]
        self.eng_n[eng] += 1
        keep = []
        for d in deps:
            p = self.ops[d]
            if p.eng == eng and not p.is_dma:
                if eng == "pe" or eng == "sp":
                    continue
                if eng != "pool" and op.ie - p.ie > 8:
                    continue
            p.has_dep = True
            keep.append(d)
        op.deps = keep
        for k in reads:
            self.rd.setdefault(k, []).append(i)
        for k in writes:
            self.lw[k] = i
            self.rd[k] = []
        self.ops.append(op)
        return i

    def emit(self, stack):
        nc = self.nc
        engobj = {"pe": nc.tensor, "act": nc.scalar, "dve": nc.vector,
                  "pool": nc.gpsimd, "sp": nc.sync}
        esem = {e: stack.enter_context(nc.semaphore("s_" + e))
                for e in ("pe", "act", "dve", "pool")}
        dsem = [stack.enter_context(nc.semaphore("d%d" % i)) for i in range(N_DMA_SEMS)]
        waited = {e: {} for e in engobj}
        cnt = {e: 0 for e in engobj}
        dcnt = [0] * N_DMA_SEMS
        rr = 0
        rrp = 0
        for op in self.ops:
            E = engobj[op.eng]
            need = {}
            for d in op.deps:
                p = self.ops[d]
                if need.get(p.semkey, 0) < p.val:
                    need[p.semkey] = p.val
            s = None
            if op.is_dma:
                if op.eng == "pool":
                    s = N_DMA_SEMS - 8 + (rrp % 8)
                    rrp += 1
                else:
                    s = rr % (N_DMA_SEMS - 8)
                    rr += 1
                if dcnt[s] > 0 and need.get(("d", s), 0) < dcnt[s]:
                    need[("d", s)] = dcnt[s]
                dcnt[s] += 16
                op.semkey = ("d", s)
                op.val = dcnt[s]
            w = waited[op.eng]
            for key, val in need.items():
                if w.get(key, 0) >= val:
                    continue
                so = dsem[key[1]] if key[0] == "d" else esem[key[1]]
                E.wait_ge(so, val)
                w[key] = val
            inst = op.fn()
            if op.is_dma:
                inst.then_inc(dsem[s], 16)
            elif op.has_dep:
                cnt[op.eng] += 1
                op.semkey = ("e", op.eng)
                op.val = cnt[op.eng]
                inst.then_inc(esem[op.eng], 1)
        for s in range(N_DMA_SEMS):
            if dcnt[s] > 0:
                nc.sync.wait_ge(dsem[s], dcnt[s])
        return cnt


class Rot:
    def __init__(self, aps, name):
        self.aps = aps
        self.name = name
        self.i = 0

    def next(self):
        k = self.i % len(self.aps)
        self.i += 1
        return self.aps[k], (self.name, k)


def _cp_layout():
    off = {}
    c = 0
    for name, n in [("ident", 128), ("maskU", 128), ("AB", 64), ("ABS", 68),
                    ("gqa_pp", 1), ("gqm_pp", 1), ("gka_pp", 1), ("gkm_rep", 64), ("gka_rep", 64),
                    ("gnorm_pp", 8), ("gmem_pp", 8), ("rows_pp", 8),
                    ("bi_pp", 1), ("bf_pp", 1), ("sm_pp", 1), ("SEL", 128), ("SEL2", 4)]:
        off[name] = (c, c + n)
        c += n
    return off, c


CP_OFF, NCP = _cp_layout()


def _make_cp(inp, core):
    cp = np.zeros((128, NCP), np.float32)

    def put(name, arr):
        a, b = CP_OFF[name]
        cp[:, a:b] = arr

    p = np.arange(128)
    put("ident", np.eye(128, dtype=np.float32))
    put("maskU", (p[:, None] <= p[None, :]).astype(np.float32))
    corr = np.zeros((128, 4, 128), np.float32)
    k = p[:, None]
    q = p[None, :]
    for h in range(4):
        c_ = np.where(k > q, -16.0 * SLOPES[h] * (k - q), 0.0)
        c_ = np.where((k // 64) > (q // 64), -BIG, c_)
        corr[:, h, :] = c_
    extra = {"corr": corr.reshape(128, 512)}
    AB = np.zeros((128, 4, 16), np.float32)
    for h in range(4):
        for r in range(16):
            AB[:, h, r] = SLOPES[h] * (p - 127 - 128 * r)
    put("AB", AB.reshape(128, 64))
    ABS = np.zeros((128, 4, 17), np.float32)
    for h in range(4):
        for t in range(17):
            ABS[:, h, t] = SLOPES[h] * np.minimum(128 * t + p - 2063, 0)
    put("ABS", ABS.reshape(128, 68))
    put("gqa_pp", inp["g_qa"][0][p % 64][:, None])
    put("gqm_pp", inp["g_qm"][0][p % 64][:, None])
    put("gka_pp", inp["g_ka"][0][p % 64][:, None])
    put("gkm_rep", np.broadcast_to(inp["g_km"][0][None, :], (128, 64)))
    put("gka_rep", np.broadcast_to(inp["g_ka"][0][None, :], (128, 64)))
    put("gnorm_pp", inp["g_norm"][0].reshape(8, 128).T)
    put("gmem_pp", inp["g_mem"][0].reshape(8, 128).T)
    rows = np.ones((128, 8), np.float32)
    rows[:, 0:4] = inp["g_subln"][0][:, None]
    rows[:, 4:6] = inp["g_mh"][0][p % 64][:, None]
    put("rows_pp", rows)
    lam4 = np.concatenate([inp["lam_q1"][0], inp["lam_k1"][0], inp["lam_q2"][0], inp["lam_k2"][0]])
    extra["lam4"] = np.ascontiguousarray(np.broadcast_to(lam4[None, :], (128, 256)))
    put("bi_pp", inp["b_i"][0][p % 4][:, None])
    put("bf_pp", inp["b_f"][0][p % 4][:, None])
    put("sm_pp", inp["state_mlstm_m"][0, core][p % 4][:, None])
    SEL = np.zeros((128, 128), np.float32)
    SEL[0:4, :] = 1.0
    put("SEL", SEL)
    SEL2 = np.zeros((128, 4), np.float32)
    SEL2[0:4, 0:4] = np.eye(4, dtype=np.float32)
    put("SEL2", SEL2)
    return cp, extra


class _Stop(Exception):
    pass


def build_program(NSEQ=4, NBLK=4, DO_SAMPLE=True, STAGE=99):
    nc = bass.Bass("TRN2", target_bir_lowering=False)
    S = Sched(nc)

    def din(name, shape):
        return nc.dram_tensor(name, shape, F32, kind="ExternalInput").ap()

    def dout(name, shape):
        return nc.dram_tensor(name, shape, F32, kind="ExternalOutput").ap()

    NS = max(NSEQ, 1)
    xp = din("xp", [NS, 2048, 1024])
    memp = din("memp", [NS, 256, 1024])
    xs = din("xs", [16, 1024])
    ck = din("ck", [2048, 512])
    cv = din("cv", [2048, 512])
    sC = din("sC", [4, 64, 64])
    sn = din("sn", [4, 64])
    cmk = din("cmk", [256, 256])
    cmv = din("cmv", [256, 256])
    w_in = din("w_in", [1024, NIN])
    w_out = din("w_out", [1024, 1024])
    w_mk = din("w_mk", [1024, 256])
    w_mv = din("w_mv", [1024, 256])
    cpd = din("cp", [128, NCP])
    corrd = din("corr", [128, 512])
    lam4d = din("lam4", [128, 256])

    yp = dout("yp", [NS, 2048, 1024])
    pk = dout("pk", [NS, 2048, 512])
    pv = dout("pv", [NS, 2048, 512])
    pC = dout("pC", [NS, 4, 64, 64])
    pn = dout("pn", [NS, 4, 64])
    pm = dout("pm", [NS, 4])
    pmk = dout("pmk", [NS, 256, 256])
    pmv = dout("pmv", [NS, 256, 256])
    ys = dout("ys", [16, 1024])
    sk = dout("sk", [16, 512])
    sv = dout("sv", [16, 512])
    sCo = dout("sCo", [4, 64, 64])
    sno = dout("sno", [4, 64])
    smo = dout("smo", [1, 4])

    st = ExitStack()

    def chk(n):
        if STAGE == n:
            raise _Stop()

    def sb(name, shape, dt=F32):
        return st.enter_context(nc.sbuf_tensor("sb_" + name, shape, dt))

    cp = sb("cp", [128, NCP])
    w_in_bf = sb("w_in_bf", [128, 8, NIN], BF16)
    w_out_bf = sb("w_out_bf", [128, 8, 1024], BF16)
    w_mkv_bf = sb("w_mkv_bf", [128, 8, 512], BF16)
    identb = sb("identb", [128, 128], BF16)
    corrb = sb("corrb", [128, 4, 128], BF16)
    zerob = sb("zerob", [128, 512], BF16)
    cm05 = sb("cm05", [128, 8])
    lamt = sb("lamt", [128, 8])
    rs8 = sb("rs8", [128, 8])
    nbf = sb("nbf", [128, 1])

    kT = sb("kT", [128, 4, 2064], BF16)
    v_aug = sb("v_aug", [128, 17, 4, 130], BF16)
    mkT = sb("mkT", [128, 2, 256], BF16)
    mv_aug = sb("mv_aug", [128, 2, 4, 66], BF16)

    qT = sb("qT", [128, 4, 512], BF16)
    mix = sb("mix", [128, 4, 1024], BF16)
    OBG = sb("OBG", [128, 4, 256], BF16)
    QKB = sb("QKB", [128, 4, 512], BF16)
    QKT = sb("QKT", [64, 8, 512], BF16)
    vb_aug = sb("vb_aug", [128, 4, 4, 66], BF16)
    qmT = sb("qmT", [128, 2, 512], BF16)
    GIF = sb("GIF", [128, 4, 8])
    WE = sb("WE", [128, 4, 8])
    C0B = sb("C0B", [64, 4, 4])
    Sst = sb("Sst", [64, 4, 66])
    Sd = sb("Sd", [64, 4, 66])
    Cs = sb("Cs", [64, 4, 66], BF16)
    Bprev = sb("Bprev", [4, 1])
    MUALL = sb("MUALL", [4, 8])
    UM = sb("UM", [4, 4])
    C0 = sb("C0", [4, 4])
    CD = sb("CD", [4, 4, 4])
    mfin = sb("mfin", [4, 1])

    XT = Rot([sb("xt%d" % i, [128, 1024])[:] for i in range(2)], "xt")
    xn = sb("xn", [128, 1024], BF16)
    HT = Rot([sb("hT%d" % i, [128, 8, 128], BF16)[:] for i in range(2)], "hT")
    ZF = Rot([sb("zf%d" % i, [128, 512])[:] for i in range(2)], "zf")
    SQ = sb("sq", [128, 512])
    KOUT = Rot([sb("kvout%d" % i, [128, 512])[:] for i in range(3)], "kvout")
    VOUT = KOUT
    TH = Rot([sb("th%d" % i, [128, 512])[:] for i in range(2)], "th")
    NB = Rot([sb("nb%d" % i, [128, 512], BF16)[:] for i in range(4)], "nb")
    PT = Rot([sb("pt%d" % i, [128, 2, 512], BF16)[:] for i in range(3)], "pt")
    STT = Rot([sb("stat%d" % i, [128, 72])[:] for i in range(4)], "stat")
    EV = sb("ev", [128, 1024])

    class _RotK(Rot):
        def next(self):
            k = self.i % len(self.aps)
            self.i += 1
            return self.aps[k], "ev"
    O1 = _RotK([EV[:, i * 128:(i + 1) * 128] for i in range(4)], "o1s")
    OO = _RotK([EV[:, 512 + i * 128:512 + (i + 1) * 128] for i in range(4)], "oo")
    AT = sb("AT", [128, 4, 128], BF16)
    VW = [sb("vw%d" % i, [128, 4, 66], BF16) for i in range(4)]
    HN = sb("HN", [128, 4, 64])
    OM = HN
    GA, KGA = SQ[0:4, :], "sq"
    GB, KGB = TH.aps[0][0:4, :], ("th", 0)
    GU, KGU = TH.aps[1][0:4, :], ("th", 1)
    GW, KGW = ZF.aps[0][0:4, :], ("zf", 0)
    GE, KGE = ZF.aps[1][0:4, :], ("zf", 1)

    PBIG = st.enter_context(nc.psum_tensor("pbig", [128, 4096], F32))
    PB = [PBIG[:, i * 512:(i + 1) * 512] for i in range(8)]
    MM = Rot([PB[0], PB[1], PB[2]], "pb_mm")
    MM.keys = [("pb", 0), ("pb", 1), ("pb", 2)]
    ACC = [PB[3], PB[4], PB[5]]
    ACCK = [("pb", 3), ("pb", 4), ("pb", 5)]
    TBf = PB[6]
    TB = PB[6].bitcast(BF16)
    TBK = ("pb", 6)
    TF = PB[7]
    TFK = ("pb", 7)

    def mmnext():
        k = MM.i % 3
        MM.i += 1
        return PB[k], ("pb", k)

    def cpc(name, a=None, b=None):
        o, e = CP_OFF[name]
        if a is None:
            return cp[:, o:e]
        return cp[:, o + a:o + b]

    def dma(out, in_, r=(), w=(), q="sp", **kw):
        if q == "sp":
            S.add("sp", lambda: nc.sync.dma_start(out=out, in_=in_, **kw), r, w, dma=True)
        else:
            S.add("pool", lambda: nc.gpsimd.dma_start(out=out, in_=in_, **kw), r, w, dma=True)

    def mm(out, lhsT, rhs, start, stop, r, w):
        S.add("pe", lambda: nc.tensor.matmul(out, lhsT=lhsT, rhs=rhs, start=start, stop=stop,
                                             skip_group_check=True), r, w)

    def tr(out, in_, ident, r, w):
        S.add("pe", lambda: nc.tensor.transpose(out=out, in_=in_, identity=ident), r, w)

    def act(out, in_, func, r, w, bias=0.0, scale=1.0, accum=None):
        if accum is None:
            S.add("act", lambda: nc.scalar.activation(out=out, in_=in_, func=func, bias=bias, scale=scale), r, w)
        else:
            S.add("act", lambda: nc.scalar.activation(out=out, in_=in_, func=func, bias=bias, scale=scale,
                                                      accum_out=accum), r, w)

    def engo(e):
        return nc.vector if e == "dve" else nc.gpsimd

    def ts(e, out, in0, s1, s2, op0, op1, r, w):
        if s2 is None:
            S.add(e, lambda: engo(e).tensor_scalar(out=out, in0=in0, scalar1=s1, scalar2=None, op0=op0), r, w)
        else:
            S.add(e, lambda: engo(e).tensor_scalar(out=out, in0=in0, scalar1=s1, scalar2=s2, op0=op0, op1=op1), r, w)

    def tt(e, out, in0, in1, op, r, w):
        S.add(e, lambda: engo(e).tensor_tensor(out=out, in0=in0, in1=in1, op=op), r, w)

    def stt(out, in0, scalar, in1, op0, op1, r, w):
        S.add("dve", lambda: nc.vector.scalar_tensor_tensor(out=out, in0=in0, scalar=scalar, in1=in1,
                                                            op0=op0, op1=op1), r, w)

    def cpy(e, out, in_, r, w):
        if e == "act":
            S.add("act", lambda: nc.scalar.copy(out=out, in_=in_), r, w)
        else:
            S.add(e, lambda: engo(e).tensor_copy(out=out, in_=in_), r, w)

    def memset(e, ap, val, w):
        S.add(e, lambda: engo(e).memset(ap, val), (), w)

    def red(out, in_, op, r, w):
        S.add("dve", lambda: nc.vector.tensor_reduce(out=out, in_=in_, axis=AX.X, op=op), r, w)

    def recip(out, in_, r, w):
        S.add("dve", lambda: nc.vector.reciprocal(out=out, in_=in_), r, w)

    def scan(out, d0, d1, init, op0, op1, r, w):
        S.add("dve", lambda: nc.vector.tensor_tensor_scan(out=out, data0=d0, data1=d1, initial=init,
                                                          op0=op0, op1=op1), r, w)

    def rstd_from_ss(stt_ap, kst, c_ss, c_tmp, c_out, n, T, inv):
        ts("dve", stt_ap[0:T, c_tmp:c_tmp + n], stt_ap[0:T, c_ss:c_ss + n], inv, EPS, ALU.mult, ALU.add,
           [kst], [kst])
        tt("pool", stt_ap[0:T, c_out:c_out + n], stt_ap[0:T, c_tmp:c_tmp + n], cm05[0:T, 0:n], ALU.pow,
           [kst, "cm05"], [kst])

    dma(cp[:], cpd, w=["cp"])
    cpy("dve", identb[:], cpc("ident"), ["cp"], ["identb"])
    identf = cpc("ident")
    dma(TH.aps[0][:, :], corrd, w=[("th", 0)])
    cpy("dve", corrb[:].rearrange("p h q -> p (h q)"), TH.aps[0][:, :], [("th", 0)], ["corrb"])
    dma(TH.aps[1][:, 0:256], lam4d, w=[("th", 1)])
    memset("pool", zerob[:], 0.0, ["zerob"])
    memset("pool", cm05[:], -0.5, ["cm05"])
    memset("pool", v_aug[:, :, :, 128:129], 1.0, ["v_ones"])
    memset("pool", vb_aug[:, :, :, 64:65], 1.0, ["vb_ones"])
    memset("pool", mv_aug[:, :, :, 64:65], 1.0, ["mv_ones"])
    ts("dve", rs8[:, 0:4], cpc("rows_pp", 0, 4), 0.5 * (1.0 - LAM_INIT), None, ALU.mult, None, ["cp"], ["rs8"])
    ts("dve", rs8[:, 4:6], cpc("rows_pp", 4, 6), 0.25, None, ALU.mult, None, ["cp"], ["rs8"])
    ts("dve", rs8[:, 6:8], cpc("rows_pp", 6, 8), 0.5, None, ALU.mult, None, ["cp"], ["rs8"])
    ts("dve", nbf[:], cpc("bf_pp"), -1.0, None, ALU.mult, None, ["cp"], ["nbf"])
    wv = w_in.rearrange("(k p) n -> p k n", p=128)
    wov = w_out.rearrange("(k p) n -> p k n", p=128)
    wkv = w_mk.rearrange("(k p) n -> p k n", p=128)
    wvv = w_mv.rearrange("(k p) n -> p k n", p=128)

    WST = Rot([XT.aps[0][:, 0:512], XT.aps[1][:, 0:512], ZF.aps[0], ZF.aps[1], KOUT.aps[0], KOUT.aps[1],
               KOUT.aps[2]], "wst")
    WSTK = [("xt", 0), ("xt", 1), ("zf", 0), ("zf", 1), ("kvout", 0), ("kvout", 1), ("kvout", 2)]

    def wload(dst, src, n, scal, key, extra=None):
        if n > 512:
            h_ = n // 2
            wload(dst[:, 0:h_], src[:, 0:h_], h_, scal, key, extra)
            wload(dst[:, h_:n], src[:, h_:n], n - h_, scal, key, extra)
            return
        i_ = WST.i % len(WST.aps)
        xt, _ = WST.next()
        kx = WSTK[i_]
        dma(xt[:, 0:n], src, w=[kx])
        if extra is None:
            ts("dve", dst, xt[:, 0:n], scal, None, ALU.mult, None, [kx, "cp", "rs8"], [key])
        else:
            ts("dve", dst, xt[:, 0:n], scal, extra, ALU.mult, ALU.mult, [kx, "cp", "rs8"], [key])

    def load_w_late():
        for kc in range(8):
            wload(w_out_bf[:, kc, :], wov[:, kc, :], 1024, rs8[:, kc:kc + 1], ("w_out", kc))
        for kc in range(8):
            wload(w_mkv_bf[:, kc, 0:256], wkv[:, kc, :], 256, cpc("gmem_pp", kc, kc + 1), ("w_mkv", kc, 0))
            wload(w_mkv_bf[:, kc, 256:512], wvv[:, kc, :], 256, cpc("gmem_pp", kc, kc + 1), ("w_mkv", kc, 1))

    for kc in range(8):
        g = cpc("gnorm_pp", kc, kc + 1)
        wload(w_in_bf[:, kc, 0:1024], wv[:, kc, 0:1024], 1024, g, ("w_in", kc, 0))
        wload(w_in_bf[:, kc, 1024:2048], wv[:, kc, 1024:2048], 1024, g, ("w_in", kc, 1))
        wload(w_in_bf[:, kc, 2048:2304], wv[:, kc, 2048:2304], 256, g, ("w_in", kc, 2))
        wload(w_in_bf[:, kc, 2304:2560], wv[:, kc, 2304:2560], 256, g, ("w_in", kc, 2), 0.125)
        wload(w_in_bf[:, kc, 2560:3072], wv[:, kc, 2560:3072], 512, g, ("w_in", kc, 2))
        wload(w_in_bf[:, kc, 3072:3840], wv[:, kc, 3080:3848], 768, g, ("w_in", kc, 3))
        wload(w_in_bf[:, kc, 3840:3848], wv[:, kc, 3072:3080], 8, g, ("w_in", kc, 3))
    if not DO_SAMPLE:
        load_w_late()
    W_IN_K = [("w_in", kc, i) for kc in range(8) for i in range(4)]
    W_OUT_K = [("w_out", kc) for kc in range(8)]
    W_MKV_K = [("w_mkv", kc, i) for kc in range(8) for i in range(2)]
    l4 = TH.aps[1]
    tt("dve", SQ[:, 0:64], l4[:, 0:64], l4[:, 64:128], ALU.mult, [("th", 1)], ["sq"])
    red(lamt[:, 0:1], SQ[:, 0:64], ALU.add, ["sq"], ["lamt"])
    tt("dve", SQ[:, 64:128], l4[:, 128:192], l4[:, 192:256], ALU.mult, [("th", 1)], ["sq"])
    red(lamt[:, 1:2], SQ[:, 64:128], ALU.add, ["sq"], ["lamt"])
    act(lamt[:, 2:4], lamt[:, 0:2], AF.Exp, ["lamt"], ["lamt"])
    stt(lamt[:, 4:5], lamt[:, 2:3], LAM_INIT, lamt[:, 3:4], ALU.add, ALU.subtract, ["lamt"], ["lamt"])
    ts("dve", lamt[:, 5:6], lamt[:, 4:5], -1.0, None, ALU.mult, None, ["lamt"], ["lamt"])

    def front_load(src_ap, T):
        xt, kx = XT.next()
        dma(xt[0:T, :], src_ap, w=[kx])
        return xt, kx

    def front_a1(xt, kx, T):
        sa, ks = STT.next()
        act(xn[0:T, :], xt[0:T, :], AF.Square, [kx], ["xn", ks], accum=sa[0:T, 0:1])
        rstd_from_ss(sa, ks, 0, 1, 2, 1, T, 1.0 / 1024)
        return sa, ks

    def front_a2(xt, kx, sa, ks, T):
        ts("dve", xn[0:T, :], xt[0:T, :], sa[0:T, 2:3], None, ALU.mult, None, [kx, ks], ["xn"])

    def front_a(xt, kx, T):
        sa, ks = front_a1(xt, kx, T)
        front_a2(xt, kx, sa, ks, T)
        return sa, ks

    def front_b(xt, kx, sa, ks, T):
        for kc in range(8):
            tr(TB[:, kc * 128:kc * 128 + T], xn[0:T, kc * 128:(kc + 1) * 128], identb[0:T, 0:T],
               ["xn", "identb"], [TBK])
        hT, kh = HT.next()
        cpy("act", hT[:, :, 0:T], TB.rearrange("p (k t) -> p k t", k=8)[:, :, 0:T], [TBK], [kh])
        return xt, kx, hT, kh, sa, ks

    def front_compute(xt, kx, T):
        sa, ks = front_a(xt, kx, T)
        return front_b(xt, kx, sa, ks, T)

    def front(src_ap, T):
        xt, kx = front_load(src_ap, T)
        return front_compute(xt, kx, T)

    def group_norm_rstd(src, ksrc, T, ng, sa, ks, base):
        tt("dve", SQ[0:T, 0:ng * 64], src, src, ALU.mult, [ksrc], ["sq"])
        red(sa[0:T, base:base + ng], SQ[0:T, 0:ng * 64].rearrange("p (g d) -> p g d", d=64), ALU.add,
            ["sq"], [ks])
        rstd_from_ss(sa, ks, base, base + ng, base + 2 * ng, ng, T, 1.0 / 64)
        return sa[0:T, base + 2 * ng:base + 3 * ng]

    def phase1_tile(fr, T, ti, j, k_dst, v_dst, later, hook=None):
        xt, kx, hT, kh, sa, ks = fr
        tok = slice(j * 128, j * 128 + T)
        ktok = slice(ti * 128, ti * 128 + T)
        for g in range(8):
            c0 = g * 512
            n = min(512, NIN - c0)
            ps, kp = mmnext()
            for kc in range(8):
                mm(ps[0:T, 0:n], hT[:, kc, 0:T], w_in_bf[:, kc, c0:c0 + n], kc == 0, kc == 7,
                   [kh] + W_IN_K, [kp])
            for it in later:
                it[0] -= 1
            ready = [it for it in later if it[0] <= 0]
            for it in ready:
                later.remove(it)
            for it in ready:
                it[1]()
            if hook is not None:
                hook(g)
            if g == 0 or g == 1:
                zf, kz = ZF.next()
                cpy("act", zf[0:T, :], ps[0:T, 0:512], [kp], [kz])
                r8 = group_norm_rstd(zf[0:T, :], kz, T, 8, sa, ks, 8 if g == 0 else 32)

                def b01(g=g, zf=zf, kz=kz, r8=r8):
                    nb, kn = NB.next()
                    if g == 0:
                        tt("pool", nb[0:T, :].rearrange("p (g d) -> p g d", d=64),
                           zf[0:T, :].rearrange("p (g d) -> p g d", d=64),
                           r8.unsqueeze(2).to_broadcast([T, 8, 64]), ALU.mult, [kz, ks], [kn])
                    else:
                        tt("dve", nb[0:T, :].rearrange("p (g d) -> p g d", d=64),
                           zf[0:T, :].rearrange("p (g d) -> p g d", d=64),
                           r8.unsqueeze(2).to_broadcast([T, 8, 64]), ALU.mult, [kz, ks], [kn])
                        ko, kk = KOUT.next()
                        tt("pool", ko[0:T, :].rearrange("p (g d) -> p g d", d=64),
                           zf[0:T, :].rearrange("p (g d) -> p g d", d=64),
                           r8.unsqueeze(2).to_broadcast([T, 8, 64]), ALU.mult, [kz, ks], [kk])
                        tt("pool", ko[0:T, :].rearrange("p (g d) -> p g d", d=64),
                           ko[0:T, :].rearrange("p (g d) -> p g d", d=64),
                           cpc("gka_rep")[0:T, :].unsqueeze(1).to_broadcast([T, 8, 64]), ALU.mult,
                           [kk, "cp"], [kk])
                        dma(k_dst, ko[0:T, :], r=[kk])

                    def d01(nb=nb, kn=kn):
                        for h in range(4):
                            tr(TB[:, h * 128:h * 128 + T], nb[0:T, h * 128:(h + 1) * 128], identb[0:T, 0:T],
                               [kn, "identb"], [TBK])
                        src = TB[:, 0:512].rearrange("p (h t) -> p h t", h=4)[:, :, 0:T]
                        if g == 0:
                            ts("dve", qT[:, :, tok], src, cpc("gqa_pp"), None, ALU.mult, None, [TBK, "cp"],
                               [("qT", j)])
                        else:
                            act(kT[:, :, ktok], src, AF.Copy, [TBK, "cp"], [("kT", ti)], scale=cpc("gka_pp"))
                    later.append([DELAY, d01])
                later.append([2, b01])
            elif g == 2:
                vo, kv = VOUT.next()
                cpy("act", vo[0:T, :], ps[0:T, 0:512], [kp], [kv])
                dma(v_dst, vo[0:T, :], r=[kv])
                cpy("dve", v_aug[0:T, ti, :, 0:128], vo[0:T, :].rearrange("p (h d) -> p h d", h=4),
                    [kv, "v_ones"], [("v", ti)])
            elif g == 3:
                th, kt = TH.next()
                act(th[0:T, :], ps[0:T, 0:512], AF.Tanh, [kp], [kt], scale=0.5)
                stt(mix[0:T, j, 0:512], th[0:T, :], 1.0, ps[0:T, 0:512], ALU.add, ALU.mult,
                    [kt, kp], [("mix", j, 0)])
            elif g == 4:
                cpy("act", QKB[0:T, j, :], ps[0:T, 0:512], [kp], [("QKB", j)])

                def d4():
                    for i in range(8):
                        tr(TB[0:64, i * 128:i * 128 + T], QKB[0:T, j, i * 64:(i + 1) * 64], identb[0:T, 0:T],
                           [("QKB", j), "identb"], [TBK])
                    cpy("dve", QKT[:, :, tok], TB[0:64, :].rearrange("p (h t) -> p h t", h=8)[:, :, 0:T],
                        [TBK], [("QKT", j)])
                later.append([DELAY, d4])
            elif g == 5:
                cpy("dve", vb_aug[0:T, j, :, 0:64], ps[0:T, 0:256].rearrange("p (h d) -> p h d", h=4),
                    [kp, "vb_ones"], [("vb", j)])
                act(OBG[0:T, j, :], ps[0:T, 256:512], AF.Tanh, [kp], [("OBG", j)], scale=0.5)
            elif g == 6:
                th, kt = TH.next()
                act(th[0:T, 0:256], ps[0:T, 0:256], AF.Tanh, [kp], [kt], scale=0.5)
                stt(th[0:T, 256:512], th[0:T, 0:256], 1.0, ps[0:T, 0:256], ALU.add, ALU.mult, [kt, kp], [kt])
                stt(mix[0:T, j, 512:768], OBG[0:T, j, :], 1.0, th[0:T, 256:512], ALU.add, ALU.mult,
                    [("OBG", j), kt], [("mix", j, 1)])
                zf, kz = ZF.next()
                cpy("act", zf[0:T, 0:256], ps[0:T, 256:512], [kp], [kz])
                r4 = group_norm_rstd(zf[0:T, 0:256], kz, T, 4, sa, ks, 56)

                def b6(zf=zf, kz=kz, r4=r4):
                    nb, kn = NB.next()
                    tt("pool", nb[0:T, 0:256].rearrange("p (g d) -> p g d", d=64),
                       zf[0:T, 0:256].rearrange("p (g d) -> p g d", d=64),
                       r4.unsqueeze(2).to_broadcast([T, 4, 64]), ALU.mult, [kz, ks], [kn])

                    def d6(nb=nb, kn=kn):
                        for hp in range(2):
                            tr(TB[:, hp * 128:hp * 128 + T], nb[0:T, hp * 128:(hp + 1) * 128], identb[0:T, 0:T],
                               [kn, "identb"], [TBK])
                        ts("dve", qmT[:, :, tok], TB[:, 0:256].rearrange("p (h t) -> p h t", h=2)[:, :, 0:T],
                           cpc("gqm_pp"), None, ALU.mult, None, [TBK, "cp"], [("qmT", j)])
                    later.append([DELAY, d6])
                later.append([2, b6])
            else:
                th, kt = TH.next()
                act(th[0:T, 0:256], ps[0:T, 0:256], AF.Tanh, [kp], [kt], scale=0.5)
                stt(mix[0:T, j, 768:1024], th[0:T, 0:256], 1.0, ps[0:T, 0:256], ALU.add, ALU.mult,
                    [kt, kp], [("mix", j, 2)])
                cpy("dve", GIF[0:T, j, :], ps[0:T, 256:264], [kp], [("GIF", j)])

    def phase1_block(tiles, T, pre=None):
        later = []
        if pre is None:
            xt, kx = front_load(tiles[0][0], T)
            fr = front_compute(xt, kx, T)
        else:
            fr = front_b(pre[0], pre[1], pre[2], pre[3], T)
        for idx, (src_ap, ti, j, k_dst, v_dst) in enumerate(tiles):
            nxt = {}
            hook = None
            if idx + 1 < len(tiles):
                nx = front_load(tiles[idx + 1][0], T)

                def hook(g, nx=nx, nxt=nxt):
                    if g == 0:
                        nxt["a"] = front_a1(nx[0], nx[1], T)
                    elif g == 2:
                        front_a2(nx[0], nx[1], nxt["a"][0], nxt["a"][1], T)
                    elif g == 5:
                        nxt["fr"] = front_b(nx[0], nx[1], nxt["a"][0], nxt["a"][1], T)
            phase1_tile(fr, T, ti, j, k_dst, v_dst, later, hook)
            fr = nxt.get("fr")
        while later:
            later.pop(0)[1]()

    def gates_block(T, ntile):
        NQ = ntile * 128
        ibT = TF[0:4, 0:NQ]
        fbT = TBf[0:4, 0:NQ]
        for j in range(ntile):
            tr(TF[0:4, j * 128:j * 128 + T], GIF[0:T, j, 0:4], identf[0:T, 0:T], [("GIF", j), "cp"], [TFK])
            tr(TBf[0:4, j * 128:j * 128 + T], GIF[0:T, j, 4:8], identf[0:T, 0:T], [("GIF", j), "cp"], [TBK])
        if T < 128:
            memset("dve", GA[:, 0:NQ], 0.0, [KGA])
        cs = slice(0, T) if ntile == 1 else slice(0, NQ)
        act(GA[:, cs], fbT[:, cs], AF.Exp, [TBK, "nbf"], [KGA], bias=nbf[0:4, 0:1], scale=-1.0)
        act(GA[:, cs], GA[:, cs], AF.Ln, [KGA], [KGA], bias=1.0)
        scan(GB[:, cs], zerob[0:4, cs], GA[:, cs], Bprev[:, 0:1], ALU.add, ALU.subtract,
             ["zerob", KGA, "Bprev"], [KGB])
        stt(GU[:, cs], ibT[:, cs], cpc("bi_pp")[0:4, :], GB[:, cs], ALU.add, ALU.subtract,
            [TFK, "cp", KGB], [KGU])
        if ntile == 1:
            red(UM[:, 0:1], GU[:, cs], ALU.max, [KGU], ["UM"])
        else:
            red(UM[:, 0:ntile], GU[:, cs].rearrange("p (c t) -> p c t", c=ntile), ALU.max, [KGU], ["UM"])
        scan(MUALL[:, 1:1 + ntile], UM[:, 0:ntile], UM[:, 0:ntile], MUALL[:, 0:1], ALU.max, ALU.max,
             ["UM", "MUALL"], ["MUALL"])
        tt("dve", C0[:, 0:ntile], MUALL[:, 0:ntile], MUALL[:, 1:1 + ntile], ALU.subtract, ["MUALL"], ["C0"])
        if ntile == 1:
            mub = MUALL[:, 1:2].to_broadcast([4, T])
            tt("dve", GW[:, cs], GU[:, cs], mub, ALU.subtract, [KGU, "MUALL"], [KGW])
            tt("dve", GE[:, cs], GB[:, cs], mub, ALU.add, [KGB, "MUALL"], [KGE])
        else:
            mub = MUALL[:, 1:1 + ntile].unsqueeze(2).to_broadcast([4, ntile, 128])
            tt("dve", GW[:, cs].rearrange("p (c t) -> p c t", c=ntile),
               GU[:, cs].rearrange("p (c t) -> p c t", c=ntile), mub, ALU.subtract, [KGU, "MUALL"], [KGW])
            tt("dve", GE[:, cs].rearrange("p (c t) -> p c t", c=ntile),
               GB[:, cs].rearrange("p (c t) -> p c t", c=ntile), mub, ALU.add, [KGB, "MUALL"], [KGE])
        return lambda: gates_part2(T, ntile, cs, NQ)

    def gates_part2(T, ntile, cs, NQ):
        act(C0[:, 0:ntile], C0[:, 0:ntile], AF.Exp, ["C0"], ["C0"])
        act(GW[:, cs], GW[:, cs], AF.Exp, [KGW], [KGW])
        act(GE[:, cs], GE[:, cs], AF.Exp, [KGE], [KGE], scale=-1.0)
        for j in range(ntile):
            tr(TF[0:T, j * 8:j * 8 + 4], GW[0:4, j * 128:j * 128 + T], identf[0:4, 0:4], [KGW, "cp"], [TFK])
            tr(TF[0:T, j * 8 + 4:j * 8 + 8], GE[0:4, j * 128:j * 128 + T], identf[0:4, 0:4], [KGE, "cp"], [TFK])
        cpy("dve", WE[0:T, 0:ntile, :], TF[0:T, 0:ntile * 8].rearrange("p (c e) -> p c e", e=8), [TFK], ["WE"])
        tt("dve", CD[:, 0:ntile, :], C0[:, 0:ntile].unsqueeze(2).to_broadcast([4, ntile, 4]),
           cpc("SEL2")[0:4, :].unsqueeze(1).to_broadcast([4, ntile, 4]), ALU.mult, ["C0", "cp"], ["CD"])
        mm(TBf[0:64, 0:ntile * 4], cpc("SEL")[0:4, 0:64], CD[:, 0:ntile, :].rearrange("p c e -> p (c e)"), True, True,
           ["cp", "CD"], [TBK])
        cpy("dve", C0B[:, 0:ntile, :], TBf[0:64, 0:ntile * 4].rearrange("p (c e) -> p c e", e=4), [TBK], ["C0B"])
        last = T - 1 if ntile == 1 else NQ - 1
        cpy("dve", Bprev[:, 0:1], GB[:, last:last + 1], [KGB], ["Bprev"])
        cpy("dve", MUALL[:, 0:1], MUALL[:, ntile:ntile + 1], ["MUALL"], ["MUALL"])

    def attention_block(T, ntile, ktiles, sample):
        NQ = ntile * T
        ABK = [4, 5, 6]
        qkeys = [("qT", jj) for jj in range(ntile)]
        steps = [(h, t, nk, dsub) for h in range(4) for (t, nk, dsub) in ktiles]
        state = {"n": 0}

        def emit_S(st_):
            h, t, nk, dsub = st_
            c0 = 0 if dsub is None else dsub * T
            ncol = NQ - c0
            b0 = 2 * (state["n"] % 2)
            state["n"] += 1
            kps = [("pb", b0), ("pb", b0 + 1)]
            for m in range(2):
                pr = slice(64 * m, 64 * m + 64)
                mm(PB[b0 + m][0:nk, 0:ncol], kT[pr, h, t * 128:t * 128 + nk], qT[pr, h, c0:NQ],
                   True, dsub is None, [("kT", t)] + qkeys, [kps[m]])
            if dsub is not None:
                for m in range(2):
                    mm(PB[b0 + m][0:nk, 0:T], identb[0:nk, 0:nk], corrb[0:nk, h, 0:T], False, True,
                       ["identb", "corrb"], [kps[m]])
            if KEEPWARM and not sample:
                mm(PB[7][:, :], zerob[:, 0:128], zerob[:, :], True, True, ["zerob"], [("pb", 7)])
            pt, kpt = PT.next()
            if sample or h != 0:
                calls = [(c0, NQ)]
            else:
                calls = [(max(c, c0), c + 256) for c in range(0, NQ, 256) if c + 256 > c0]
            pair = PBIG[0:nk, b0 * 512:(b0 + 2) * 512].rearrange("p (m c) -> p m c", m=2)
            for (ca, cb) in calls:
                if sample:
                    bias = cpc("ABS")[0:nk, h * 17 + t:h * 17 + t + 1]
                else:
                    tref = ktiles[-1][0] - (ntile - 1) + (cb - 1) // 128
                    bias = cpc("AB")[0:nk, h * 16 + (tref - t):h * 16 + (tref - t) + 1]
                act(pt[0:nk, :, ca - c0:cb - c0], pair[:, :, ca - c0:cb - c0], AF.Exp, kps + ["cp"], [kpt],
                    bias=bias, scale=0.125)
            return pt, kpt, c0

        def emit_PV(st_, pt, kpt, c0):
            h, t, nk, dsub = st_
            for m in range(2):
                for i in range(ntile):
                    if i * T < c0:
                        continue
                    a_ = m * 4 + i
                    bnk = ABK[a_ // 3]
                    o = (a_ % 3) * 129
                    mm(PB[bnk][0:T, o:o + 129], pt[0:nk, m, i * T - c0:(i + 1) * T - c0],
                       v_aug[0:nk, t, h, 0:129], False, True, [kpt, ("v", t), "v_ones"], [("pb", bnk)])

        def evac(h):
            bset = ABK
            sa, ks = STT.next()
            if ntile == 4:
                for b_ in range(3):
                    nacc = 3 if b_ < 2 else 2
                    recip(sa[0:T, 3 * b_:3 * b_ + nacc], PB[bset[b_]][0:T, 128:129 * nacc:129],
                          [("pb", bset[b_])], [ks])
                for b_ in range(3):
                    nacc = 3 if b_ < 2 else 2
                    cpy("dve", EV[0:T, 384 * b_:384 * b_ + 128 * nacc].rearrange("p (a d) -> p a d", d=128),
                        PB[bset[b_]][0:T, 0:129 * nacc].rearrange("p (a e) -> p a e", e=129)[:, :, 0:128],
                        [("pb", bset[b_])], ["ev"])
            else:
                recip(sa[0:T, 0:1], PB[bset[0]][0:T, 128:129], [("pb", bset[0])], [ks])
                recip(sa[0:T, 4:5], PB[bset[1]][0:T, 129 + 128:129 + 129], [("pb", bset[1])], [ks])
                cpy("dve", EV[0:T, 0:128], PB[bset[0]][0:T, 0:128], [("pb", bset[0])], ["ev"])
                cpy("dve", EV[0:T, 512:640], PB[bset[1]][0:T, 129:257], [("pb", bset[1])], ["ev"])
            tt("dve", sa[0:T, 4:4 + ntile], sa[0:T, 4:4 + ntile], lamt[0:T, 5:6].to_broadcast([T, ntile]),
               ALU.mult, [ks, "lamt"], [ks])
            for i in range(ntile):
                e1 = EV[0:T, i * 128:(i + 1) * 128]
                e2 = EV[0:T, (4 + i) * 128:(5 + i) * 128]
                ts("dve", e1, e1, sa[0:T, i:i + 1], None, ALU.mult, None, ["ev", ks], ["ev"])
                stt(e2, e2, sa[0:T, 4 + i:5 + i], e1, ALU.mult, ALU.add, ["ev", ks], ["ev"])
                S.add("dve", (lambda e1=e1, e2=e2, sa=sa, i=i: nc.vector.scalar_tensor_tensor(
                    out=e1, in0=e2, scalar=1.0, in1=e2, op0=ALU.mult, op1=ALU.mult,
                    accum_out=sa[0:T, 8 + i:9 + i])), ["ev"], ["ev", ks])
            ts("dve", sa[0:T, 12:12 + ntile], sa[0:T, 8:8 + ntile], 1.0 / 128, EPS, ALU.mult, ALU.add, [ks], [ks])
            tt("pool", sa[0:T, 16:16 + ntile], sa[0:T, 12:12 + ntile], cm05[0:T, 0:ntile], ALU.pow,
               [ks, "cm05"], [ks])

            def fin(h=h, sa=sa, ks=ks):
                for ii in range(ntile):
                    stt(mix[0:T, ii, h * 128:(h + 1) * 128], EV[0:T, (4 + ii) * 128:(5 + ii) * 128],
                        sa[0:T, 16 + ii:17 + ii], mix[0:T, ii, h * 128:(h + 1) * 128], ALU.mult, ALU.mult,
                        ["ev", ks, ("mix", ii, 0)], [("mix", ii, 0)])
            return fin

        prev = None
        fins = []
        for k, st_ in enumerate(steps):
            h = st_[0]
            new_head = (k == 0 or steps[k - 1][0] != h)
            pt, kpt, c0 = emit_S(st_)
            if prev is not None:
                emit_PV(*prev)
                if new_head:
                    fins.append(evac(prev[0][0]))
            if new_head:
                if NWARM and k > 0 and not sample:
                    for _ in range(NWARM):
                        mm(PB[7][:, :], zerob[:, 0:128], zerob[:, :], True, True, ["zerob"], [("pb", 7)])
                for b_ in range(3):
                    mm(PB[ABK[b_]][0:T, :], zerob[:, 0:T], zerob[:, :], True, True, ["zerob"], [("pb", ABK[b_])])
            if len(fins) > 0 and not new_head and (k == 0 or steps[k - 2][0] == h):
                for f_ in fins:
                    f_()
                del fins[:]
            prev = (st_, pt, kpt, c0)
        emit_PV(*prev)
        fins.append(evac(prev[0][0]))
        for f_ in fins:
            f_()

    def mem_block(T, ntile):
        NQ = ntile * T
        banks = [(ACC[0], ACCK[0]), (ACC[1], ACCK[1]), (ACC[2], ACCK[2]), (TF, TFK)]
        for i in range(ntile):
            mm(banks[i][0][0:T, :], zerob[:, 0:T], zerob[:, :], True, True, ["zerob"], [banks[i][1]])
        pendm = []
        for h in range(4):
            pr = slice(64 * (h % 2), 64 * (h % 2) + 64)
            for nt in range(2):
                ps, kp = mmnext()
                mm(ps[:, 0:NQ], mkT[pr, h // 2, nt * 128:(nt + 1) * 128], qmT[pr, h // 2, 0:NQ], True, True,
                   ["mkT"] + [("qmT", jj) for jj in range(ntile)], [kp])
                pt, kpt = PT.next()
                act(pt[:, 0, 0:NQ], ps[:, 0:NQ], AF.Exp, [kp], [kpt], scale=0.125)
                pendm.append((h, nt, pt, kpt))
                if len(pendm) > 2:
                    ph, pnt, ppt, pkpt = pendm.pop(0)
                    for i in range(ntile):
                        mm(banks[i][0][0:T, ph * 65:(ph + 1) * 65], ppt[:, 0, i * T:(i + 1) * T],
                           mv_aug[:, pnt, ph, 0:65], False, True, [pkpt, "mv", "mv_ones"], [banks[i][1]])
        while pendm:
            ph, pnt, ppt, pkpt = pendm.pop(0)
            for i in range(ntile):
                mm(banks[i][0][0:T, ph * 65:(ph + 1) * 65], ppt[:, 0, i * T:(i + 1) * T],
                   mv_aug[:, pnt, ph, 0:65], False, True, [pkpt, "mv", "mv_ones"], [banks[i][1]])
        for i in range(ntile):
            bk, kb = banks[i]
            sa, ks = STT.next()
            recip(sa[0:T, 0:4], bk[0:T, 64:260:65], [kb], [ks])
            tt("dve", OM[0:T, :, :], bk[0:T, 0:260].rearrange("p (h e) -> p h e", e=65)[:, :, 0:64],
               sa[0:T, 0:4].unsqueeze(2).to_broadcast([T, 4, 64]), ALU.mult, [kb, ks], ["HN"])
            tt("dve", mix[0:T, i, 768:1024], OM[0:T, :, :].rearrange("p h d -> p (h d)"),
               mix[0:T, i, 768:1024], ALU.mult, ["HN", ("mix", i, 2)], [("mix", i, 2)])

    def mlstm_block(T, ntile):
        pend_a = []
        pend_b = []
        for j in range(ntile):
            tt("pool", VW[j][0:T, :, 0:65], vb_aug[0:T, j, :, 0:65],
               WE[0:T, j, 0:4].unsqueeze(2).to_broadcast([T, 4, 65]), ALU.mult,
               [("vb", j), "vb_ones", "WE"], [("vw", j)])
        for j in range(ntile):
            tok = slice(j * 128, j * 128 + T)
            vw = VW[j]
            kvw = ("vw", j)
            tt("dve", Sd[:, :, 0:65], Sst[:, :, 0:65], C0B[:, j, :].unsqueeze(2).to_broadcast([64, 4, 65]), ALU.mult,
               ["Sst", "C0B"], ["Sd"])
            cpy("act", Cs[:, :, 0:65], Sd[:, :, 0:65], ["Sd"], ["Cs"])
            psA, kA = mmnext()
            for h in range(4):
                mm(psA[0:T, h * 128:h * 128 + T], QKT[:, 4 + h, tok], QKT[:, h, tok], True, True,
                   [("QKT", j)], [kA])
            psS, kS = ACC[2], ACCK[2]
            for h in range(4):
                mm(psS[0:64, h * 65:(h + 1) * 65], QKB[0:T, j, 256 + h * 64:256 + (h + 1) * 64],
                   vw[0:T, h, 0:65], True, True, [("QKB", j), kvw], [kS])
            tt("dve", Sst[:, :, 0:65], Sd[:, :, 0:65], psS[0:64, 0:260].rearrange("p (c e) -> p c e", e=65), ALU.add,
               ["Sd", kS], ["Sst"])
            tt("dve", AT[0:T, :, 0:T], psA[0:T, :].rearrange("p (h t) -> p h t", h=4)[:, :, 0:T],
               cpc("maskU")[0:T, 0:T].unsqueeze(1).to_broadcast([T, 4, T]), ALU.mult, [kA, "cp"], ["AT"])
            psO, kO = ACC[j % 2], ACCK[j % 2]
            for h in range(4):
                mm(psO[0:T, h * 65:(h + 1) * 65], AT[0:T, h, 0:T], vw[0:T, h, 0:65], True, False, ["AT", kvw], [kO])
                mm(psO[0:T, h * 65:(h + 1) * 65], QKT[:, h, tok], Cs[:, h, 0:65], False, True,
                   [("QKT", j), "Cs"], [kO])

            def evac_a(j=j, psO=psO, kO=kO):
                sa, ks = STT.next()
                cpy("dve", sa[0:T, 20:24], psO[0:T, 64:260:65], [kO], [ks])
                stt(sa[0:T, 24:28], sa[0:T, 20:24], -1.0, sa[0:T, 20:24], ALU.mult, ALU.max, [ks], [ks])
                tt("dve", sa[0:T, 0:4], sa[0:T, 24:28], WE[0:T, j, 4:8], ALU.max, [ks, "WE"], [ks])
                recip(sa[0:T, 4:8], sa[0:T, 0:4], [ks], [ks])
                tt("dve", HN[0:T, :, :], psO[0:T, 0:260].rearrange("p (h e) -> p h e", e=65)[:, :, 0:64],
                   sa[0:T, 4:8].unsqueeze(2).to_broadcast([T, 4, 64]), ALU.mult, [kO, ks], ["HN"])
                r4 = group_norm_rstd(HN[0:T, :, :].rearrange("p h d -> p (h d)"), "HN", T, 4, sa, ks, 8)

                def evac_b():
                    tt("dve", HN[0:T, :, :], HN[0:T, :, :], r4.unsqueeze(2).to_broadcast([T, 4, 64]), ALU.mult,
                       ["HN", ks], ["HN"])
                    tt("dve", mix[0:T, j, 512:768], HN[0:T, :, :].rearrange("p h d -> p (h d)"),
                       mix[0:T, j, 512:768], ALU.mult, ["HN", ("mix", j, 1)], [("mix", j, 1)])
                return evac_b
            pend_a.append(evac_a)
            if len(pend_a) == 2:
                for f_ in pend_b:
                    f_()
                del pend_b[:]
                pend_b.append(pend_a.pop(0)())
        for f_ in pend_b:
            f_()
        del pend_b[:]
        while pend_a:
            pend_a.pop(0)()()

    def phase3_block(tiles, T):
        YB = [[(PB[3], ("pb", 3)), (PB[4], ("pb", 4))], [(PB[5], ("pb", 5)), (PB[0], ("pb", 0))]]
        xts = {}

        def stage_t(idx):
            src_ap, dst_ap, j = tiles[idx]
            mT, kmT = HT.next()
            for kc in range(8):
                tr(TB[:, kc * 128:kc * 128 + T], mix[0:T, j, kc * 128:(kc + 1) * 128], identb[0:T, 0:T],
                   [("mix", j, 0), ("mix", j, 1), ("mix", j, 2), "identb"], [TBK])
            cpy("act", mT[:, :, 0:T], TB.rearrange("p (k t) -> p k t", k=8)[:, :, 0:T], [TBK], [kmT])
            xt, kx = XT.next()
            dma(xt[0:T, :], src_ap, w=[kx])
            xts[idx] = (mT, kmT, xt, kx)

        def stage_m(idx):
            src_ap, dst_ap, j = tiles[idx]
            mT, kmT, xt, kx = xts[idx]
            for half in range(2):
                ps, kp = YB[idx % 2][half]
                for kc in range(8):
                    mm(ps[0:T, :], mT[:, kc, 0:T], w_out_bf[:, kc, half * 512:(half + 1) * 512], kc == 0, kc == 7,
                       [kmT] + W_OUT_K, [kp])
                tt("dve", xt[0:T, half * 512:(half + 1) * 512], ps[0:T, :], xt[0:T, half * 512:(half + 1) * 512],
                   ALU.add, [kp, kx], [kx])
            dma(dst_ap, xt[0:T, :], r=[kx])

        stage_t(0)
        for idx in range(len(tiles)):
            if idx + 1 < len(tiles):
                stage_t(idx + 1)
            stage_m(idx)

    def memkv_seq(s):
        for nt in range(2):
            xt, kx, hT, kh, sa, ks = front(memp[s, nt * 128:(nt + 1) * 128, :], 128)
            ps, kp = mmnext()
            for kc in range(8):
                mm(ps[:, :], hT[:, kc, :], w_mkv_bf[:, kc, :], kc == 0, kc == 7, [kh] + W_MKV_K, [kp])
            zf, kz = ZF.next()
            cpy("act", zf[:, :], ps[:, :], [kp], [kz])
            dma(pmv[s, nt * 128:(nt + 1) * 128, :], zf[:, 256:512], r=[kz])
            cpy("pool", mv_aug[:, nt, :, 0:64], zf[:, 256:512].rearrange("p (h d) -> p h d", h=4),
                [kz, "mv_ones"], ["mv"])
            r4 = group_norm_rstd(zf[:, 0:256], kz, 128, 4, sa, ks, 8)
            tt("dve", zf[:, 0:256].rearrange("p (g d) -> p g d", d=64),
               zf[:, 0:256].rearrange("p (g d) -> p g d", d=64),
               r4.unsqueeze(2).to_broadcast([128, 4, 64]), ALU.mult, [kz, ks], [kz])
            ko, kk = KOUT.next()
            tt("dve", ko[:, 0:256].rearrange("p (g d) -> p g d", d=64),
               zf[:, 0:256].rearrange("p (g d) -> p g d", d=64),
               cpc("gkm_rep").unsqueeze(1).to_broadcast([128, 4, 64]), ALU.mult, [kz, "cp"], [kk])
            dma(pmk[s, nt * 128:(nt + 1) * 128, :], ko[:, 0:256], r=[kk])
            nb, kn = NB.next()
            cpy("pool", nb[:, 0:256], ko[:, 0:256], [kk], [kn])
            for hp in range(2):
                tr(TB[:, hp * 128:(hp + 1) * 128], nb[:, hp * 128:(hp + 1) * 128], identb[:, :],
                   [kn, "identb"], [TBK])
            cpy("act", mkT[:, :, nt * 128:(nt + 1) * 128], TB[:, 0:256].rearrange("p (h t) -> p h t", h=2),
                [TBK], ["mkT"])

    def state_out(dC, dn, dm):
        for h in range(4):
            dma(dC[h], Sst[:, h, 0:64], r=["Sst"])
            dma(dn[h].unsqueeze(1), Sst[:, h, 64:65], r=["Sst"])
        tt("dve", mfin[:, :], Bprev[:, 0:1], MUALL[:, 0:1], ALU.add, ["Bprev", "MUALL"], ["mfin"])
        dma(dm, mfin[:, :], r=["mfin"])

    try:
        chk(1)
        if DO_SAMPLE:
            for t in range(16):
                i1 = WST.i % len(WST.aps)
                xk, _ = WST.next()
                i2 = WST.i % len(WST.aps)
                xv, _ = WST.next()
                kx, kx2 = WSTK[i1], WSTK[i2]
                dma(xk[:, 0:512], ck[t * 128:(t + 1) * 128, :], w=[kx])
                dma(xv[:, 0:512], cv[t * 128:(t + 1) * 128, :], w=[kx2])
                nb, kn = NB.next()
                cpy("dve", nb[:, :], xk[:, 0:512], [kx], [kn])
                cpy("dve" if t % 2 == 0 else "pool", v_aug[:, t, :, 0:128],
                    xv[:, 0:512].rearrange("p (h d) -> p h d", h=4), [kx2, "v_ones"], [("v", t)])
                for h in range(4):
                    tr(TB[:, h * 128:(h + 1) * 128], nb[:, h * 128:(h + 1) * 128], identb[:, :], [kn, "identb"], [TBK])
                cpy("act", kT[:, :, t * 128:(t + 1) * 128], TB[:, 0:512].rearrange("p (h t) -> p h t", h=4),
                    [TBK], [("kT", t)])
            for nt in range(2):
                xt, kx = XT.next()
                dma(xt[:, 0:256], cmk[nt * 128:(nt + 1) * 128, :], w=[kx])
                dma(xt[:, 256:512], cmv[nt * 128:(nt + 1) * 128, :], w=[kx])
                nb, kn = NB.next()
                cpy("dve", nb[:, 0:256], xt[:, 0:256], [kx], [kn])
                cpy("pool", mv_aug[:, nt, :, 0:64], xt[:, 256:512].rearrange("p (h d) -> p h d", h=4),
                    [kx, "mv_ones"], ["mv"])
                for hp in range(2):
                    tr(TB[:, hp * 128:(hp + 1) * 128], nb[:, hp * 128:(hp + 1) * 128], identb[:, :],
                       [kn, "identb"], [TBK])
                cpy("act", mkT[:, :, nt * 128:(nt + 1) * 128], TB[:, 0:256].rearrange("p (h t) -> p h t", h=2),
                    [TBK], ["mkT"])
            chk(2)
            load_w_late()
            for h in range(4):
                dma(Sst[:, h, 0:64], sC[h], w=["Sst"])
                dma(Sst[:, h, 64:65], sn[h].unsqueeze(1), w=["Sst"])
            memset("dve", Bprev[:, :], 0.0, ["Bprev"])
            cpy("dve", MUALL[:, 0:1], cpc("sm_pp")[0:4, :], ["cp"], ["MUALL"])
            chk(3)
            phase1_block([(xs[:, :], 16, 0, sk[:, :], sv[:, :])], 16)
            chk(4)
            g2 = gates_block(16, 1)
            chk(5)
            attention_block(16, 1, [(t, 128, None) for t in range(16)] + [(16, 16, 0)], True)
            g2()
            chk(6)
            mem_block(16, 1)
            chk(7)
            mlstm_block(16, 1)
            chk(8)
            phase3_block([(xs[:, :], ys[:, :], 0)], 16)
            chk(9)
            state_out(sCo, sno, smo.rearrange("o h -> h o"))

        pre = None
        for s in range(NSEQ):
            if s == 0:
                memkv_seq(s)
            memset("dve", Sst[:, :, :], 0.0, ["Sst"])
            memset("dve", Bprev[:, :], 0.0, ["Bprev"])
            memset("dve", MUALL[:, 0:1], 0.0, ["MUALL"])
            for b in range(NBLK):
                tl = []
                for j in range(4):
                    ti = 4 * b + j
                    rows = slice(ti * 128, (ti + 1) * 128)
                    tl.append((xp[s, rows, :], ti, j, pk[s, rows, :], pv[s, rows, :]))
                phase1_block(tl, 128, pre)
                pre = None
                g2 = gates_block(128, 4)
                ktiles = [(t, 128, None) for t in range(4 * b)] + [(4 * b + i, 128, i) for i in range(4)]
                attention_block(128, 4, ktiles, False)
                mem_block(128, 4)
                g2()
                if b + 1 == NBLK and s + 1 < NSEQ:
                    memkv_seq(s + 1)
                mlstm_block(128, 4)
                nsrc = None
                if b + 1 < NBLK:
                    nsrc = xp[s, (4 * b + 4) * 128:(4 * b + 5) * 128, :]
                elif s + 1 < NSEQ:
                    nsrc = xp[s + 1, 0:128, :]
                if nsrc is not None:
                    dma(EV[:, :], nsrc, w=["ev"])
                    sa_, ks_ = front_a(EV[:, :], "ev", 128)
                    pre = (EV[:, :], "ev", sa_, ks_)
                tl3 = []
                for j in range(4):
                    ti = 4 * b + j
                    rows = slice(ti * 128, (ti + 1) * 128)
                    tl3.append((xp[s, rows, :], yp[s, rows, :], j))
                phase3_block(tl3, 128)
            state_out(pC[s], pn[s], pm[s].unsqueeze(1))

    except _Stop:
        pass

    print('sbuf bytes remaining', nc.sbuf_bytes_remaining)
    S.emit(st)
    st.close()
    return nc


_NC_CACHE = {}


def _get_nc(key=(4, 4, True)):
    if key not in _NC_CACHE:
        _NC_CACHE[key] = build_program(*key)
    return _NC_CACHE[key]


def _in_maps(inp, NSEQ=4):
    maps = []
    c32 = lambda a: np.ascontiguousarray(a, dtype=np.float32)
    for c in range(8):
        m = {
            "xp": c32(inp["x_prompt"][4 * c:4 * c + max(NSEQ, 1)]),
            "memp": c32(inp["mem_prompt"][4 * c:4 * c + max(NSEQ, 1)]),
            "xs": c32(inp["x_sample"][c]),
            "ck": c32(inp["cache_attn_k"][0, c].reshape(2048, 512)),
            "cv": c32(inp["cache_attn_v"][0, c].reshape(2048, 512)),
            "sC": c32(inp["state_mlstm_C"][0, c]),
            "sn": c32(inp["state_mlstm_n"][0, c]),
            "cmk": c32(inp["cache_mem_k"][0, c].reshape(256, 256)),
            "cmv": c32(inp["cache_mem_v"][0, c].reshape(256, 256)),
            "w_in": c32(inp["w_in"][0]),
            "w_out": c32(inp["w_out"][0]),
            "w_mk": c32(inp["w_mk"][0]),
            "w_mv": c32(inp["w_mv"][0]),
        }
        cpa, extra = _make_cp(inp, c)
        m["cp"] = cpa
        m.update(extra)
        maps.append(m)
    return maps


def kernel(**inputs):
    inp = {k: np.asarray(v) for k, v in inputs.items()}
    nc = _get_nc()
    res = run_bass_kernel_spmd(nc, _in_maps(inp), core_ids=list(range(8)))
    R = res.results
    cat = lambda name: np.concatenate([np.asarray(r[name]) for r in R], axis=0)
    stk = lambda name: np.stack([np.asarray(r[name]) for r in R], axis=0)
    y_prompt = cat("yp")
    y_sample = stk("ys")
    p_attn_k = cat("pk").reshape(1, 32, 2048, 4, 128)
    p_attn_v = cat("pv").reshape(1, 32, 2048, 4, 128)
    p_C = cat("pC").reshape(1, 32, 4, 64, 64)
    p_n = cat("pn").reshape(1, 32, 4, 64)
    p_m = cat("pm").reshape(1, 32, 4)
    p_mk = cat("pmk").reshape(1, 32, 256, 4, 64)
    p_mv = cat("pmv").reshape(1, 32, 256, 4, 64)
    s_k = stk("sk").reshape(1, 8, 16, 4, 128)
    s_v = stk("sv").reshape(1, 8, 16, 4, 128)
    s_C = stk("sCo").reshape(1, 8, 4, 64, 64)
    s_n = stk("sno").reshape(1, 8, 4, 64)
    s_m = stk("smo").reshape(1, 8, 4)
    return (y_prompt, y_sample, p_attn_k, p_attn_v, p_C, p_n, p_m, p_mk, p_mv,
            s_k, s_v, s_C, s_n, s_m)
```
